# Optimizing a Trainium2 kernel written in Bass

```python
import math
import jax, jax.numpy as jnp
from jax import lax
import numpy as np

D_MODEL = 1024
BATCH = 8
SEQ = 2048
DEPTH = 1
DEC_BATCH = 128
DEC_SEQ = 1
PAST_LEN = 16384
PAGE_SIZE = 128

HEAD_A = 64
C_A = D_MODEL // 2
H_A = C_A // HEAD_A
LORA_W = 64
LORA_A = 64
LORA_G = 128
A_COLS = 3 * C_A + LORA_W + LORA_A + LORA_G
DK = 128
DV = 128
H_B = (D_MODEL // 2) // DV
C_BK = H_B * DK
C_BV = H_B * DV
CONV_W = 4
CONV_CH = 2 * C_BK + C_BV
B_COLS = CONV_CH + C_BV + 2 * H_B
GDN_CHUNK = 64
IN_COLS = A_COLS + B_COLS + 2 * D_MODEL
D_FF = -(-8 * D_MODEL // (3 * 256)) * 256
PLE_DIM = 256
NORM_EPS = 1e-6
GN_EPS = 64e-5

kernel_name = "rwkv7_gdn_gated_hybrid_step"


def rmsnorm(x, gain):
    xf = x.astype(jnp.float32)
    y = xf * lax.rsqrt(jnp.mean(xf * xf, axis=-1, keepdims=True) + NORM_EPS)
    return (y * gain.astype(jnp.float32)).astype(x.dtype)


def l2norm(x):
    return x * lax.rsqrt(jnp.sum(x * x, axis=-1, keepdims=True) + 1e-6)


def rwkv7_scan(r, decay, k, v, a_in, b_in, s0):
    def step(S, inp):
        r_t, w_t, k_t, v_t, a_t, b_t = inp
        sa = jnp.einsum('bhvk,bhk->bhv', S, a_t)
        S = S * w_t[:, :, None, :] + sa[..., None] * b_t[:, :, None, :] + v_t[..., None] * k_t[:, :, None, :]
        return S, jnp.einsum('bhvk,bhk->bhv', S, r_t)
    xs = tuple(jnp.moveaxis(t, 1, 0) for t in (r, decay, k, v, a_in, b_in))
    S, ys = lax.scan(step, s0, xs)
    return jnp.moveaxis(ys, 0, 1), S


def gated_delta_chunked(q, k, v, g, beta, s0):
    Bn, T, H, _ = q.shape
    V = v.shape[-1]
    C = min(GDN_CHUNK, T)
    n = -(-T // C)
    pad = n * C - T

    def blocks(t):
        t = jnp.pad(t, [(0, 0), (0, pad)] + [(0, 0)] * (t.ndim - 2))
        t = t.reshape((Bn, n, C) + t.shape[2:])
        return jnp.moveaxis(jnp.moveaxis(t, 1, 0), 3, 2)

    q, k, v, g, beta = (blocks(t) for t in (q, k, v, g, beta))
    gc = jnp.cumsum(g, axis=-1)
    tril = jnp.tril(jnp.ones((C, C), bool))
    strict = jnp.tril(jnp.ones((C, C), bool), -1)
    diff = gc[..., :, None] - gc[..., None, :]
    decay = jnp.where(tril, jnp.exp(jnp.where(tril, diff, 0.0)), 0.0)
    kb = k * beta[..., None]
    vb = v * beta[..., None]
    lmat = jnp.where(strict, jnp.einsum('nbhik,nbhjk->nbhij', kb, k) * decay, 0.0)
    eye = jnp.eye(C, dtype=lmat.dtype)
    tinv = lax.linalg.triangular_solve(eye + lmat, jnp.broadcast_to(eye, lmat.shape),
                                       left_side=True, lower=True, unit_diagonal=True)
    u = tinv @ vb
    w = tinv @ (kb * jnp.exp(gc)[..., None])
    qk = jnp.where(tril, jnp.einsum('nbhik,nbhjk->nbhij', q, k) * decay, 0.0)
    qg = q * jnp.exp(gc)[..., None]
    kd = k * jnp.exp(gc[..., -1:] - gc)[..., None]
    glast = jnp.exp(gc[..., -1])

    def step(S, inp):
        u_c, w_c, qk_c, qg_c, kd_c, gl_c = inp
        v_new = u_c - w_c @ S
        o = qg_c @ S + qk_c @ v_new
        S = S * gl_c[..., None, None] + jnp.einsum('bhck,bhcv->bhkv', kd_c, v_new)
        return S, o

    S, o = lax.scan(step, s0, (u, w, qk, qg, kd, glast))
    o = jnp.swapaxes(jnp.moveaxis(o, 0, 1), 2, 3).reshape(Bn, n * C, H, V)[:, :T]
    return o, S


def hybrid_layer(h, pe, shift0, wkv0, conv0, gdn0,
                 norm_mix, w_in, mu_shift, rw_w0, rw_w2, rw_a0, rw_a2, rw_g2, rw_kk, rw_ka, rw_rk,
                 rw_ln_w, rw_ln_b, gdn_conv, gdn_a_log, gdn_dt_bias, gdn_norm,
                 w_branch_a, w_branch_b, w_out, norm_ffn, w_ffn_gate, w_ffn_up, w_ffn_down,
                 norm_ple, w_ple_gate, w_ple_proj):
    f32 = jnp.float32
    Bn, T, _ = h.shape
    act = h.dtype
    u = rmsnorm(h, norm_mix)
    proj = u @ w_in
    pa = proj[..., :A_COLS]
    pb = proj[..., A_COLS:A_COLS + B_COLS]
    pg = proj[..., A_COLS + B_COLS:]

    prev = jnp.concatenate([shift0.astype(act), pa[:, :-1]], axis=1)
    xa = (pa + (prev - pa) * mu_shift).astype(f32)
    new_shift = pa[:, -1:].astype(shift0.dtype)
    r, k, v, xw, xaa, xg = jnp.split(
        xa, [C_A, 2 * C_A, 3 * C_A, 3 * C_A + LORA_W, 3 * C_A + LORA_W + LORA_A], axis=-1)
    w_log = -jax.nn.softplus(-(rw_w0 + jnp.tanh(xw) @ rw_w2)) - 0.5
    decay = jnp.exp(-jnp.exp(w_log))
    a = jax.nn.sigmoid(rw_a0 + xaa @ rw_a2)
    g = jax.nn.sigmoid(xg) @ rw_g2
    heads = lambda t: t.reshape(Bn, T, H_A, HEAD_A)
    kk = l2norm(heads(k * rw_kk))
    k = k * (1.0 + (a - 1.0) * rw_ka)
    rh, kh, vh = heads(r), heads(k), heads(v)
    y, wkv = rwkv7_scan(rh, heads(decay), kh, vh, -kk, kk * heads(a), wkv0.astype(f32))
    mean = jnp.mean(y, axis=-1, keepdims=True)
    var = jnp.mean(jnp.square(y - mean), axis=-1, keepdims=True)
    yn = ((y - mean) * lax.rsqrt(var + GN_EPS)).reshape(Bn, T, C_A) * rw_ln_w + rw_ln_b
    bonus = (jnp.sum(rh * kh * rw_rk, axis=-1, keepdims=True) * vh).reshape(Bn, T, C_A)
    o_a = (yn + bonus) * g

    qkv = pb[..., :CONV_CH].astype(f32)
    z = pb[..., CONV_CH:CONV_CH + C_BV].astype(f32)
    b_raw = pb[..., CONV_CH + C_BV:CONV_CH + C_BV + H_B].astype(f32)
    a_raw = pb[..., CONV_CH + C_BV + H_B:].astype(f32)
    xp = jnp.concatenate([conv0.astype(f32), qkv], axis=1)
    acc = xp[:, :T] * gdn_conv[0]
    for j in range(1, CONV_W):
        acc = acc + xp[:, j:j + T] * gdn_conv[j]
    c = jax.nn.silu(acc)
    new_conv = xp[:, T:].astype(conv0.dtype)
    qb = l2norm(c[..., :C_BK].reshape(Bn, T, H_B, DK)) * (DK ** -0.5)
    kb_ = l2norm(c[..., C_BK:2 * C_BK].reshape(Bn, T, H_B, DK))
    vb_ = c[..., 2 * C_BK:].reshape(Bn, T, H_B, DV)
    beta = jax.nn.sigmoid(b_raw)
    glog = -jnp.exp(gdn_a_log.astype(f32)) * jax.nn.softplus(a_raw + gdn_dt_bias)
    ob, gdn = gated_delta_chunked(qb, kb_, vb_, glog, beta, gdn0.astype(f32))
    ob = ob * lax.rsqrt(jnp.mean(ob * ob, axis=-1, keepdims=True) + NORM_EPS) * gdn_norm
    o_b = (ob * jax.nn.silu(z.reshape(Bn, T, H_B, DV))).reshape(Bn, T, C_BV)

    gate_a = jax.nn.sigmoid(pg[..., :D_MODEL])
    gate_b = jax.nn.sigmoid(pg[..., D_MODEL:])
    mix = gate_a * (o_a.astype(act) @ w_branch_a) + gate_b * (o_b.astype(act) @ w_branch_b)
    h = h + mix @ w_out

    u2 = rmsnorm(h, norm_ffn)
    h = h + (jax.nn.silu(u2 @ w_ffn_gate) * (u2 @ w_ffn_up)) @ w_ffn_down

    h = h + jax.nn.sigmoid(rmsnorm(h, norm_ple) @ w_ple_gate) * (pe.astype(act) @ w_ple_proj)
    return h, new_shift, wkv.astype(wkv0.dtype), new_conv, gdn.astype(gdn0.dtype)


def setup_inputs(seed: int = 0) -> dict:
    key = jax.random.key(seed)
    ks = iter(jax.random.split(key, 48))
    nrm = lambda shape, s: jax.random.normal(next(ks), shape, jnp.float32) * s
    unif = lambda shape, lo, hi: jax.random.uniform(next(ks), shape, jnp.float32, lo, hi)
    L = DEPTH
    dt_init = jnp.exp(unif((L, H_B), math.log(1e-3), math.log(1e-1)))
    return {
        'x_prompt': nrm((BATCH, SEQ, D_MODEL), 1.0),
        'x_sample': nrm((DEC_BATCH, DEC_SEQ, D_MODEL), 1.0),
        'p_prompt': nrm((L, BATCH, SEQ, PLE_DIM), 1.0),
        'p_sample': nrm((L, DEC_BATCH, DEC_SEQ, PLE_DIM), 1.0),
        'state_shift': nrm((L, DEC_BATCH, 1, A_COLS), 1.0),
        'state_wkv': nrm((L, DEC_BATCH, H_A, HEAD_A, HEAD_A), 0.5),
        'state_conv': nrm((L, DEC_BATCH, CONV_W - 1, CONV_CH), 1.0),
        'state_gdn': nrm((L, DEC_BATCH, H_B, DK, DV), 0.5),
        'norm_mix': 1.0 + nrm((L, D_MODEL), 0.01),
        'w_in': nrm((L, D_MODEL, IN_COLS), D_MODEL ** -0.5),
        'mu_shift': unif((L, A_COLS), 0.0, 1.0),
        'rw_w0': unif((L, C_A), -6.0, -1.0),
        'rw_w2': nrm((L, LORA_W, C_A), 0.1),
        'rw_a0': nrm((L, C_A), 0.1),
        'rw_a2': nrm((L, LORA_A, C_A), 0.5 * LORA_A ** -0.5),
        'rw_g2': nrm((L, LORA_G, C_A), LORA_G ** -0.5),
        'rw_kk': 0.85 + nrm((L, C_A), 0.02),
        'rw_ka': 1.0 + nrm((L, C_A), 0.02),
        'rw_rk': nrm((L, H_A, HEAD_A), 0.1),
        'rw_ln_w': 1.0 + nrm((L, C_A), 0.01),
        'rw_ln_b': nrm((L, C_A), 0.01),
        'gdn_conv': nrm((L, CONV_W, CONV_CH), 0.5),
        'gdn_a_log': jnp.log(unif((L, H_B), 1.0, 16.0)),
        'gdn_dt_bias': dt_init + jnp.log(-jnp.expm1(-dt_init)),
        'gdn_norm': 1.0 + nrm((L, DV), 0.01),
        'w_branch_a': nrm((L, C_A, D_MODEL), C_A ** -0.5),
        'w_branch_b': nrm((L, C_BV, D_MODEL), C_BV ** -0.5),
        'w_out': nrm((L, D_MODEL, D_MODEL), D_MODEL ** -0.5),
        'norm_ffn': 1.0 + nrm((L, D_MODEL), 0.01),
        'w_ffn_gate': nrm((L, D_MODEL, D_FF), D_MODEL ** -0.5),
        'w_ffn_up': nrm((L, D_MODEL, D_FF), D_MODEL ** -0.5),
        'w_ffn_down': nrm((L, D_FF, D_MODEL), D_FF ** -0.5),
        'norm_ple': 1.0 + nrm((L, D_MODEL), 0.01),
        'w_ple_gate': nrm((L, D_MODEL, D_MODEL), D_MODEL ** -0.5),
        'w_ple_proj': nrm((L, PLE_DIM, D_MODEL), PLE_DIM ** -0.5),
        'norm_final': 1.0 + nrm((D_MODEL,), 0.01),
    }


def reference(x_prompt, x_sample, p_prompt, p_sample, state_shift, state_wkv, state_conv, state_gdn,
              norm_mix, w_in, mu_shift, rw_w0, rw_w2, rw_a0, rw_a2, rw_g2, rw_kk, rw_ka, rw_rk,
              rw_ln_w, rw_ln_b, gdn_conv, gdn_a_log, gdn_dt_bias, gdn_norm,
              w_branch_a, w_branch_b, w_out, norm_ffn, w_ffn_gate, w_ffn_up, w_ffn_down,
              norm_ple, w_ple_gate, w_ple_proj, norm_final):
    hp, hs = x_prompt, x_sample
    bp = x_prompt.shape[0]
    st_p, st_s = [], []
    for i in range(DEPTH):
        lw = (norm_mix[i], w_in[i], mu_shift[i], rw_w0[i], rw_w2[i], rw_a0[i], rw_a2[i], rw_g2[i],
              rw_kk[i], rw_ka[i], rw_rk[i], rw_ln_w[i], rw_ln_b[i], gdn_conv[i], gdn_a_log[i],
              gdn_dt_bias[i], gdn_norm[i], w_branch_a[i], w_branch_b[i], w_out[i], norm_ffn[i],
              w_ffn_gate[i], w_ffn_up[i], w_ffn_down[i], norm_ple[i], w_ple_gate[i], w_ple_proj[i])
        z_shift = jnp.zeros((bp, 1, A_COLS), state_shift.dtype)
        z_wkv = jnp.zeros((bp, H_A, HEAD_A, HEAD_A), state_wkv.dtype)
        z_conv = jnp.zeros((bp, CONV_W - 1, CONV_CH), state_conv.dtype)
        z_gdn = jnp.zeros((bp, H_B, DK, DV), state_gdn.dtype)
        hp, *sp = hybrid_layer(hp, p_prompt[i], z_shift, z_wkv, z_conv, z_gdn, *lw)
        hs, *ss = hybrid_layer(hs, p_sample[i], state_shift[i], state_wkv[i], state_conv[i], state_gdn[i], *lw)
        st_p.append(sp)
        st_s.append(ss)
    y_prompt = rmsnorm(hp, norm_final)
    y_sample = rmsnorm(hs, norm_final)
    shift_prompt = jnp.stack([s[0] for s in st_p])
    wkv_prompt = jnp.stack([s[1] for s in st_p])
    conv_prompt = jnp.stack([s[2] for s in st_p])
    gdn_prompt = jnp.stack([s[3] for s in st_p])
    shift_sample = jnp.stack([s[0] for s in st_s])
    wkv_sample = jnp.stack([s[1] for s in st_s])
    conv_sample = jnp.stack([s[2] for s in st_s])
    gdn_sample = jnp.stack([s[3] for s in st_s])
    return (y_prompt, y_sample, shift_prompt, wkv_prompt, conv_prompt, gdn_prompt,
            shift_sample, wkv_sample, conv_sample, gdn_sample)
```

```python
import contextlib
import numpy as np
import concourse.bass as bass
import concourse.mybir as mybir
from concourse.bass_utils import run_bass_kernel_spmd
from concourse.alu_op_type import AluOpType as ALU

F32 = mybir.dt.float32
BF16 = mybir.dt.bfloat16
AF = mybir.ActivationFunctionType

D = 1024
NS = 16
NP = 256
TCN = NP // 128
CW = 5896
DFF = 2816
C0 = float(np.exp(-0.5))


class _Stop(Exception):
    pass


STOP = [None]
NSLOT = [2]
ONLY = ['rg']
OFFS = [30]


def CKP(name):
    if STOP[0] == name:
        raise _Stop()


class Buf:
    def __init__(self, t):
        self.t = t
        self.w = None
        self.r = {}


class KB:
    def __init__(self, nc, es):
        self.nc = nc
        self.es = es
        self.eng = {'pe': nc.tensor, 'dve': nc.vector, 'act': nc.scalar, 'pool': nc.gpsimd, 'sp': nc.sync}
        self.semh = {}
        self.cnt = {}
        self.seen = {e: {} for e in self.eng}
        for e in self.eng:
            self.semh[e] = es.enter_context(nc.semaphore("s_" + e))
            self.cnt[e] = 0
        self.ndma = 20
        self.dcur = 0
        for i in range(self.ndma):
            k = "d%d" % i
            self.semh[k] = es.enter_context(nc.semaphore("s_" + k))
            self.cnt[k] = 0
        self.nbuf = 0
        self.out_events = []

    def sb(self, shape, dt=F32, name=None):
        self.nbuf += 1
        t = self.es.enter_context(self.nc.sbuf_tensor(name or ("b%d" % self.nbuf), list(shape), dt))
        return Buf(t)

    def psb(self, name):
        t = self.es.enter_context(self.nc.psum_tensor(name, [128, 512], F32))
        return Buf(t)

    def _wait(self, e, deps):
        engine = self.eng[e]
        for (s, v) in deps:
            if self.seen[e].get(s, 0) >= v:
                continue
            engine.wait_ge(self.semh[s], v)
            self.seen[e][s] = v

    def _deps(self, e, reads, writes):
        deps = []
        for b in reads:
            if b.w is not None:
                if not (e == 'pe' and b.w[0] == 'pe'):
                    deps.append(b.w)
        for b in writes:
            if b.w is not None and b.w[0] != e:
                deps.append(b.w)
            for s, v in b.r.items():
                if s != e:
                    deps.append((s, v))
        return deps

    def _record(self, ev, reads, writes):
        for b in reads:
            b.r[ev[0]] = max(b.r.get(ev[0], 0), ev[1])
        for b in writes:
            b.w = ev
            b.r = {}

    def op(self, e, fn, reads=(), writes=(), inc=True):
        self._wait(e, self._deps(e, reads, writes))
        ins = fn(self.eng[e])
        if inc:
            self.cnt[e] += 1
            ins.then_inc(self.semh[e], 1)
            ev = (e, self.cnt[e])
        else:
            ev = (e, self.cnt[e] + 1)
        self._record(ev, reads, writes)
        return ev

    def dma(self, q, out, in_, reads=(), writes=(), is_out=False):
        k = "d%d" % self.dcur
        self.dcur = (self.dcur + 1) % self.ndma
        deps = self._deps(q, reads, writes)
        if self.cnt[k] > 0:
            deps.append((k, self.cnt[k]))
        self._wait(q, deps)
        self.cnt[k] += 16
        with self.nc.allow_non_contiguous_dma(reason="layout"):
            self.eng[q].dma_start(out=out, in_=in_).then_inc(self.semh[k], 16)
        ev = (k, self.cnt[k])
        self._record(ev, reads, writes)
        if is_out:
            self.out_events.append(ev)
        return ev

    def barrier(self):
        evs = []
        for e in self.eng:
            if self.cnt[e] > 0:
                evs.append((e, self.cnt[e]))
        for i in range(self.ndma):
            k = "d%d" % i
            if self.cnt[k] > 0:
                evs.append((k, self.cnt[k]))
        for e in self.eng:
            self._wait(e, [ev for ev in evs if ev[0] != e])


def build(n_ptiles, dbg=()):
    TP = n_ptiles * NP
    nc = bass.Bass("TRN2", target_bir_lowering=False)

    def din(name, shape):
        return nc.dram_tensor(name, list(shape), F32, kind="ExternalInput").ap()

    def dout(name, shape):
        return nc.dram_tensor(name, list(shape), F32, kind="ExternalOutput").ap()

    xP = din("xP", [TP, D]); xS = din("xS", [NS, D])
    pP = din("pP", [TP, 256]); pS = din("pS", [NS, 256])
    stShift = din("stShift", [NS, 1792]); stWkv = din("stWkv", [NS, 8, 64, 64])
    stConv = din("stConv", [NS, 3, 1536]); stGdn = din("stGdn", [NS, 4, 128, 128])
    w_in = din("w_in", [D, CW]); w_barep = din("w_barep", [D, 1024])
    c64 = din("c64", [64, 80]); c128 = din("c128", [128, 64]); convw = din("convw", [128, 48])
    w2 = din("w2", [64, 512]); a2 = din("a2", [64, 512]); g2 = din("g2", [128, 512])
    w_bra = din("w_bra", [512, D]); w_brb = din("w_brb", [512, D]); w_out = din("w_out", [D, D])
    w_fg = din("w_fg", [D, DFF]); w_fu = din("w_fu", [D, DFF]); w_fd = din("w_fd", [DFF, D])
    w_pg = din("w_pg", [D, D]); w_pp = din("w_pp", [256, D])
    gfin = din("gfin", [128, D]); cmat = din("cmat", [128, 1024]); cmat2 = din("cmat2", [128, 192]); c128rd = din("c128r", [128, 64])
    yP = dout("yP", [TP, D]); yS = dout("yS", [NS, D])
    oShiftP = dout("oShiftP", [1, 1792]); oWkvP = dout("oWkvP", [1, 8, 64, 64])
    oConvP = dout("oConvP", [1, 3, 1536]); oGdnP = dout("oGdnP", [1, 4, 128, 128])
    oShiftS = dout("oShiftS", [NS, 1792]); oWkvS = dout("oWkvS", [NS, 8, 64, 64])
    oConvS = dout("oConvS", [NS, 3, 1536]); oGdnS = dout("oGdnS", [NS, 4, 128, 128])
    dbg_outs = {}
    for nm, shp in dbg:
        dbg_outs[nm] = nc.dram_tensor("dbg_" + nm, list(shp), BF16 if nm in ("oa", "ob", "mix") else F32, kind="ExternalOutput").ap()

    es = contextlib.ExitStack()
    with es:
        kb = KB(nc, es)
        op = kb.op
        cm = kb.sb([128, 1024]); kb.dma('sp', cm.t[:], cmat, writes=[cm])
        ident = cm.t[:, 0:128]
        maskS = cm.t[0:64, 128:192]
        maskI = cm.t[0:64, 192:256]
        negI = cm.t[0:64, 256:320]
        ones = cm.t[:, 320:448]
        scanm = cm.t[:, 448:960]
        maskS2 = cm.t[:, 128:192]; maskI2 = cm.t[:, 192:256]; maskL2 = cm.t[:, 960:1024]
        c64b = kb.sb([64, 80]); kb.dma('sp', c64b.t[:], c64, writes=[c64b])
        c128b = kb.sb([128, 64]); kb.dma('sp', c128b.t[:], c128, writes=[c128b])
        cvw = kb.sb([128, 48]); kb.dma('sp', cvw.t[:], convw, writes=[cvw])
        w2b = kb.sb([64, 512], BF16); kb.dma('pool', w2b.t[:], w2, writes=[w2b])
        a2b = kb.sb([64, 512], BF16); kb.dma('pool', a2b.t[:], a2, writes=[a2b])
        g2b = kb.sb([128, 512], BF16); kb.dma('pool', g2b.t[:], g2, writes=[g2b])
        gfb = kb.sb([128, D]); kb.dma('sp', gfb.t[:], gfin, writes=[gfb])
        MU_RKV, MU_XW, MU_XAA, W0, A0, KKc, KAc, RKc, LNW = 0, 24, 25, 26, 34, 42, 50, 58, 66
        MU_XG, GMIX, GFFN, GPLE, GDNN, ALOG, DTB, LNB = 0, 1, 9, 17, 25, 26, 30, 40
        c128r = kb.sb([128, 64]); kb.dma('sp', c128r.t[:], c128rd, writes=[c128r])
        cm2 = kb.sb([128, 192]); kb.dma('sp', cm2.t[:], cmat2, writes=[cm2])
        bones = cm2.t[:, 0:128]
        cm16 = kb.sb([128, 128], BF16)
        op('dve', lambda e: e.tensor_copy(out=cm16.t[:], in_=cm.t[:, 0:128]), [cm], [cm16])
        id16 = cm16.t[:, :]
        on16 = kb.sb([128, 256], BF16)
        op('dve', lambda e: e.tensor_copy(out=on16.t[:, 0:128], in_=cm.t[:, 320:448]), [cm], [on16])
        op('dve', lambda e: e.tensor_copy(out=on16.t[:, 128:256], in_=cm2.t[:, 0:128]), [cm2], [on16])
        ones16 = on16.t[:, 0:128]; bones16 = on16.t[:, 128:256]
        ident2 = cm2.t[:, 128:192]
        W0r, A0r, KKr, KAr, RKr, LNWr, LNBr = 12, 16, 20, 24, 28, 32, 36
        omu128r = kb.sb([128, 12])
        op('dve', lambda e: e.tensor_scalar(out=omu128r.t[:], in0=c128r.t[:, 0:12], scalar1=-1.0, scalar2=1.0, op0=ALU.mult, op1=ALU.add), [c128r], [omu128r])
        omu64 = kb.sb([64, 26])
        op('dve', lambda e: e.tensor_scalar(out=omu64.t[:], in0=c64b.t[:, 0:26], scalar1=-1.0, scalar2=1.0, op0=ALU.mult, op1=ALU.add), [c64b], [omu64])
        omu128 = kb.sb([128, 1])
        op('dve', lambda e: e.tensor_scalar(out=omu128.t[:], in0=c128b.t[:, 0:1], scalar1=-1.0, scalar2=1.0, op0=ALU.mult, op1=ALU.add), [c128b], [omu128])
        nexpA = kb.sb([128, 4])
        op('act', lambda e: e.activation(out=nexpA.t[:], in_=c128b.t[:, ALOG:ALOG + 4], func=AF.Exp), [c128b], [nexpA])
        op('dve', lambda e: e.tensor_scalar(out=nexpA.t[:], in0=nexpA.t[:], scalar1=-1.0, scalar2=None, op0=ALU.mult), [nexpA], [nexpA])

        ps = [kb.psb("ps%d" % i) for i in range(8)]
        pcur = [0]

        def PS():
            b = ps[pcur[0]]
            pcur[0] = (pcur[0] + 1) % 8
            return b
        pscur = [0, 0]

        def PSslot(slot):
            b = ps[slot * 3 + pscur[slot]]
            pscur[slot] = (pscur[slot] + 1) % 3
            return b

        NW = 5
        wring = [kb.sb([128, 8, 256], BF16, "wr%d" % i) for i in range(NW)]
        wcur = [0]

        def wload(w, r0, nk, pk, c0, ncols):
            b = wring[wcur[0]]
            wcur[0] = (wcur[0] + 1) % NW
            src = w[r0:r0 + nk * pk, c0:c0 + ncols].rearrange("(k p) c -> p k c", p=pk)
            kb.dma('pool', b.t[0:pk, 0:nk, 0:ncols], src, writes=[b])
            return b

        conv = {}

        def convert(name, src, R, Cc):
            dst = nc.dram_tensor("cw_" + name, [R, Cc], BF16, kind="Internal").ap()
            b = Buf(dst)
            for c0 in range(0, Cc, 1024):
                w_ = min(1024, Cc - c0)
                kb.dma('pool', dst[:, c0:c0 + w_], src[:, c0:c0 + w_], writes=[b])
            conv[name] = (dst, b)

        def wloadc(name, r0, nk, pk, c0, ncols):
            dst_, cb_ = conv[name]
            b = wring[wcur[0]]
            wcur[0] = (wcur[0] + 1) % NW
            src = dst_[r0:r0 + nk * pk, c0:c0 + ncols].rearrange("(k p) c -> p k c", p=pk)
            kb.dma('pool', b.t[0:pk, 0:nk, 0:ncols], src, reads=[cb_], writes=[b])
            return b

        RW = 11500
        region = kb.sb([128, RW], name="region")
        bumpP = [0]; bumpS = [0]

        def rview(bump, width, pat=None, parts=128, **kw):
            a_ = bump[0]; bump[0] += width
            assert bump[0] <= RW, (bump[0], RW)
            ap = region.t[0:parts, a_:a_ + width]
            if pat is not None:
                ap = ap.rearrange(pat, **kw)
            return Buf(ap)

        xtok = kb.sb([128, TCN, D], name="xtok")
        xs = kb.sb([128, D], name="xs")
        uT = kb.sb([128, 8, NP], BF16, name="uT")
        ss = kb.sb([128, 8], name="ss")
        car_rkv = kb.sb([128, 12], name="car_rkv"); car_xw = kb.sb([64, 2], name="car_xw"); car_xg = kb.sb([128, 2], name="car_xg")
        car_cv = kb.sb([128, 12, 4], name="car_cv")
        for b_ in (car_rkv, car_xw, car_xg, car_cv):
            op('dve', lambda e, b_=b_: e.memset(b_.t[:], 0.0), [], [b_])
        stR = [kb.sb([128, 64], name="stR%d" % i) for i in range(4)]
        stG = [kb.sb([128, 128], name="stG%d" % i) for i in range(4)]
        for b_ in stR + stG:
            op('dve', lambda e, b_=b_: e.memset(b_.t[:], 0.0), [], [b_])
        stRs = rview(bumpS, NS * 64, "p (s v) -> p s v", v=64)
        stGs = rview(bumpS, NS * 128, "p (s v) -> p s v", v=128)
        tw = kb.sb([64, NP], BF16, name="tw"); xaaS = kb.sb([64, NP], BF16, name="xaaS"); sgS = kb.sb([128, NP], BF16, name="sgS")
        oa = kb.sb([128, 4, NP], BF16, name="oa"); ob = kb.sb([128, 4, NP], BF16, name="ob")
        mixT = kb.sb([128, 8, NP], BF16, name="mixT")
        hfT = kb.sb([128, 22, NP], BF16, name="hfT")
        peT = kb.sb([128, 2, NP], BF16, name="peT")
        ptok = kb.sb([128, TCN, 256], name="ptok")
        prevS = rview(bumpS, 16 * NS, "p (g s) -> p g s", s=NS)
        rawS = rview(bumpS, 16 * NS, "p (g s) -> p g s", s=NS)
        histS = rview(bumpS, 36 * NS, "p (g s) -> p g s", s=NS)
        rawC = rview(bumpS, 12 * NS, "p (g s) -> p g s", s=NS)
        tokS = rview(bumpS, 1792, parts=NS)
        tokC = rview(bumpS, 1536, parts=NS)
        NHB = 7
        hb = [[kb.sb([128, NP + 4], name="hb%d" % i) for i in range(NHB)], [rview(bumpP, NP + 4) for i in range(6)]]
        hcur = [0, 0]

        def HB(slot=0):
            r_ = hb[slot]
            b = r_[hcur[slot]]
            hcur[slot] = (hcur[slot] + 1) % len(r_)
            return b
        named = {}

        def NB(name, width=NP + 4, slot=0):
            key = (name, slot)
            if key not in named:
                if slot == 0:
                    named[key] = kb.sb([128, width], name="n_" + name)
                elif slot == 1:
                    named[key] = rview(bumpP, width)
                else:
                    named[key] = rview(bumpS, width)
            return named[key]

        def act(out, in_, func, reads, writes, bias=None, scale=None, **kw):
            kws = dict(kw)
            if bias is not None:
                kws['bias'] = bias
            if scale is not None:
                kws['scale'] = scale
            return op('act', lambda e: e.activation(out=out, in_=in_, func=func, **kws), reads, writes)

        def tt(out, in0, in1, o, reads, writes, eng='dve'):
            return op(eng, lambda e: e.tensor_tensor(out=out, in0=in0, in1=in1, op=o), reads, writes)

        def ts(out, in0, s1, s2, o0, o1, reads, writes, eng='dve'):
            if o1 is None:
                return op(eng, lambda e: e.tensor_scalar(out=out, in0=in0, scalar1=s1, scalar2=None, op0=o0), reads, writes)
            return op(eng, lambda e: e.tensor_scalar(out=out, in0=in0, scalar1=s1, scalar2=s2, op0=o0, op1=o1), reads, writes)

        def stt(out, in0, sc, in1, o0, o1, reads, writes):
            return op('dve', lambda e: e.scalar_tensor_tensor(out=out, in0=in0, scalar=sc, in1=in1, op0=o0, op1=o1), reads, writes)

        def mm(pb, out, lhsT, rhs, reads, start, stop):
            return op('pe', lambda e: e.matmul(out, lhsT, rhs, start=start, stop=stop), reads, [pb], inc=stop)

        def tr(pb, out, in_, rows, reads, inc=True):
            return op('pe', lambda e: e.transpose(out, in_, ident[0:rows, 0:rows]), list(reads) + [cm], [pb], inc=inc)

        def rsqrt(out, in_, scale, eps, reads, writes):
            act(out, in_, AF.Ln, reads, writes, bias=eps, scale=scale)
            act(out, out, AF.Exp, writes, writes, scale=-0.5)

        def dbgdump(name, ap_sb, bufs):
            if name in dbg_outs:
                kb.dma('sp', dbg_outs[name], ap_sb, reads=bufs, is_out=True)

        def norm_T(N, rows_list, gcol):
            TC = len(rows_list)
            for tc, rows in enumerate(rows_list):
                act(xs.t[:rows, :], xtok.t[:rows, tc, :], AF.Square, [xtok], [xs, ss], accum_out=ss.t[:rows, tc:tc + 1])
            mr = max(rows_list)
            rsqrt(ss.t[:mr, 4:4 + TC], ss.t[:mr, 0:TC], 1.0 / D, 1e-6, [ss], [ss])
            for kh in range(2):
                pbs = [PS() for _ in range(4)]
                for tc, rows in enumerate(rows_list):
                    ts(xs.t[:rows, :], xtok.t[:rows, tc, :], ss.t[:rows, 4 + tc:5 + tc], None, ALU.mult, None, [xtok, ss], [xs])
                    for k4 in range(4):
                        k = kh * 4 + k4
                        tr(pbs[k4], pbs[k4].t[:, tc * 128:tc * 128 + rows], xs.t[:rows, k * 128:(k + 1) * 128], rows, [xs])
                for k4 in range(4):
                    k = kh * 4 + k4
                    if k % 2 == 0:
                        ts(uT.t[:, k, :N], pbs[k4].t[:, :N], c128b.t[:, gcol + k:gcol + k + 1], None, ALU.mult, None, [pbs[k4], c128b], [uT])
                    else:
                        act(uT.t[:, k, :N], pbs[k4].t[:, :N], AF.Copy, [pbs[k4], c128b], [uT], scale=c128b.t[:, gcol + k:gcol + k + 1])

        def proj_fm(pb, out_ap, wt, coff, M, N, nk=8, rhsb=None, pk=128):
            rb = rhsb or uT
            for k in range(nk):
                mm(pb, out_ap, wt.t[0:pk, k, coff:coff + M], rb.t[0:pk, k, :N], [wt, rb], k == 0, k == nk - 1)

        def shift(P, N, pb, psap, mu, omu, carry, out_b, out_ap, is_s, prev_ap=None, raw_ap=None, raw_b=None, tmp=HB):
            t1 = tmp()
            if is_s:
                act(t1.t[:P, :N], prev_ap, AF.Copy, [prevS], [t1], scale=mu)
                act(raw_ap, psap, AF.Copy, [pb], [raw_b])
            else:
                act(t1.t[:P, 1:N], psap[:, 0:N - 1], AF.Copy, [pb], [t1], scale=mu)
                act(t1.t[:P, 0:1], carry[1], AF.Copy, [carry[0]], [t1], scale=mu)
                act(carry[1], psap[:, N - 1:N], AF.Copy, [pb], [carry[0]])
            stt(out_ap, psap, omu, t1.t[:P, :N], ALU.mult, ALU.add, [pb, t1], [out_b])

        named16 = {}

        def NB16(name, width=NP + 4, slot=0):
            key = (name, slot)
            if key not in named16:
                if slot == 0:
                    named16[key] = kb.sb([128, width], BF16, name="h_" + name)
                else:
                    a_ = bumpP[0]; bumpP[0] += (width + 1) // 2
                    assert bumpP[0] <= RW
                    named16[key] = Buf(region.t[:, a_:a_ + (width + 1) // 2].bitcast(BF16))
            return named16[key]
        hb16 = [[NB16("r16_%d" % i, 132, sl_) for i in range(4)] for sl_ in (0, 1)]
        hbw = [[NB16("w16_%d" % i, NP + 4, sl_) for i in range(2)] for sl_ in (0, 1)]
        hbwcur = [0, 0]

        def HBW(slot=0):
            b = hbw[slot][hbwcur[slot]]
            hbwcur[slot] = (hbwcur[slot] + 1) % 2
            return b
        h16cur = [0, 0]

        def HB16(slot):
            b = hb16[slot][h16cur[slot]]
            h16cur[slot] = (h16cur[slot] + 1) % 4
            return b

        natS = rview(bumpS, NS * 128, "p (s h k) -> p s h k", parts=64, h=2, k=64)

        def scan(slot, Kd, Vd, N, C, nseq, QsT, RsT, PbT, KbT, VT, A, WcB, wc_ap, st_b, st_ap, ypb):
            nch = N // C
            cps = nch // nseq
            W = nch * C

            def tmaj(srcb, Fd, name):
                dst = NB16(name, 512, slot=slot)
                per = 512 // Fd
                for c0 in range(0, nch, per):
                    pb = PSslot(slot)
                    n = min(per, nch - c0)
                    for c in range(c0, c0 + n):
                        tr(pb, pb.t[0:C, (c - c0) * Fd:(c - c0 + 1) * Fd], srcb.t[0:Fd, c * C:(c + 1) * C], Fd, [srcb], inc=(c == c0 + n - 1))
                    act(dst.t[0:C, c0 * Fd:(c0 + n) * Fd], pb.t[0:C, 0:n * Fd], AF.Copy, [pb], [dst])
                return dst
            pre_t = (nch * max(Kd, Vd) <= 512)
            if pre_t:
                pbt = tmaj(PbT, Kd, 'Pb'); kbt = tmaj(KbT, Kd, 'Kb'); vt = tmaj(VT, Vd, 'Vt')
                gP = lambda c: pbt.t[0:C, c * Kd:(c + 1) * Kd]
                gK = lambda c: kbt.t[0:C, c * Kd:(c + 1) * Kd]
                gV = lambda c: vt.t[0:C, c * Vd:(c + 1) * Vd]
            yield
            TinvT = None
            if C > 1:
                NTb = A['NT']; NT16 = A['NT16']
                Xb = NB16('X0', slot=slot); pb = PSslot(slot)
                for c in range(nch):
                    tr(pb, pb.t[0:C, c * C:(c + 1) * C], NTb.t[0:C, c * C:(c + 1) * C], C, [NTb], inc=(c == nch - 1))
                act(Xb.t[0:C, 0:W], pb.t[0:C, 0:W], AF.Copy, [pb], [Xb])
                XTb = NT16
                PT = NB('PT0', slot=slot)
                tt(PT.t[0:C, 0:W].rearrange("p (c i) -> p c i", i=C), NTb.t[0:C, 0:W].rearrange("p (c i) -> p c i", i=C),
                   cm.t[0:C, 0:C].unsqueeze(1).broadcast_to([C, nch, C]), ALU.add, [NTb, cm], [PT])
                PTh = NB16('PTh0', slot=slot)
                act(PTh.t[0:C, 0:W], PT.t[0:C, 0:W], AF.Copy, [PT], [PTh])
                yield
                nlev = {64: 5, 2: 0}[C]
                for lv in range(1, nlev + 1):
                    pbx = PSslot(slot)
                    for c in range(nch):
                        sl = slice(c * C, (c + 1) * C)
                        op('pe', lambda e, sl=sl, pbx=pbx, XTb=XTb, Xb=Xb: e.matmul(pbx.t[0:C, sl], XTb.t[0:C, sl], Xb.t[0:C, sl], start=True, stop=True), [XTb, Xb], [pbx], inc=(c == nch - 1))
                    Xn = NB16('X%d' % (lv % 2), slot=slot)
                    XTn = None
                    if lv < nlev:
                        pbt_ = PSslot(slot)
                        for c in range(nch):
                            sl = slice(c * C, (c + 1) * C)
                            op('pe', lambda e, sl=sl, pbt_=pbt_, XTb=XTb, Xb=Xb: e.matmul(pbt_.t[0:C, sl], Xb.t[0:C, sl], XTb.t[0:C, sl], start=True, stop=True), [XTb, Xb], [pbt_], inc=(c == nch - 1))
                    act(Xn.t[0:C, 0:W], pbx.t[0:C, 0:W], AF.Copy, [pbx], [Xn])
                    yield
                    if lv < nlev:
                        XTn = NB16('XT%d' % (lv % 2), slot=slot)
                        ts(XTn.t[0:C, 0:W], pbt_.t[0:C, 0:W], 1.0, None, ALU.mult, None, [pbt_], [XTn])
                    pbp = PSslot(slot)
                    for c in range(nch):
                        sl = slice(c * C, (c + 1) * C)
                        op('pe', lambda e, sl=sl, pbp=pbp, Xn=Xn, PTh=PTh: e.matmul(pbp.t[0:C, sl], Xn.t[0:C, sl], PTh.t[0:C, sl], start=True, stop=True), [Xn, PTh], [pbp], inc=(c == nch - 1))
                    PTn = NB('PT%d' % (lv % 2), slot=slot)
                    tt(PTn.t[0:C, 0:W], PT.t[0:C, 0:W], pbp.t[0:C, 0:W], ALU.add, [PT, pbp], [PTn])
                    PThn = NB16('PTh%d' % (lv % 2), slot=slot)
                    act(PThn.t[0:C, 0:W], PTn.t[0:C, 0:W], AF.Copy, [PTn], [PThn])
                    PT = PTn; PTh = PThn
                    yield
                    Xb = Xn
                    if lv < nlev:
                        XTb = XTn
                PT = PTh
                TinvT = PT
            for s in range(nseq):
                for cc in range(cps):
                    c = s * cps + cc
                    sl = slice(c * C, (c + 1) * C)
                    stap = st_ap(s)
                    if pre_t:
                        aP = gP(c); aK = gK(c); aV = gV(c)
                    else:
                        pb = PSslot(slot)
                        tr(pb, pb.t[0:C, 0:Kd], PbT.t[0:Kd, sl], Kd, [PbT], inc=False)
                        tr(pb, pb.t[0:C, 128:128 + Kd], KbT.t[0:Kd, sl], Kd, [KbT], inc=False)
                        tr(pb, pb.t[0:C, 256:256 + Vd], VT.t[0:Vd, sl], Vd, [VT])
                        row = HB(slot)
                        act(row.t[0:C, 0:256], pb.t[0:C, 0:256], AF.Copy, [pb], [row])
                        rowv = HB(slot)
                        act(rowv.t[0:C, 0:128], pb.t[0:C, 256:384], AF.Copy, [pb], [rowv])
                        aP = row.t[0:C, 0:Kd]; aK = row.t[0:C, 128:128 + Kd]; aV = rowv.t[0:C, 0:Vd]
                        pbt = row; kbt = row; vt = rowv
                    zp = PSslot(slot)
                    if C > 1:
                        op('pe', lambda e: e.matmul(zp.t[0:C, 0:Vd], QsT.t[0:Kd, sl], stap, start=True, stop=False), [QsT, st_b], [zp], inc=False)
                        op('pe', lambda e: e.matmul(zp.t[0:C, 0:Vd], A['LakT'].t[0:C, sl], aV, start=False, stop=True), [A['LakT'], vt], [zp])
                        yield
                        Zs = HB16(slot)
                        act(Zs.t[0:C, 0:Vd], zp.t[0:C, 0:Vd], AF.Copy, [zp], [Zs])
                        up = PSslot(slot)
                        op('pe', lambda e: e.matmul(up.t[0:C, 0:Vd], TinvT.t[0:C, sl], Zs.t[0:C, 0:Vd], start=True, stop=True), [TinvT, Zs], [up])
                        yield
                        U = HB16(slot)
                        ts(U.t[0:C, 0:Vd], up.t[0:C, 0:Vd], 1.0, None, ALU.mult, None, [up], [U])
                    else:
                        op('pe', lambda e: e.matmul(zp.t[0:C, 0:Vd], QsT.t[0:Kd, sl], stap, start=True, stop=True), [QsT, st_b], [zp])
                        U = HB(slot)
                        act(U.t[0:C, 0:Vd], zp.t[0:C, 0:Vd], AF.Copy, [zp], [U])
                    op('pe', lambda e: e.matmul(ypb.t[0:Vd, sl], stap, RsT.t[0:Kd, sl], start=True, stop=False), [st_b, RsT], [ypb], inc=False)
                    op('pe', lambda e: e.matmul(ypb.t[0:Vd, sl], U.t[0:C, 0:Vd], A['ArbT'].t[0:C, sl], start=False, stop=False), [U, A['ArbT']], [ypb], inc=False)
                    op('pe', lambda e: e.matmul(ypb.t[0:Vd, sl], aV, A['ArkT'].t[0:C, sl], start=False, stop=True), [vt, A['ArkT']], [ypb])
                    yield
                    sp_ = PSslot(slot)
                    op('pe', lambda e: e.matmul(sp_.t[0:Kd, 0:Vd], aP, U.t[0:C, 0:Vd], start=True, stop=False), [pbt, U], [sp_], inc=False)
                    op('pe', lambda e: e.matmul(sp_.t[0:Kd, 0:Vd], aK, aV, start=False, stop=True), [kbt, vt], [sp_])
                    stt(stap, stap, wc_ap(c), sp_.t[0:Kd, 0:Vd], ALU.mult, ALU.add, [st_b, WcB, sp_], [st_b])
                    yield

        def scan_sample(Kd, Vd, QsT, RsT, PbT, KbT, VT, arb, ark, WcB, wc16, stX):
            n = NS
            QR = NB('sQR', 36, slot=2); PK = NB('sPK', 36, slot=2); UV = NB('sUV', 36, slot=2)
            q3 = QR.t[0:Kd, 0:2 * n].rearrange("p (s two) -> p s two", two=2)
            p3 = PK.t[0:Kd, 0:2 * n].rearrange("p (s two) -> p s two", two=2)
            u3 = UV.t[0:Vd, 0:2 * n].rearrange("p (s two) -> p s two", two=2)
            act(q3[:, :, 0], QsT.t[0:Kd, 0:n], AF.Copy, [QsT], [QR])
            ts(q3[:, :, 1], RsT.t[0:Kd, 0:n], 1.0, None, ALU.mult, None, [RsT], [QR])
            act(p3[:, :, 0], PbT.t[0:Kd, 0:n], AF.Copy, [PbT], [PK])
            ts(p3[:, :, 1], KbT.t[0:Kd, 0:n], 1.0, None, ALU.mult, None, [KbT], [PK])
            ts(u3[:, :, 1], VT.t[0:Vd, 0:n], 1.0, None, ALU.mult, None, [VT], [UV])
            pq = PS()
            for s_ in range(n):
                op('pe', lambda e, s_=s_: e.matmul(pq.t[0:Vd, 2 * s_:2 * s_ + 2], stX.t[:, s_, :], QR.t[0:Kd, 2 * s_:2 * s_ + 2], start=True, stop=True),
                   [stX, QR], [pq], inc=(s_ == n - 1))
            pq3 = pq.t[0:Vd, 0:2 * n].rearrange("p (s two) -> p s two", two=2)
            act(u3[:, :, 0], pq3[:, :, 0], AF.Copy, [pq], [UV])
            Y = HB(); t2 = HB()
            tt(Y.t[0:Vd, 0:n], pq3[:, :, 0], arb.t[0:Vd, 0:n], ALU.mult, [pq, arb], [Y])
            tt(t2.t[0:Vd, 0:n], VT.t[0:Vd, 0:n], ark.t[0:Vd, 0:n], ALU.mult, [VT, ark], [t2])
            tt(Y.t[0:Vd, 0:n], Y.t[0:Vd, 0:n], t2.t[0:Vd, 0:n], ALU.add, [Y, t2], [Y])
            tt(Y.t[0:Vd, 0:n], Y.t[0:Vd, 0:n], pq3[:, :, 1], ALU.add, [Y, pq], [Y])
            gs = 512 // max(Kd, Vd)
            pkr = NB('sPKr', 512, slot=2); uvr = NB('sUVr', 512, slot=2)
            for g0 in range(0, n, gs):
                pt = PS()
                for j in range(gs):
                    s_ = g0 + j
                    tr(pt, pt.t[0:2, j * Kd:(j + 1) * Kd], PK.t[0:Kd, 2 * s_:2 * s_ + 2], Kd, [PK], inc=(j == gs - 1))
                act(pkr.t[0:2, 0:gs * Kd], pt.t[0:2, 0:gs * Kd], AF.Copy, [pt], [pkr])
                pu = PS()
                for j in range(gs):
                    s_ = g0 + j
                    tr(pu, pu.t[0:2, j * Vd:(j + 1) * Vd], UV.t[0:Vd, 2 * s_:2 * s_ + 2], Vd, [UV], inc=(j == gs - 1))
                ts(uvr.t[0:2, 0:gs * Vd], pu.t[0:2, 0:gs * Vd], 1.0, None, ALU.mult, None, [pu], [uvr])
                pp = PS()
                for j in range(gs):
                    op('pe', lambda e, j=j: e.matmul(pp.t[0:Kd, j * Vd:(j + 1) * Vd], pkr.t[0:2, j * Kd:(j + 1) * Kd], uvr.t[0:2, j * Vd:(j + 1) * Vd], start=True, stop=True),
                       [pkr, uvr], [pp], inc=(j == gs - 1))
                sv = stX.t[:, g0:g0 + gs, :]
                tt(sv, sv, wc16[:, g0:g0 + gs].unsqueeze(2).broadcast_to([Kd, gs, Vd]), ALU.mult, [stX, WcB], [stX])
                tt(sv, sv, pp.t[0:Kd, 0:gs * Vd].rearrange("p (s v) -> p s v", v=Vd), ALU.add, [stX, pp], [stX])
            return Y

        def scan2(slot, nh, Kd, Vd, N, C, QsT, RsT, PbT, KbT, VT, A, WcB, wc_ap, st_b, ypb):
            nch = N // C
            nb = nh * nch
            W = nb * C

            def tmaj(srcb, name):
                dst = NB16(name, 516, slot=slot)
                pb = PSslot(slot)
                for c in range(nch):
                    tr(pb, pb.t[0:C, c * 128:(c + 1) * 128], srcb.t[0:128, c * C:(c + 1) * C], 128, [srcb], inc=(c == nch - 1))
                act(dst.t[0:C, 0:nch * 128], pb.t[0:C, 0:nch * 128], AF.Copy, [pb], [dst])
                return dst
            pbt = tmaj(PbT, 'Pb'); kbt = tmaj(KbT, 'Kb'); vt = tmaj(VT, 'Vt')
            yield
            NTb = A['NT']; NT16 = A['NT16']
            Xb = NB16('X0', 516, slot=slot); pb = PSslot(slot)
            for b in range(nb):
                tr(pb, pb.t[0:C, b * C:(b + 1) * C], NTb.t[0:C, b * C:(b + 1) * C], C, [NTb], inc=(b == nb - 1))
            act(Xb.t[0:C, 0:W], pb.t[0:C, 0:W], AF.Copy, [pb], [Xb])
            XTb = NT16
            PT = NB('PT0', 516, slot=slot)
            tt(PT.t[0:C, 0:W].rearrange("p (c i) -> p c i", i=C), NTb.t[0:C, 0:W].rearrange("p (c i) -> p c i", i=C),
               cm.t[0:C, 0:C].unsqueeze(1).broadcast_to([C, nb, C]), ALU.add, [NTb, cm], [PT])
            PTh = NB16('PTh0', 516, slot=slot)
            act(PTh.t[0:C, 0:W], PT.t[0:C, 0:W], AF.Copy, [PT], [PTh])
            yield
            nlev = 5
            for lv in range(1, nlev + 1):
                pbx = PSslot(slot)
                for b in range(nb):
                    sl = slice(b * C, (b + 1) * C)
                    op('pe', lambda e, sl=sl, pbx=pbx, XTb=XTb, Xb=Xb: e.matmul(pbx.t[0:C, sl], XTb.t[0:C, sl], Xb.t[0:C, sl], start=True, stop=True), [XTb, Xb], [pbx], inc=(b == nb - 1))
                Xn = NB16('X%d' % (lv % 2), 516, slot=slot)
                XTn = None
                if lv < nlev:
                    pbt_ = PSslot(slot)
                    for b in range(nb):
                        sl = slice(b * C, (b + 1) * C)
                        op('pe', lambda e, sl=sl, pbt_=pbt_, XTb=XTb, Xb=Xb: e.matmul(pbt_.t[0:C, sl], Xb.t[0:C, sl], XTb.t[0:C, sl], start=True, stop=True), [XTb, Xb], [pbt_], inc=(b == nb - 1))
                act(Xn.t[0:C, 0:W], pbx.t[0:C, 0:W], AF.Copy, [pbx], [Xn])
                yield
                if lv < nlev:
                    XTn = NB16('XT%d' % (lv % 2), 516, slot=slot)
                    ts(XTn.t[0:C, 0:W], pbt_.t[0:C, 0:W], 1.0, None, ALU.mult, None, [pbt_], [XTn])
                pbp = PSslot(slot)
                for b in range(nb):
                    sl = slice(b * C, (b + 1) * C)
                    op('pe', lambda e, sl=sl, pbp=pbp, Xn=Xn, PTh=PTh: e.matmul(pbp.t[0:C, sl], Xn.t[0:C, sl], PTh.t[0:C, sl], start=True, stop=True), [Xn, PTh], [pbp], inc=(b == nb - 1))
                PTn = NB('PT%d' % (lv % 2), 516, slot=slot)
                tt(PTn.t[0:C, 0:W], PT.t[0:C, 0:W], pbp.t[0:C, 0:W], ALU.add, [PT, pbp], [PTn])
                PThn = NB16('PTh%d' % (lv % 2), 516, slot=slot)
                act(PThn.t[0:C, 0:W], PTn.t[0:C, 0:W], AF.Copy, [PTn], [PThn])
                PT = PTn; PTh = PThn
                yield
                Xb = Xn
                if lv < nlev:
                    XTb = XTn
            TinvT = PTh
            LakT = A['LakT']; ArbT = A['ArbT']; ArkT = A['ArkT']
            for c in range(nch):
                sl = slice(c * C, (c + 1) * C)
                zp = PSslot(slot)
                for hl in range(nh):
                    pr = slice(hl * Kd, (hl + 1) * Kd); vc = slice(hl * Vd, (hl + 1) * Vd)
                    bsl = slice((hl * nch + c) * C, (hl * nch + c + 1) * C)
                    aV = vt.t[0:C, c * 128 + hl * Vd:c * 128 + (hl + 1) * Vd]
                    op('pe', lambda e, pr=pr, vc=vc: e.matmul(zp.t[0:C, vc], QsT.t[pr, sl], st_b.t[pr, :], start=True, stop=False), [QsT, st_b], [zp], inc=False)
                    op('pe', lambda e, vc=vc, bsl=bsl, aV=aV: e.matmul(zp.t[0:C, vc], LakT.t[0:C, bsl], aV, start=False, stop=True), [LakT, vt], [zp], inc=(hl == nh - 1))
                yield
                Zs = HB16(slot)
                act(Zs.t[0:C, 0:128], zp.t[0:C, 0:128], AF.Copy, [zp], [Zs])
                up = PSslot(slot)
                for hl in range(nh):
                    vc = slice(hl * Vd, (hl + 1) * Vd)
                    bsl = slice((hl * nch + c) * C, (hl * nch + c + 1) * C)
                    op('pe', lambda e, vc=vc, bsl=bsl: e.matmul(up.t[0:C, vc], TinvT.t[0:C, bsl], Zs.t[0:C, vc], start=True, stop=True), [TinvT, Zs], [up], inc=(hl == nh - 1))
                yield
                U = HB16(slot)
                ts(U.t[0:C, 0:128], up.t[0:C, 0:128], 1.0, None, ALU.mult, None, [up], [U])
                for hl in range(nh):
                    pr = slice(hl * Kd, (hl + 1) * Kd); vc = slice(hl * Vd, (hl + 1) * Vd)
                    bsl = slice((hl * nch + c) * C, (hl * nch + c + 1) * C)
                    aV = vt.t[0:C, c * 128 + hl * Vd:c * 128 + (hl + 1) * Vd]
                    op('pe', lambda e, pr=pr, vc=vc: e.matmul(ypb.t[vc, sl], st_b.t[pr, :], RsT.t[pr, sl], start=True, stop=False), [st_b, RsT], [ypb], inc=False)
                    op('pe', lambda e, vc=vc, bsl=bsl: e.matmul(ypb.t[vc, sl], U.t[0:C, vc], ArbT.t[0:C, bsl], start=False, stop=False), [U, ArbT], [ypb], inc=False)
                    op('pe', lambda e, vc=vc, bsl=bsl, aV=aV: e.matmul(ypb.t[vc, sl], aV, ArkT.t[0:C, bsl], start=False, stop=True), [vt, ArkT], [ypb], inc=(hl == nh - 1))
                yield
                sp_ = PSslot(slot)
                for hl in range(nh):
                    pr = slice(hl * Kd, (hl + 1) * Kd); vc = slice(hl * Vd, (hl + 1) * Vd)
                    aP = pbt.t[0:C, c * 128 + hl * Kd:c * 128 + (hl + 1) * Kd]
                    aK = kbt.t[0:C, c * 128 + hl * Kd:c * 128 + (hl + 1) * Kd]
                    aV = vt.t[0:C, c * 128 + hl * Vd:c * 128 + (hl + 1) * Vd]
                    op('pe', lambda e, pr=pr, vc=vc, aP=aP: e.matmul(sp_.t[pr, 0:Vd], aP, U.t[0:C, vc], start=True, stop=False), [pbt, U], [sp_], inc=False)
                    op('pe', lambda e, pr=pr, aK=aK, aV=aV: e.matmul(sp_.t[pr, 0:Vd], aK, aV, start=False, stop=True), [kbt, vt], [sp_], inc=(hl == nh - 1))
                stt(st_b.t[:, :], st_b.t[:, :], wc_ap(c), sp_.t[0:128, 0:Vd], ALU.mult, ALU.add, [st_b, WcB, sp_], [st_b])
                yield

        def scan_sample2(nh, Kd, Vd, QsT, RsT, PbT, KbT, VT, arb, ark, WcB, wc16, stX):
            n = NS
            QR = NB('sQR', 36, slot=2); PK = NB('sPK', 36, slot=2); UV = NB('sUV', 36, slot=2)
            q3 = QR.t[:, 0:2 * n].rearrange("p (s two) -> p s two", two=2)
            p3 = PK.t[:, 0:2 * n].rearrange("p (s two) -> p s two", two=2)
            u3 = UV.t[:, 0:2 * n].rearrange("p (s two) -> p s two", two=2)
            act(q3[:, :, 0], QsT.t[:, 0:n], AF.Copy, [QsT], [QR])
            ts(q3[:, :, 1], RsT.t[:, 0:n], 1.0, None, ALU.mult, None, [RsT], [QR])
            act(p3[:, :, 0], PbT.t[:, 0:n], AF.Copy, [PbT], [PK])
            ts(p3[:, :, 1], KbT.t[:, 0:n], 1.0, None, ALU.mult, None, [KbT], [PK])
            ts(u3[:, :, 1], VT.t[:, 0:n], 1.0, None, ALU.mult, None, [VT], [UV])
            pq = PS()
            for s_ in range(n):
                for hl in range(nh):
                    pr = slice(hl * Kd, (hl + 1) * Kd); vc = slice(hl * Vd, (hl + 1) * Vd)
                    op('pe', lambda e, s_=s_, pr=pr, vc=vc: e.matmul(pq.t[vc, 2 * s_:2 * s_ + 2], stX.t[pr, s_, :], QR.t[pr, 2 * s_:2 * s_ + 2], start=True, stop=True),
                       [stX, QR], [pq], inc=(s_ == n - 1 and hl == nh - 1))
            pq3 = pq.t[:, 0:2 * n].rearrange("p (s two) -> p s two", two=2)
            act(u3[:, :, 0], pq3[:, :, 0], AF.Copy, [pq], [UV])
            Y = HB(); t2 = HB()
            tt(Y.t[:, 0:n], pq3[:, :, 0], arb.t[:, 0:n], ALU.mult, [pq, arb], [Y])
            tt(t2.t[:, 0:n], VT.t[:, 0:n], ark.t[:, 0:n], ALU.mult, [VT, ark], [t2])
            tt(Y.t[:, 0:n], Y.t[:, 0:n], t2.t[:, 0:n], ALU.add, [Y, t2], [Y])
            tt(Y.t[:, 0:n], Y.t[:, 0:n], pq3[:, :, 1], ALU.add, [Y, pq], [Y])
            gs = 4
            pkr = NB('sPKr', 512, slot=2); uvr = NB('sUVr', 512, slot=2)
            for g0 in range(0, n, gs):
                pt = PS()
                for j in range(gs):
                    s_ = g0 + j
                    tr(pt, pt.t[0:2, j * 128:(j + 1) * 128], PK.t[:, 2 * s_:2 * s_ + 2], 128, [PK], inc=(j == gs - 1))
                act(pkr.t[0:2, 0:gs * 128], pt.t[0:2, 0:gs * 128], AF.Copy, [pt], [pkr])
                pu = PS()
                for j in range(gs):
                    s_ = g0 + j
                    tr(pu, pu.t[0:2, j * 128:(j + 1) * 128], UV.t[:, 2 * s_:2 * s_ + 2], 128, [UV], inc=(j == gs - 1))
                ts(uvr.t[0:2, 0:gs * 128], pu.t[0:2, 0:gs * 128], 1.0, None, ALU.mult, None, [pu], [uvr])
                pp = PS()
                for j in range(gs):
                    for hl in range(nh):
                        pr = slice(hl * Kd, (hl + 1) * Kd)
                        op('pe', lambda e, j=j, hl=hl, pr=pr: e.matmul(pp.t[pr, j * Vd:(j + 1) * Vd], pkr.t[0:2, j * 128 + hl * Kd:j * 128 + (hl + 1) * Kd],
                                                                    uvr.t[0:2, j * 128 + hl * Vd:j * 128 + (hl + 1) * Vd], start=True, stop=True),
                           [pkr, uvr], [pp], inc=(j == gs - 1 and hl == nh - 1))
                sv = stX.t[:, g0:g0 + gs, :]
                tt(sv, sv, wc16[:, g0:g0 + gs].unsqueeze(2).broadcast_to([128, gs, Vd]), ALU.mult, [stX, WcB], [stX])
                tt(sv, sv, pp.t[:, 0:gs * Vd].rearrange("p (s v) -> p s v", v=Vd), ALU.add, [stX, pp], [stX])
            return Y

        def scan_pair(slot, N, C, QsT, RsT, Pb16, Kb16, V16, A, WcB, wc_ap, st_b, ypb):
            nch = N // C
            W = nch * C
            H2 = (slice(0, 64), slice(64, 128))

            def tmaj(src16, name):
                dst = NB16(name, slot=slot)
                pb = PSslot(slot)
                for hl in range(2):
                    pr = H2[hl]
                    for c in range(nch):
                        sl = slice(c * C, (c + 1) * C)
                        op('pe', lambda e, pr=pr, sl=sl: e.matmul(pb.t[pr, sl], src16.t[pr, sl], id16[pr, pr], start=True, stop=True), [src16, cm16], [pb],
                           inc=(hl == 1 and c == nch - 1))
                act(dst.t[:, 0:W], pb.t[:, 0:W], AF.Copy, [pb], [dst])
                return dst
            pbt = tmaj(Pb16, 'Pb'); kbt = tmaj(Kb16, 'Kb'); vt = tmaj(V16, 'Vt')
            yield
            NTb = A['NT']; XTb = A['NT16']; Xb = A['Nn16']
            PT = NB('PT0', slot=slot)
            tt(PT.t[:, 0:W].rearrange("p (c i) -> p c i", i=C), NTb.t[:, 0:W].rearrange("p (c i) -> p c i", i=C),
               ident2.unsqueeze(1).broadcast_to([128, nch, C]), ALU.add, [NTb, cm], [PT])
            PTh = NB16('PTh0', slot=slot)
            act(PTh.t[:, 0:W], PT.t[:, 0:W], AF.Copy, [PT], [PTh])
            yield
            nlev = 5

            def mm8(pbo, L_, R_):
                for hl in range(2):
                    pr = H2[hl]
                    for c in range(nch):
                        sl = slice(c * C, (c + 1) * C)
                        op('pe', lambda e, pr=pr, sl=sl: e.matmul(pbo.t[pr, sl], L_.t[pr, sl], R_.t[pr, sl], start=True, stop=True), [L_, R_], [pbo],
                           inc=(hl == 1 and c == nch - 1))
            for lv in range(1, nlev + 1):
                pbx = PSslot(slot)
                mm8(pbx, XTb, Xb)
                Xn = NB16('X%d' % (lv % 2), slot=slot)
                XTn = None
                if lv < nlev:
                    pbt_ = PSslot(slot)
                    mm8(pbt_, Xb, XTb)
                act(Xn.t[:, 0:W], pbx.t[:, 0:W], AF.Copy, [pbx], [Xn])
                yield
                if lv < nlev:
                    XTn = NB16('XT%d' % (lv % 2), slot=slot)
                    ts(XTn.t[:, 0:W], pbt_.t[:, 0:W], 1.0, None, ALU.mult, None, [pbt_], [XTn])
                pbp = PSslot(slot)
                mm8(pbp, Xn, PTh)
                PTn = NB('PT%d' % (lv % 2), slot=slot)
                tt(PTn.t[:, 0:W], PT.t[:, 0:W], pbp.t[:, 0:W], ALU.add, [PT, pbp], [PTn])
                PThn = NB16('PTh%d' % (lv % 2), slot=slot)
                act(PThn.t[:, 0:W], PTn.t[:, 0:W], AF.Copy, [PTn], [PThn])
                PT = PTn; PTh = PThn
                yield
                Xb = Xn
                if lv < nlev:
                    XTb = XTn
            TinvT = PTh
            LakT = A['LakT']; ArbT = A['ArbT']; ArkT = A['ArkT']
            for c in range(nch):
                sl = slice(c * C, (c + 1) * C)
                zp = PSslot(slot)
                for hl in range(2):
                    pr = H2[hl]
                    op('pe', lambda e, pr=pr: e.matmul(zp.t[pr, 0:64], QsT.t[pr, sl], st_b.t[pr, :], start=True, stop=False), [QsT, st_b], [zp], inc=False)
                    op('pe', lambda e, pr=pr: e.matmul(zp.t[pr, 0:64], LakT.t[pr, sl], vt.t[pr, sl], start=False, stop=True), [LakT, vt], [zp], inc=(hl == 1))
                yield
                Zs = HB16(slot)
                act(Zs.t[:, 0:64], zp.t[:, 0:64], AF.Copy, [zp], [Zs])
                up = PSslot(slot)
                for hl in range(2):
                    pr = H2[hl]
                    op('pe', lambda e, pr=pr: e.matmul(up.t[pr, 0:64], TinvT.t[pr, sl], Zs.t[pr, 0:64], start=True, stop=True), [TinvT, Zs], [up], inc=(hl == 1))
                yield
                U = HB16(slot)
                ts(U.t[:, 0:64], up.t[:, 0:64], 1.0, None, ALU.mult, None, [up], [U])
                for hl in range(2):
                    pr = H2[hl]
                    op('pe', lambda e, pr=pr: e.matmul(ypb.t[pr, sl], st_b.t[pr, :], RsT.t[pr, sl], start=True, stop=False), [st_b, RsT], [ypb], inc=False)
                    op('pe', lambda e, pr=pr: e.matmul(ypb.t[pr, sl], U.t[pr, 0:64], ArbT.t[pr, sl], start=False, stop=False), [U, ArbT], [ypb], inc=False)
                    op('pe', lambda e, pr=pr: e.matmul(ypb.t[pr, sl], vt.t[pr, sl], ArkT.t[pr, sl], start=False, stop=True), [vt, ArkT], [ypb], inc=(hl == 1))
                yield
                sp_ = PSslot(slot)
                for hl in range(2):
                    pr = H2[hl]
                    op('pe', lambda e, pr=pr: e.matmul(sp_.t[pr, 0:64], pbt.t[pr, sl], U.t[pr, 0:64], start=True, stop=False), [pbt, U], [sp_], inc=False)
                    op('pe', lambda e, pr=pr: e.matmul(sp_.t[pr, 0:64], kbt.t[pr, sl], vt.t[pr, sl], start=False, stop=True), [kbt, vt], [sp_], inc=(hl == 1))
                stt(st_b.t[:, :], st_b.t[:, :], wc_ap(c), sp_.t[:, 0:64], ALU.mult, ALU.add, [st_b, WcB, sp_], [st_b])
                yield

        def scan_sample_pair(QsT, RsT, PbT, KbT, VT, arb, ark, WcB, wc16, stX):
            n = NS
            H2 = (slice(0, 64), slice(64, 128))
            QR = NB('sQR', 36, slot=2); PK = NB('sPK', 36, slot=2); UV = NB('sUV', 36, slot=2)
            q3 = QR.t[:, 0:2 * n].rearrange("p (s two) -> p s two", two=2)
            p3 = PK.t[:, 0:2 * n].rearrange("p (s two) -> p s two", two=2)
            u3 = UV.t[:, 0:2 * n].rearrange("p (s two) -> p s two", two=2)
            act(q3[:, :, 0], QsT.t[:, 0:n], AF.Copy, [QsT], [QR])
            ts(q3[:, :, 1], RsT.t[:, 0:n], 1.0, None, ALU.mult, None, [RsT], [QR])
            act(p3[:, :, 0], PbT.t[:, 0:n], AF.Copy, [PbT], [PK])
            ts(p3[:, :, 1], KbT.t[:, 0:n], 1.0, None, ALU.mult, None, [KbT], [PK])
            ts(u3[:, :, 1], VT.t[:, 0:n], 1.0, None, ALU.mult, None, [VT], [UV])
            pq = PS()
            for s_ in range(n):
                for hl in range(2):
                    pr = H2[hl]
                    op('pe', lambda e, s_=s_, pr=pr: e.matmul(pq.t[pr, 2 * s_:2 * s_ + 2], stX.t[pr, s_, :], QR.t[pr, 2 * s_:2 * s_ + 2], start=True, stop=True),
                       [stX, QR], [pq], inc=(s_ == n - 1 and hl == 1))
            pq3 = pq.t[:, 0:2 * n].rearrange("p (s two) -> p s two", two=2)
            act(u3[:, :, 0], pq3[:, :, 0], AF.Copy, [pq], [UV])
            Y = HB(); t2 = HB()
            tt(Y.t[:, 0:n], pq3[:, :, 0], arb.t[:, 0:n], ALU.mult, [pq, arb], [Y])
            tt(t2.t[:, 0:n], VT.t[:, 0:n], ark.t[:, 0:n], ALU.mult, [VT, ark], [t2])
            tt(Y.t[:, 0:n], Y.t[:, 0:n], t2.t[:, 0:n], ALU.add, [Y, t2], [Y])
            tt(Y.t[:, 0:n], Y.t[:, 0:n], pq3[:, :, 1], ALU.add, [Y, pq], [Y])
            gs = 8
            pkr = NB('sPKr', 512, slot=2); uvr = NB('sUVr', 512, slot=2)
            for g0 in range(0, n, gs):
                pt = PS(); pu = PS()
                for j in range(gs):
                    s_ = g0 + j
                    for hl in range(2):
                        pr = H2[hl]; p2 = slice(hl * 64, hl * 64 + 2)
                        op('pe', lambda e, j=j, s_=s_, pr=pr, p2=p2: e.matmul(pt.t[p2, j * 64:(j + 1) * 64], PK.t[pr, 2 * s_:2 * s_ + 2], ident[pr, pr], start=True, stop=True),
                           [PK, cm], [pt], inc=(j == gs - 1 and hl == 1))
                for j in range(gs):
                    s_ = g0 + j
                    for hl in range(2):
                        pr = H2[hl]; p2 = slice(hl * 64, hl * 64 + 2)
                        op('pe', lambda e, j=j, s_=s_, pr=pr, p2=p2: e.matmul(pu.t[p2, j * 64:(j + 1) * 64], UV.t[pr, 2 * s_:2 * s_ + 2], ident[pr, pr], start=True, stop=True),
                           [UV, cm], [pu], inc=(j == gs - 1 and hl == 1))
                for hl in range(2):
                    p2 = slice(hl * 64, hl * 64 + 2)
                    act(pkr.t[p2, 0:gs * 64], pt.t[p2, 0:gs * 64], AF.Copy, [pt], [pkr])
                    ts(uvr.t[p2, 0:gs * 64], pu.t[p2, 0:gs * 64], 1.0, None, ALU.mult, None, [pu], [uvr])
                pp = PS()
                for j in range(gs):
                    for hl in range(2):
                        pr = H2[hl]; p2 = slice(hl * 64, hl * 64 + 2)
                        op('pe', lambda e, j=j, pr=pr, p2=p2: e.matmul(pp.t[pr, j * 64:(j + 1) * 64], pkr.t[p2, j * 64:(j + 1) * 64], uvr.t[p2, j * 64:(j + 1) * 64], start=True, stop=True),
                           [pkr, uvr], [pp], inc=(j == gs - 1 and hl == 1))
                sv = stX.t[:, g0:g0 + gs, :]
                tt(sv, sv, wc16[:, g0:g0 + gs].unsqueeze(2).broadcast_to([128, gs, 64]), ALU.mult, [stX, WcB], [stX])
                tt(sv, sv, pp.t[:, 0:gs * 64].rearrange("p (s v) -> p s v", v=64), ALU.add, [stX, pp], [stX])
            return Y

        def do_tile(is_s, t0):
            pfx = 's_' if is_s else ''
            N = NS if is_s else NP
            C = 1 if is_s else 64
            nseq = NS if is_s else 1
            nch = N // C
            rows_list = [NS] if is_s else [128] * TCN
            xsrc = xS if is_s else xP
            psrc = pS if is_s else pP
            ydst = yS if is_s else yP
            last = is_s or (t0 + NP >= TP)
            for tc, rows in enumerate(rows_list):
                kb.dma('sp', xtok.t[:rows, tc, :], xsrc[t0 + tc * 128:t0 + tc * 128 + rows, :], writes=[xtok])
                kb.dma('sp', ptok.t[:rows, tc, :], psrc[t0 + tc * 128:t0 + tc * 128 + rows, :], writes=[ptok])
            CKP(pfx + 'load')
            norm_T(N, rows_list, GMIX)
            CKP(pfx + 'norm')
            if is_s:
                kb.dma('sp', tokS.t[:], stShift, writes=[tokS])
                for g in range(15):
                    pb = PS()
                    if g < 12:
                        col = (g // 4) * 512 + (g % 4) * 128; P_ = 128
                    elif g == 12:
                        col = 1536; P_ = 64
                    elif g == 13:
                        col = 1600; P_ = 64
                    else:
                        col = 1664; P_ = 128
                    tr(pb, pb.t[0:P_, 0:NS], tokS.t[0:NS, col:col + P_], NS, [tokS])
                    act(prevS.t[0:P_, g, :], pb.t[0:P_, 0:NS], AF.Copy, [pb], [prevS])
                for j in range(3):
                    kb.dma('sp', tokC.t[:], stConv[:, j, :], writes=[tokC])
                    for g in range(12):
                        pb = PS()
                        tr(pb, pb.t[0:128, 0:NS], tokC.t[0:NS, g * 128:(g + 1) * 128], NS, [tokC])
                        act(histS.t[:, j * 12 + g, :], pb.t[:, 0:NS], AF.Copy, [pb], [histS])
            wt = wload(w_in, 0, 8, 128, 1536, 256)
            pb = PS(); proj_fm(pb, pb.t[0:64, :N], wt, 0, 64, N)
            xw = HB()
            shift(64, N, pb, pb.t[0:64, :N], c64b.t[:, MU_XW:MU_XW + 1], omu64.t[:, 24:25], (car_xw, car_xw.t[:, 0:1]), xw, xw.t[0:64, :N], is_s,
                  prev_ap=prevS.t[0:64, 12, :], raw_ap=rawS.t[0:64, 12, :], raw_b=rawS)
            act(tw.t[:, :N], xw.t[0:64, :N], AF.Tanh, [xw], [tw])
            pb = PS(); proj_fm(pb, pb.t[0:64, :N], wt, 64, 64, N)
            shift(64, N, pb, pb.t[0:64, :N], c64b.t[:, MU_XAA:MU_XAA + 1], omu64.t[:, 25:26], (car_xw, car_xw.t[:, 1:2]), xaaS, xaaS.t[:, :N], is_s,
                  prev_ap=prevS.t[0:64, 13, :], raw_ap=rawS.t[0:64, 13, :], raw_b=rawS)
            pb = PS(); proj_fm(pb, pb.t[:, :N], wt, 128, 128, N)
            xg = HB()
            shift(128, N, pb, pb.t[:, :N], c128b.t[:, MU_XG:MU_XG + 1], omu128.t[:, 0:1], (car_xg, car_xg.t[:, 0:1]), xg, xg.t[:, :N], is_s,
                  prev_ap=prevS.t[:, 14, :], raw_ap=rawS.t[:, 14, :], raw_b=rawS)
            act(sgS.t[:, :N], xg.t[:, :N], AF.Sigmoid, [xg], [sgS])

            pend = []
            if (not is_s) and t0 == 0:
                pend = [lambda: convert('bra', w_bra, 512, D), lambda: convert('brb', w_brb, 512, D),
                        lambda: convert('gate', w_in[:, 3848:5896], D, 2048), lambda: convert('out', w_out, D, D),
                        lambda: convert('fg', w_fg, D, DFF), lambda: convert('fu', w_fu, D, DFF), lambda: convert('fd', w_fd, DFF, D),
                        lambda: convert('pg', w_pg, D, D), lambda: convert('pp', w_pp, 256, D)]
            CKP(pfx + 'lora')
            def rwkv_gen(hp, slot):
                T = lambda: HB(slot)
                PSs = lambda: PSslot(slot)
                wt1 = wload(w_in, 0, 8, 128, hp * 384, 256)
                wt2 = wload(w_in, 0, 8, 128, hp * 384 + 256, 128)
                rkv = []
                for j, nm_ in enumerate(('r', 'k', 'v')):
                    g = hp * 3 + j; gc_ = j * 4 + hp
                    pb = PSs()
                    if j < 2:
                        proj_fm(pb, pb.t[:, :N], wt1, j * 128, 128, N)
                    else:
                        proj_fm(pb, pb.t[:, :N], wt2, 0, 128, N)
                    o_ = NB(nm_, slot=slot)
                    shift(128, N, pb, pb.t[:, :N], c128r.t[:, g:g + 1], omu128r.t[:, g:g + 1], (car_rkv, car_rkv.t[:, gc_:gc_ + 1]), o_, o_.t[:, :N], is_s,
                          prev_ap=prevS.t[:, gc_, :], raw_ap=rawS.t[:, gc_, :], raw_b=rawS, tmp=T)
                    rkv.append(o_)
                r_, k_, v_ = rkv
                hs = slice(hp * 128, (hp + 1) * 128)
                cc_ = lambda base: c128r.t[:, base + hp:base + hp + 1]
                yield
                pb = PSs()
                mm(pb, pb.t[:, :N], w2b.t[:, hs], tw.t[:, :N], [w2b, tw], True, True)
                sg = T(); act(sg.t[:, :N], pb.t[:, :N], AF.Sigmoid, [pb, c128r], [sg], bias=cc_(W0r))
                yield
                cws = T()
                if C > 1:
                    op('dve', lambda e: e.tensor_tensor_scan(out=cws.t[:, :N], data0=scanm[:, :N], data1=sg.t[:, :N], initial=0.0, op0=ALU.mult, op1=ALU.add), [sg, cm], [cws])
                else:
                    ts(cws.t[:, :N], sg.t[:, :N], 1.0, None, ALU.mult, None, [sg], [cws])
                yield
                ew = NB('ew', slot=slot); act(ew.t[:, :N], cws.t[:, :N], AF.Exp, [cws], [ew], scale=-C0)
                ewi = NB('ewi', slot=slot); act(ewi.t[:, :N], cws.t[:, :N], AF.Exp, [cws], [ewi], scale=C0)
                yield
                cx = T(); tt(cx.t[:, :N], cws.t[:, :N], sg.t[:, :N], ALU.subtract, [cws, sg], [cx])
                ewx = NB('ewx', slot=slot); act(ewx.t[:, :N], cx.t[:, :N], AF.Exp, [cx], [ewx], scale=-C0)
                yield
                dC = T()
                cw3 = cws.t[:, :N].rearrange("p (c i) -> p c i", i=C)
                tt(dC.t[:, :N].rearrange("p (c i) -> p c i", i=C), cw3[:, :, C - 1:C].broadcast_to([128, nch, C]), cw3, ALU.subtract, [cws], [dC])
                ewC = NB('ewC', slot=slot); act(ewC.t[:, :N], dC.t[:, :N], AF.Exp, [dC], [ewC], scale=-C0)
                yield
                pb = PSs()
                mm(pb, pb.t[:, :N], a2b.t[:, hs], xaaS.t[:, :N], [a2b, xaaS], True, True)
                a_ = NB('a', slot=slot); act(a_.t[:, :N], pb.t[:, :N], AF.Sigmoid, [pb, c128r], [a_], bias=cc_(A0r))
                yield
                pb = PSs()
                mm(pb, pb.t[:, :N], g2b.t[:, hs], sgS.t[:, :N], [g2b, sgS], True, True)
                g_ = NB('g', slot=slot); act(g_.t[:, :N], pb.t[:, :N], AF.Copy, [pb], [g_])
                yield
                kks = T(); ts(kks.t[:, :N], k_.t[:, :N], cc_(KKr), None, ALU.mult, None, [k_, c128r], [kks])
                sq = HBW(slot); act(sq.t[:, :N], kks.t[:, :N], AF.Square, [kks], [sq])
                yield
                pb = PSs()
                mm(pb, pb.t[:, :N], bones16, sq.t[:, :N], [on16, sq], True, True)
                rs = T(); rsqrt(rs.t[:, :N], pb.t[:, :N], 1.0, 1e-6, [pb], [rs])
                yield
                kk = NB('kk', slot=slot); tt(kk.t[:, :N], kks.t[:, :N], rs.t[:, :N], ALU.mult, [kks, rs], [kk])
                t1 = T(); ts(t1.t[:, :N], a_.t[:, :N], -1.0, cc_(KAr), ALU.add, ALU.mult, [a_, c128r], [t1])
                yield
                km = NB('km', slot=slot); stt(km.t[:, :N], t1.t[:, :N], 1.0, k_.t[:, :N], ALU.add, ALU.mult, [t1, k_], [km])
                kka = NB('kka', slot=slot); tt(kka.t[:, :N], kk.t[:, :N], a_.t[:, :N], ALU.mult, [kk, a_], [kka])
                yield
                QsT = NB('QsT', slot=slot); stt(QsT.t[:, :N], kk.t[:, :N], -1.0, ewx.t[:, :N], ALU.mult, ALU.mult, [kk, ewx], [QsT])
                RsT = NB('RsT', slot=slot); tt(RsT.t[:, :N], r_.t[:, :N], ew.t[:, :N], ALU.mult, [r_, ew], [RsT])
                yield
                PnT = NB16('PnT', slot=slot); tt(PnT.t[:, :N], kka.t[:, :N], ewi.t[:, :N], ALU.mult, [kka, ewi], [PnT])
                KnT = NB16('KnT', slot=slot); tt(KnT.t[:, :N], km.t[:, :N], ewi.t[:, :N], ALU.mult, [km, ewi], [KnT])
                yield
                PbT = NB('PbT', slot=slot); tt(PbT.t[:, :N], kka.t[:, :N], ewC.t[:, :N], ALU.mult, [kka, ewC], [PbT])
                KbT = NB('KbT', slot=slot); tt(KbT.t[:, :N], km.t[:, :N], ewC.t[:, :N], ALU.mult, [km, ewC], [KbT])
                yield
                rk = HBW(slot); stt(rk.t[:, :N], r_.t[:, :N], cc_(RKr), km.t[:, :N], ALU.mult, ALU.mult, [r_, c128r, km], [rk])
                pb = PSs()
                mm(pb, pb.t[:, :N], bones16, rk.t[:, :N], [on16, rk], True, True)
                bon = NB('bon', slot=slot); tt(bon.t[:, :N], pb.t[:, :N], v_.t[:, :N], ALU.mult, [pb, v_], [bon])
                yield
                if is_s:
                    for hl_ in range(2):
                        kb.dma('sp', natS.t[0:64, :, hl_, :], stWkv[:, 2 * hp + hl_].rearrange("s v k -> v s k"), writes=[natS])
                    for s0 in range(0, NS, 4):
                        pb = PSs()
                        for s_ in range(s0, s0 + 4):
                            tr(pb, pb.t[:, (s_ - s0) * 64:(s_ - s0 + 1) * 64], natS.t[0:64, s_, :, :], 64, [natS], inc=(s_ == s0 + 3))
                        act(stRs.t[:, s0:s0 + 4, :], pb.t[:, 0:256].rearrange("k (s v) -> k s v", v=64), AF.Copy, [pb], [stRs])
                    stb = stRs
                    pr1 = T(); tt(pr1.t[:, :N], PnT.t[:, :N], RsT.t[:, :N], ALU.mult, [PnT, RsT], [pr1])
                    pr2 = T(); tt(pr2.t[:, :N], KnT.t[:, :N], RsT.t[:, :N], ALU.mult, [KnT, RsT], [pr2])
                    pb = PSs(); mm(pb, pb.t[:, :N], bones, pr1.t[:, :N], [cm2, pr1], True, True)
                    arb = T(); act(arb.t[:, :N], pb.t[:, :N], AF.Copy, [pb], [arb])
                    pb = PSs(); mm(pb, pb.t[:, :N], bones, pr2.t[:, :N], [cm2, pr2], True, True)
                    ark = T(); act(ark.t[:, :N], pb.t[:, :N], AF.Copy, [pb], [ark])
                    y = scan_sample_pair(QsT, RsT, PbT, KbT, v_, arb, ark, ew, ew.t[:, 0:NS], stRs)
                else:
                    CKP(pfx + 'rprep')
                    A = {}
                    H2 = (slice(0, 64), slice(64, 128))

                    def amat(name, Lb, Rb, mask):
                        pb = PSs()
                        for hl in range(2):
                            pr = H2[hl]
                            for c in range(nch):
                                sl = slice(c * C, (c + 1) * C)
                                op('pe', lambda e, sl=sl, pr=pr: e.matmul(pb.t[pr, sl], Lb.t[pr, sl], Rb.t[pr, sl], start=True, stop=True), [Lb, Rb], [pb],
                                   inc=(hl == 1 and c == nch - 1))
                        o_ = NB(name, slot=slot) if name == 'NT' else NB16(name, slot=slot)
                        tt(o_.t[:, 0:N].rearrange("p (c i) -> p c i", i=C), pb.t[:, 0:N].rearrange("p (c i) -> p c i", i=C),
                           mask.unsqueeze(1).broadcast_to([128, nch, C]), ALU.mult, [pb, cm], [o_])
                        A[name] = o_
                        if name == 'NT':
                            n16 = NB16('NT16', slot=slot)
                            act(n16.t[:, 0:N], o_.t[:, 0:N], AF.Copy, [o_], [n16])
                            A['NT16'] = n16
                    Qs16 = NB16('Qs16', slot=slot); act(Qs16.t[:, :N], QsT.t[:, :N], AF.Copy, [QsT], [Qs16])
                    Rs16 = NB16('Rs16', slot=slot); ts(Rs16.t[:, :N], RsT.t[:, :N], 1.0, None, ALU.mult, None, [RsT], [Rs16])
                    yield
                    amat('NT', PnT, Qs16, maskS2)
                    yield
                    amat('Nn16', Qs16, PnT, maskL2)
                    yield
                    amat('LakT', KnT, Qs16, maskS2)
                    yield
                    amat('ArbT', PnT, Rs16, maskI2)
                    yield
                    amat('ArkT', KnT, Rs16, maskI2)
                    yield
                    Pb16 = NB16('Pb16', slot=slot); act(Pb16.t[:, :N], PbT.t[:, :N], AF.Copy, [PbT], [Pb16])
                    Kb16 = NB16('Kb16', slot=slot); ts(Kb16.t[:, :N], KbT.t[:, :N], 1.0, None, ALU.mult, None, [KbT], [Kb16])
                    V16 = NB16('V16', slot=slot); act(V16.t[:, :N], v_.t[:, :N], AF.Copy, [v_], [V16])
                    yield
                    CKP(pfx + 'ramat')
                    stb = stR[hp]
                    ypb = ps[6 + slot]
                    yield from scan_pair(slot, N, C, QsT, RsT, Pb16, Kb16, V16, A, ew, lambda c: ew.t[:, (c + 1) * C - 1:(c + 1) * C], stb, ypb)
                    CKP(pfx + 'rscan')
                    y = T(); act(y.t[:, :N], ypb.t[:, :N], AF.Copy, [ypb], [y])
                yield
                pb = PSs(); mm(pb, pb.t[:, :N], bones, y.t[:, :N], [cm2, y], True, True)
                d_ = T(); stt(d_.t[:, :N], pb.t[:, :N], -1.0 / 64, y.t[:, :N], ALU.mult, ALU.add, [pb, y], [d_])
                yield
                d2 = HBW(slot); act(d2.t[:, :N], d_.t[:, :N], AF.Square, [d_], [d2])
                pb = PSs(); mm(pb, pb.t[:, :N], bones16, d2.t[:, :N], [on16, d2], True, True)
                rs2 = T(); rsqrt(rs2.t[:, :N], pb.t[:, :N], 1.0 / 64, 64e-5, [pb], [rs2])
                yield
                yn = T(); tt(yn.t[:, :N], d_.t[:, :N], rs2.t[:, :N], ALU.mult, [d_, rs2], [yn])
                ts(yn.t[:, :N], yn.t[:, :N], cc_(LNWr), cc_(LNBr), ALU.mult, ALU.add, [yn, c128r], [yn])
                yield
                tt(yn.t[:, :N], yn.t[:, :N], bon.t[:, :N], ALU.add, [yn, bon], [yn])
                tt(oa.t[:, hp, :N], yn.t[:, :N], g_.t[:, :N], ALU.mult, [yn, g_], [oa])
                yield
                if last:
                    if is_s:
                        for s0 in range(0, NS, 4):
                            pb = PSs()
                            for s_ in range(s0, s0 + 4):
                                tr(pb, pb.t[0:64, (s_ - s0) * 128:(s_ - s0 + 1) * 128], stRs.t[:, s_, :], 128, [stRs], inc=(s_ == s0 + 3))
                            act(natS.t[0:64, s0:s0 + 4, :, :], pb.t[0:64, 0:512].rearrange("v (s h k) -> v s h k", h=2, k=64), AF.Copy, [pb], [natS])
                        for hl_ in range(2):
                            kb.dma('sp', oWkvS[:, 2 * hp + hl_].rearrange("s v k -> v s k"), natS.t[0:64, :, hl_, :], reads=[natS], is_out=True)
                    else:
                        pb = PSs()
                        tr(pb, pb.t[0:64, 0:128], stR[hp].t[:, :], 128, [stR[hp]])
                        stg_ = T()
                        act(stg_.t[0:64, 0:128], pb.t[0:64, 0:128], AF.Copy, [pb], [stg_])
                        kb.dma('sp', oWkvP[0, 2 * hp:2 * hp + 2].rearrange("h v k -> v h k"), stg_.t[0:64, 0:128].rearrange("v (h k) -> v h k", k=64), reads=[stg_], is_out=True)

            CKP(pfx + 'rwkv')
            def gdn_gen(h, slot):
                T = lambda: HB(slot)
                PSs = lambda: PSslot(slot)
                wtb = wload(w_barep, 0, 8, 128, h * 128, 128)
                wta = wload(w_barep, 0, 8, 128, 512 + h * 128, 128)
                pb = PSs(); proj_fm(pb, pb.t[:, :N], wtb, 0, 128, N)
                Bb = NB('a', slot=slot); act(Bb.t[:, :N], pb.t[:, :N], AF.Sigmoid, [pb], [Bb])
                pb = PSs(); proj_fm(pb, pb.t[:, :N], wta, 0, 128, N)
                e1 = T(); act(e1.t[:, :N], pb.t[:, :N], AF.Exp, [pb, c128b], [e1], bias=c128b.t[:, DTB + h:DTB + h + 1])
                act(e1.t[:, :N], e1.t[:, :N], AF.Ln, [e1], [e1], bias=1.0)
                gl = T(); ts(gl.t[:, :N], e1.t[:, :N], nexpA.t[:, h:h + 1], None, ALU.mult, None, [e1, nexpA], [gl])
                G = NB('ewi', slot=slot)
                if C > 1:
                    op('dve', lambda e, G=G, gl=gl: e.tensor_tensor_scan(out=G.t[:, :N], data0=scanm[:, :N], data1=gl.t[:, :N], initial=0.0, op0=ALU.mult, op1=ALU.add), [gl, cm], [G])
                else:
                    ts(G.t[:, :N], gl.t[:, :N], 1.0, None, ALU.mult, None, [gl], [G])
                wt1 = wload(w_in, 0, 8, 128, 1792 + h * 512, 256)
                wt2 = wload(w_in, 0, 8, 128, 1792 + h * 512 + 256, 256)
                cs = []
                for j, nm_ in enumerate(('r', 'k', 'v')):
                    g = j * 4 + h
                    pb = PSs()
                    if j < 2:
                        proj_fm(pb, pb.t[:, :N], wt1, j * 128, 128, N)
                    else:
                        proj_fm(pb, pb.t[:, :N], wt2, 0, 128, N)
                    acc = T()
                    cwc = lambda tap, g=g: cvw.t[:, tap * 12 + g:tap * 12 + g + 1]
                    if is_s:
                        act(rawC.t[:, g, :], pb.t[:, :N], AF.Copy, [pb], [rawC])
                        ts(acc.t[:, :N], histS.t[:, 0 * 12 + g, :], cwc(0), None, ALU.mult, None, [histS, cvw], [acc])
                        stt(acc.t[:, :N], histS.t[:, 1 * 12 + g, :], cwc(1), acc.t[:, :N], ALU.mult, ALU.add, [histS, cvw, acc], [acc])
                        stt(acc.t[:, :N], histS.t[:, 2 * 12 + g, :], cwc(2), acc.t[:, :N], ALU.mult, ALU.add, [histS, cvw, acc], [acc])
                        stt(acc.t[:, :N], pb.t[:, :N], cwc(3), acc.t[:, :N], ALU.mult, ALU.add, [pb, cvw, acc], [acc])
                    else:
                        raw = T()
                        act(raw.t[:, 3:N + 3], pb.t[:, :N], AF.Copy, [pb], [raw])
                        act(raw.t[:, 0:3], car_cv.t[:, g, 0:3], AF.Copy, [car_cv], [raw])
                        act(car_cv.t[:, g, 0:3], raw.t[:, N:N + 3], AF.Copy, [raw], [car_cv])
                        ts(acc.t[:, :N], raw.t[:, 0:N], cwc(0), None, ALU.mult, None, [raw, cvw], [acc])
                        for tap in range(1, 4):
                            stt(acc.t[:, :N], raw.t[:, tap:tap + N], cwc(tap), acc.t[:, :N], ALU.mult, ALU.add, [raw, cvw, acc], [acc])
                    c_ = NB(nm_, slot=slot); act(c_.t[:, :N], acc.t[:, :N], AF.Silu, [acc], [c_])
                    cs.append(c_)
                pbz = PSs(); proj_fm(pbz, pbz.t[:, :N], wt2, 128, 128, N)
                zz = NB('g', slot=slot); act(zz.t[:, :N], pbz.t[:, :N], AF.Silu, [pbz], [zz])
                cq, ck, cv_ = cs
                yield

                def l2n(src, scale, name):
                    sq = HBW(slot); act(sq.t[:, :N], src.t[:, :N], AF.Square, [src], [sq])
                    pb = PSs(); mm(pb, pb.t[:, :N], ones16, sq.t[:, :N], [on16, sq], True, True)
                    rs = T(); rsqrt(rs.t[:, :N], pb.t[:, :N], 1.0, 1e-6, [pb], [rs])
                    o_ = NB(name, slot=slot); stt(o_.t[:, :N], src.t[:, :N], scale, rs.t[:, :N], ALU.mult, ALU.mult, [src, rs], [o_])
                    return o_
                qn = l2n(cq, float(128 ** -0.5), 'kk'); kn = l2n(ck, 1.0, 'km')
                yield
                N2, C2, nch2 = N, C, nch
                yield
                eg = NB('ew', slot=slot); act(eg.t[:, :N2], G.t[:, :N2], AF.Exp, [G], [eg])
                yield
                QsT = NB('QsT', slot=slot); tt(QsT.t[:, :N2], kn.t[:, :N2], eg.t[:, :N2], ALU.mult, [kn, eg], [QsT])
                yield
                RsT = NB('RsT', slot=slot); tt(RsT.t[:, :N2], qn.t[:, :N2], eg.t[:, :N2], ALU.mult, [qn, eg], [RsT])
                yield
                dC = T()
                yield
                G3 = G.t[:, :N2].rearrange("p (c i) -> p c i", i=C2)
                yield
                tt(dC.t[:, :N2].rearrange("p (c i) -> p c i", i=C2), G3[:, :, C2 - 1:C2].broadcast_to([128, nch2, C2]), G3, ALU.subtract, [G], [dC])
                yield
                egC = T(); act(egC.t[:, :N2], dC.t[:, :N2], AF.Exp, [dC], [egC])
                yield
                bE = T(); tt(bE.t[:, :N2], Bb.t[:, :N2], egC.t[:, :N2], ALU.mult, [Bb, egC], [bE])
                yield
                KbT = NB('KbT', slot=slot); tt(KbT.t[:, :N2], kn.t[:, :N2], bE.t[:, :N2], ALU.mult, [kn, bE], [KbT])
                yield
                PbT = NB('PbT', slot=slot); ts(PbT.t[:, :N2], KbT.t[:, :N2], -1.0, None, ALU.mult, None, [KbT], [PbT])
                yield
                if is_s:
                    kb.dma('sp', stGs.t[:], stGdn[:, h].rearrange("s k v -> k s v"), writes=[stGs])
                    stb = stGs
                    kq = T(); tt(kq.t[:, :N], kn.t[:, :N], qn.t[:, :N], ALU.mult, [kn, qn], [kq])
                    pb = PSs(); mm(pb, pb.t[:, :N], ones, kq.t[:, :N], [cm, kq], True, True)
                    arb = T(); stt(arb.t[:, :N], pb.t[:, :N], -1.0, Bb.t[:, :N], ALU.mult, ALU.mult, [pb, Bb], [arb])
                    ark = T(); ts(ark.t[:, :N], arb.t[:, :N], -1.0, None, ALU.mult, None, [arb], [ark])
                    o_ = scan_sample2(1, 128, 128, QsT, RsT, PbT, KbT, cv_, arb, ark, eg, eg.t[:, 0:NS], stGs)
                else:
                    gcT = NB('ewx', slot=slot); nbT = NB('ewC', slot=slot)
                    for (src, dst, sc) in ((G, gcT, 1.0), (Bb, nbT, -1.0)):
                        pb = PSs()
                        for c in range(nch2):
                            tr(pb, pb.t[0:C2, c * 32:(c + 1) * 32], src.t[0:32, c * C2:(c + 1) * C2], 32, [src], inc=(c == nch2 - 1))
                        ts(dst.t[0:C2, 0:nch2], pb.t[0:C2, 0:nch2 * 32].rearrange("p (c i) -> p c i", i=32)[:, :, 0], sc, None, ALU.mult, None, [pb], [dst])
                    E = NB('kka', slot=slot)
                    E3 = E.t[0:C2, :N2].rearrange("p (c i) -> p c i", i=C2)
                    tt(E3, G.t[0:C2, :N2].rearrange("p (c i) -> p c i", i=C2), gcT.t[0:C2, 0:nch2].unsqueeze(2).broadcast_to([C2, nch2, C2]), ALU.subtract, [G, gcT], [E])
                    tt(E3, E3, negI[0:C2, 0:C].unsqueeze(1).broadcast_to([C2, nch2, C2]), ALU.add, [E, cm], [E])
                    act(E.t[0:C2, :N2], E.t[0:C2, :N2], AF.Exp, [E], [E])
                    nb3 = nbT.t[0:C2, 0:nch2].unsqueeze(2).broadcast_to([C2, nch2, C2])
                    A = {}

                    def gmat(Rb, strict, n1, n2):
                        pb = PSs()
                        for c in range(nch2):
                            sl = slice(c * C2, (c + 1) * C2)
                            op('pe', lambda e, sl=sl: e.matmul(pb.t[0:C2, sl], kn.t[:, sl], Rb.t[:, sl], start=True, stop=True), [kn, Rb], [pb], inc=(c == nch2 - 1))
                        o_ = NB('NT' if strict else 'A32', slot=slot); o3 = o_.t[0:C2, :N2].rearrange("p (c i) -> p c i", i=C2)
                        tt(o3, pb.t[0:C2, :N2].rearrange("p (c i) -> p c i", i=C2), E3, ALU.mult, [pb, E], [o_])
                        if strict:
                            tt(o3, o3, maskS[0:C2, 0:C].unsqueeze(1).broadcast_to([C2, nch2, C2]), ALU.mult, [o_, cm], [o_])
                        tt(o3, o3, nb3, ALU.mult, [o_, nbT], [o_])
                        p_ = NB16(n1, slot=slot); act(p_.t[0:C2, :N2], o_.t[0:C2, :N2], AF.Copy, [o_], [p_])
                        n_ = NB16(n2, slot=slot); ts(n_.t[0:C2, :N2], o_.t[0:C2, :N2], -1.0, None, ALU.mult, None, [o_], [n_])
                        return o_, p_, n_
                    A['NT'], A['NT16'], A['LakT'] = gmat(kn, True, 'NT16', 'LakT')
                    _, A['ArbT'], A['ArkT'] = gmat(qn, False, 'ArbT', 'ArkT')
                    if is_s:
                        kb.dma('sp', stGs.t[:], stGdn[:, h].rearrange("s k v -> k s v"), writes=[stGs])
                        stb = stGs; stap = lambda s: stGs.t[:, s, :]
                    else:
                        stb = stG[h]; stap = lambda s, h=h: stG[h].t[:, :]
                    ypb = ps[6 + slot]
                    yield from scan2(slot, 1, 128, 128, N, C, QsT, RsT, PbT, KbT, cv_, A, eg, lambda c: eg.t[:, (c + 1) * C - 1:(c + 1) * C], stG[h], ypb)
                    osrc = ypb.t[:, 0:N2].rearrange("p (s two) -> p s two", two=2)[:, :, 0] if is_s else ypb.t[:, :N]
                    o_ = T(); act(o_.t[:, :N], osrc, AF.Copy, [ypb], [o_])
                sq = HBW(slot); act(sq.t[:, :N], o_.t[:, :N], AF.Square, [o_], [sq])
                yield
                pb = PSs(); mm(pb, pb.t[:, :N], ones16, sq.t[:, :N], [on16, sq], True, True)
                yield
                rs = T(); rsqrt(rs.t[:, :N], pb.t[:, :N], 1.0 / 128, 1e-6, [pb], [rs])
                yield
                on = T(); stt(on.t[:, :N], o_.t[:, :N], c128b.t[:, GDNN:GDNN + 1], rs.t[:, :N], ALU.mult, ALU.mult, [o_, c128b, rs], [on])
                yield
                tt(ob.t[:, h, :N], on.t[:, :N], zz.t[:, :N], ALU.mult, [on, zz], [ob])
                yield
                if last:
                    dstG = oGdnS if is_s else oGdnP
                    if is_s:
                        kb.dma('sp', dstG[:, h].rearrange("s k v -> k s v"), stGs.t[:], reads=[stGs], is_out=True)
                    else:
                        kb.dma('sp', dstG[0, h], stG[h].t[:, :], reads=[stG[h]], is_out=True)

            order = [x_ for x_ in [('r', 0), ('g', 0), ('r', 1), ('g', 1), ('r', 2), ('g', 2), ('r', 3), ('g', 3)] if x_[0] in ONLY[0]]
            mk = lambda kind, hh: (lambda sl_: (rwkv_gen(hh, sl_) if kind == 'r' else gdn_gen(hh, sl_)))
            queue = [mk(k_, h_) for (k_, h_) in order]
            if is_s or NSLOT[0] == 1:
                for f_ in queue:
                    for _ in f_(0):
                        pass
            else:
                active = [None, None]
                rounds = 0
                qs = [[mk(k_, h_) for (k_, h_) in order if k_ == 'r'], [mk(k_, h_) for (k_, h_) in order if k_ == 'g']]
                while qs[0] or qs[1] or any(a_ is not None for a_ in active):
                    rounds += 1
                    for sl_ in (0, 1):
                        if active[sl_] is None and qs[sl_] and not (sl_ == 1 and rounds < OFFS[0]):
                            active[sl_] = qs[sl_].pop(0)(sl_)
                            next(active[sl_])
                            for _ in range(2):
                                if pend:
                                    pend.pop(0)()
                        if active[sl_] is not None:
                            try:
                                next(active[sl_])
                            except StopIteration:
                                active[sl_] = None
            while pend:
                pend.pop(0)()
            if is_s:
                dbgdump('oa', oa.t[:, :, 0:NS], [oa]); dbgdump('ob', ob.t[:, :, 0:NS], [ob])
            CKP(pfx + 'gdn')
            for cb in range(4):
                wa = wloadc('bra', 0, 4, 128, cb * 256, 256)
                wb = wloadc('brb', 0, 4, 128, cb * 256, 256)
                wga = wloadc('gate', 0, 8, 128, cb * 256, 256)
                wgb = wloadc('gate', 0, 8, 128, 1024 + cb * 256, 256)
                for m2 in range(2):
                    m = cb * 2 + m2
                    pa_ = PS(); proj_fm(pa_, pa_.t[:, :N], wa, m2 * 128, 128, N, nk=4, rhsb=oa)
                    pbb = PS(); proj_fm(pbb, pbb.t[:, :N], wb, m2 * 128, 128, N, nk=4, rhsb=ob)
                    pga = PS(); proj_fm(pga, pga.t[:, :N], wga, m2 * 128, 128, N)
                    pgb = PS(); proj_fm(pgb, pgb.t[:, :N], wgb, m2 * 128, 128, N)
                    ga = HB(); act(ga.t[:, :N], pga.t[:, :N], AF.Sigmoid, [pga], [ga])
                    gb = HB(); act(gb.t[:, :N], pgb.t[:, :N], AF.Sigmoid, [pgb], [gb])
                    t1 = HB(); tt(t1.t[:, :N], ga.t[:, :N], pa_.t[:, :N], ALU.mult, [ga, pa_], [t1])
                    t2 = HB(); tt(t2.t[:, :N], gb.t[:, :N], pbb.t[:, :N], ALU.mult, [gb, pbb], [t2])
                    tt(mixT.t[:, m, :N], t1.t[:, :N], t2.t[:, :N], ALU.add, [t1, t2], [mixT])

            def tok_out(wname, nkc_list, lhsb, epilogue):
                for cb in range(4):
                    pbs = [PS() for _ in rows_list]
                    k0 = 0
                    tot = sum(nkc_list)
                    for nk in nkc_list:
                        wt = wloadc(wname, k0 * 128, nk, 128, cb * 256, 256)
                        for tc, rows in enumerate(rows_list):
                            for k in range(nk):
                                kk_ = k0 + k
                                mm(pbs[tc], pbs[tc].t[0:rows, 0:256], lhsb.t[:, kk_, tc * 128:tc * 128 + rows], wt.t[:, k, :], [lhsb, wt], kk_ == 0, kk_ == tot - 1)
                        k0 += nk
                    for tc, rows in enumerate(rows_list):
                        epilogue(tc, rows, cb, pbs[tc])

            def resid_add(tc, rows, cb, pb):
                sl = slice(cb * 256, (cb + 1) * 256)
                tt(xtok.t[0:rows, tc, sl], xtok.t[0:rows, tc, sl], pb.t[0:rows, 0:256], ALU.add, [xtok, pb], [xtok])

            if is_s:
                dbgdump('mix', mixT.t[:, :, 0:NS], [mixT])
            tok_out('out', [8], mixT, resid_add)
            if is_s:
                dbgdump('h1', xtok.t[0:NS, 0, :], [xtok])
            CKP(pfx + 'merge')
            norm_T(N, rows_list, GFFN)
            for cb in range(11):
                wg = wloadc('fg', 0, 8, 128, cb * 256, 256)
                wu = wloadc('fu', 0, 8, 128, cb * 256, 256)
                for m2 in range(2):
                    pg_ = PS(); proj_fm(pg_, pg_.t[:, :N], wg, m2 * 128, 128, N)
                    pu_ = PS(); proj_fm(pu_, pu_.t[:, :N], wu, m2 * 128, 128, N)
                    sg = HB(); act(sg.t[:, :N], pg_.t[:, :N], AF.Silu, [pg_], [sg])
                    tt(hfT.t[:, cb * 2 + m2, :N], sg.t[:, :N], pu_.t[:, :N], ALU.mult, [sg, pu_], [hfT])
            tok_out('fd', [8, 8, 6], hfT, resid_add)
            if is_s:
                dbgdump('h2', xtok.t[0:NS, 0, :], [xtok])
            CKP(pfx + 'ffn')
            norm_T(N, rows_list, GPLE)
            for tc, rows in enumerate(rows_list):
                for k in range(2):
                    pb = PS()
                    tr(pb, pb.t[:, 0:rows], ptok.t[0:rows, tc, k * 128:(k + 1) * 128], rows, [ptok])
                    act(peT.t[:, k, tc * 128:tc * 128 + rows], pb.t[:, 0:rows], AF.Copy, [pb], [peT])
            for cb in range(4):
                wg = wloadc('pg', 0, 8, 128, cb * 256, 256)
                wp = wloadc('pp', 0, 2, 128, cb * 256, 256)
                for tc, rows in enumerate(rows_list):
                    pg_ = PS(); pp_ = PS()
                    for k in range(8):
                        mm(pg_, pg_.t[0:rows, 0:256], uT.t[:, k, tc * 128:tc * 128 + rows], wg.t[:, k, :], [uT, wg], k == 0, k == 7)
                    for k in range(2):
                        mm(pp_, pp_.t[0:rows, 0:256], peT.t[:, k, tc * 128:tc * 128 + rows], wp.t[:, k, :], [peT, wp], k == 0, k == 1)
                    sg = HB(); act(sg.t[0:rows, 0:256], pg_.t[0:rows, 0:256], AF.Sigmoid, [pg_], [sg])
                    tt(sg.t[0:rows, 0:256], sg.t[0:rows, 0:256], pp_.t[0:rows, 0:256], ALU.mult, [sg, pp_], [sg])
                    sl = slice(cb * 256, (cb + 1) * 256)
                    tt(xtok.t[0:rows, tc, sl], xtok.t[0:rows, tc, sl], sg.t[0:rows, 0:256], ALU.add, [xtok, sg], [xtok])
            if is_s:
                dbgdump('h3', xtok.t[0:NS, 0, :], [xtok])
            CKP(pfx + 'ple')
            for tc, rows in enumerate(rows_list):
                act(xs.t[:rows, :], xtok.t[:rows, tc, :], AF.Square, [xtok], [xs, ss], accum_out=ss.t[:rows, tc:tc + 1])
            mr = max(rows_list); TC = len(rows_list)
            rsqrt(ss.t[:mr, 4:4 + TC], ss.t[:mr, 0:TC], 1.0 / D, 1e-6, [ss], [ss])
            for tc, rows in enumerate(rows_list):
                stt(xs.t[:rows, :], xtok.t[:rows, tc, :], ss.t[:rows, 4 + tc:5 + tc], gfb.t[:rows, :], ALU.mult, ALU.mult, [xtok, ss, gfb], [xs])
                kb.dma('sp', ydst[t0 + tc * 128:t0 + tc * 128 + rows, :], xs.t[:rows, :], reads=[xs], is_out=True)

        try:
          for ti in range(n_ptiles):
            do_tile(False, ti * NP)
          CKP('ptiles')
          pb = PS()
          tr(pb, pb.t[0:12, 0:128], car_rkv.t[:, 0:12], 128, [car_rkv])
          o1 = HB(); act(o1.t[0:12, 0:128], pb.t[0:12, 0:128], AF.Copy, [pb], [o1])
          kb.dma('sp', oShiftP[0, 0:1536].rearrange("(g p) -> g p", p=128), o1.t[0:12, 0:128], reads=[o1], is_out=True)
          pb = PS()
          tr(pb, pb.t[0:2, 0:64], car_xw.t[:, 0:2], 64, [car_xw])
          o2 = HB(); act(o2.t[0:2, 0:64], pb.t[0:2, 0:64], AF.Copy, [pb], [o2])
          kb.dma('sp', oShiftP[0, 1536:1664].rearrange("(g p) -> g p", p=64), o2.t[0:2, 0:64], reads=[o2], is_out=True)
          pb = PS()
          tr(pb, pb.t[0:2, 0:128], car_xg.t[:, 0:2], 128, [car_xg])
          o3 = HB(); act(o3.t[0:2, 0:128], pb.t[0:2, 0:128], AF.Copy, [pb], [o3])
          kb.dma('sp', oShiftP[0:1, 1664:1792], o3.t[0:1, 0:128], reads=[o3], is_out=True)
          for g3 in range(3):
              pb = PS()
              for gg in range(4):
                  g = g3 * 4 + gg
                  tr(pb, pb.t[0:4, gg * 128:(gg + 1) * 128], car_cv.t[:, g, :], 128, [car_cv], inc=(gg == 3))
              cst = NB(('Pb', 'Kb', 'Vt')[g3], 512, slot=0)
              act(cst.t[0:4, 0:512], pb.t[0:4, 0:512], AF.Copy, [pb], [cst])
              kb.dma('sp', oConvP[0, :, g3 * 512:(g3 + 1) * 512], cst.t[0:3, 0:512], reads=[cst], is_out=True)
          CKP('pouts')
          kb.barrier()
          do_tile(True, 0)
          CKP('stile')
          for jb in range(4):
              pb = PS()
              if jb < 3:
                  for h_ in range(4):
                      tr(pb, pb.t[0:NS, h_ * 128:(h_ + 1) * 128], rawS.t[:, jb * 4 + h_, :], 128, [rawS], inc=(h_ == 3))
                  act(tokS.t[0:NS, jb * 512:(jb + 1) * 512], pb.t[0:NS, 0:512], AF.Copy, [pb], [tokS])
              else:
                  tr(pb, pb.t[0:NS, 0:64], rawS.t[0:64, 12, :], 64, [rawS], inc=False)
                  tr(pb, pb.t[0:NS, 64:128], rawS.t[0:64, 13, :], 64, [rawS], inc=False)
                  tr(pb, pb.t[0:NS, 128:256], rawS.t[0:128, 14, :], 128, [rawS])
                  act(tokS.t[0:NS, 1536:1792], pb.t[0:NS, 0:256], AF.Copy, [pb], [tokS])
          kb.dma('sp', oShiftS[:, :], tokS.t[0:NS, :], reads=[tokS], is_out=True)
          kb.dma('sp', oConvS[:, 0:2, :], stConv[:, 1:3, :], is_out=True)
          for g3 in range(3):
              pb = PS()
              for gg in range(4):
                  g = g3 * 4 + gg
                  tr(pb, pb.t[0:NS, gg * 128:(gg + 1) * 128], rawC.t[:, g, :], 128, [rawC], inc=(gg == 3))
              act(tokC.t[0:NS, g3 * 512:(g3 + 1) * 512], pb.t[0:NS, 0:512], AF.Copy, [pb], [tokC])
          kb.dma('sp', oConvS[:, 2, :], tokC.t[0:NS, 0:1536], reads=[tokC], is_out=True)
        except _Stop:
            pass
        kb._wait('sp', kb.out_events)
    return nc


def host_consts():
    cm = np.zeros((128, 1024), np.float32)
    cm[:, 0:128] = np.eye(128, dtype=np.float32)
    j = np.arange(64)[:, None]; i = np.arange(64)[None, :]
    for h0 in (0, 64):
        cm[h0:h0 + 64, 128:192] = (j < i)
        cm[h0:h0 + 64, 192:256] = (j <= i)
        cm[h0:h0 + 64, 960:1024] = (j > i)
    cm[0:64, 256:320] = np.where(j <= i, 0.0, -30000.0)
    cm[:, 320:448] = 1.0
    sm = np.ones(512, np.float32); sm[::64] = 0.0
    cm[:, 448:960] = sm[None, :]
    return cm


def prep_weights(inp):
    w_in = inp['w_in'][0]
    perm = []
    for hp in range(4):
        for j in range(3):
            perm.extend(range(j * 512 + hp * 128, j * 512 + hp * 128 + 128))
    perm.extend(range(1536, 1792))
    for h in range(4):
        for j in range(3):
            perm.extend(range(1792 + j * 512 + h * 128, 1792 + j * 512 + h * 128 + 128))
        perm.extend(range(3328 + h * 128, 3328 + h * 128 + 128))
    perm.extend(range(3840, 5896))
    w_in_p = np.ascontiguousarray(w_in[:, perm])
    w_barep = np.ascontiguousarray(np.repeat(w_in[:, 3840:3848], 128, axis=1))
    mu = inp['mu_shift'][0]
    c64 = np.zeros((64, 80), np.float32)
    for h in range(8):
        for j in range(3):
            c64[:, h * 3 + j] = mu[j * 512 + h * 64: j * 512 + h * 64 + 64]
    c64[:, 24] = mu[1536:1600]; c64[:, 25] = mu[1600:1664]
    def hcol(v):
        return np.ascontiguousarray(v.reshape(8, 64).T)
    c64[:, 26:34] = hcol(inp['rw_w0'][0]); c64[:, 34:42] = hcol(inp['rw_a0'][0])
    c64[:, 42:50] = hcol(inp['rw_kk'][0]); c64[:, 50:58] = hcol(inp['rw_ka'][0])
    c64[:, 58:66] = hcol(inp['rw_rk'][0].reshape(-1)); c64[:, 66:74] = hcol(inp['rw_ln_w'][0])
    c128 = np.zeros((128, 64), np.float32)
    c128[:, 0] = mu[1664:1792]
    c128[:, 1:9] = inp['norm_mix'][0].reshape(8, 128).T
    c128[:, 9:17] = inp['norm_ffn'][0].reshape(8, 128).T
    c128[:, 17:25] = inp['norm_ple'][0].reshape(8, 128).T
    c128[:, 25] = inp['gdn_norm'][0]
    c128[:, 26:30] = inp['gdn_a_log'][0][None, :]
    c128[:, 30:34] = inp['gdn_dt_bias'][0][None, :]
    c128[0:64, 40:48] = hcol(inp['rw_ln_b'][0])
    c128r = np.zeros((128, 64), np.float32)
    for hp in range(4):
        for j in range(3):
            c128r[:, hp * 3 + j] = mu[j * 512 + hp * 128: j * 512 + hp * 128 + 128]
    pcol = lambda v: np.ascontiguousarray(np.asarray(v).reshape(4, 128).T)
    c128r[:, 12:16] = pcol(inp['rw_w0'][0]); c128r[:, 16:20] = pcol(inp['rw_a0'][0])
    c128r[:, 20:24] = pcol(inp['rw_kk'][0]); c128r[:, 24:28] = pcol(inp['rw_ka'][0])
    c128r[:, 28:32] = pcol(inp['rw_rk'][0].reshape(-1)); c128r[:, 32:36] = pcol(inp['rw_ln_w'][0]); c128r[:, 36:40] = pcol(inp['rw_ln_b'][0])
    cmat2 = np.zeros((128, 192), np.float32)
    cmat2[0:64, 0:64] = 1.0; cmat2[64:128, 64:128] = 1.0
    cmat2[0:64, 128:192] = np.eye(64); cmat2[64:128, 128:192] = np.eye(64)
    cv = inp['gdn_conv'][0]
    convw = np.zeros((128, 48), np.float32)
    for tap in range(4):
        convw[:, tap * 12:(tap + 1) * 12] = cv[tap].reshape(12, 128).T
    return dict(
        w_in=w_in_p, w_barep=w_barep, c64=c64, c128=c128, convw=convw,
        w2=np.ascontiguousarray(inp['rw_w2'][0]), a2=np.ascontiguousarray(inp['rw_a2'][0]), g2=np.ascontiguousarray(inp['rw_g2'][0]),
        w_bra=np.ascontiguousarray(inp['w_branch_a'][0]), w_brb=np.ascontiguousarray(inp['w_branch_b'][0]),
        w_out=np.ascontiguousarray(inp['w_out'][0]),
        w_fg=np.ascontiguousarray(inp['w_ffn_gate'][0]), w_fu=np.ascontiguousarray(inp['w_ffn_up'][0]),
        w_fd=np.ascontiguousarray(inp['w_ffn_down'][0]),
        w_pg=np.ascontiguousarray(inp['w_ple_gate'][0]), w_pp=np.ascontiguousarray(inp['w_ple_proj'][0]),
        gfin=np.ascontiguousarray(np.broadcast_to(inp['norm_final'][None, :], (128, D))),
        cmat=host_consts(), cmat2=cmat2, c128r=c128r,
    )


def make_in_maps(inp, n_cores, TP):
    shared = prep_weights(inp)
    maps = []
    for c in range(n_cores):
        m = dict(shared)
        m['xP'] = np.ascontiguousarray(inp['x_prompt'][c, :TP])
        m['pP'] = np.ascontiguousarray(inp['p_prompt'][0, c, :TP])
        sl = slice(c * NS, (c + 1) * NS)
        m['xS'] = np.ascontiguousarray(inp['x_sample'][sl, 0])
        m['pS'] = np.ascontiguousarray(inp['p_sample'][0, sl, 0])
        m['stShift'] = np.ascontiguousarray(inp['state_shift'][0, sl, 0])
        m['stWkv'] = np.ascontiguousarray(inp['state_wkv'][0, sl])
        m['stConv'] = np.ascontiguousarray(inp['state_conv'][0, sl])
        m['stGdn'] = np.ascontiguousarray(inp['state_gdn'][0, sl])
        maps.append(m)
    return maps


def gather(results, n_cores):
    cat = lambda k: np.concatenate([r[k] for r in results], axis=0)
    yP = np.stack([r['yP'] for r in results], axis=0)
    yS = cat('yS')[:, None, :]
    return (yP, yS,
            cat('oShiftP')[None, :, None, :], cat('oWkvP')[None], cat('oConvP')[None], cat('oGdnP')[None],
            cat('oShiftS')[None, :, None, :], cat('oWkvS')[None], cat('oConvS')[None], cat('oGdnS')[None])


def kernel(**inputs):
    inp = {k: np.asarray(v) for k, v in inputs.items()}
    n = 8
    TP = inp['x_prompt'].shape[1]
    nc = build(TP // NP)
    maps = make_in_maps(inp, n, TP)
    res = run_bass_kernel_spmd(nc, maps, core_ids=list(range(n)))
    outs = gather(res.results, n)
    return tuple(np.ascontiguousarray(o, dtype=np.float32) for o in outs)
```

```python
import contextlib
import numpy as np
import concourse.bass as bass
import concourse.mybir as mybir
from concourse.bass_utils import run_bass_kernel_spmd
from concourse.alu_op_type import AluOpType as ALU

F32 = mybir.dt.float32
BF16 = mybir.dt.bfloat16
AF = mybir.ActivationFunctionType

D = 1024
NS = 16
NP = 256
TCN = NP // 128
CW = 5896
DFF = 2816
C0 = float(np.exp(-0.5))


class _Stop(Exception):
    pass


STOP = [None]
NSLOT = [2]
ONLY = ['rg']
OFFS = [30]


def CKP(name):
    if STOP[0] == name:
        raise _Stop()


class Buf:
    def __init__(self, t):
        self.t = t
        self.w = None
        self.r = {}


class KB:
    def __init__(self, nc, es):
        self.nc = nc
        self.es = es
        self.eng = {'pe': nc.tensor, 'dve': nc.vector, 'act': nc.scalar, 'pool': nc.gpsimd, 'sp': nc.sync}
        self.semh = {}
        self.cnt = {}
        self.seen = {e: {} for e in self.eng}
        for e in self.eng:
            self.semh[e] = es.enter_context(nc.semaphore("s_" + e))
            self.cnt[e] = 0
        self.ndma = 20
        self.dcur = 0
        for i in range(self.ndma):
            k = "d%d" % i
            self.semh[k] = es.enter_context(nc.semaphore("s_" + k))
            self.cnt[k] = 0
        self.nbuf = 0
        self.out_events = []

    def sb(self, shape, dt=F32, name=None):
        self.nbuf += 1
        t = self.es.enter_context(self.nc.sbuf_tensor(name or ("b%d" % self.nbuf), list(shape), dt))
        return Buf(t)

    def psb(self, name):
        t = self.es.enter_context(self.nc.psum_tensor(name, [128, 512], F32))
        return Buf(t)

    def _wait(self, e, deps):
        engine = self.eng[e]
        for (s, v) in deps:
            if self.seen[e].get(s, 0) >= v:
                continue
            engine.wait_ge(self.semh[s], v)
            self.seen[e][s] = v

    def _deps(self, e, reads, writes):
        deps = []
        for b in reads:
            if b.w is not None:
                if not (e == 'pe' and b.w[0] == 'pe'):
                    deps.append(b.w)
        for b in writes:
            if b.w is not None and b.w[0] != e:
                deps.append(b.w)
            for s, v in b.r.items():
                if s != e:
                    deps.append((s, v))
        return deps

    def _record(self, ev, reads, writes):
        for b in reads:
            b.r[ev[0]] = max(b.r.get(ev[0], 0), ev[1])
        for b in writes:
            b.w = ev
            b.r = {}

    def op(self, e, fn, reads=(), writes=(), inc=True):
        self._wait(e, self._deps(e, reads, writes))
        ins = fn(self.eng[e])
        if inc:
            self.cnt[e] += 1
            ins.then_inc(self.semh[e], 1)
            ev = (e, self.cnt[e])
        else:
            ev = (e, self.cnt[e] + 1)
        self._record(ev, reads, writes)
        return ev

    def dma(self, q, out, in_, reads=(), writes=(), is_out=False):
        k = "d%d" % self.dcur
        self.dcur = (self.dcur + 1) % self.ndma
        deps = self._deps(q, reads, writes)
        if self.cnt[k] > 0:
            deps.append((k, self.cnt[k]))
        self._wait(q, deps)
        self.cnt[k] += 16
        with self.nc.allow_non_contiguous_dma(reason="layout"):
            self.eng[q].dma_start(out=out, in_=in_).then_inc(self.semh[k], 16)
        ev = (k, self.cnt[k])
        self._record(ev, reads, writes)
        if is_out:
            self.out_events.append(ev)
        return ev

    def barrier(self):
        evs = []
        for e in self.eng:
            if self.cnt[e] > 0:
                evs.append((e, self.cnt[e]))
        for i in range(self.ndma):
            k = "d%d" % i
            if self.cnt[k] > 0:
                evs.append((k, self.cnt[k]))
        for e in self.eng:
            self._wait(e, [ev for ev in evs if ev[0] != e])


def build(n_ptiles, dbg=()):
    TP = n_ptiles * NP
    nc = bass.Bass("TRN2", target_bir_lowering=False)

    def din(name, shape):
        return nc.dram_tensor(name, list(shape), F32, kind="ExternalInput").ap()

    def dout(name, shape):
        return nc.dram_tensor(name, list(shape), F32, kind="ExternalOutput").ap()

    xP = din("xP", [TP, D]); xS = din("xS", [NS, D])
    pP = din("pP", [TP, 256]); pS = din("pS", [NS, 256])
    stShift = din("stShift", [NS, 1792]); stWkv = din("stWkv", [NS, 8, 64, 64])
    stConv = din("stConv", [NS, 3, 1536]); stGdn = din("stGdn", [NS, 4, 128, 128])
    w_in = din("w_in", [D, CW]); w_barep = din("w_barep", [D, 1024])
    c64 = din("c64", [64, 80]); c128 = din("c128", [128, 64]); convw = din("convw", [128, 48])
    w2 = din("w2", [64, 512]); a2 = din("a2", [64, 512]); g2 = din("g2", [128, 512])
    w_bra = din("w_bra", [512, D]); w_brb = din("w_brb", [512, D]); w_out = din("w_out", [D, D])
    w_fg = din("w_fg", [D, DFF]); w_fu = din("w_fu", [D, DFF]); w_fd = din("w_fd", [DFF, D])
    w_pg = din("w_pg", [D, D]); w_pp = din("w_pp", [256, D])
    gfin = din("gfin", [128, D]); cmat = din("cmat", [128, 1024]); cmat2 = din("cmat2", [128, 192]); c128rd = din("c128r", [128, 64])
    yP = dout("yP", [TP, D]); yS = dout("yS", [NS, D])
    oShiftP = dout("oShiftP", [1, 1792]); oWkvP = dout("oWkvP", [1, 8, 64, 64])
    oConvP = dout("oConvP", [1, 3, 1536]); oGdnP = dout("oGdnP", [1, 4, 128, 128])
    oShiftS = dout("oShiftS", [NS, 1792]); oWkvS = dout("oWkvS", [NS, 8, 64, 64])
    oConvS = dout("oConvS", [NS, 3, 1536]); oGdnS = dout("oGdnS", [NS, 4, 128, 128])
    dbg_outs = {}
    for nm, shp in dbg:
        dbg_outs[nm] = nc.dram_tensor("dbg_" + nm, list(shp), BF16 if nm in ("oa", "ob", "mix") else F32, kind="ExternalOutput").ap()

    es = contextlib.ExitStack()
    with es:
        kb = KB(nc, es)
        op = kb.op
        cm = kb.sb([128, 1024]); kb.dma('sp', cm.t[:], cmat, writes=[cm])
        ident = cm.t[:, 0:128]
        maskS = cm.t[0:64, 128:192]
        maskI = cm.t[0:64, 192:256]
        negI = cm.t[0:64, 256:320]
        ones = cm.t[:, 320:448]
        scanm = cm.t[:, 448:960]
        maskS2 = cm.t[:, 128:192]; maskI2 = cm.t[:, 192:256]; maskL2 = cm.t[:, 960:1024]
        c64b = kb.sb([64, 80]); kb.dma('sp', c64b.t[:], c64, writes=[c64b])
        c128b = kb.sb([128, 64]); kb.dma('sp', c128b.t[:], c128, writes=[c128b])
        cvw = kb.sb([128, 48]); kb.dma('sp', cvw.t[:], convw, writes=[cvw])
        w2b = kb.sb([64, 512], BF16); kb.dma('pool', w2b.t[:], w2, writes=[w2b])
        a2b = kb.sb([64, 512], BF16); kb.dma('pool', a2b.t[:], a2, writes=[a2b])
        g2b = kb.sb([128, 512], BF16); kb.dma('pool', g2b.t[:], g2, writes=[g2b])
        gfb = kb.sb([128, D]); kb.dma('sp', gfb.t[:], gfin, writes=[gfb])
        MU_RKV, MU_XW, MU_XAA, W0, A0, KKc, KAc, RKc, LNW = 0, 24, 25, 26, 34, 42, 50, 58, 66
        MU_XG, GMIX, GFFN, GPLE, GDNN, ALOG, DTB, LNB = 0, 1, 9, 17, 25, 26, 30, 40
        c128r = kb.sb([128, 64]); kb.dma('sp', c128r.t[:], c128rd, writes=[c128r])
        cm2 = kb.sb([128, 192]); kb.dma('sp', cm2.t[:], cmat2, writes=[cm2])
        bones = cm2.t[:, 0:128]
        cm16 = kb.sb([128, 128], BF16)
        op('dve', lambda e: e.tensor_copy(out=cm16.t[:], in_=cm.t[:, 0:128]), [cm], [cm16])
        id16 = cm16.t[:, :]
        on16 = kb.sb([128, 256], BF16)
        op('dve', lambda e: e.tensor_copy(out=on16.t[:, 0:128], in_=cm.t[:, 320:448]), [cm], [on16])
        op('dve', lambda e: e.tensor_copy(out=on16.t[:, 128:256], in_=cm2.t[:, 0:128]), [cm2], [on16])
        ones16 = on16.t[:, 0:128]; bones16 = on16.t[:, 128:256]
        ident2 = cm2.t[:, 128:192]
        W0r, A0r, KKr, KAr, RKr, LNWr, LNBr = 12, 16, 20, 24, 28, 32, 36
        omu128r = kb.sb([128, 12])
        op('dve', lambda e: e.tensor_scalar(out=omu128r.t[:], in0=c128r.t[:, 0:12], scalar1=-1.0, scalar2=1.0, op0=ALU.mult, op1=ALU.add), [c128r], [omu128r])
        omu64 = kb.sb([64, 26])
        op('dve', lambda e: e.tensor_scalar(out=omu64.t[:], in0=c64b.t[:, 0:26], scalar1=-1.0, scalar2=1.0, op0=ALU.mult, op1=ALU.add), [c64b], [omu64])
        omu128 = kb.sb([128, 1])
        op('dve', lambda e: e.tensor_scalar(out=omu128.t[:], in0=c128b.t[:, 0:1], scalar1=-1.0, scalar2=1.0, op0=ALU.mult, op1=ALU.add), [c128b], [omu128])
        nexpA = kb.sb([128, 4])
        op('act', lambda e: e.activation(out=nexpA.t[:], in_=c128b.t[:, ALOG:ALOG + 4], func=AF.Exp), [c128b], [nexpA])
        op('dve', lambda e: e.tensor_scalar(out=nexpA.t[:], in0=nexpA.t[:], scalar1=-1.0, scalar2=None, op0=ALU.mult), [nexpA], [nexpA])

        ps = [kb.psb("ps%d" % i) for i in range(8)]
        pcur = [0]

        def PS():
            b = ps[pcur[0]]
            pcur[0] = (pcur[0] + 1) % 8
            return b
        pscur = [0, 0]

        def PSslot(slot):
            b = ps[slot * 3 + pscur[slot]]
            pscur[slot] = (pscur[slot] + 1) % 3
            return b

        NW = 5
        wring = [kb.sb([128, 8, 256], BF16, "wr%d" % i) for i in range(NW)]
        wcur = [0]

        def wload(w, r0, nk, pk, c0, ncols):
            b = wring[wcur[0]]
            wcur[0] = (wcur[0] + 1) % NW
            src = w[r0:r0 + nk * pk, c0:c0 + ncols].rearrange("(k p) c -> p k c", p=pk)
            kb.dma('pool', b.t[0:pk, 0:nk, 0:ncols], src, writes=[b])
            return b

        conv = {}

        def convert(name, src, R, Cc):
            dst = nc.dram_tensor("cw_" + name, [R, Cc], BF16, kind="Internal").ap()
            b = Buf(dst)
            for c0 in range(0, Cc, 1024):
                w_ = min(1024, Cc - c0)
                kb.dma('pool', dst[:, c0:c0 + w_], src[:, c0:c0 + w_], writes=[b])
            conv[name] = (dst, b)

        def wloadc(name, r0, nk, pk, c0, ncols):
            dst_, cb_ = conv[name]
            b = wring[wcur[0]]
            wcur[0] = (wcur[0] + 1) % NW
            src = dst_[r0:r0 + nk * pk, c0:c0 + ncols].rearrange("(k p) c -> p k c", p=pk)
            kb.dma('pool', b.t[0:pk, 0:nk, 0:ncols], src, reads=[cb_], writes=[b])
            return b

        RW = 11500
        region = kb.sb([128, RW], name="region")
        bumpP = [0]; bumpS = [0]

        def rview(bump, width, pat=None, parts=128, **kw):
            a_ = bump[0]; bump[0] += width
            assert bump[0] <= RW, (bump[0], RW)
            ap = region.t[0:parts, a_:a_ + width]
            if pat is not None:
                ap = ap.rearrange(pat, **kw)
            return Buf(ap)

        xtok = kb.sb([128, TCN, D], name="xtok")
        xs = kb.sb([128, D], name="xs")
        uT = kb.sb([128, 8, NP], BF16, name="uT")
        ss = kb.sb([128, 8], name="ss")
        car_rkv = kb.sb([128, 12], name="car_rkv"); car_xw = kb.sb([64, 2], name="car_xw"); car_xg = kb.sb([128, 2], name="car_xg")
        car_cv = kb.sb([128, 12, 4], name="car_cv")
        for b_ in (car_rkv, car_xw, car_xg, car_cv):
            op('dve', lambda e, b_=b_: e.memset(b_.t[:], 0.0), [], [b_])
        stR = [kb.sb([128, 64], name="stR%d" % i) for i in range(4)]
        stG = [kb.sb([128, 128], name="stG%d" % i) for i in range(4)]
        for b_ in stR + stG:
            op('dve', lambda e, b_=b_: e.memset(b_.t[:], 0.0), [], [b_])
        stRs = rview(bumpS, NS * 64, "p (s v) -> p s v", v=64)
        stGs = rview(bumpS, NS * 128, "p (s v) -> p s v", v=128)
        tw = kb.sb([64, NP], BF16, name="tw"); xaaS = kb.sb([64, NP], BF16, name="xaaS"); sgS = kb.sb([128, NP], BF16, name="sgS")
        oa = kb.sb([128, 4, NP], BF16, name="oa"); ob = kb.sb([128, 4, NP], BF16, name="ob")
        mixT = kb.sb([128, 8, NP], BF16, name="mixT")
        hfT = kb.sb([128, 22, NP], BF16, name="hfT")
        peT = kb.sb([128, 2, NP], BF16, name="peT")
        ptok = kb.sb([128, TCN, 256], name="ptok")
        prevS = rview(bumpS, 16 * NS, "p (g s) -> p g s", s=NS)
        rawS = rview(bumpS, 16 * NS, "p (g s) -> p g s", s=NS)
        histS = rview(bumpS, 36 * NS, "p (g s) -> p g s", s=NS)
        rawC = rview(bumpS, 12 * NS, "p (g s) -> p g s", s=NS)
        tokS = rview(bumpS, 1792, parts=NS)
        tokC = rview(bumpS, 1536, parts=NS)
        NHB = 7
        hb = [[kb.sb([128, NP + 4], name="hb%d" % i) for i in range(NHB)], [rview(bumpP, NP + 4) for i in range(6)]]
        hcur = [0, 0]

        def HB(slot=0):
            r_ = hb[slot]
            b = r_[hcur[slot]]
            hcur[slot] = (hcur[slot] + 1) % len(r_)
            return b
        named = {}

        def NB(name, width=NP + 4, slot=0):
            key = (name, slot)
            if key not in named:
                if slot == 0:
                    named[key] = kb.sb([128, width], name="n_" + name)
                elif slot == 1:
                    named[key] = rview(bumpP, width)
                else:
                    named[key] = rview(bumpS, width)
            return named[key]

        def act(out, in_, func, reads, writes, bias=None, scale=None, **kw):
            kws = dict(kw)
            if bias is not None:
                kws['bias'] = bias
            if scale is not None:
                kws['scale'] = scale
            return op('act', lambda e: e.activation(out=out, in_=in_, func=func, **kws), reads, writes)

        def tt(out, in0, in1, o, reads, writes, eng='dve'):
            return op(eng, lambda e: e.tensor_tensor(out=out, in0=in0, in1=in1, op=o), reads, writes)

        def ts(out, in0, s1, s2, o0, o1, reads, writes, eng='dve'):
            if o1 is None:
                return op(eng, lambda e: e.tensor_scalar(out=out, in0=in0, scalar1=s1, scalar2=None, op0=o0), reads, writes)
            return op(eng, lambda e: e.tensor_scalar(out=out, in0=in0, scalar1=s1, scalar2=s2, op0=o0, op1=o1), reads, writes)

        def stt(out, in0, sc, in1, o0, o1, reads, writes):
            return op('dve', lambda e: e.scalar_tensor_tensor(out=out, in0=in0, scalar=sc, in1=in1, op0=o0, op1=o1), reads, writes)

        def mm(pb, out, lhsT, rhs, reads, start, stop):
            return op('pe', lambda e: e.matmul(out, lhsT, rhs, start=start, stop=stop), reads, [pb], inc=stop)

        def tr(pb, out, in_, rows, reads, inc=True):
            return op('pe', lambda e: e.transpose(out, in_, ident[0:rows, 0:rows]), list(reads) + [cm], [pb], inc=inc)

        def rsqrt(out, in_, scale, eps, reads, writes):
            act(out, in_, AF.Ln, reads, writes, bias=eps, scale=scale)
            act(out, out, AF.Exp, writes, writes, scale=-0.5)

        def dbgdump(name, ap_sb, bufs):
            if name in dbg_outs:
                kb.dma('sp', dbg_outs[name], ap_sb, reads=bufs, is_out=True)

        def norm_T(N, rows_list, gcol):
            TC = len(rows_list)
            for tc, rows in enumerate(rows_list):
                act(xs.t[:rows, :], xtok.t[:rows, tc, :], AF.Square, [xtok], [xs, ss], accum_out=ss.t[:rows, tc:tc + 1])
            mr = max(rows_list)
            rsqrt(ss.t[:mr, 4:4 + TC], ss.t[:mr, 0:TC], 1.0 / D, 1e-6, [ss], [ss])
            pbs = [PS() for _ in range(8)]
            for tc, rows in enumerate(rows_list):
                ts(xs.t[:rows, :], xtok.t[:rows, tc, :], ss.t[:rows, 4 + tc:5 + tc], None, ALU.mult, None, [xtok, ss], [xs])
                for k in range(8):
                    tr(pbs[k], pbs[k].t[:, tc * 128:tc * 128 + rows], xs.t[:rows, k * 128:(k + 1) * 128], rows, [xs])
            for k in range(8):
                if k % 2 == 0:
                    ts(uT.t[:, k, :N], pbs[k].t[:, :N], c128b.t[:, gcol + k:gcol + k + 1], None, ALU.mult, None, [pbs[k], c128b], [uT])
                else:
                    act(uT.t[:, k, :N], pbs[k].t[:, :N], AF.Copy, [pbs[k], c128b], [uT], scale=c128b.t[:, gcol + k:gcol + k + 1])

        def proj_fm(pb, out_ap, wt, coff, M, N, nk=8, rhsb=None, pk=128):
            rb = rhsb or uT
            for k in range(nk):
                mm(pb, out_ap, wt.t[0:pk, k, coff:coff + M], rb.t[0:pk, k, :N], [wt, rb], k == 0, k == nk - 1)

        def shift(P, N, pb, psap, mu, omu, carry, out_b, out_ap, is_s, prev_ap=None, raw_ap=None, raw_b=None, tmp=HB):
            t1 = tmp()
            if is_s:
                act(t1.t[:P, :N], prev_ap, AF.Copy, [prevS], [t1], scale=mu)
                act(raw_ap, psap, AF.Copy, [pb], [raw_b])
            else:
                act(t1.t[:P, 1:N], psap[:, 0:N - 1], AF.Copy, [pb], [t1], scale=mu)
                act(t1.t[:P, 0:1], carry[1], AF.Copy, [carry[0]], [t1], scale=mu)
                act(carry[1], psap[:, N - 1:N], AF.Copy, [pb], [carry[0]])
            stt(out_ap, psap, omu, t1.t[:P, :N], ALU.mult, ALU.add, [pb, t1], [out_b])

        named16 = {}

        def NB16(name, width=NP + 4, slot=0):
            key = (name, slot)
            if key not in named16:
                if slot == 0:
                    named16[key] = kb.sb([128, width], BF16, name="h_" + name)
                else:
                    a_ = bumpP[0]; bumpP[0] += (width + 1) // 2
                    assert bumpP[0] <= RW
                    named16[key] = Buf(region.t[:, a_:a_ + (width + 1) // 2].bitcast(BF16))
            return named16[key]
        hb16 = [[NB16("r16_%d" % i, 132, sl_) for i in range(4)] for sl_ in (0, 1)]
        hbw = [[NB16("w16_%d" % i, NP + 4, sl_) for i in range(2)] for sl_ in (0, 1)]
        hbwcur = [0, 0]

        def HBW(slot=0):
            b = hbw[slot][hbwcur[slot]]
            hbwcur[slot] = (hbwcur[slot] + 1) % 2
            return b
        h16cur = [0, 0]

        def HB16(slot):
            b = hb16[slot][h16cur[slot]]
            h16cur[slot] = (h16cur[slot] + 1) % 4
            return b

        natS = rview(bumpS, NS * 128, "p (s h k) -> p s h k", parts=64, h=2, k=64)

        def scan(slot, Kd, Vd, N, C, nseq, QsT, RsT, PbT, KbT, VT, A, WcB, wc_ap, st_b, st_ap, ypb):
            nch = N // C
            cps = nch // nseq
            W = nch * C

            def tmaj(srcb, Fd, name):
                dst = NB16(name, 512, slot=slot)
                per = 512 // Fd
                for c0 in range(0, nch, per):
                    pb = PSslot(slot)
                    n = min(per, nch - c0)
                    for c in range(c0, c0 + n):
                        tr(pb, pb.t[0:C, (c - c0) * Fd:(c - c0 + 1) * Fd], srcb.t[0:Fd, c * C:(c + 1) * C], Fd, [srcb], inc=(c == c0 + n - 1))
                    act(dst.t[0:C, c0 * Fd:(c0 + n) * Fd], pb.t[0:C, 0:n * Fd], AF.Copy, [pb], [dst])
                return dst
            pre_t = (nch * max(Kd, Vd) <= 512)
            if pre_t:
                pbt = tmaj(PbT, Kd, 'Pb'); kbt = tmaj(KbT, Kd, 'Kb'); vt = tmaj(VT, Vd, 'Vt')
                gP = lambda c: pbt.t[0:C, c * Kd:(c + 1) * Kd]
                gK = lambda c: kbt.t[0:C, c * Kd:(c + 1) * Kd]
                gV = lambda c: vt.t[0:C, c * Vd:(c + 1) * Vd]
            yield
            TinvT = None
            if C > 1:
                NTb = A['NT']; NT16 = A['NT16']
                Xb = NB16('X0', slot=slot); pb = PSslot(slot)
                for c in range(nch):
                    tr(pb, pb.t[0:C, c * C:(c + 1) * C], NTb.t[0:C, c * C:(c + 1) * C], C, [NTb], inc=(c == nch - 1))
                act(Xb.t[0:C, 0:W], pb.t[0:C, 0:W], AF.Copy, [pb], [Xb])
                XTb = NT16
                PT = NB('PT0', slot=slot)
                tt(PT.t[0:C, 0:W].rearrange("p (c i) -> p c i", i=C), NTb.t[0:C, 0:W].rearrange("p (c i) -> p c i", i=C),
                   cm.t[0:C, 0:C].unsqueeze(1).broadcast_to([C, nch, C]), ALU.add, [NTb, cm], [PT])
                PTh = NB16('PTh0', slot=slot)
                act(PTh.t[0:C, 0:W], PT.t[0:C, 0:W], AF.Copy, [PT], [PTh])
                yield
                nlev = {64: 5, 2: 0}[C]
                for lv in range(1, nlev + 1):
                    pbx = PSslot(slot)
                    for c in range(nch):
                        sl = slice(c * C, (c + 1) * C)
                        op('pe', lambda e, sl=sl, pbx=pbx, XTb=XTb, Xb=Xb: e.matmul(pbx.t[0:C, sl], XTb.t[0:C, sl], Xb.t[0:C, sl], start=True, stop=True), [XTb, Xb], [pbx], inc=(c == nch - 1))
                    Xn = NB16('X%d' % (lv % 2), slot=slot)
                    XTn = None
                    if lv < nlev:
                        pbt_ = PSslot(slot)
                        for c in range(nch):
                            sl = slice(c * C, (c + 1) * C)
                            op('pe', lambda e, sl=sl, pbt_=pbt_, XTb=XTb, Xb=Xb: e.matmul(pbt_.t[0:C, sl], Xb.t[0:C, sl], XTb.t[0:C, sl], start=True, stop=True), [XTb, Xb], [pbt_], inc=(c == nch - 1))
                    act(Xn.t[0:C, 0:W], pbx.t[0:C, 0:W], AF.Copy, [pbx], [Xn])
                    yield
                    if lv < nlev:
                        XTn = NB16('XT%d' % (lv % 2), slot=slot)
                        ts(XTn.t[0:C, 0:W], pbt_.t[0:C, 0:W], 1.0, None, ALU.mult, None, [pbt_], [XTn])
                    pbp = PSslot(slot)
                    for c in range(nch):
                        sl = slice(c * C, (c + 1) * C)
                        op('pe', lambda e, sl=sl, pbp=pbp, Xn=Xn, PTh=PTh: e.matmul(pbp.t[0:C, sl], Xn.t[0:C, sl], PTh.t[0:C, sl], start=True, stop=True), [Xn, PTh], [pbp], inc=(c == nch - 1))
                    PTn = NB('PT%d' % (lv % 2), slot=slot)
                    tt(PTn.t[0:C, 0:W], PT.t[0:C, 0:W], pbp.t[0:C, 0:W], ALU.add, [PT, pbp], [PTn])
                    PThn = NB16('PTh%d' % (lv % 2), slot=slot)
                    act(PThn.t[0:C, 0:W], PTn.t[0:C, 0:W], AF.Copy, [PTn], [PThn])
                    PT = PTn; PTh = PThn
                    yield
                    Xb = Xn
                    if lv < nlev:
                        XTb = XTn
                PT = PTh
                TinvT = PT
            for s in range(nseq):
                for cc in range(cps):
                    c = s * cps + cc
                    sl = slice(c * C, (c + 1) * C)
                    stap = st_ap(s)
                    if pre_t:
                        aP = gP(c); aK = gK(c); aV = gV(c)
                    else:
                        pb = PSslot(slot)
                        tr(pb, pb.t[0:C, 0:Kd], PbT.t[0:Kd, sl], Kd, [PbT], inc=False)
                        tr(pb, pb.t[0:C, 128:128 + Kd], KbT.t[0:Kd, sl], Kd, [KbT], inc=False)
                        tr(pb, pb.t[0:C, 256:256 + Vd], VT.t[0:Vd, sl], Vd, [VT])
                        row = HB(slot)
                        act(row.t[0:C, 0:256], pb.t[0:C, 0:256], AF.Copy, [pb], [row])
                        rowv = HB(slot)
                        act(rowv.t[0:C, 0:128], pb.t[0:C, 256:384], AF.Copy, [pb], [rowv])
                        aP = row.t[0:C, 0:Kd]; aK = row.t[0:C, 128:128 + Kd]; aV = rowv.t[0:C, 0:Vd]
                        pbt = row; kbt = row; vt = rowv
                    zp = PSslot(slot)
                    if C > 1:
                        op('pe', lambda e: e.matmul(zp.t[0:C, 0:Vd], QsT.t[0:Kd, sl], stap, start=True, stop=False), [QsT, st_b], [zp], inc=False)
                        op('pe', lambda e: e.matmul(zp.t[0:C, 0:Vd], A['LakT'].t[0:C, sl], aV, start=False, stop=True), [A['LakT'], vt], [zp])
                        yield
                        Zs = HB16(slot)
                        act(Zs.t[0:C, 0:Vd], zp.t[0:C, 0:Vd], AF.Copy, [zp], [Zs])
                        up = PSslot(slot)
                        op('pe', lambda e: e.matmul(up.t[0:C, 0:Vd], TinvT.t[0:C, sl], Zs.t[0:C, 0:Vd], start=True, stop=True), [TinvT, Zs], [up])
                        yield
                        U = HB16(slot)
                        ts(U.t[0:C, 0:Vd], up.t[0:C, 0:Vd], 1.0, None, ALU.mult, None, [up], [U])
                    else:
                        op('pe', lambda e: e.matmul(zp.t[0:C, 0:Vd], QsT.t[0:Kd, sl], stap, start=True, stop=True), [QsT, st_b], [zp])
                        U = HB(slot)
                        act(U.t[0:C, 0:Vd], zp.t[0:C, 0:Vd], AF.Copy, [zp], [U])
                    op('pe', lambda e: e.matmul(ypb.t[0:Vd, sl], stap, RsT.t[0:Kd, sl], start=True, stop=False), [st_b, RsT], [ypb], inc=False)
                    op('pe', lambda e: e.matmul(ypb.t[0:Vd, sl], U.t[0:C, 0:Vd], A['ArbT'].t[0:C, sl], start=False, stop=False), [U, A['ArbT']], [ypb], inc=False)
                    op('pe', lambda e: e.matmul(ypb.t[0:Vd, sl], aV, A['ArkT'].t[0:C, sl], start=False, stop=True), [vt, A['ArkT']], [ypb])
                    yield
                    sp_ = PSslot(slot)
                    op('pe', lambda e: e.matmul(sp_.t[0:Kd, 0:Vd], aP, U.t[0:C, 0:Vd], start=True, stop=False), [pbt, U], [sp_], inc=False)
                    op('pe', lambda e: e.matmul(sp_.t[0:Kd, 0:Vd], aK, aV, start=False, stop=True), [kbt, vt], [sp_])
                    stt(stap, stap, wc_ap(c), sp_.t[0:Kd, 0:Vd], ALU.mult, ALU.add, [st_b, WcB, sp_], [st_b])
                    yield

        def scan_sample(Kd, Vd, QsT, RsT, PbT, KbT, VT, arb, ark, WcB, wc16, stX):
            n = NS
            QR = NB('sQR', 36, slot=2); PK = NB('sPK', 36, slot=2); UV = NB('sUV', 36, slot=2)
            q3 = QR.t[0:Kd, 0:2 * n].rearrange("p (s two) -> p s two", two=2)
            p3 = PK.t[0:Kd, 0:2 * n].rearrange("p (s two) -> p s two", two=2)
            u3 = UV.t[0:Vd, 0:2 * n].rearrange("p (s two) -> p s two", two=2)
            act(q3[:, :, 0], QsT.t[0:Kd, 0:n], AF.Copy, [QsT], [QR])
            ts(q3[:, :, 1], RsT.t[0:Kd, 0:n], 1.0, None, ALU.mult, None, [RsT], [QR])
            act(p3[:, :, 0], PbT.t[0:Kd, 0:n], AF.Copy, [PbT], [PK])
            ts(p3[:, :, 1], KbT.t[0:Kd, 0:n], 1.0, None, ALU.mult, None, [KbT], [PK])
            ts(u3[:, :, 1], VT.t[0:Vd, 0:n], 1.0, None, ALU.mult, None, [VT], [UV])
            pq = PS()
            for s_ in range(n):
                op('pe', lambda e, s_=s_: e.matmul(pq.t[0:Vd, 2 * s_:2 * s_ + 2], stX.t[:, s_, :], QR.t[0:Kd, 2 * s_:2 * s_ + 2], start=True, stop=True),
                   [stX, QR], [pq], inc=(s_ == n - 1))
            pq3 = pq.t[0:Vd, 0:2 * n].rearrange("p (s two) -> p s two", two=2)
            act(u3[:, :, 0], pq3[:, :, 0], AF.Copy, [pq], [UV])
            Y = HB(); t2 = HB()
            tt(Y.t[0:Vd, 0:n], pq3[:, :, 0], arb.t[0:Vd, 0:n], ALU.mult, [pq, arb], [Y])
            tt(t2.t[0:Vd, 0:n], VT.t[0:Vd, 0:n], ark.t[0:Vd, 0:n], ALU.mult, [VT, ark], [t2])
            tt(Y.t[0:Vd, 0:n], Y.t[0:Vd, 0:n], t2.t[0:Vd, 0:n], ALU.add, [Y, t2], [Y])
            tt(Y.t[0:Vd, 0:n], Y.t[0:Vd, 0:n], pq3[:, :, 1], ALU.add, [Y, pq], [Y])
            gs = 512 // max(Kd, Vd)
            pkr = NB('sPKr', 512, slot=2); uvr = NB('sUVr', 512, slot=2)
            for g0 in range(0, n, gs):
                pt = PS()
                for j in range(gs):
                    s_ = g0 + j
                    tr(pt, pt.t[0:2, j * Kd:(j + 1) * Kd], PK.t[0:Kd, 2 * s_:2 * s_ + 2], Kd, [PK], inc=(j == gs - 1))
                act(pkr.t[0:2, 0:gs * Kd], pt.t[0:2, 0:gs * Kd], AF.Copy, [pt], [pkr])
                pu = PS()
                for j in range(gs):
                    s_ = g0 + j
                    tr(pu, pu.t[0:2, j * Vd:(j + 1) * Vd], UV.t[0:Vd, 2 * s_:2 * s_ + 2], Vd, [UV], inc=(j == gs - 1))
                ts(uvr.t[0:2, 0:gs * Vd], pu.t[0:2, 0:gs * Vd], 1.0, None, ALU.mult, None, [pu], [uvr])
                pp = PS()
                for j in range(gs):
                    op('pe', lambda e, j=j: e.matmul(pp.t[0:Kd, j * Vd:(j + 1) * Vd], pkr.t[0:2, j * Kd:(j + 1) * Kd], uvr.t[0:2, j * Vd:(j + 1) * Vd], start=True, stop=True),
                       [pkr, uvr], [pp], inc=(j == gs - 1))
                sv = stX.t[:, g0:g0 + gs, :]
                tt(sv, sv, wc16[:, g0:g0 + gs].unsqueeze(2).broadcast_to([Kd, gs, Vd]), ALU.mult, [stX, WcB], [stX])
                tt(sv, sv, pp.t[0:Kd, 0:gs * Vd].rearrange("p (s v) -> p s v", v=Vd), ALU.add, [stX, pp], [stX])
            return Y

        def scan2(slot, nh, Kd, Vd, N, C, QsT, RsT, PbT, KbT, VT, A, WcB, wc_ap, st_b, ypb):
            nch = N // C
            nb = nh * nch
            W = nb * C

            def tmaj(srcb, name):
                dst = NB16(name, 516, slot=slot)
                pb = PSslot(slot)
                for c in range(nch):
                    tr(pb, pb.t[0:C, c * 128:(c + 1) * 128], srcb.t[0:128, c * C:(c + 1) * C], 128, [srcb], inc=(c == nch - 1))
                act(dst.t[0:C, 0:nch * 128], pb.t[0:C, 0:nch * 128], AF.Copy, [pb], [dst])
                return dst
            pbt = tmaj(PbT, 'Pb'); kbt = tmaj(KbT, 'Kb'); vt = tmaj(VT, 'Vt')
            yield
            NTb = A['NT']; NT16 = A['NT16']
            Xb = NB16('X0', 516, slot=slot); pb = PSslot(slot)
            for b in range(nb):
                tr(pb, pb.t[0:C, b * C:(b + 1) * C], NTb.t[0:C, b * C:(b + 1) * C], C, [NTb], inc=(b == nb - 1))
            act(Xb.t[0:C, 0:W], pb.t[0:C, 0:W], AF.Copy, [pb], [Xb])
            XTb = NT16
            PT = NB('PT0', 516, slot=slot)
            tt(PT.t[0:C, 0:W].rearrange("p (c i) -> p c i", i=C), NTb.t[0:C, 0:W].rearrange("p (c i) -> p c i", i=C),
               cm.t[0:C, 0:C].unsqueeze(1).broadcast_to([C, nb, C]), ALU.add, [NTb, cm], [PT])
            PTh = NB16('PTh0', 516, slot=slot)
            act(PTh.t[0:C, 0:W], PT.t[0:C, 0:W], AF.Copy, [PT], [PTh])
            yield
            nlev = 5
            for lv in range(1, nlev + 1):
                pbx = PSslot(slot)
                for b in range(nb):
                    sl = slice(b * C, (b + 1) * C)
                    op('pe', lambda e, sl=sl, pbx=pbx, XTb=XTb, Xb=Xb: e.matmul(pbx.t[0:C, sl], XTb.t[0:C, sl], Xb.t[0:C, sl], start=True, stop=True), [XTb, Xb], [pbx], inc=(b == nb - 1))
                Xn = NB16('X%d' % (lv % 2), 516, slot=slot)
                XTn = None
                if lv < nlev:
                    pbt_ = PSslot(slot)
                    for b in range(nb):
                        sl = slice(b * C, (b + 1) * C)
                        op('pe', lambda e, sl=sl, pbt_=pbt_, XTb=XTb, Xb=Xb: e.matmul(pbt_.t[0:C, sl], Xb.t[0:C, sl], XTb.t[0:C, sl], start=True, stop=True), [XTb, Xb], [pbt_], inc=(b == nb - 1))
                act(Xn.t[0:C, 0:W], pbx.t[0:C, 0:W], AF.Copy, [pbx], [Xn])
                yield
                if lv < nlev:
                    XTn = NB16('XT%d' % (lv % 2), 516, slot=slot)
                    ts(XTn.t[0:C, 0:W], pbt_.t[0:C, 0:W], 1.0, None, ALU.mult, None, [pbt_], [XTn])
                pbp = PSslot(slot)
                for b in range(nb):
                    sl = slice(b * C, (b + 1) * C)
                    op('pe', lambda e, sl=sl, pbp=pbp, Xn=Xn, PTh=PTh: e.matmul(pbp.t[0:C, sl], Xn.t[0:C, sl], PTh.t[0:C, sl], start=True, stop=True), [Xn, PTh], [pbp], inc=(b == nb - 1))
                PTn = NB('PT%d' % (lv % 2), 516, slot=slot)
                tt(PTn.t[0:C, 0:W], PT.t[0:C, 0:W], pbp.t[0:C, 0:W], ALU.add, [PT, pbp], [PTn])
                PThn = NB16('PTh%d' % (lv % 2), 516, slot=slot)
                act(PThn.t[0:C, 0:W], PTn.t[0:C, 0:W], AF.Copy, [PTn], [PThn])
                PT = PTn; PTh = PThn
                yield
                Xb = Xn
                if lv < nlev:
                    XTb = XTn
            TinvT = PTh
            LakT = A['LakT']; ArbT = A['ArbT']; ArkT = A['ArkT']
            for c in range(nch):
                sl = slice(c * C, (c + 1) * C)
                zp = PSslot(slot)
                for hl in range(nh):
                    pr = slice(hl * Kd, (hl + 1) * Kd); vc = slice(hl * Vd, (hl + 1) * Vd)
                    bsl = slice((hl * nch + c) * C, (hl * nch + c + 1) * C)
                    aV = vt.t[0:C, c * 128 + hl * Vd:c * 128 + (hl + 1) * Vd]
                    op('pe', lambda e, pr=pr, vc=vc: e.matmul(zp.t[0:C, vc], QsT.t[pr, sl], st_b.t[pr, :], start=True, stop=False), [QsT, st_b], [zp], inc=False)
                    op('pe', lambda e, vc=vc, bsl=bsl, aV=aV: e.matmul(zp.t[0:C, vc], LakT.t[0:C, bsl], aV, start=False, stop=True), [LakT, vt], [zp], inc=(hl == nh - 1))
                yield
                Zs = HB16(slot)
                act(Zs.t[0:C, 0:128], zp.t[0:C, 0:128], AF.Copy, [zp], [Zs])
                up = PSslot(slot)
                for hl in range(nh):
                    vc = slice(hl * Vd, (hl + 1) * Vd)
                    bsl = slice((hl * nch + c) * C, (hl * nch + c + 1) * C)
                    op('pe', lambda e, vc=vc, bsl=bsl: e.matmul(up.t[0:C, vc], TinvT.t[0:C, bsl], Zs.t[0:C, vc], start=True, stop=True), [TinvT, Zs], [up], inc=(hl == nh - 1))
                yield
                U = HB16(slot)
                ts(U.t[0:C, 0:128], up.t[0:C, 0:128], 1.0, None, ALU.mult, None, [up], [U])
                for hl in range(nh):
                    pr = slice(hl * Kd, (hl + 1) * Kd); vc = slice(hl * Vd, (hl + 1) * Vd)
                    bsl = slice((hl * nch + c) * C, (hl * nch + c + 1) * C)
                    aV = vt.t[0:C, c * 128 + hl * Vd:c * 128 + (hl + 1) * Vd]
                    op('pe', lambda e, pr=pr, vc=vc: e.matmul(ypb.t[vc, sl], st_b.t[pr, :], RsT.t[pr, sl], start=True, stop=False), [st_b, RsT], [ypb], inc=False)
                    op('pe', lambda e, vc=vc, bsl=bsl: e.matmul(ypb.t[vc, sl], U.t[0:C, vc], ArbT.t[0:C, bsl], start=False, stop=False), [U, ArbT], [ypb], inc=False)
                    op('pe', lambda e, vc=vc, bsl=bsl, aV=aV: e.matmul(ypb.t[vc, sl], aV, ArkT.t[0:C, bsl], start=False, stop=True), [vt, ArkT], [ypb], inc=(hl == nh - 1))
                yield
                sp_ = PSslot(slot)
                for hl in range(nh):
                    pr = slice(hl * Kd, (hl + 1) * Kd); vc = slice(hl * Vd, (hl + 1) * Vd)
                    aP = pbt.t[0:C, c * 128 + hl * Kd:c * 128 + (hl + 1) * Kd]
                    aK = kbt.t[0:C, c * 128 + hl * Kd:c * 128 + (hl + 1) * Kd]
                    aV = vt.t[0:C, c * 128 + hl * Vd:c * 128 + (hl + 1) * Vd]
                    op('pe', lambda e, pr=pr, vc=vc, aP=aP: e.matmul(sp_.t[pr, 0:Vd], aP, U.t[0:C, vc], start=True, stop=False), [pbt, U], [sp_], inc=False)
                    op('pe', lambda e, pr=pr, aK=aK, aV=aV: e.matmul(sp_.t[pr, 0:Vd], aK, aV, start=False, stop=True), [kbt, vt], [sp_], inc=(hl == nh - 1))
                stt(st_b.t[:, :], st_b.t[:, :], wc_ap(c), sp_.t[0:128, 0:Vd], ALU.mult, ALU.add, [st_b, WcB, sp_], [st_b])
                yield

        def scan_sample2(nh, Kd, Vd, QsT, RsT, PbT, KbT, VT, arb, ark, WcB, wc16, stX):
            n = NS
            QR = NB('sQR', 36, slot=2); PK = NB('sPK', 36, slot=2); UV = NB('sUV', 36, slot=2)
            q3 = QR.t[:, 0:2 * n].rearrange("p (s two) -> p s two", two=2)
            p3 = PK.t[:, 0:2 * n].rearrange("p (s two) -> p s two", two=2)
            u3 = UV.t[:, 0:2 * n].rearrange("p (s two) -> p s two", two=2)
            act(q3[:, :, 0], QsT.t[:, 0:n], AF.Copy, [QsT], [QR])
            ts(q3[:, :, 1], RsT.t[:, 0:n], 1.0, None, ALU.mult, None, [RsT], [QR])
            act(p3[:, :, 0], PbT.t[:, 0:n], AF.Copy, [PbT], [PK])
            ts(p3[:, :, 1], KbT.t[:, 0:n], 1.0, None, ALU.mult, None, [KbT], [PK])
            ts(u3[:, :, 1], VT.t[:, 0:n], 1.0, None, ALU.mult, None, [VT], [UV])
            pq = PS()
            for s_ in range(n):
                for hl in range(nh):
                    pr = slice(hl * Kd, (hl + 1) * Kd); vc = slice(hl * Vd, (hl + 1) * Vd)
                    op('pe', lambda e, s_=s_, pr=pr, vc=vc: e.matmul(pq.t[vc, 2 * s_:2 * s_ + 2], stX.t[pr, s_, :], QR.t[pr, 2 * s_:2 * s_ + 2], start=True, stop=True),
                       [stX, QR], [pq], inc=(s_ == n - 1 and hl == nh - 1))
            pq3 = pq.t[:, 0:2 * n].rearrange("p (s two) -> p s two", two=2)
            act(u3[:, :, 0], pq3[:, :, 0], AF.Copy, [pq], [UV])
            Y = HB(); t2 = HB()
            tt(Y.t[:, 0:n], pq3[:, :, 0], arb.t[:, 0:n], ALU.mult, [pq, arb], [Y])
            tt(t2.t[:, 0:n], VT.t[:, 0:n], ark.t[:, 0:n], ALU.mult, [VT, ark], [t2])
            tt(Y.t[:, 0:n], Y.t[:, 0:n], t2.t[:, 0:n], ALU.add, [Y, t2], [Y])
            tt(Y.t[:, 0:n], Y.t[:, 0:n], pq3[:, :, 1], ALU.add, [Y, pq], [Y])
            gs = 4
            pkr = NB('sPKr', 512, slot=2); uvr = NB('sUVr', 512, slot=2)
            for g0 in range(0, n, gs):
                pt = PS()
                for j in range(gs):
                    s_ = g0 + j
                    tr(pt, pt.t[0:2, j * 128:(j + 1) * 128], PK.t[:, 2 * s_:2 * s_ + 2], 128, [PK], inc=(j == gs - 1))
                act(pkr.t[0:2, 0:gs * 128], pt.t[0:2, 0:gs * 128], AF.Copy, [pt], [pkr])
                pu = PS()
                for j in range(gs):
                    s_ = g0 + j
                    tr(pu, pu.t[0:2, j * 128:(j + 1) * 128], UV.t[:, 2 * s_:2 * s_ + 2], 128, [UV], inc=(j == gs - 1))
                ts(uvr.t[0:2, 0:gs * 128], pu.t[0:2, 0:gs * 128], 1.0, None, ALU.mult, None, [pu], [uvr])
                pp = PS()
                for j in range(gs):
                    for hl in range(nh):
                        pr = slice(hl * Kd, (hl + 1) * Kd)
                        op('pe', lambda e, j=j, hl=hl, pr=pr: e.matmul(pp.t[pr, j * Vd:(j + 1) * Vd], pkr.t[0:2, j * 128 + hl * Kd:j * 128 + (hl + 1) * Kd],
                                                                    uvr.t[0:2, j * 128 + hl * Vd:j * 128 + (hl + 1) * Vd], start=True, stop=True),
                           [pkr, uvr], [pp], inc=(j == gs - 1 and hl == nh - 1))
                sv = stX.t[:, g0:g0 + gs, :]
                tt(sv, sv, wc16[:, g0:g0 + gs].unsqueeze(2).broadcast_to([128, gs, Vd]), ALU.mult, [stX, WcB], [stX])
                tt(sv, sv, pp.t[:, 0:gs * Vd].rearrange("p (s v) -> p s v", v=Vd), ALU.add, [stX, pp], [stX])
            return Y

        def scan_pair(slot, N, C, QsT, RsT, Pb16, Kb16, V16, A, WcB, wc_ap, st_b, ypb):
            nch = N // C
            W = nch * C
            H2 = (slice(0, 64), slice(64, 128))

            def tmaj(src16, name):
                dst = NB16(name, slot=slot)
                pb = PSslot(slot)
                for hl in range(2):
                    pr = H2[hl]
                    for c in range(nch):
                        sl = slice(c * C, (c + 1) * C)
                        op('pe', lambda e, pr=pr, sl=sl: e.matmul(pb.t[pr, sl], src16.t[pr, sl], id16[pr, pr], start=True, stop=True), [src16, cm16], [pb],
                           inc=(hl == 1 and c == nch - 1))
                act(dst.t[:, 0:W], pb.t[:, 0:W], AF.Copy, [pb], [dst])
                return dst
            pbt = tmaj(Pb16, 'Pb'); kbt = tmaj(Kb16, 'Kb'); vt = tmaj(V16, 'Vt')
            yield
            NTb = A['NT']; XTb = A['NT16']; Xb = A['Nn16']
            PT = NB('PT0', slot=slot)
            tt(PT.t[:, 0:W].rearrange("p (c i) -> p c i", i=C), NTb.t[:, 0:W].rearrange("p (c i) -> p c i", i=C),
               ident2.unsqueeze(1).broadcast_to([128, nch, C]), ALU.add, [NTb, cm], [PT])
            PTh = NB16('PTh0', slot=slot)
            act(PTh.t[:, 0:W], PT.t[:, 0:W], AF.Copy, [PT], [PTh])
            yield
            nlev = 5

            def mm8(pbo, L_, R_):
                for hl in range(2):
                    pr = H2[hl]
                    for c in range(nch):
                        sl = slice(c * C, (c + 1) * C)
                        op('pe', lambda e, pr=pr, sl=sl: e.matmul(pbo.t[pr, sl], L_.t[pr, sl], R_.t[pr, sl], start=True, stop=True), [L_, R_], [pbo],
                           inc=(hl == 1 and c == nch - 1))
            for lv in range(1, nlev + 1):
                pbx = PSslot(slot)
                mm8(pbx, XTb, Xb)
                Xn = NB16('X%d' % (lv % 2), slot=slot)
                XTn = None
                if lv < nlev:
                    pbt_ = PSslot(slot)
                    mm8(pbt_, Xb, XTb)
                act(Xn.t[:, 0:W], pbx.t[:, 0:W], AF.Copy, [pbx], [Xn])
                yield
                if lv < nlev:
                    XTn = NB16('XT%d' % (lv % 2), slot=slot)
                    ts(XTn.t[:, 0:W], pbt_.t[:, 0:W], 1.0, None, ALU.mult, None, [pbt_], [XTn])
                pbp = PSslot(slot)
                mm8(pbp, Xn, PTh)
                PTn = NB('PT%d' % (lv % 2), slot=slot)
                tt(PTn.t[:, 0:W], PT.t[:, 0:W], pbp.t[:, 0:W], ALU.add, [PT, pbp], [PTn])
                PThn = NB16('PTh%d' % (lv % 2), slot=slot)
                act(PThn.t[:, 0:W], PTn.t[:, 0:W], AF.Copy, [PTn], [PThn])
                PT = PTn; PTh = PThn
                yield
                Xb = Xn
                if lv < nlev:
                    XTb = XTn
            TinvT = PTh
            LakT = A['LakT']; ArbT = A['ArbT']; ArkT = A['ArkT']
            for c in range(nch):
                sl = slice(c * C, (c + 1) * C)
                zp = PSslot(slot)
                for hl in range(2):
                    pr = H2[hl]
                    op('pe', lambda e, pr=pr: e.matmul(zp.t[pr, 0:64], QsT.t[pr, sl], st_b.t[pr, :], start=True, stop=False), [QsT, st_b], [zp], inc=False)
                    op('pe', lambda e, pr=pr: e.matmul(zp.t[pr, 0:64], LakT.t[pr, sl], vt.t[pr, sl], start=False, stop=True), [LakT, vt], [zp], inc=(hl == 1))
                yield
                Zs = HB16(slot)
                act(Zs.t[:, 0:64], zp.t[:, 0:64], AF.Copy, [zp], [Zs])
                up = PSslot(slot)
                for hl in range(2):
                    pr = H2[hl]
                    op('pe', lambda e, pr=pr: e.matmul(up.t[pr, 0:64], TinvT.t[pr, sl], Zs.t[pr, 0:64], start=True, stop=True), [TinvT, Zs], [up], inc=(hl == 1))
                yield
                U = HB16(slot)
                ts(U.t[:, 0:64], up.t[:, 0:64], 1.0, None, ALU.mult, None, [up], [U])
                for hl in range(2):
                    pr = H2[hl]
                    op('pe', lambda e, pr=pr: e.matmul(ypb.t[pr, sl], st_b.t[pr, :], RsT.t[pr, sl], start=True, stop=False), [st_b, RsT], [ypb], inc=False)
                    op('pe', lambda e, pr=pr: e.matmul(ypb.t[pr, sl], U.t[pr, 0:64], ArbT.t[pr, sl], start=False, stop=False), [U, ArbT], [ypb], inc=False)
                    op('pe', lambda e, pr=pr: e.matmul(ypb.t[pr, sl], vt.t[pr, sl], ArkT.t[pr, sl], start=False, stop=True), [vt, ArkT], [ypb], inc=(hl == 1))
                yield
                sp_ = PSslot(slot)
                for hl in range(2):
                    pr = H2[hl]
                    op('pe', lambda e, pr=pr: e.matmul(sp_.t[pr, 0:64], pbt.t[pr, sl], U.t[pr, 0:64], start=True, stop=False), [pbt, U], [sp_], inc=False)
                    op('pe', lambda e, pr=pr: e.matmul(sp_.t[pr, 0:64], kbt.t[pr, sl], vt.t[pr, sl], start=False, stop=True), [kbt, vt], [sp_], inc=(hl == 1))
                stt(st_b.t[:, :], st_b.t[:, :], wc_ap(c), sp_.t[:, 0:64], ALU.mult, ALU.add, [st_b, WcB, sp_], [st_b])
                yield

        def scan_sample_pair(QsT, RsT, PbT, KbT, VT, arb, ark, WcB, wc16, stX):
            n = NS
            H2 = (slice(0, 64), slice(64, 128))
            QR = NB('sQR', 36, slot=2); PK = NB('sPK', 36, slot=2); UV = NB('sUV', 36, slot=2)
            q3 = QR.t[:, 0:2 * n].rearrange("p (s two) -> p s two", two=2)
            p3 = PK.t[:, 0:2 * n].rearrange("p (s two) -> p s two", two=2)
            u3 = UV.t[:, 0:2 * n].rearrange("p (s two) -> p s two", two=2)
            act(q3[:, :, 0], QsT.t[:, 0:n], AF.Copy, [QsT], [QR])
            ts(q3[:, :, 1], RsT.t[:, 0:n], 1.0, None, ALU.mult, None, [RsT], [QR])
            act(p3[:, :, 0], PbT.t[:, 0:n], AF.Copy, [PbT], [PK])
            ts(p3[:, :, 1], KbT.t[:, 0:n], 1.0, None, ALU.mult, None, [KbT], [PK])
            ts(u3[:, :, 1], VT.t[:, 0:n], 1.0, None, ALU.mult, None, [VT], [UV])
            pq = PS()
            for s_ in range(n):
                for hl in range(2):
                    pr = H2[hl]
                    op('pe', lambda e, s_=s_, pr=pr: e.matmul(pq.t[pr, 2 * s_:2 * s_ + 2], stX.t[pr, s_, :], QR.t[pr, 2 * s_:2 * s_ + 2], start=True, stop=True),
                       [stX, QR], [pq], inc=(s_ == n - 1 and hl == 1))
            pq3 = pq.t[:, 0:2 * n].rearrange("p (s two) -> p s two", two=2)
            act(u3[:, :, 0], pq3[:, :, 0], AF.Copy, [pq], [UV])
            Y = HB(); t2 = HB()
            tt(Y.t[:, 0:n], pq3[:, :, 0], arb.t[:, 0:n], ALU.mult, [pq, arb], [Y])
            tt(t2.t[:, 0:n], VT.t[:, 0:n], ark.t[:, 0:n], ALU.mult, [VT, ark], [t2])
            tt(Y.t[:, 0:n], Y.t[:, 0:n], t2.t[:, 0:n], ALU.add, [Y, t2], [Y])
            tt(Y.t[:, 0:n], Y.t[:, 0:n], pq3[:, :, 1], ALU.add, [Y, pq], [Y])
            gs = 8
            pkr = NB('sPKr', 512, slot=2); uvr = NB('sUVr', 512, slot=2)
            for g0 in range(0, n, gs):
                pt = PS(); pu = PS()
                for j in range(gs):
                    s_ = g0 + j
                    for hl in range(2):
                        pr = H2[hl]; p2 = slice(hl * 64, hl * 64 + 2)
                        op('pe', lambda e, j=j, s_=s_, pr=pr, p2=p2: e.matmul(pt.t[p2, j * 64:(j + 1) * 64], PK.t[pr, 2 * s_:2 * s_ + 2], ident[pr, pr], start=True, stop=True),
                           [PK, cm], [pt], inc=(j == gs - 1 and hl == 1))
                for j in range(gs):
                    s_ = g0 + j
                    for hl in range(2):
                        pr = H2[hl]; p2 = slice(hl * 64, hl * 64 + 2)
                        op('pe', lambda e, j=j, s_=s_, pr=pr, p2=p2: e.matmul(pu.t[p2, j * 64:(j + 1) * 64], UV.t[pr, 2 * s_:2 * s_ + 2], ident[pr, pr], start=True, stop=True),
                           [UV, cm], [pu], inc=(j == gs - 1 and hl == 1))
                for hl in range(2):
                    p2 = slice(hl * 64, hl * 64 + 2)
                    act(pkr.t[p2, 0:gs * 64], pt.t[p2, 0:gs * 64], AF.Copy, [pt], [pkr])
                    ts(uvr.t[p2, 0:gs * 64], pu.t[p2, 0:gs * 64], 1.0, None, ALU.mult, None, [pu], [uvr])
                pp = PS()
                for j in range(gs):
                    for hl in range(2):
                        pr = H2[hl]; p2 = slice(hl * 64, hl * 64 + 2)
                        op('pe', lambda e, j=j, pr=pr, p2=p2: e.matmul(pp.t[pr, j * 64:(j + 1) * 64], pkr.t[p2, j * 64:(j + 1) * 64], uvr.t[p2, j * 64:(j + 1) * 64], start=True, stop=True),
                           [pkr, uvr], [pp], inc=(j == gs - 1 and hl == 1))
                sv = stX.t[:, g0:g0 + gs, :]
                tt(sv, sv, wc16[:, g0:g0 + gs].unsqueeze(2).broadcast_to([128, gs, 64]), ALU.mult, [stX, WcB], [stX])
                tt(sv, sv, pp.t[:, 0:gs * 64].rearrange("p (s v) -> p s v", v=64), ALU.add, [stX, pp], [stX])
            return Y

        def do_tile(is_s, t0):
            pfx = 's_' if is_s else ''
            N = NS if is_s else NP
            C = 1 if is_s else 64
            nseq = NS if is_s else 1
            nch = N // C
            rows_list = [NS] if is_s else [128] * TCN
            xsrc = xS if is_s else xP
            psrc = pS if is_s else pP
            ydst = yS if is_s else yP
            last = is_s or (t0 + NP >= TP)
            for tc, rows in enumerate(rows_list):
                kb.dma('sp', xtok.t[:rows, tc, :], xsrc[t0 + tc * 128:t0 + tc * 128 + rows, :], writes=[xtok])
                kb.dma('sp', ptok.t[:rows, tc, :], psrc[t0 + tc * 128:t0 + tc * 128 + rows, :], writes=[ptok])
            CKP(pfx + 'load')
            norm_T(N, rows_list, GMIX)
            CKP(pfx + 'norm')
            if is_s:
                kb.dma('sp', tokS.t[:], stShift, writes=[tokS])
                for g in range(15):
                    pb = PS()
                    if g < 12:
                        col = (g // 4) * 512 + (g % 4) * 128; P_ = 128
                    elif g == 12:
                        col = 1536; P_ = 64
                    elif g == 13:
                        col = 1600; P_ = 64
                    else:
                        col = 1664; P_ = 128
                    tr(pb, pb.t[0:P_, 0:NS], tokS.t[0:NS, col:col + P_], NS, [tokS])
                    act(prevS.t[0:P_, g, :], pb.t[0:P_, 0:NS], AF.Copy, [pb], [prevS])
                for j in range(3):
                    kb.dma('sp', tokC.t[:], stConv[:, j, :], writes=[tokC])
                    for g in range(12):
                        pb = PS()
                        tr(pb, pb.t[0:128, 0:NS], tokC.t[0:NS, g * 128:(g + 1) * 128], NS, [tokC])
                        act(histS.t[:, j * 12 + g, :], pb.t[:, 0:NS], AF.Copy, [pb], [histS])
            wt = wload(w_in, 0, 8, 128, 1536, 256)
            pb = PS(); proj_fm(pb, pb.t[0:64, :N], wt, 0, 64, N)
            xw = HB()
            shift(64, N, pb, pb.t[0:64, :N], c64b.t[:, MU_XW:MU_XW + 1], omu64.t[:, 24:25], (car_xw, car_xw.t[:, 0:1]), xw, xw.t[0:64, :N], is_s,
                  prev_ap=prevS.t[0:64, 12, :], raw_ap=rawS.t[0:64, 12, :], raw_b=rawS)
            act(tw.t[:, :N], xw.t[0:64, :N], AF.Tanh, [xw], [tw])
            pb = PS(); proj_fm(pb, pb.t[0:64, :N], wt, 64, 64, N)
            shift(64, N, pb, pb.t[0:64, :N], c64b.t[:, MU_XAA:MU_XAA + 1], omu64.t[:, 25:26], (car_xw, car_xw.t[:, 1:2]), xaaS, xaaS.t[:, :N], is_s,
                  prev_ap=prevS.t[0:64, 13, :], raw_ap=rawS.t[0:64, 13, :], raw_b=rawS)
            pb = PS(); proj_fm(pb, pb.t[:, :N], wt, 128, 128, N)
            xg = HB()
            shift(128, N, pb, pb.t[:, :N], c128b.t[:, MU_XG:MU_XG + 1], omu128.t[:, 0:1], (car_xg, car_xg.t[:, 0:1]), xg, xg.t[:, :N], is_s,
                  prev_ap=prevS.t[:, 14, :], raw_ap=rawS.t[:, 14, :], raw_b=rawS)
            act(sgS.t[:, :N], xg.t[:, :N], AF.Sigmoid, [xg], [sgS])

            pend = []
            if (not is_s) and t0 == 0:
                pend = [lambda: convert('bra', w_bra, 512, D), lambda: convert('brb', w_brb, 512, D),
                        lambda: convert('gate', w_in[:, 3848:5896], D, 2048), lambda: convert('out', w_out, D, D),
                        lambda: convert('fg', w_fg, D, DFF), lambda: convert('fu', w_fu, D, DFF), lambda: convert('fd', w_fd, DFF, D),
                        lambda: convert('pg', w_pg, D, D), lambda: convert('pp', w_pp, 256, D)]
            CKP(pfx + 'lora')
            def rwkv_gen(hp, slot):
                T = lambda: HB(slot)
                PSs = lambda: PSslot(slot)
                wt1 = wload(w_in, 0, 8, 128, hp * 384, 256)
                wt2 = wload(w_in, 0, 8, 128, hp * 384 + 256, 128)
                rkv = []
                for j, nm_ in enumerate(('r', 'k', 'v')):
                    g = hp * 3 + j; gc_ = j * 4 + hp
                    pb = PSs()
                    if j < 2:
                        proj_fm(pb, pb.t[:, :N], wt1, j * 128, 128, N)
                    else:
                        proj_fm(pb, pb.t[:, :N], wt2, 0, 128, N)
                    o_ = NB(nm_, slot=slot)
                    shift(128, N, pb, pb.t[:, :N], c128r.t[:, g:g + 1], omu128r.t[:, g:g + 1], (car_rkv, car_rkv.t[:, gc_:gc_ + 1]), o_, o_.t[:, :N], is_s,
                          prev_ap=prevS.t[:, gc_, :], raw_ap=rawS.t[:, gc_, :], raw_b=rawS, tmp=T)
                    rkv.append(o_)
                r_, k_, v_ = rkv
                hs = slice(hp * 128, (hp + 1) * 128)
                cc_ = lambda base: c128r.t[:, base + hp:base + hp + 1]
                yield
                pb = PSs()
                mm(pb, pb.t[:, :N], w2b.t[:, hs], tw.t[:, :N], [w2b, tw], True, True)
                sg = T(); act(sg.t[:, :N], pb.t[:, :N], AF.Sigmoid, [pb, c128r], [sg], bias=cc_(W0r))
                yield
                cws = T()
                if C > 1:
                    op('dve', lambda e: e.tensor_tensor_scan(out=cws.t[:, :N], data0=scanm[:, :N], data1=sg.t[:, :N], initial=0.0, op0=ALU.mult, op1=ALU.add), [sg, cm], [cws])
                else:
                    ts(cws.t[:, :N], sg.t[:, :N], 1.0, None, ALU.mult, None, [sg], [cws])
                yield
                ew = NB('ew', slot=slot); act(ew.t[:, :N], cws.t[:, :N], AF.Exp, [cws], [ew], scale=-C0)
                ewi = NB('ewi', slot=slot); act(ewi.t[:, :N], cws.t[:, :N], AF.Exp, [cws], [ewi], scale=C0)
                yield
                cx = T(); tt(cx.t[:, :N], cws.t[:, :N], sg.t[:, :N], ALU.subtract, [cws, sg], [cx])
                ewx = NB('ewx', slot=slot); act(ewx.t[:, :N], cx.t[:, :N], AF.Exp, [cx], [ewx], scale=-C0)
                yield
                dC = T()
                cw3 = cws.t[:, :N].rearrange("p (c i) -> p c i", i=C)
                tt(dC.t[:, :N].rearrange("p (c i) -> p c i", i=C), cw3[:, :, C - 1:C].broadcast_to([128, nch, C]), cw3, ALU.subtract, [cws], [dC])
                ewC = NB('ewC', slot=slot); act(ewC.t[:, :N], dC.t[:, :N], AF.Exp, [dC], [ewC], scale=-C0)
                yield
                pb = PSs()
                mm(pb, pb.t[:, :N], a2b.t[:, hs], xaaS.t[:, :N], [a2b, xaaS], True, True)
                a_ = NB('a', slot=slot); act(a_.t[:, :N], pb.t[:, :N], AF.Sigmoid, [pb, c128r], [a_], bias=cc_(A0r))
                yield
                pb = PSs()
                mm(pb, pb.t[:, :N], g2b.t[:, hs], sgS.t[:, :N], [g2b, sgS], True, True)
                g_ = NB('g', slot=slot); act(g_.t[:, :N], pb.t[:, :N], AF.Copy, [pb], [g_])
                yield
                kks = T(); ts(kks.t[:, :N], k_.t[:, :N], cc_(KKr), None, ALU.mult, None, [k_, c128r], [kks])
                sq = HBW(slot); act(sq.t[:, :N], kks.t[:, :N], AF.Square, [kks], [sq])
                yield
                pb = PSs()
                mm(pb, pb.t[:, :N], bones16, sq.t[:, :N], [on16, sq], True, True)
                rs = T(); rsqrt(rs.t[:, :N], pb.t[:, :N], 1.0, 1e-6, [pb], [rs])
                yield
                kk = NB('kk', slot=slot); tt(kk.t[:, :N], kks.t[:, :N], rs.t[:, :N], ALU.mult, [kks, rs], [kk])
                t1 = T(); ts(t1.t[:, :N], a_.t[:, :N], -1.0, cc_(KAr), ALU.add, ALU.mult, [a_, c128r], [t1])
                yield
                km = NB('km', slot=slot); stt(km.t[:, :N], t1.t[:, :N], 1.0, k_.t[:, :N], ALU.add, ALU.mult, [t1, k_], [km])
                kka = NB('kka', slot=slot); tt(kka.t[:, :N], kk.t[:, :N], a_.t[:, :N], ALU.mult, [kk, a_], [kka])
                yield
                QsT = NB('QsT', slot=slot); stt(QsT.t[:, :N], kk.t[:, :N], -1.0, ewx.t[:, :N], ALU.mult, ALU.mult, [kk, ewx], [QsT])
                RsT = NB('RsT', slot=slot); tt(RsT.t[:, :N], r_.t[:, :N], ew.t[:, :N], ALU.mult, [r_, ew], [RsT])
                yield
                PnT = NB16('PnT', slot=slot); tt(PnT.t[:, :N], kka.t[:, :N], ewi.t[:, :N], ALU.mult, [kka, ewi], [PnT])
                KnT = NB16('KnT', slot=slot); tt(KnT.t[:, :N], km.t[:, :N], ewi.t[:, :N], ALU.mult, [km, ewi], [KnT])
                yield
                PbT = NB('PbT', slot=slot); tt(PbT.t[:, :N], kka.t[:, :N], ewC.t[:, :N], ALU.mult, [kka, ewC], [PbT])
                KbT = NB('KbT', slot=slot); tt(KbT.t[:, :N], km.t[:, :N], ewC.t[:, :N], ALU.mult, [km, ewC], [KbT])
                yield
                rk = HBW(slot); stt(rk.t[:, :N], r_.t[:, :N], cc_(RKr), km.t[:, :N], ALU.mult, ALU.mult, [r_, c128r, km], [rk])
                pb = PSs()
                mm(pb, pb.t[:, :N], bones16, rk.t[:, :N], [on16, rk], True, True)
                bon = NB('bon', slot=slot); tt(bon.t[:, :N], pb.t[:, :N], v_.t[:, :N], ALU.mult, [pb, v_], [bon])
                yield
                if is_s:
                    for hl_ in range(2):
                        kb.dma('sp', natS.t[0:64, :, hl_, :], stWkv[:, 2 * hp + hl_].rearrange("s v k -> v s k"), writes=[natS])
                    for s0 in range(0, NS, 4):
                        pb = PSs()
                        for s_ in range(s0, s0 + 4):
                            tr(pb, pb.t[:, (s_ - s0) * 64:(s_ - s0 + 1) * 64], natS.t[0:64, s_, :, :], 64, [natS], inc=(s_ == s0 + 3))
                        act(stRs.t[:, s0:s0 + 4, :], pb.t[:, 0:256].rearrange("k (s v) -> k s v", v=64), AF.Copy, [pb], [stRs])
                    stb = stRs
                    pr1 = T(); tt(pr1.t[:, :N], PnT.t[:, :N], RsT.t[:, :N], ALU.mult, [PnT, RsT], [pr1])
                    pr2 = T(); tt(pr2.t[:, :N], KnT.t[:, :N], RsT.t[:, :N], ALU.mult, [KnT, RsT], [pr2])
                    pb = PSs(); mm(pb, pb.t[:, :N], bones, pr1.t[:, :N], [cm2, pr1], True, True)
                    arb = T(); act(arb.t[:, :N], pb.t[:, :N], AF.Copy, [pb], [arb])
                    pb = PSs(); mm(pb, pb.t[:, :N], bones, pr2.t[:, :N], [cm2, pr2], True, True)
                    ark = T(); act(ark.t[:, :N], pb.t[:, :N], AF.Copy, [pb], [ark])
                    y = scan_sample_pair(QsT, RsT, PbT, KbT, v_, arb, ark, ew, ew.t[:, 0:NS], stRs)
                else:
                    CKP(pfx + 'rprep')
                    A = {}
                    H2 = (slice(0, 64), slice(64, 128))

                    def amat(name, Lb, Rb, mask):
                        pb = PSs()
                        for hl in range(2):
                            pr = H2[hl]
                            for c in range(nch):
                                sl = slice(c * C, (c + 1) * C)
                                op('pe', lambda e, sl=sl, pr=pr: e.matmul(pb.t[pr, sl], Lb.t[pr, sl], Rb.t[pr, sl], start=True, stop=True), [Lb, Rb], [pb],
                                   inc=(hl == 1 and c == nch - 1))
                        o_ = NB(name, slot=slot) if name == 'NT' else NB16(name, slot=slot)
                        tt(o_.t[:, 0:N].rearrange("p (c i) -> p c i", i=C), pb.t[:, 0:N].rearrange("p (c i) -> p c i", i=C),
                           mask.unsqueeze(1).broadcast_to([128, nch, C]), ALU.mult, [pb, cm], [o_])
                        A[name] = o_
                        if name == 'NT':
                            n16 = NB16('NT16', slot=slot)
                            act(n16.t[:, 0:N], o_.t[:, 0:N], AF.Copy, [o_], [n16])
                            A['NT16'] = n16
                    Qs16 = NB16('Qs16', slot=slot); act(Qs16.t[:, :N], QsT.t[:, :N], AF.Copy, [QsT], [Qs16])
                    Rs16 = NB16('Rs16', slot=slot); ts(Rs16.t[:, :N], RsT.t[:, :N], 1.0, None, ALU.mult, None, [RsT], [Rs16])
                    yield
                    amat('NT', PnT, Qs16, maskS2)
                    yield
                    amat('Nn16', Qs16, PnT, maskL2)
                    yield
                    amat('LakT', KnT, Qs16, maskS2)
                    yield
                    amat('ArbT', PnT, Rs16, maskI2)
                    yield
                    amat('ArkT', KnT, Rs16, maskI2)
                    yield
                    Pb16 = NB16('Pb16', slot=slot); act(Pb16.t[:, :N], PbT.t[:, :N], AF.Copy, [PbT], [Pb16])
                    Kb16 = NB16('Kb16', slot=slot); ts(Kb16.t[:, :N], KbT.t[:, :N], 1.0, None, ALU.mult, None, [KbT], [Kb16])
                    V16 = NB16('V16', slot=slot); act(V16.t[:, :N], v_.t[:, :N], AF.Copy, [v_], [V16])
                    yield
                    CKP(pfx + 'ramat')
                    stb = stR[hp]
                    ypb = ps[6 + slot]
                    yield from scan_pair(slot, N, C, QsT, RsT, Pb16, Kb16, V16, A, ew, lambda c: ew.t[:, (c + 1) * C - 1:(c + 1) * C], stb, ypb)
                    CKP(pfx + 'rscan')
                    y = T(); act(y.t[:, :N], ypb.t[:, :N], AF.Copy, [ypb], [y])
                yield
                pb = PSs(); mm(pb, pb.t[:, :N], bones, y.t[:, :N], [cm2, y], True, True)
                d_ = T(); stt(d_.t[:, :N], pb.t[:, :N], -1.0 / 64, y.t[:, :N], ALU.mult, ALU.add, [pb, y], [d_])
                yield
                d2 = HBW(slot); act(d2.t[:, :N], d_.t[:, :N], AF.Square, [d_], [d2])
                pb = PSs(); mm(pb, pb.t[:, :N], bones16, d2.t[:, :N], [on16, d2], True, True)
                rs2 = T(); rsqrt(rs2.t[:, :N], pb.t[:, :N], 1.0 / 64, 64e-5, [pb], [rs2])
                yield
                yn = T(); tt(yn.t[:, :N], d_.t[:, :N], rs2.t[:, :N], ALU.mult, [d_, rs2], [yn])
                ts(yn.t[:, :N], yn.t[:, :N], cc_(LNWr), cc_(LNBr), ALU.mult, ALU.add, [yn, c128r], [yn])
                yield
                tt(yn.t[:, :N], yn.t[:, :N], bon.t[:, :N], ALU.add, [yn, bon], [yn])
                tt(oa.t[:, hp, :N], yn.t[:, :N], g_.t[:, :N], ALU.mult, [yn, g_], [oa])
                yield
                if last:
                    if is_s:
                        for s0 in range(0, NS, 4):
                            pb = PSs()
                            for s_ in range(s0, s0 + 4):
                                tr(pb, pb.t[0:64, (s_ - s0) * 128:(s_ - s0 + 1) * 128], stRs.t[:, s_, :], 128, [stRs], inc=(s_ == s0 + 3))
                            act(natS.t[0:64, s0:s0 + 4, :, :], pb.t[0:64, 0:512].rearrange("v (s h k) -> v s h k", h=2, k=64), AF.Copy, [pb], [natS])
                        for hl_ in range(2):
                            kb.dma('sp', oWkvS[:, 2 * hp + hl_].rearrange("s v k -> v s k"), natS.t[0:64, :, hl_, :], reads=[natS], is_out=True)
                    else:
                        pb = PSs()
                        tr(pb, pb.t[0:64, 0:128], stR[hp].t[:, :], 128, [stR[hp]])
                        stg_ = T()
                        act(stg_.t[0:64, 0:128], pb.t[0:64, 0:128], AF.Copy, [pb], [stg_])
                        kb.dma('sp', oWkvP[0, 2 * hp:2 * hp + 2].rearrange("h v k -> v h k"), stg_.t[0:64, 0:128].rearrange("v (h k) -> v h k", k=64), reads=[stg_], is_out=True)

            CKP(pfx + 'rwkv')
            def gdn_gen(h, slot):
                T = lambda: HB(slot)
                PSs = lambda: PSslot(slot)
                wtb = wload(w_barep, 0, 8, 128, h * 128, 128)
                wta = wload(w_barep, 0, 8, 128, 512 + h * 128, 128)
                pb = PSs(); proj_fm(pb, pb.t[:, :N], wtb, 0, 128, N)
                Bb = NB('a', slot=slot); act(Bb.t[:, :N], pb.t[:, :N], AF.Sigmoid, [pb], [Bb])
                pb = PSs(); proj_fm(pb, pb.t[:, :N], wta, 0, 128, N)
                e1 = T(); act(e1.t[:, :N], pb.t[:, :N], AF.Exp, [pb, c128b], [e1], bias=c128b.t[:, DTB + h:DTB + h + 1])
                act(e1.t[:, :N], e1.t[:, :N], AF.Ln, [e1], [e1], bias=1.0)
                gl = T(); ts(gl.t[:, :N], e1.t[:, :N], nexpA.t[:, h:h + 1], None, ALU.mult, None, [e1, nexpA], [gl])
                G = NB('ewi', slot=slot)
                if C > 1:
                    op('dve', lambda e, G=G, gl=gl: e.tensor_tensor_scan(out=G.t[:, :N], data0=scanm[:, :N], data1=gl.t[:, :N], initial=0.0, op0=ALU.mult, op1=ALU.add), [gl, cm], [G])
                else:
                    ts(G.t[:, :N], gl.t[:, :N], 1.0, None, ALU.mult, None, [gl], [G])
                wt1 = wload(w_in, 0, 8, 128, 1792 + h * 512, 256)
                wt2 = wload(w_in, 0, 8, 128, 1792 + h * 512 + 256, 256)
                cs = []
                for j, nm_ in enumerate(('r', 'k', 'v')):
                    g = j * 4 + h
                    pb = PSs()
                    if j < 2:
                        proj_fm(pb, pb.t[:, :N], wt1, j * 128, 128, N)
                    else:
                        proj_fm(pb, pb.t[:, :N], wt2, 0, 128, N)
                    acc = T()
                    cwc = lambda tap, g=g: cvw.t[:, tap * 12 + g:tap * 12 + g + 1]
                    if is_s:
                        act(rawC.t[:, g, :], pb.t[:, :N], AF.Copy, [pb], [rawC])
                        ts(acc.t[:, :N], histS.t[:, 0 * 12 + g, :], cwc(0), None, ALU.mult, None, [histS, cvw], [acc])
                        stt(acc.t[:, :N], histS.t[:, 1 * 12 + g, :], cwc(1), acc.t[:, :N], ALU.mult, ALU.add, [histS, cvw, acc], [acc])
                        stt(acc.t[:, :N], histS.t[:, 2 * 12 + g, :], cwc(2), acc.t[:, :N], ALU.mult, ALU.add, [histS, cvw, acc], [acc])
                        stt(acc.t[:, :N], pb.t[:, :N], cwc(3), acc.t[:, :N], ALU.mult, ALU.add, [pb, cvw, acc], [acc])
                    else:
                        raw = T()
                        act(raw.t[:, 3:N + 3], pb.t[:, :N], AF.Copy, [pb], [raw])
                        act(raw.t[:, 0:3], car_cv.t[:, g, 0:3], AF.Copy, [car_cv], [raw])
                        act(car_cv.t[:, g, 0:3], raw.t[:, N:N + 3], AF.Copy, [raw], [car_cv])
                        ts(acc.t[:, :N], raw.t[:, 0:N], cwc(0), None, ALU.mult, None, [raw, cvw], [acc])
                        for tap in range(1, 4):
                            stt(acc.t[:, :N], raw.t[:, tap:tap + N], cwc(tap), acc.t[:, :N], ALU.mult, ALU.add, [raw, cvw, acc], [acc])
                    c_ = NB(nm_, slot=slot); act(c_.t[:, :N], acc.t[:, :N], AF.Silu, [acc], [c_])
                    cs.append(c_)
                pbz = PSs(); proj_fm(pbz, pbz.t[:, :N], wt2, 128, 128, N)
                zz = NB('g', slot=slot); act(zz.t[:, :N], pbz.t[:, :N], AF.Silu, [pbz], [zz])
                cq, ck, cv_ = cs
                yield

                def l2n(src, scale, name):
                    sq = HBW(slot); act(sq.t[:, :N], src.t[:, :N], AF.Square, [src], [sq])
                    pb = PSs(); mm(pb, pb.t[:, :N], ones16, sq.t[:, :N], [on16, sq], True, True)
                    rs = T(); rsqrt(rs.t[:, :N], pb.t[:, :N], 1.0, 1e-6, [pb], [rs])
                    o_ = NB(name, slot=slot); stt(o_.t[:, :N], src.t[:, :N], scale, rs.t[:, :N], ALU.mult, ALU.mult, [src, rs], [o_])
                    return o_
                qn = l2n(cq, float(128 ** -0.5), 'kk'); kn = l2n(ck, 1.0, 'km')
                yield
                N2, C2, nch2 = N, C, nch
                yield
                eg = NB('ew', slot=slot); act(eg.t[:, :N2], G.t[:, :N2], AF.Exp, [G], [eg])
                yield
                QsT = NB('QsT', slot=slot); tt(QsT.t[:, :N2], kn.t[:, :N2], eg.t[:, :N2], ALU.mult, [kn, eg], [QsT])
                yield
                RsT = NB('RsT', slot=slot); tt(RsT.t[:, :N2], qn.t[:, :N2], eg.t[:, :N2], ALU.mult, [qn, eg], [RsT])
                yield
                dC = T()
                yield
                G3 = G.t[:, :N2].rearrange("p (c i) -> p c i", i=C2)
                yield
                tt(dC.t[:, :N2].rearrange("p (c i) -> p c i", i=C2), G3[:, :, C2 - 1:C2].broadcast_to([128, nch2, C2]), G3, ALU.subtract, [G], [dC])
                yield
                egC = T(); act(egC.t[:, :N2], dC.t[:, :N2], AF.Exp, [dC], [egC])
                yield
                bE = T(); tt(bE.t[:, :N2], Bb.t[:, :N2], egC.t[:, :N2], ALU.mult, [Bb, egC], [bE])
                yield
                KbT = NB('KbT', slot=slot); tt(KbT.t[:, :N2], kn.t[:, :N2], bE.t[:, :N2], ALU.mult, [kn, bE], [KbT])
                yield
                PbT = NB('PbT', slot=slot); ts(PbT.t[:, :N2], KbT.t[:, :N2], -1.0, None, ALU.mult, None, [KbT], [PbT])
                yield
                if is_s:
                    kb.dma('sp', stGs.t[:], stGdn[:, h].rearrange("s k v -> k s v"), writes=[stGs])
                    stb = stGs
                    kq = T(); tt(kq.t[:, :N], kn.t[:, :N], qn.t[:, :N], ALU.mult, [kn, qn], [kq])
                    pb = PSs(); mm(pb, pb.t[:, :N], ones, kq.t[:, :N], [cm, kq], True, True)
                    arb = T(); stt(arb.t[:, :N], pb.t[:, :N], -1.0, Bb.t[:, :N], ALU.mult, ALU.mult, [pb, Bb], [arb])
                    ark = T(); ts(ark.t[:, :N], arb.t[:, :N], -1.0, None, ALU.mult, None, [arb], [ark])
                    o_ = scan_sample2(1, 128, 128, QsT, RsT, PbT, KbT, cv_, arb, ark, eg, eg.t[:, 0:NS], stGs)
                else:
                    gcT = NB('ewx', slot=slot); nbT = NB('ewC', slot=slot)
                    for (src, dst, sc) in ((G, gcT, 1.0), (Bb, nbT, -1.0)):
                        pb = PSs()
                        for c in range(nch2):
                            tr(pb, pb.t[0:C2, c * 32:(c + 1) * 32], src.t[0:32, c * C2:(c + 1) * C2], 32, [src], inc=(c == nch2 - 1))
                        ts(dst.t[0:C2, 0:nch2], pb.t[0:C2, 0:nch2 * 32].rearrange("p (c i) -> p c i", i=32)[:, :, 0], sc, None, ALU.mult, None, [pb], [dst])
                    E = NB('kka', slot=slot)
                    E3 = E.t[0:C2, :N2].rearrange("p (c i) -> p c i", i=C2)
                    tt(E3, G.t[0:C2, :N2].rearrange("p (c i) -> p c i", i=C2), gcT.t[0:C2, 0:nch2].unsqueeze(2).broadcast_to([C2, nch2, C2]), ALU.subtract, [G, gcT], [E])
                    tt(E3, E3, negI[0:C2, 0:C].unsqueeze(1).broadcast_to([C2, nch2, C2]), ALU.add, [E, cm], [E])
                    act(E.t[0:C2, :N2], E.t[0:C2, :N2], AF.Exp, [E], [E])
                    nb3 = nbT.t[0:C2, 0:nch2].unsqueeze(2).broadcast_to([C2, nch2, C2])
                    A = {}

                    def gmat(Rb, strict, n1, n2):
                        pb = PSs()
                        for c in range(nch2):
                            sl = slice(c * C2, (c + 1) * C2)
                            op('pe', lambda e, sl=sl: e.matmul(pb.t[0:C2, sl], kn.t[:, sl], Rb.t[:, sl], start=True, stop=True), [kn, Rb], [pb], inc=(c == nch2 - 1))
                        o_ = NB('NT' if strict else 'A32', slot=slot); o3 = o_.t[0:C2, :N2].rearrange("p (c i) -> p c i", i=C2)
                        tt(o3, pb.t[0:C2, :N2].rearrange("p (c i) -> p c i", i=C2), E3, ALU.mult, [pb, E], [o_])
                        if strict:
                            tt(o3, o3, maskS[0:C2, 0:C].unsqueeze(1).broadcast_to([C2, nch2, C2]), ALU.mult, [o_, cm], [o_])
                        tt(o3, o3, nb3, ALU.mult, [o_, nbT], [o_])
                        p_ = NB16(n1, slot=slot); act(p_.t[0:C2, :N2], o_.t[0:C2, :N2], AF.Copy, [o_], [p_])
                        n_ = NB16(n2, slot=slot); ts(n_.t[0:C2, :N2], o_.t[0:C2, :N2], -1.0, None, ALU.mult, None, [o_], [n_])
                        return o_, p_, n_
                    A['NT'], A['NT16'], A['LakT'] = gmat(kn, True, 'NT16', 'LakT')
                    _, A['ArbT'], A['ArkT'] = gmat(qn, False, 'ArbT', 'ArkT')
                    if is_s:
                        kb.dma('sp', stGs.t[:], stGdn[:, h].rearrange("s k v -> k s v"), writes=[stGs])
                        stb = stGs; stap = lambda s: stGs.t[:, s, :]
                    else:
                        stb = stG[h]; stap = lambda s, h=h: stG[h].t[:, :]
                    ypb = ps[6 + slot]
                    yield from scan2(slot, 1, 128, 128, N, C, QsT, RsT, PbT, KbT, cv_, A, eg, lambda c: eg.t[:, (c + 1) * C - 1:(c + 1) * C], stG[h], ypb)
                    osrc = ypb.t[:, 0:N2].rearrange("p (s two) -> p s two", two=2)[:, :, 0] if is_s else ypb.t[:, :N]
                    o_ = T(); act(o_.t[:, :N], osrc, AF.Copy, [ypb], [o_])
                sq = HBW(slot); act(sq.t[:, :N], o_.t[:, :N], AF.Square, [o_], [sq])
                yield
                pb = PSs(); mm(pb, pb.t[:, :N], ones16, sq.t[:, :N], [on16, sq], True, True)
                yield
                rs = T(); rsqrt(rs.t[:, :N], pb.t[:, :N], 1.0 / 128, 1e-6, [pb], [rs])
                yield
                on = T(); stt(on.t[:, :N], o_.t[:, :N], c128b.t[:, GDNN:GDNN + 1], rs.t[:, :N], ALU.mult, ALU.mult, [o_, c128b, rs], [on])
                yield
                tt(ob.t[:, h, :N], on.t[:, :N], zz.t[:, :N], ALU.mult, [on, zz], [ob])
                yield
                if last:
                    dstG = oGdnS if is_s else oGdnP
                    if is_s:
                        kb.dma('sp', dstG[:, h].rearrange("s k v -> k s v"), stGs.t[:], reads=[stGs], is_out=True)
                    else:
                        kb.dma('sp', dstG[0, h], stG[h].t[:, :], reads=[stG[h]], is_out=True)

            order = [x_ for x_ in [('r', 0), ('g', 0), ('r', 1), ('g', 1), ('r', 2), ('g', 2), ('r', 3), ('g', 3)] if x_[0] in ONLY[0]]
            mk = lambda kind, hh: (lambda sl_: (rwkv_gen(hh, sl_) if kind == 'r' else gdn_gen(hh, sl_)))
            queue = [mk(k_, h_) for (k_, h_) in order]
            if is_s or NSLOT[0] == 1:
                for f_ in queue:
                    for _ in f_(0):
                        pass
            else:
                active = [None, None]
                rounds = 0
                qs = [[mk(k_, h_) for (k_, h_) in order if k_ == 'r'], [mk(k_, h_) for (k_, h_) in order if k_ == 'g']]
                while qs[0] or qs[1] or any(a_ is not None for a_ in active):
                    rounds += 1
                    for sl_ in (0, 1):
                        if active[sl_] is None and qs[sl_] and not (sl_ == 1 and rounds < OFFS[0]):
                            active[sl_] = qs[sl_].pop(0)(sl_)
                            next(active[sl_])
                            for _ in range(2):
                                if pend:
                                    pend.pop(0)()
                        if active[sl_] is not None:
                            try:
                                next(active[sl_])
                            except StopIteration:
                                active[sl_] = None
            while pend:
                pend.pop(0)()
            if is_s:
                dbgdump('oa', oa.t[:, :, 0:NS], [oa]); dbgdump('ob', ob.t[:, :, 0:NS], [ob])
            CKP(pfx + 'gdn')
            for cb in range(4):
                wa = wloadc('bra', 0, 4, 128, cb * 256, 256)
                wb = wloadc('brb', 0, 4, 128, cb * 256, 256)
                wga = wloadc('gate', 0, 8, 128, cb * 256, 256)
                wgb = wloadc('gate', 0, 8, 128, 1024 + cb * 256, 256)
                for m2 in range(2):
                    m = cb * 2 + m2
                    pa_ = PS(); proj_fm(pa_, pa_.t[:, :N], wa, m2 * 128, 128, N, nk=4, rhsb=oa)
                    pbb = PS(); proj_fm(pbb, pbb.t[:, :N], wb, m2 * 128, 128, N, nk=4, rhsb=ob)
                    pga = PS(); proj_fm(pga, pga.t[:, :N], wga, m2 * 128, 128, N)
                    pgb = PS(); proj_fm(pgb, pgb.t[:, :N], wgb, m2 * 128, 128, N)
                    ga = HB(); act(ga.t[:, :N], pga.t[:, :N], AF.Sigmoid, [pga], [ga])
                    gb = HB(); act(gb.t[:, :N], pgb.t[:, :N], AF.Sigmoid, [pgb], [gb])
                    t1 = HB(); tt(t1.t[:, :N], ga.t[:, :N], pa_.t[:, :N], ALU.mult, [ga, pa_], [t1])
                    t2 = HB(); tt(t2.t[:, :N], gb.t[:, :N], pbb.t[:, :N], ALU.mult, [gb, pbb], [t2])
                    tt(mixT.t[:, m, :N], t1.t[:, :N], t2.t[:, :N], ALU.add, [t1, t2], [mixT])

            def tok_out(wname, nkc_list, lhsb, epilogue):
                for cb in range(4):
                    pbs = [PS() for _ in rows_list]
                    k0 = 0
                    tot = sum(nkc_list)
                    for nk in nkc_list:
                        wt = wloadc(wname, k0 * 128, nk, 128, cb * 256, 256)
                        for tc, rows in enumerate(rows_list):
                            for k in range(nk):
                                kk_ = k0 + k
                                mm(pbs[tc], pbs[tc].t[0:rows, 0:256], lhsb.t[:, kk_, tc * 128:tc * 128 + rows], wt.t[:, k, :], [lhsb, wt], kk_ == 0, kk_ == tot - 1)
                        k0 += nk
                    for tc, rows in enumerate(rows_list):
                        epilogue(tc, rows, cb, pbs[tc])

            def resid_add(tc, rows, cb, pb):
                sl = slice(cb * 256, (cb + 1) * 256)
                tt(xtok.t[0:rows, tc, sl], xtok.t[0:rows, tc, sl], pb.t[0:rows, 0:256], ALU.add, [xtok, pb], [xtok])

            if is_s:
                dbgdump('mix', mixT.t[:, :, 0:NS], [mixT])
            tok_out('out', [8], mixT, resid_add)
            if is_s:
                dbgdump('h1', xtok.t[0:NS, 0, :], [xtok])
            CKP(pfx + 'merge')
            norm_T(N, rows_list, GFFN)
            for cb in range(11):
                wg = wloadc('fg', 0, 8, 128, cb * 256, 256)
                wu = wloadc('fu', 0, 8, 128, cb * 256, 256)
                for m2 in range(2):
                    pg_ = PS(); proj_fm(pg_, pg_.t[:, :N], wg, m2 * 128, 128, N)
                    pu_ = PS(); proj_fm(pu_, pu_.t[:, :N], wu, m2 * 128, 128, N)
                    sg = HB(); act(sg.t[:, :N], pg_.t[:, :N], AF.Silu, [pg_], [sg])
                    tt(hfT.t[:, cb * 2 + m2, :N], sg.t[:, :N], pu_.t[:, :N], ALU.mult, [sg, pu_], [hfT])
            tok_out('fd', [8, 8, 6], hfT, resid_add)
            if is_s:
                dbgdump('h2', xtok.t[0:NS, 0, :], [xtok])
            CKP(pfx + 'ffn')
            norm_T(N, rows_list, GPLE)
            for tc, rows in enumerate(rows_list):
                for k in range(2):
                    pb = PS()
                    tr(pb, pb.t[:, 0:rows], ptok.t[0:rows, tc, k * 128:(k + 1) * 128], rows, [ptok])
                    act(peT.t[:, k, tc * 128:tc * 128 + rows], pb.t[:, 0:rows], AF.Copy, [pb], [peT])
            for cb in range(4):
                wg = wloadc('pg', 0, 8, 128, cb * 256, 256)
                wp = wloadc('pp', 0, 2, 128, cb * 256, 256)
                for tc, rows in enumerate(rows_list):
                    pg_ = PS(); pp_ = PS()
                    for k in range(8):
                        mm(pg_, pg_.t[0:rows, 0:256], uT.t[:, k, tc * 128:tc * 128 + rows], wg.t[:, k, :], [uT, wg], k == 0, k == 7)
                    for k in range(2):
                        mm(pp_, pp_.t[0:rows, 0:256], peT.t[:, k, tc * 128:tc * 128 + rows], wp.t[:, k, :], [peT, wp], k == 0, k == 1)
                    sg = HB(); act(sg.t[0:rows, 0:256], pg_.t[0:rows, 0:256], AF.Sigmoid, [pg_], [sg])
                    tt(sg.t[0:rows, 0:256], sg.t[0:rows, 0:256], pp_.t[0:rows, 0:256], ALU.mult, [sg, pp_], [sg])
                    sl = slice(cb * 256, (cb + 1) * 256)
                    tt(xtok.t[0:rows, tc, sl], xtok.t[0:rows, tc, sl], sg.t[0:rows, 0:256], ALU.add, [xtok, sg], [xtok])
            if is_s:
                dbgdump('h3', xtok.t[0:NS, 0, :], [xtok])
            CKP(pfx + 'ple')
            for tc, rows in enumerate(rows_list):
                act(xs.t[:rows, :], xtok.t[:rows, tc, :], AF.Square, [xtok], [xs, ss], accum_out=ss.t[:rows, tc:tc + 1])
            mr = max(rows_list); TC = len(rows_list)
            rsqrt(ss.t[:mr, 4:4 + TC], ss.t[:mr, 0:TC], 1.0 / D, 1e-6, [ss], [ss])
            for tc, rows in enumerate(rows_list):
                stt(xs.t[:rows, :], xtok.t[:rows, tc, :], ss.t[:rows, 4 + tc:5 + tc], gfb.t[:rows, :], ALU.mult, ALU.mult, [xtok, ss, gfb], [xs])
                kb.dma('sp', ydst[t0 + tc * 128:t0 + tc * 128 + rows, :], xs.t[:rows, :], reads=[xs], is_out=True)

        try:
          for ti in range(n_ptiles):
            do_tile(False, ti * NP)
          CKP('ptiles')
          pb = PS()
          tr(pb, pb.t[0:12, 0:128], car_rkv.t[:, 0:12], 128, [car_rkv])
          o1 = HB(); act(o1.t[0:12, 0:128], pb.t[0:12, 0:128], AF.Copy, [pb], [o1])
          kb.dma('sp', oShiftP[0, 0:1536].rearrange("(g p) -> g p", p=128), o1.t[0:12, 0:128], reads=[o1], is_out=True)
          pb = PS()
          tr(pb, pb.t[0:2, 0:64], car_xw.t[:, 0:2], 64, [car_xw])
          o2 = HB(); act(o2.t[0:2, 0:64], pb.t[0:2, 0:64], AF.Copy, [pb], [o2])
          kb.dma('sp', oShiftP[0, 1536:1664].rearrange("(g p) -> g p", p=64), o2.t[0:2, 0:64], reads=[o2], is_out=True)
          pb = PS()
          tr(pb, pb.t[0:2, 0:128], car_xg.t[:, 0:2], 128, [car_xg])
          o3 = HB(); act(o3.t[0:2, 0:128], pb.t[0:2, 0:128], AF.Copy, [pb], [o3])
          kb.dma('sp', oShiftP[0:1, 1664:1792], o3.t[0:1, 0:128], reads=[o3], is_out=True)
          for g3 in range(3):
              pb = PS()
              for gg in range(4):
                  g = g3 * 4 + gg
                  tr(pb, pb.t[0:4, gg * 128:(gg + 1) * 128], car_cv.t[:, g, :], 128, [car_cv], inc=(gg == 3))
              cst = NB(('Pb', 'Kb', 'Vt')[g3], 512, slot=0)
              act(cst.t[0:4, 0:512], pb.t[0:4, 0:512], AF.Copy, [pb], [cst])
              kb.dma('sp', oConvP[0, :, g3 * 512:(g3 + 1) * 512], cst.t[0:3, 0:512], reads=[cst], is_out=True)
          CKP('pouts')
          kb.barrier()
          do_tile(True, 0)
          CKP('stile')
          for jb in range(4):
              pb = PS()
              if jb < 3:
                  for h_ in range(4):
                      tr(pb, pb.t[0:NS, h_ * 128:(h_ + 1) * 128], rawS.t[:, jb * 4 + h_, :], 128, [rawS], inc=(h_ == 3))
                  act(tokS.t[0:NS, jb * 512:(jb + 1) * 512], pb.t[0:NS, 0:512], AF.Copy, [pb], [tokS])
              else:
                  tr(pb, pb.t[0:NS, 0:64], rawS.t[0:64, 12, :], 64, [rawS], inc=False)
                  tr(pb, pb.t[0:NS, 64:128], rawS.t[0:64, 13, :], 64, [rawS], inc=False)
                  tr(pb, pb.t[0:NS, 128:256], rawS.t[0:128, 14, :], 128, [rawS])
                  act(tokS.t[0:NS, 1536:1792], pb.t[0:NS, 0:256], AF.Copy, [pb], [tokS])
          kb.dma('sp', oShiftS[:, :], tokS.t[0:NS, :], reads=[tokS], is_out=True)
          kb.dma('sp', oConvS[:, 0:2, :], stConv[:, 1:3, :], is_out=True)
          for g3 in range(3):
              pb = PS()
              for gg in range(4):
                  g = g3 * 4 + gg
                  tr(pb, pb.t[0:NS, gg * 128:(gg + 1) * 128], rawC.t[:, g, :], 128, [rawC], inc=(gg == 3))
              act(tokC.t[0:NS, g3 * 512:(g3 + 1) * 512], pb.t[0:NS, 0:512], AF.Copy, [pb], [tokC])
          kb.dma('sp', oConvS[:, 2, :], tokC.t[0:NS, 0:1536], reads=[tokC], is_out=True)
        except _Stop:
            pass
        kb._wait('sp', kb.out_events)
    return nc


def host_consts():
    cm = np.zeros((128, 1024), np.float32)
    cm[:, 0:128] = np.eye(128, dtype=np.float32)
    j = np.arange(64)[:, None]; i = np.arange(64)[None, :]
    for h0 in (0, 64):
        cm[h0:h0 + 64, 128:192] = (j < i)
        cm[h0:h0 + 64, 192:256] = (j <= i)
        cm[h0:h0 + 64, 960:1024] = (j > i)
    cm[0:64, 256:320] = np.where(j <= i, 0.0, -30000.0)
    cm[:, 320:448] = 1.0
    sm = np.ones(512, np.float32); sm[::64] = 0.0
    cm[:, 448:960] = sm[None, :]
    return cm


def prep_weights(inp):
    w_in = inp['w_in'][0]
    perm = []
    for hp in range(4):
        for j in range(3):
            perm.extend(range(j * 512 + hp * 128, j * 512 + hp * 128 + 128))
    perm.extend(range(1536, 1792))
    for h in range(4):
        for j in range(3):
            perm.extend(range(1792 + j * 512 + h * 128, 1792 + j * 512 + h * 128 + 128))
        perm.extend(range(3328 + h * 128, 3328 + h * 128 + 128))
    perm.extend(range(3840, 5896))
    w_in_p = np.ascontiguousarray(w_in[:, perm])
    w_barep = np.ascontiguousarray(np.repeat(w_in[:, 3840:3848], 128, axis=1))
    mu = inp['mu_shift'][0]
    c64 = np.zeros((64, 80), np.float32)
    for h in range(8):
        for j in range(3):
            c64[:, h * 3 + j] = mu[j * 512 + h * 64: j * 512 + h * 64 + 64]
    c64[:, 24] = mu[1536:1600]; c64[:, 25] = mu[1600:1664]
    def hcol(v):
        return np.ascontiguousarray(v.reshape(8, 64).T)
    c64[:, 26:34] = hcol(inp['rw_w0'][0]); c64[:, 34:42] = hcol(inp['rw_a0'][0])
    c64[:, 42:50] = hcol(inp['rw_kk'][0]); c64[:, 50:58] = hcol(inp['rw_ka'][0])
    c64[:, 58:66] = hcol(inp['rw_rk'][0].reshape(-1)); c64[:, 66:74] = hcol(inp['rw_ln_w'][0])
    c128 = np.zeros((128, 64), np.float32)
    c128[:, 0] = mu[1664:1792]
    c128[:, 1:9] = inp['norm_mix'][0].reshape(8, 128).T
    c128[:, 9:17] = inp['norm_ffn'][0].reshape(8, 128).T
    c128[:, 17:25] = inp['norm_ple'][0].reshape(8, 128).T
    c128[:, 25] = inp['gdn_norm'][0]
    c128[:, 26:30] = inp['gdn_a_log'][0][None, :]
    c128[:, 30:34] = inp['gdn_dt_bias'][0][None, :]
    c128[0:64, 40:48] = hcol(inp['rw_ln_b'][0])
    c128r = np.zeros((128, 64), np.float32)
    for hp in range(4):
        for j in range(3):
            c128r[:, hp * 3 + j] = mu[j * 512 + hp * 128: j * 512 + hp * 128 + 128]
    pcol = lambda v: np.ascontiguousarray(np.asarray(v).reshape(4, 128).T)
    c128r[:, 12:16] = pcol(inp['rw_w0'][0]); c128r[:, 16:20] = pcol(inp['rw_a0'][0])
    c128r[:, 20:24] = pcol(inp['rw_kk'][0]); c128r[:, 24:28] = pcol(inp['rw_ka'][0])
    c128r[:, 28:32] = pcol(inp['rw_rk'][0].reshape(-1)); c128r[:, 32:36] = pcol(inp['rw_ln_w'][0]); c128r[:, 36:40] = pcol(inp['rw_ln_b'][0])
    cmat2 = np.zeros((128, 192), np.float32)
    cmat2[0:64, 0:64] = 1.0; cmat2[64:128, 64:128] = 1.0
    cmat2[0:64, 128:192] = np.eye(64); cmat2[64:128, 128:192] = np.eye(64)
    cv = inp['gdn_conv'][0]
    convw = np.zeros((128, 48), np.float32)
    for tap in range(4):
        convw[:, tap * 12:(tap + 1) * 12] = cv[tap].reshape(12, 128).T
    return dict(
        w_in=w_in_p, w_barep=w_barep, c64=c64, c128=c128, convw=convw,
        w2=np.ascontiguousarray(inp['rw_w2'][0]), a2=np.ascontiguousarray(inp['rw_a2'][0]), g2=np.ascontiguousarray(inp['rw_g2'][0]),
        w_bra=np.ascontiguousarray(inp['w_branch_a'][0]), w_brb=np.ascontiguousarray(inp['w_branch_b'][0]),
        w_out=np.ascontiguousarray(inp['w_out'][0]),
        w_fg=np.ascontiguousarray(inp['w_ffn_gate'][0]), w_fu=np.ascontiguousarray(inp['w_ffn_up'][0]),
        w_fd=np.ascontiguousarray(inp['w_ffn_down'][0]),
        w_pg=np.ascontiguousarray(inp['w_ple_gate'][0]), w_pp=np.ascontiguousarray(inp['w_ple_proj'][0]),
        gfin=np.ascontiguousarray(np.broadcast_to(inp['norm_final'][None, :], (128, D))),
        cmat=host_consts(), cmat2=cmat2, c128r=c128r,
    )


def make_in_maps(inp, n_cores, TP):
    shared = prep_weights(inp)
    maps = []
    for c in range(n_cores):
        m = dict(shared)
        m['xP'] = np.ascontiguousarray(inp['x_prompt'][c, :TP])
        m['pP'] = np.ascontiguousarray(inp['p_prompt'][0, c, :TP])
        sl = slice(c * NS, (c + 1) * NS)
        m['xS'] = np.ascontiguousarray(inp['x_sample'][sl, 0])
        m['pS'] = np.ascontiguousarray(inp['p_sample'][0, sl, 0])
        m['stShift'] = np.ascontiguousarray(inp['state_shift'][0, sl, 0])
        m['stWkv'] = np.ascontiguousarray(inp['state_wkv'][0, sl])
        m['stConv'] = np.ascontiguousarray(inp['state_conv'][0, sl])
        m['stGdn'] = np.ascontiguousarray(inp['state_gdn'][0, sl])
        maps.append(m)
    return maps


def gather(results, n_cores):
    cat = lambda k: np.concatenate([r[k] for r in results], axis=0)
    yP = np.stack([r['yP'] for r in results], axis=0)
    yS = cat('yS')[:, None, :]
    return (yP, yS,
            cat('oShiftP')[None, :, None, :], cat('oWkvP')[None], cat('oConvP')[None], cat('oGdnP')[None],
            cat('oShiftS')[None, :, None, :], cat('oWkvS')[None], cat('oConvS')[None], cat('oGdnS')[None])


def kernel(**inputs):
    inp = {k: np.asarray(v) for k, v in inputs.items()}
    n = 8
    TP = inp['x_prompt'].shape[1]
    nc = build(TP // NP)
    maps = make_in_maps(inp, n, TP)
    res = run_bass_kernel_spmd(nc, maps, core_ids=list(range(n)))
    outs = gather(res.results, n)
    return tuple(np.ascontiguousarray(o, dtype=np.float32) for o in outs)
```

```python
import contextlib
import numpy as np
import concourse.bass as bass
import concourse.mybir as mybir
from concourse.bass_utils import run_bass_kernel_spmd
from concourse.alu_op_type import AluOpType as ALU

F32 = mybir.dt.float32
BF16 = mybir.dt.bfloat16
AF = mybir.ActivationFunctionType

D = 1024
NS = 16
NP = 256
TCN = NP // 128
CW = 5896
DFF = 2816
C0 = float(np.exp(-0.5))


class _Stop(Exception):
    pass


STOP = [None]
NSLOT = [2]
ONLY = ['rg']
OFFS = [30]


def CKP(name):
    if STOP[0] == name:
        raise _Stop()


class Buf:
    def __init__(self, t):
        self.t = t
        self.w = None
        self.r = {}


class KB:
    def __init__(self, nc, es):
        self.nc = nc
        self.es = es
        self.eng = {'pe': nc.tensor, 'dve': nc.vector, 'act': nc.scalar, 'pool': nc.gpsimd, 'sp': nc.sync}
        self.semh = {}
        self.cnt = {}
        self.seen = {e: {} for e in self.eng}
        for e in self.eng:
            self.semh[e] = es.enter_context(nc.semaphore("s_" + e))
            self.cnt[e] = 0
        self.ndma = 20
        self.dcur = 0
        for i in range(self.ndma):
            k = "d%d" % i
            self.semh[k] = es.enter_context(nc.semaphore("s_" + k))
            self.cnt[k] = 0
        self.nbuf = 0
        self.out_events = []

    def sb(self, shape, dt=F32, name=None):
        self.nbuf += 1
        t = self.es.enter_context(self.nc.sbuf_tensor(name or ("b%d" % self.nbuf), list(shape), dt))
        return Buf(t)

    def psb(self, name):
        t = self.es.enter_context(self.nc.psum_tensor(name, [128, 512], F32))
        return Buf(t)

    def _wait(self, e, deps):
        engine = self.eng[e]
        for (s, v) in deps:
            if self.seen[e].get(s, 0) >= v:
                continue
            engine.wait_ge(self.semh[s], v)
            self.seen[e][s] = v

    def _deps(self, e, reads, writes):
        deps = []
        for b in reads:
            if b.w is not None:
                if not (e == 'pe' and b.w[0] == 'pe'):
                    deps.append(b.w)
        for b in writes:
            if b.w is not None and b.w[0] != e:
                deps.append(b.w)
            for s, v in b.r.items():
                if s != e:
                    deps.append((s, v))
        return deps

    def _record(self, ev, reads, writes):
        for b in reads:
            b.r[ev[0]] = max(b.r.get(ev[0], 0), ev[1])
        for b in writes:
            b.w = ev
            b.r = {}

    def op(self, e, fn, reads=(), writes=(), inc=True):
        self._wait(e, self._deps(e, reads, writes))
        ins = fn(self.eng[e])
        if inc:
            self.cnt[e] += 1
            ins.then_inc(self.semh[e], 1)
            ev = (e, self.cnt[e])
        else:
            ev = (e, self.cnt[e] + 1)
        self._record(ev, reads, writes)
        return ev

    def dma(self, q, out, in_, reads=(), writes=(), is_out=False):
        k = "d%d" % self.dcur
        self.dcur = (self.dcur + 1) % self.ndma
        deps = self._deps(q, reads, writes)
        if self.cnt[k] > 0:
            deps.append((k, self.cnt[k]))
        self._wait(q, deps)
        self.cnt[k] += 16
        with self.nc.allow_non_contiguous_dma(reason="layout"):
            self.eng[q].dma_start(out=out, in_=in_).then_inc(self.semh[k], 16)
        ev = (k, self.cnt[k])
        self._record(ev, reads, writes)
        if is_out:
            self.out_events.append(ev)
        return ev

    def barrier(self):
        evs = []
        for e in self.eng:
            if self.cnt[e] > 0:
                evs.append((e, self.cnt[e]))
        for i in range(self.ndma):
            k = "d%d" % i
            if self.cnt[k] > 0:
                evs.append((k, self.cnt[k]))
        for e in self.eng:
            self._wait(e, [ev for ev in evs if ev[0] != e])


def build(n_ptiles, dbg=()):
    TP = n_ptiles * NP
    nc = bass.Bass("TRN2", target_bir_lowering=False)

    def din(name, shape):
        return nc.dram_tensor(name, list(shape), F32, kind="ExternalInput").ap()

    def dout(name, shape):
        return nc.dram_tensor(name, list(shape), F32, kind="ExternalOutput").ap()

    xP = din("xP", [TP, D]); xS = din("xS", [NS, D])
    pP = din("pP", [TP, 256]); pS = din("pS", [NS, 256])
    stShift = din("stShift", [NS, 1792]); stWkv = din("stWkv", [NS, 8, 64, 64])
    stConv = din("stConv", [NS, 3, 1536]); stGdn = din("stGdn", [NS, 4, 128, 128])
    w_in = din("w_in", [D, CW]); w_barep = din("w_barep", [D, 1024])
    c64 = din("c64", [64, 80]); c128 = din("c128", [128, 64]); convw = din("convw", [128, 48])
    w2 = din("w2", [64, 512]); a2 = din("a2", [64, 512]); g2 = din("g2", [128, 512])
    w_bra = din("w_bra", [512, D]); w_brb = din("w_brb", [512, D]); w_out = din("w_out", [D, D])
    w_fg = din("w_fg", [D, DFF]); w_fu = din("w_fu", [D, DFF]); w_fd = din("w_fd", [DFF, D])
    w_pg = din("w_pg", [D, D]); w_pp = din("w_pp", [256, D])
    gfin = din("gfin", [128, D]); cmat = din("cmat", [128, 1024]); cmat2 = din("cmat2", [128, 192]); c128rd = din("c128r", [128, 64])
    yP = dout("yP", [TP, D]); yS = dout("yS", [NS, D])
    oShiftP = dout("oShiftP", [1, 1792]); oWkvP = dout("oWkvP", [1, 8, 64, 64])
    oConvP = dout("oConvP", [1, 3, 1536]); oGdnP = dout("oGdnP", [1, 4, 128, 128])
    oShiftS = dout("oShiftS", [NS, 1792]); oWkvS = dout("oWkvS", [NS, 8, 64, 64])
    oConvS = dout("oConvS", [NS, 3, 1536]); oGdnS = dout("oGdnS", [NS, 4, 128, 128])
    dbg_outs = {}
    for nm, shp in dbg:
        dbg_outs[nm] = nc.dram_tensor("dbg_" + nm, list(shp), BF16 if nm in ("oa", "ob", "mix") else F32, kind="ExternalOutput").ap()

    es = contextlib.ExitStack()
    with es:
        kb = KB(nc, es)
        op = kb.op
        cm = kb.sb([128, 1024]); kb.dma('sp', cm.t[:], cmat, writes=[cm])
        ident = cm.t[:, 0:128]
        maskS = cm.t[0:64, 128:192]
        maskI = cm.t[0:64, 192:256]
        negI = cm.t[0:64, 256:320]
        ones = cm.t[:, 320:448]
        scanm = cm.t[:, 448:960]
        maskS2 = cm.t[:, 128:192]; maskI2 = cm.t[:, 192:256]; maskL2 = cm.t[:, 960:1024]
        c64b = kb.sb([64, 80]); kb.dma('sp', c64b.t[:], c64, writes=[c64b])
        c128b = kb.sb([128, 64]); kb.dma('sp', c128b.t[:], c128, writes=[c128b])
        cvw = kb.sb([128, 48]); kb.dma('sp', cvw.t[:], convw, writes=[cvw])
        w2b = kb.sb([64, 512], BF16); kb.dma('pool', w2b.t[:], w2, writes=[w2b])
        a2b = kb.sb([64, 512], BF16); kb.dma('pool', a2b.t[:], a2, writes=[a2b])
        g2b = kb.sb([128, 512], BF16); kb.dma('pool', g2b.t[:], g2, writes=[g2b])
        gfb = kb.sb([128, D]); kb.dma('sp', gfb.t[:], gfin, writes=[gfb])
        MU_RKV, MU_XW, MU_XAA, W0, A0, KKc, KAc, RKc, LNW = 0, 24, 25, 26, 34, 42, 50, 58, 66
        MU_XG, GMIX, GFFN, GPLE, GDNN, ALOG, DTB, LNB = 0, 1, 9, 17, 25, 26, 30, 40
        c128r = kb.sb([128, 64]); kb.dma('sp', c128r.t[:], c128rd, writes=[c128r])
        cm2 = kb.sb([128, 192]); kb.dma('sp', cm2.t[:], cmat2, writes=[cm2])
        bones = cm2.t[:, 0:128]
        cm16 = kb.sb([128, 128], BF16)
        op('dve', lambda e: e.tensor_copy(out=cm16.t[:], in_=cm.t[:, 0:128]), [cm], [cm16])
        id16 = cm16.t[:, :]
        on16 = kb.sb([128, 256], BF16)
        op('dve', lambda e: e.tensor_copy(out=on16.t[:, 0:128], in_=cm.t[:, 320:448]), [cm], [on16])
        op('dve', lambda e: e.tensor_copy(out=on16.t[:, 128:256], in_=cm2.t[:, 0:128]), [cm2], [on16])
        ones16 = on16.t[:, 0:128]; bones16 = on16.t[:, 128:256]
        ident2 = cm2.t[:, 128:192]
        W0r, A0r, KKr, KAr, RKr, LNWr, LNBr = 12, 16, 20, 24, 28, 32, 36
        omu128r = kb.sb([128, 12])
        op('dve', lambda e: e.tensor_scalar(out=omu128r.t[:], in0=c128r.t[:, 0:12], scalar1=-1.0, scalar2=1.0, op0=ALU.mult, op1=ALU.add), [c128r], [omu128r])
        omu64 = kb.sb([64, 26])
        op('dve', lambda e: e.tensor_scalar(out=omu64.t[:], in0=c64b.t[:, 0:26], scalar1=-1.0, scalar2=1.0, op0=ALU.mult, op1=ALU.add), [c64b], [omu64])
        omu128 = kb.sb([128, 1])
        op('dve', lambda e: e.tensor_scalar(out=omu128.t[:], in0=c128b.t[:, 0:1], scalar1=-1.0, scalar2=1.0, op0=ALU.mult, op1=ALU.add), [c128b], [omu128])
        nexpA = kb.sb([128, 4])
        op('act', lambda e: e.activation(out=nexpA.t[:], in_=c128b.t[:, ALOG:ALOG + 4], func=AF.Exp), [c128b], [nexpA])
        op('dve', lambda e: e.tensor_scalar(out=nexpA.t[:], in0=nexpA.t[:], scalar1=-1.0, scalar2=None, op0=ALU.mult), [nexpA], [nexpA])

        ps = [kb.psb("ps%d" % i) for i in range(8)]
        pcur = [0]

        def PS():
            b = ps[pcur[0]]
            pcur[0] = (pcur[0] + 1) % 8
            return b
        pscur = [0, 0]

        def PSslot(slot):
            b = ps[slot * 3 + pscur[slot]]
            pscur[slot] = (pscur[slot] + 1) % 3
            return b

        NW = 5
        wring = [kb.sb([128, 8, 256], BF16, "wr%d" % i) for i in range(NW)]
        wcur = [0]

        def wload(w, r0, nk, pk, c0, ncols):
            b = wring[wcur[0]]
            wcur[0] = (wcur[0] + 1) % NW
            src = w[r0:r0 + nk * pk, c0:c0 + ncols].rearrange("(k p) c -> p k c", p=pk)
            kb.dma('pool', b.t[0:pk, 0:nk, 0:ncols], src, writes=[b])
            return b

        conv = {}

        def convert(name, src, R, Cc):
            dst = nc.dram_tensor("cw_" + name, [R, Cc], BF16, kind="Internal").ap()
            b = Buf(dst)
            for c0 in range(0, Cc, 1024):
                w_ = min(1024, Cc - c0)
                kb.dma('pool', dst[:, c0:c0 + w_], src[:, c0:c0 + w_], writes=[b])
            conv[name] = (dst, b)

        def wloadc(name, r0, nk, pk, c0, ncols):
            dst_, cb_ = conv[name]
            b = wring[wcur[0]]
            wcur[0] = (wcur[0] + 1) % NW
            src = dst_[r0:r0 + nk * pk, c0:c0 + ncols].rearrange("(k p) c -> p k c", p=pk)
            kb.dma('pool', b.t[0:pk, 0:nk, 0:ncols], src, reads=[cb_], writes=[b])
            return b

        RW = 11500
        region = kb.sb([128, RW], name="region")
        bumpP = [0]; bumpS = [0]

        def rview(bump, width, pat=None, parts=128, **kw):
            a_ = bump[0]; bump[0] += width
            assert bump[0] <= RW, (bump[0], RW)
            ap = region.t[0:parts, a_:a_ + width]
            if pat is not None:
                ap = ap.rearrange(pat, **kw)
            return Buf(ap)

        xtok = kb.sb([128, TCN, D], name="xtok")
        xs = kb.sb([128, D], name="xs")
        uT = kb.sb([128, 8, NP], BF16, name="uT")
        ss = kb.sb([128, 8], name="ss")
        car_rkv = kb.sb([128, 12], name="car_rkv"); car_xw = kb.sb([64, 2], name="car_xw"); car_xg = kb.sb([128, 2], name="car_xg")
        car_cv = kb.sb([128, 12, 4], name="car_cv")
        for b_ in (car_rkv, car_xw, car_xg, car_cv):
            op('dve', lambda e, b_=b_: e.memset(b_.t[:], 0.0), [], [b_])
        stR = [kb.sb([128, 64], name="stR%d" % i) for i in range(4)]
        stG = [kb.sb([128, 128], name="stG%d" % i) for i in range(4)]
        for b_ in stR + stG:
            op('dve', lambda e, b_=b_: e.memset(b_.t[:], 0.0), [], [b_])
        stRs = rview(bumpS, NS * 64, "p (s v) -> p s v", v=64)
        stGs = rview(bumpS, NS * 128, "p (s v) -> p s v", v=128)
        tw = kb.sb([64, NP], BF16, name="tw"); xaaS = kb.sb([64, NP], BF16, name="xaaS"); sgS = kb.sb([128, NP], BF16, name="sgS")
        oa = kb.sb([128, 4, NP], BF16, name="oa"); ob = kb.sb([128, 4, NP], BF16, name="ob")
        mixT = kb.sb([128, 8, NP], BF16, name="mixT")
        hfT = kb.sb([128, 22, NP], BF16, name="hfT")
        peT = kb.sb([128, 2, NP], BF16, name="peT")
        ptok = kb.sb([128, TCN, 256], name="ptok")
        prevS = rview(bumpS, 16 * NS, "p (g s) -> p g s", s=NS)
        rawS = rview(bumpS, 16 * NS, "p (g s) -> p g s", s=NS)
        histS = rview(bumpS, 36 * NS, "p (g s) -> p g s", s=NS)
        rawC = rview(bumpS, 12 * NS, "p (g s) -> p g s", s=NS)
        tokS = rview(bumpS, 1792, parts=NS)
        tokC = rview(bumpS, 1536, parts=NS)
        NHB = 7
        hb = [[kb.sb([128, NP + 4], name="hb%d" % i) for i in range(NHB)], [rview(bumpP, NP + 4) for i in range(6)]]
        hcur = [0, 0]

        def HB(slot=0):
            r_ = hb[slot]
            b = r_[hcur[slot]]
            hcur[slot] = (hcur[slot] + 1) % len(r_)
            return b
        named = {}

        def NB(name, width=NP + 4, slot=0):
            key = (name, slot)
            if key not in named:
                if slot == 0:
                    named[key] = kb.sb([128, width], name="n_" + name)
                elif slot == 1:
                    named[key] = rview(bumpP, width)
                else:
                    named[key] = rview(bumpS, width)
            return named[key]

        def act(out, in_, func, reads, writes, bias=None, scale=None, **kw):
            kws = dict(kw)
            if bias is not None:
                kws['bias'] = bias
            if scale is not None:
                kws['scale'] = scale
            return op('act', lambda e: e.activation(out=out, in_=in_, func=func, **kws), reads, writes)

        def tt(out, in0, in1, o, reads, writes, eng='dve'):
            return op(eng, lambda e: e.tensor_tensor(out=out, in0=in0, in1=in1, op=o), reads, writes)

        def ts(out, in0, s1, s2, o0, o1, reads, writes, eng='dve'):
            if o1 is None:
                return op(eng, lambda e: e.tensor_scalar(out=out, in0=in0, scalar1=s1, scalar2=None, op0=o0), reads, writes)
            return op(eng, lambda e: e.tensor_scalar(out=out, in0=in0, scalar1=s1, scalar2=s2, op0=o0, op1=o1), reads, writes)

        def stt(out, in0, sc, in1, o0, o1, reads, writes):
            return op('dve', lambda e: e.scalar_tensor_tensor(out=out, in0=in0, scalar=sc, in1=in1, op0=o0, op1=o1), reads, writes)

        def mm(pb, out, lhsT, rhs, reads, start, stop):
            return op('pe', lambda e: e.matmul(out, lhsT, rhs, start=start, stop=stop), reads, [pb], inc=stop)

        def tr(pb, out, in_, rows, reads, inc=True):
            return op('pe', lambda e: e.transpose(out, in_, ident[0:rows, 0:rows]), list(reads) + [cm], [pb], inc=inc)

        def rsqrt(out, in_, scale, eps, reads, writes):
            act(out, in_, AF.Ln, reads, writes, bias=eps, scale=scale)
            act(out, out, AF.Exp, writes, writes, scale=-0.5)

        def dbgdump(name, ap_sb, bufs):
            if name in dbg_outs:
                kb.dma('sp', dbg_outs[name], ap_sb, reads=bufs, is_out=True)

        def norm_T(N, rows_list, gcol):
            TC = len(rows_list)
            for tc, rows in enumerate(rows_list):
                act(xs.t[:rows, :], xtok.t[:rows, tc, :], AF.Square, [xtok], [xs, ss], accum_out=ss.t[:rows, tc:tc + 1])
            mr = max(rows_list)
            rsqrt(ss.t[:mr, 4:4 + TC], ss.t[:mr, 0:TC], 1.0 / D, 1e-6, [ss], [ss])
            pbs = [PS() for _ in range(8)]
            for tc, rows in enumerate(rows_list):
                ts(xs.t[:rows, :], xtok.t[:rows, tc, :], ss.t[:rows, 4 + tc:5 + tc], None, ALU.mult, None, [xtok, ss], [xs])
                for k in range(8):
                    tr(pbs[k], pbs[k].t[:, tc * 128:tc * 128 + rows], xs.t[:rows, k * 128:(k + 1) * 128], rows, [xs])
            for k in range(8):
                if k % 2 == 0:
                    ts(uT.t[:, k, :N], pbs[k].t[:, :N], c128b.t[:, gcol + k:gcol + k + 1], None, ALU.mult, None, [pbs[k], c128b], [uT])
                else:
                    act(uT.t[:, k, :N], pbs[k].t[:, :N], AF.Copy, [pbs[k], c128b], [uT], scale=c128b.t[:, gcol + k:gcol + k + 1])

        def proj_fm(pb, out_ap, wt, coff, M, N, nk=8, rhsb=None, pk=128):
            rb = rhsb or uT
            for k in range(nk):
                mm(pb, out_ap, wt.t[0:pk, k, coff:coff + M], rb.t[0:pk, k, :N], [wt, rb], k == 0, k == nk - 1)

        def shift(P, N, pb, psap, mu, omu, carry, out_b, out_ap, is_s, prev_ap=None, raw_ap=None, raw_b=None, tmp=HB):
            t1 = tmp()
            if is_s:
                act(t1.t[:P, :N], prev_ap, AF.Copy, [prevS], [t1], scale=mu)
                act(raw_ap, psap, AF.Copy, [pb], [raw_b])
            else:
                act(t1.t[:P, 1:N], psap[:, 0:N - 1], AF.Copy, [pb], [t1], scale=mu)
                act(t1.t[:P, 0:1], carry[1], AF.Copy, [carry[0]], [t1], scale=mu)
                act(carry[1], psap[:, N - 1:N], AF.Copy, [pb], [carry[0]])
            stt(out_ap, psap, omu, t1.t[:P, :N], ALU.mult, ALU.add, [pb, t1], [out_b])

        named16 = {}

        def NB16(name, width=NP + 4, slot=0):
            key = (name, slot)
            if key not in named16:
                if slot == 0:
                    named16[key] = kb.sb([128, width], BF16, name="h_" + name)
                else:
                    a_ = bumpP[0]; bumpP[0] += (width + 1) // 2
                    assert bumpP[0] <= RW
                    named16[key] = Buf(region.t[:, a_:a_ + (width + 1) // 2].bitcast(BF16))
            return named16[key]
        hb16 = [[NB16("r16_%d" % i, 132, sl_) for i in range(4)] for sl_ in (0, 1)]
        hbw = [[NB16("w16_%d" % i, NP + 4, sl_) for i in range(2)] for sl_ in (0, 1)]
        hbwcur = [0, 0]

        def HBW(slot=0):
            b = hbw[slot][hbwcur[slot]]
            hbwcur[slot] = (hbwcur[slot] + 1) % 2
            return b
        h16cur = [0, 0]

        def HB16(slot):
            b = hb16[slot][h16cur[slot]]
            h16cur[slot] = (h16cur[slot] + 1) % 4
            return b

        natS = rview(bumpS, NS * 128, "p (s h k) -> p s h k", parts=64, h=2, k=64)

        def scan(slot, Kd, Vd, N, C, nseq, QsT, RsT, PbT, KbT, VT, A, WcB, wc_ap, st_b, st_ap, ypb):
            nch = N // C
            cps = nch // nseq
            W = nch * C

            def tmaj(srcb, Fd, name):
                dst = NB16(name, 512, slot=slot)
                per = 512 // Fd
                for c0 in range(0, nch, per):
                    pb = PSslot(slot)
                    n = min(per, nch - c0)
                    for c in range(c0, c0 + n):
                        tr(pb, pb.t[0:C, (c - c0) * Fd:(c - c0 + 1) * Fd], srcb.t[0:Fd, c * C:(c + 1) * C], Fd, [srcb], inc=(c == c0 + n - 1))
                    act(dst.t[0:C, c0 * Fd:(c0 + n) * Fd], pb.t[0:C, 0:n * Fd], AF.Copy, [pb], [dst])
                return dst
            pre_t = (nch * max(Kd, Vd) <= 512)
            if pre_t:
                pbt = tmaj(PbT, Kd, 'Pb'); kbt = tmaj(KbT, Kd, 'Kb'); vt = tmaj(VT, Vd, 'Vt')
                gP = lambda c: pbt.t[0:C, c * Kd:(c + 1) * Kd]
                gK = lambda c: kbt.t[0:C, c * Kd:(c + 1) * Kd]
                gV = lambda c: vt.t[0:C, c * Vd:(c + 1) * Vd]
            yield
            TinvT = None
            if C > 1:
                NTb = A['NT']; NT16 = A['NT16']
                Xb = NB16('X0', slot=slot); pb = PSslot(slot)
                for c in range(nch):
                    tr(pb, pb.t[0:C, c * C:(c + 1) * C], NTb.t[0:C, c * C:(c + 1) * C], C, [NTb], inc=(c == nch - 1))
                act(Xb.t[0:C, 0:W], pb.t[0:C, 0:W], AF.Copy, [pb], [Xb])
                XTb = NT16
                PT = NB('PT0', slot=slot)
                tt(PT.t[0:C, 0:W].rearrange("p (c i) -> p c i", i=C), NTb.t[0:C, 0:W].rearrange("p (c i) -> p c i", i=C),
                   cm.t[0:C, 0:C].unsqueeze(1).broadcast_to([C, nch, C]), ALU.add, [NTb, cm], [PT])
                PTh = NB16('PTh0', slot=slot)
                act(PTh.t[0:C, 0:W], PT.t[0:C, 0:W], AF.Copy, [PT], [PTh])
                yield
                nlev = {64: 5, 2: 0}[C]
                for lv in range(1, nlev + 1):
                    pbx = PSslot(slot)
                    for c in range(nch):
                        sl = slice(c * C, (c + 1) * C)
                        op('pe', lambda e, sl=sl, pbx=pbx, XTb=XTb, Xb=Xb: e.matmul(pbx.t[0:C, sl], XTb.t[0:C, sl], Xb.t[0:C, sl], start=True, stop=True), [XTb, Xb], [pbx], inc=(c == nch - 1))
                    Xn = NB16('X%d' % (lv % 2), slot=slot)
                    XTn = None
                    if lv < nlev:
                        pbt_ = PSslot(slot)
                        for c in range(nch):
                            sl = slice(c * C, (c + 1) * C)
                            op('pe', lambda e, sl=sl, pbt_=pbt_, XTb=XTb, Xb=Xb: e.matmul(pbt_.t[0:C, sl], Xb.t[0:C, sl], XTb.t[0:C, sl], start=True, stop=True), [XTb, Xb], [pbt_], inc=(c == nch - 1))
                    act(Xn.t[0:C, 0:W], pbx.t[0:C, 0:W], AF.Copy, [pbx], [Xn])
                    yield
                    if lv < nlev:
                        XTn = NB16('XT%d' % (lv % 2), slot=slot)
                        ts(XTn.t[0:C, 0:W], pbt_.t[0:C, 0:W], 1.0, None, ALU.mult, None, [pbt_], [XTn])
                    pbp = PSslot(slot)
                    for c in range(nch):
                        sl = slice(c * C, (c + 1) * C)
                        op('pe', lambda e, sl=sl, pbp=pbp, Xn=Xn, PTh=PTh: e.matmul(pbp.t[0:C, sl], Xn.t[0:C, sl], PTh.t[0:C, sl], start=True, stop=True), [Xn, PTh], [pbp], inc=(c == nch - 1))
                    PTn = NB('PT%d' % (lv % 2), slot=slot)
                    tt(PTn.t[0:C, 0:W], PT.t[0:C, 0:W], pbp.t[0:C, 0:W], ALU.add, [PT, pbp], [PTn])
                    PThn = NB16('PTh%d' % (lv % 2), slot=slot)
                    act(PThn.t[0:C, 0:W], PTn.t[0:C, 0:W], AF.Copy, [PTn], [PThn])
                    PT = PTn; PTh = PThn
                    yield
                    Xb = Xn
                    if lv < nlev:
                        XTb = XTn
                PT = PTh
                TinvT = PT
            for s in range(nseq):
                for cc in range(cps):
                    c = s * cps + cc
                    sl = slice(c * C, (c + 1) * C)
                    stap = st_ap(s)
                    if pre_t:
                        aP = gP(c); aK = gK(c); aV = gV(c)
                    else:
                        pb = PSslot(slot)
                        tr(pb, pb.t[0:C, 0:Kd], PbT.t[0:Kd, sl], Kd, [PbT], inc=False)
                        tr(pb, pb.t[0:C, 128:128 + Kd], KbT.t[0:Kd, sl], Kd, [KbT], inc=False)
                        tr(pb, pb.t[0:C, 256:256 + Vd], VT.t[0:Vd, sl], Vd, [VT])
                        row = HB(slot)
                        act(row.t[0:C, 0:256], pb.t[0:C, 0:256], AF.Copy, [pb], [row])
                        rowv = HB(slot)
                        act(rowv.t[0:C, 0:128], pb.t[0:C, 256:384], AF.Copy, [pb], [rowv])
                        aP = row.t[0:C, 0:Kd]; aK = row.t[0:C, 128:128 + Kd]; aV = rowv.t[0:C, 0:Vd]
                        pbt = row; kbt = row; vt = rowv
                    zp = PSslot(slot)
                    if C > 1:
                        op('pe', lambda e: e.matmul(zp.t[0:C, 0:Vd], QsT.t[0:Kd, sl], stap, start=True, stop=False), [QsT, st_b], [zp], inc=False)
                        op('pe', lambda e: e.matmul(zp.t[0:C, 0:Vd], A['LakT'].t[0:C, sl], aV, start=False, stop=True), [A['LakT'], vt], [zp])
                        yield
                        Zs = HB16(slot)
                        act(Zs.t[0:C, 0:Vd], zp.t[0:C, 0:Vd], AF.Copy, [zp], [Zs])
                        up = PSslot(slot)
                        op('pe', lambda e: e.matmul(up.t[0:C, 0:Vd], TinvT.t[0:C, sl], Zs.t[0:C, 0:Vd], start=True, stop=True), [TinvT, Zs], [up])
                        yield
                        U = HB16(slot)
                        ts(U.t[0:C, 0:Vd], up.t[0:C, 0:Vd], 1.0, None, ALU.mult, None, [up], [U])
                    else:
                        op('pe', lambda e: e.matmul(zp.t[0:C, 0:Vd], QsT.t[0:Kd, sl], stap, start=True, stop=True), [QsT, st_b], [zp])
                        U = HB(slot)
                        act(U.t[0:C, 0:Vd], zp.t[0:C, 0:Vd], AF.Copy, [zp], [U])
                    op('pe', lambda e: e.matmul(ypb.t[0:Vd, sl], stap, RsT.t[0:Kd, sl], start=True, stop=False), [st_b, RsT], [ypb], inc=False)
                    op('pe', lambda e: e.matmul(ypb.t[0:Vd, sl], U.t[0:C, 0:Vd], A['ArbT'].t[0:C, sl], start=False, stop=False), [U, A['ArbT']], [ypb], inc=False)
                    op('pe', lambda e: e.matmul(ypb.t[0:Vd, sl], aV, A['ArkT'].t[0:C, sl], start=False, stop=True), [vt, A['ArkT']], [ypb])
                    yield
                    sp_ = PSslot(slot)
                    op('pe', lambda e: e.matmul(sp_.t[0:Kd, 0:Vd], aP, U.t[0:C, 0:Vd], start=True, stop=False), [pbt, U], [sp_], inc=False)
                    op('pe', lambda e: e.matmul(sp_.t[0:Kd, 0:Vd], aK, aV, start=False, stop=True), [kbt, vt], [sp_])
                    stt(stap, stap, wc_ap(c), sp_.t[0:Kd, 0:Vd], ALU.mult, ALU.add, [st_b, WcB, sp_], [st_b])
                    yield

        def scan_sample(Kd, Vd, QsT, RsT, PbT, KbT, VT, arb, ark, WcB, wc16, stX):
            n = NS
            QR = NB('sQR', 36, slot=2); PK = NB('sPK', 36, slot=2); UV = NB('sUV', 36, slot=2)
            q3 = QR.t[0:Kd, 0:2 * n].rearrange("p (s two) -> p s two", two=2)
            p3 = PK.t[0:Kd, 0:2 * n].rearrange("p (s two) -> p s two", two=2)
            u3 = UV.t[0:Vd, 0:2 * n].rearrange("p (s two) -> p s two", two=2)
            act(q3[:, :, 0], QsT.t[0:Kd, 0:n], AF.Copy, [QsT], [QR])
            ts(q3[:, :, 1], RsT.t[0:Kd, 0:n], 1.0, None, ALU.mult, None, [RsT], [QR])
            act(p3[:, :, 0], PbT.t[0:Kd, 0:n], AF.Copy, [PbT], [PK])
            ts(p3[:, :, 1], KbT.t[0:Kd, 0:n], 1.0, None, ALU.mult, None, [KbT], [PK])
            ts(u3[:, :, 1], VT.t[0:Vd, 0:n], 1.0, None, ALU.mult, None, [VT], [UV])
            pq = PS()
            for s_ in range(n):
                op('pe', lambda e, s_=s_: e.matmul(pq.t[0:Vd, 2 * s_:2 * s_ + 2], stX.t[:, s_, :], QR.t[0:Kd, 2 * s_:2 * s_ + 2], start=True, stop=True),
                   [stX, QR], [pq], inc=(s_ == n - 1))
            pq3 = pq.t[0:Vd, 0:2 * n].rearrange("p (s two) -> p s two", two=2)
            act(u3[:, :, 0], pq3[:, :, 0], AF.Copy, [pq], [UV])
            Y = HB(); t2 = HB()
            tt(Y.t[0:Vd, 0:n], pq3[:, :, 0], arb.t[0:Vd, 0:n], ALU.mult, [pq, arb], [Y])
            tt(t2.t[0:Vd, 0:n], VT.t[0:Vd, 0:n], ark.t[0:Vd, 0:n], ALU.mult, [VT, ark], [t2])
            tt(Y.t[0:Vd, 0:n], Y.t[0:Vd, 0:n], t2.t[0:Vd, 0:n], ALU.add, [Y, t2], [Y])
            tt(Y.t[0:Vd, 0:n], Y.t[0:Vd, 0:n], pq3[:, :, 1], ALU.add, [Y, pq], [Y])
            gs = 512 // max(Kd, Vd)
            pkr = NB('sPKr', 512, slot=2); uvr = NB('sUVr', 512, slot=2)
            for g0 in range(0, n, gs):
                pt = PS()
                for j in range(gs):
                    s_ = g0 + j
                    tr(pt, pt.t[0:2, j * Kd:(j + 1) * Kd], PK.t[0:Kd, 2 * s_:2 * s_ + 2], Kd, [PK], inc=(j == gs - 1))
                act(pkr.t[0:2, 0:gs * Kd], pt.t[0:2, 0:gs * Kd], AF.Copy, [pt], [pkr])
                pu = PS()
                for j in range(gs):
                    s_ = g0 + j
                    tr(pu, pu.t[0:2, j * Vd:(j + 1) * Vd], UV.t[0:Vd, 2 * s_:2 * s_ + 2], Vd, [UV], inc=(j == gs - 1))
                ts(uvr.t[0:2, 0:gs * Vd], pu.t[0:2, 0:gs * Vd], 1.0, None, ALU.mult, None, [pu], [uvr])
                pp = PS()
                for j in range(gs):
                    op('pe', lambda e, j=j: e.matmul(pp.t[0:Kd, j * Vd:(j + 1) * Vd], pkr.t[0:2, j * Kd:(j + 1) * Kd], uvr.t[0:2, j * Vd:(j + 1) * Vd], start=True, stop=True),
                       [pkr, uvr], [pp], inc=(j == gs - 1))
                sv = stX.t[:, g0:g0 + gs, :]
                tt(sv, sv, wc16[:, g0:g0 + gs].unsqueeze(2).broadcast_to([Kd, gs, Vd]), ALU.mult, [stX, WcB], [stX])
                tt(sv, sv, pp.t[0:Kd, 0:gs * Vd].rearrange("p (s v) -> p s v", v=Vd), ALU.add, [stX, pp], [stX])
            return Y

        def scan2(slot, nh, Kd, Vd, N, C, QsT, RsT, PbT, KbT, VT, A, WcB, wc_ap, st_b, ypb):
            nch = N // C
            nb = nh * nch
            W = nb * C

            def tmaj(srcb, name):
                dst = NB16(name, 516, slot=slot)
                pb = PSslot(slot)
                for c in range(nch):
                    tr(pb, pb.t[0:C, c * 128:(c + 1) * 128], srcb.t[0:128, c * C:(c + 1) * C], 128, [srcb], inc=(c == nch - 1))
                act(dst.t[0:C, 0:nch * 128], pb.t[0:C, 0:nch * 128], AF.Copy, [pb], [dst])
                return dst
            pbt = tmaj(PbT, 'Pb'); kbt = tmaj(KbT, 'Kb'); vt = tmaj(VT, 'Vt')
            yield
            NTb = A['NT']; NT16 = A['NT16']
            Xb = NB16('X0', 516, slot=slot); pb = PSslot(slot)
            for b in range(nb):
                tr(pb, pb.t[0:C, b * C:(b + 1) * C], NTb.t[0:C, b * C:(b + 1) * C], C, [NTb], inc=(b == nb - 1))
            act(Xb.t[0:C, 0:W], pb.t[0:C, 0:W], AF.Copy, [pb], [Xb])
            XTb = NT16
            PT = NB('PT0', 516, slot=slot)
            tt(PT.t[0:C, 0:W].rearrange("p (c i) -> p c i", i=C), NTb.t[0:C, 0:W].rearrange("p (c i) -> p c i", i=C),
               cm.t[0:C, 0:C].unsqueeze(1).broadcast_to([C, nb, C]), ALU.add, [NTb, cm], [PT])
            PTh = NB16('PTh0', 516, slot=slot)
            act(PTh.t[0:C, 0:W], PT.t[0:C, 0:W], AF.Copy, [PT], [PTh])
            yield
            nlev = 5
            for lv in range(1, nlev + 1):
                pbx = PSslot(slot)
                for b in range(nb):
                    sl = slice(b * C, (b + 1) * C)
                    op('pe', lambda e, sl=sl, pbx=pbx, XTb=XTb, Xb=Xb: e.matmul(pbx.t[0:C, sl], XTb.t[0:C, sl], Xb.t[0:C, sl], start=True, stop=True), [XTb, Xb], [pbx], inc=(b == nb - 1))
                Xn = NB16('X%d' % (lv % 2), 516, slot=slot)
                XTn = None
                if lv < nlev:
                    pbt_ = PSslot(slot)
                    for b in range(nb):
                        sl = slice(b * C, (b + 1) * C)
                        op('pe', lambda e, sl=sl, pbt_=pbt_, XTb=XTb, Xb=Xb: e.matmul(pbt_.t[0:C, sl], Xb.t[0:C, sl], XTb.t[0:C, sl], start=True, stop=True), [XTb, Xb], [pbt_], inc=(b == nb - 1))
                act(Xn.t[0:C, 0:W], pbx.t[0:C, 0:W], AF.Copy, [pbx], [Xn])
                yield
                if lv < nlev:
                    XTn = NB16('XT%d' % (lv % 2), 516, slot=slot)
                    ts(XTn.t[0:C, 0:W], pbt_.t[0:C, 0:W], 1.0, None, ALU.mult, None, [pbt_], [XTn])
                pbp = PSslot(slot)
                for b in range(nb):
                    sl = slice(b * C, (b + 1) * C)
                    op('pe', lambda e, sl=sl, pbp=pbp, Xn=Xn, PTh=PTh: e.matmul(pbp.t[0:C, sl], Xn.t[0:C, sl], PTh.t[0:C, sl], start=True, stop=True), [Xn, PTh], [pbp], inc=(b == nb - 1))
                PTn = NB('PT%d' % (lv % 2), 516, slot=slot)
                tt(PTn.t[0:C, 0:W], PT.t[0:C, 0:W], pbp.t[0:C, 0:W], ALU.add, [PT, pbp], [PTn])
                PThn = NB16('PTh%d' % (lv % 2), 516, slot=slot)
                act(PThn.t[0:C, 0:W], PTn.t[0:C, 0:W], AF.Copy, [PTn], [PThn])
                PT = PTn; PTh = PThn
                yield
                Xb = Xn
                if lv < nlev:
                    XTb = XTn
            TinvT = PTh
            LakT = A['LakT']; ArbT = A['ArbT']; ArkT = A['ArkT']
            for c in range(nch):
                sl = slice(c * C, (c + 1) * C)
                zp = PSslot(slot)
                for hl in range(nh):
                    pr = slice(hl * Kd, (hl + 1) * Kd); vc = slice(hl * Vd, (hl + 1) * Vd)
                    bsl = slice((hl * nch + c) * C, (hl * nch + c + 1) * C)
                    aV = vt.t[0:C, c * 128 + hl * Vd:c * 128 + (hl + 1) * Vd]
                    op('pe', lambda e, pr=pr, vc=vc: e.matmul(zp.t[0:C, vc], QsT.t[pr, sl], st_b.t[pr, :], start=True, stop=False), [QsT, st_b], [zp], inc=False)
                    op('pe', lambda e, vc=vc, bsl=bsl, aV=aV: e.matmul(zp.t[0:C, vc], LakT.t[0:C, bsl], aV, start=False, stop=True), [LakT, vt], [zp], inc=(hl == nh - 1))
                yield
                Zs = HB16(slot)
                act(Zs.t[0:C, 0:128], zp.t[0:C, 0:128], AF.Copy, [zp], [Zs])
                up = PSslot(slot)
                for hl in range(nh):
                    vc = slice(hl * Vd, (hl + 1) * Vd)
                    bsl = slice((hl * nch + c) * C, (hl * nch + c + 1) * C)
                    op('pe', lambda e, vc=vc, bsl=bsl: e.matmul(up.t[0:C, vc], TinvT.t[0:C, bsl], Zs.t[0:C, vc], start=True, stop=True), [TinvT, Zs], [up], inc=(hl == nh - 1))
                yield
                U = HB16(slot)
                ts(U.t[0:C, 0:128], up.t[0:C, 0:128], 1.0, None, ALU.mult, None, [up], [U])
                for hl in range(nh):
                    pr = slice(hl * Kd, (hl + 1) * Kd); vc = slice(hl * Vd, (hl + 1) * Vd)
                    bsl = slice((hl * nch + c) * C, (hl * nch + c + 1) * C)
                    aV = vt.t[0:C, c * 128 + hl * Vd:c * 128 + (hl + 1) * Vd]
                    op('pe', lambda e, pr=pr, vc=vc: e.matmul(ypb.t[vc, sl], st_b.t[pr, :], RsT.t[pr, sl], start=True, stop=False), [st_b, RsT], [ypb], inc=False)
                    op('pe', lambda e, vc=vc, bsl=bsl: e.matmul(ypb.t[vc, sl], U.t[0:C, vc], ArbT.t[0:C, bsl], start=False, stop=False), [U, ArbT], [ypb], inc=False)
                    op('pe', lambda e, vc=vc, bsl=bsl, aV=aV: e.matmul(ypb.t[vc, sl], aV, ArkT.t[0:C, bsl], start=False, stop=True), [vt, ArkT], [ypb], inc=(hl == nh - 1))
                yield
                sp_ = PSslot(slot)
                for hl in range(nh):
                    pr = slice(hl * Kd, (hl + 1) * Kd); vc = slice(hl * Vd, (hl + 1) * Vd)
                    aP = pbt.t[0:C, c * 128 + hl * Kd:c * 128 + (hl + 1) * Kd]
                    aK = kbt.t[0:C, c * 128 + hl * Kd:c * 128 + (hl + 1) * Kd]
                    aV = vt.t[0:C, c * 128 + hl * Vd:c * 128 + (hl + 1) * Vd]
                    op('pe', lambda e, pr=pr, vc=vc, aP=aP: e.matmul(sp_.t[pr, 0:Vd], aP, U.t[0:C, vc], start=True, stop=False), [pbt, U], [sp_], inc=False)
                    op('pe', lambda e, pr=pr, aK=aK, aV=aV: e.matmul(sp_.t[pr, 0:Vd], aK, aV, start=False, stop=True), [kbt, vt], [sp_], inc=(hl == nh - 1))
                stt(st_b.t[:, :], st_b.t[:, :], wc_ap(c), sp_.t[0:128, 0:Vd], ALU.mult, ALU.add, [st_b, WcB, sp_], [st_b])
                yield

        def scan_sample2(nh, Kd, Vd, QsT, RsT, PbT, KbT, VT, arb, ark, WcB, wc16, stX):
            n = NS
            QR = NB('sQR', 36, slot=2); PK = NB('sPK', 36, slot=2); UV = NB('sUV', 36, slot=2)
            q3 = QR.t[:, 0:2 * n].rearrange("p (s two) -> p s two", two=2)
            p3 = PK.t[:, 0:2 * n].rearrange("p (s two) -> p s two", two=2)
            u3 = UV.t[:, 0:2 * n].rearrange("p (s two) -> p s two", two=2)
            act(q3[:, :, 0], QsT.t[:, 0:n], AF.Copy, [QsT], [QR])
            ts(q3[:, :, 1], RsT.t[:, 0:n], 1.0, None, ALU.mult, None, [RsT], [QR])
            act(p3[:, :, 0], PbT.t[:, 0:n], AF.Copy, [PbT], [PK])
            ts(p3[:, :, 1], KbT.t[:, 0:n], 1.0, None, ALU.mult, None, [KbT], [PK])
            ts(u3[:, :, 1], VT.t[:, 0:n], 1.0, None, ALU.mult, None, [VT], [UV])
            pq = PS()
            for s_ in range(n):
                for hl in range(nh):
                    pr = slice(hl * Kd, (hl + 1) * Kd); vc = slice(hl * Vd, (hl + 1) * Vd)
                    op('pe', lambda e, s_=s_, pr=pr, vc=vc: e.matmul(pq.t[vc, 2 * s_:2 * s_ + 2], stX.t[pr, s_, :], QR.t[pr, 2 * s_:2 * s_ + 2], start=True, stop=True),
                       [stX, QR], [pq], inc=(s_ == n - 1 and hl == nh - 1))
            pq3 = pq.t[:, 0:2 * n].rearrange("p (s two) -> p s two", two=2)
            act(u3[:, :, 0], pq3[:, :, 0], AF.Copy, [pq], [UV])
            Y = HB(); t2 = HB()
            tt(Y.t[:, 0:n], pq3[:, :, 0], arb.t[:, 0:n], ALU.mult, [pq, arb], [Y])
            tt(t2.t[:, 0:n], VT.t[:, 0:n], ark.t[:, 0:n], ALU.mult, [VT, ark], [t2])
            tt(Y.t[:, 0:n], Y.t[:, 0:n], t2.t[:, 0:n], ALU.add, [Y, t2], [Y])
            tt(Y.t[:, 0:n], Y.t[:, 0:n], pq3[:, :, 1], ALU.add, [Y, pq], [Y])
            gs = 4
            pkr = NB('sPKr', 512, slot=2); uvr = NB('sUVr', 512, slot=2)
            for g0 in range(0, n, gs):
                pt = PS()
                for j in range(gs):
                    s_ = g0 + j
                    tr(pt, pt.t[0:2, j * 128:(j + 1) * 128], PK.t[:, 2 * s_:2 * s_ + 2], 128, [PK], inc=(j == gs - 1))
                act(pkr.t[0:2, 0:gs * 128], pt.t[0:2, 0:gs * 128], AF.Copy, [pt], [pkr])
                pu = PS()
                for j in range(gs):
                    s_ = g0 + j
                    tr(pu, pu.t[0:2, j * 128:(j + 1) * 128], UV.t[:, 2 * s_:2 * s_ + 2], 128, [UV], inc=(j == gs - 1))
                ts(uvr.t[0:2, 0:gs * 128], pu.t[0:2, 0:gs * 128], 1.0, None, ALU.mult, None, [pu], [uvr])
                pp = PS()
                for j in range(gs):
                    for hl in range(nh):
                        pr = slice(hl * Kd, (hl + 1) * Kd)
                        op('pe', lambda e, j=j, hl=hl, pr=pr: e.matmul(pp.t[pr, j * Vd:(j + 1) * Vd], pkr.t[0:2, j * 128 + hl * Kd:j * 128 + (hl + 1) * Kd],
                                                                    uvr.t[0:2, j * 128 + hl * Vd:j * 128 + (hl + 1) * Vd], start=True, stop=True),
                           [pkr, uvr], [pp], inc=(j == gs - 1 and hl == nh - 1))
                sv = stX.t[:, g0:g0 + gs, :]
                tt(sv, sv, wc16[:, g0:g0 + gs].unsqueeze(2).broadcast_to([128, gs, Vd]), ALU.mult, [stX, WcB], [stX])
                tt(sv, sv, pp.t[:, 0:gs * Vd].rearrange("p (s v) -> p s v", v=Vd), ALU.add, [stX, pp], [stX])
            return Y

        def scan_pair(slot, N, C, QsT, RsT, Pb16, Kb16, V16, A, WcB, wc_ap, st_b, ypb):
            nch = N // C
            W = nch * C
            H2 = (slice(0, 64), slice(64, 128))

            def tmaj(src16, name):
                dst = NB16(name, slot=slot)
                pb = PSslot(slot)
                for hl in range(2):
                    pr = H2[hl]
                    for c in range(nch):
                        sl = slice(c * C, (c + 1) * C)
                        op('pe', lambda e, pr=pr, sl=sl: e.matmul(pb.t[pr, sl], src16.t[pr, sl], id16[pr, pr], start=True, stop=True), [src16, cm16], [pb],
                           inc=(hl == 1 and c == nch - 1))
                act(dst.t[:, 0:W], pb.t[:, 0:W], AF.Copy, [pb], [dst])
                return dst
            pbt = tmaj(Pb16, 'Pb'); kbt = tmaj(Kb16, 'Kb'); vt = tmaj(V16, 'Vt')
            yield
            NTb = A['NT']; XTb = A['NT16']; Xb = A['Nn16']
            PT = NB('PT0', slot=slot)
            tt(PT.t[:, 0:W].rearrange("p (c i) -> p c i", i=C), NTb.t[:, 0:W].rearrange("p (c i) -> p c i", i=C),
               ident2.unsqueeze(1).broadcast_to([128, nch, C]), ALU.add, [NTb, cm], [PT])
            PTh = NB16('PTh0', slot=slot)
            act(PTh.t[:, 0:W], PT.t[:, 0:W], AF.Copy, [PT], [PTh])
            yield
            nlev = 5

            def mm8(pbo, L_, R_):
                for hl in range(2):
                    pr = H2[hl]
                    for c in range(nch):
                        sl = slice(c * C, (c + 1) * C)
                        op('pe', lambda e, pr=pr, sl=sl: e.matmul(pbo.t[pr, sl], L_.t[pr, sl], R_.t[pr, sl], start=True, stop=True), [L_, R_], [pbo],
                           inc=(hl == 1 and c == nch - 1))
            for lv in range(1, nlev + 1):
                pbx = PSslot(slot)
                mm8(pbx, XTb, Xb)
                Xn = NB16('X%d' % (lv % 2), slot=slot)
                XTn = None
                if lv < nlev:
                    pbt_ = PSslot(slot)
                    mm8(pbt_, Xb, XTb)
                act(Xn.t[:, 0:W], pbx.t[:, 0:W], AF.Copy, [pbx], [Xn])
                yield
                if lv < nlev:
                    XTn = NB16('XT%d' % (lv % 2), slot=slot)
                    ts(XTn.t[:, 0:W], pbt_.t[:, 0:W], 1.0, None, ALU.mult, None, [pbt_], [XTn])
                pbp = PSslot(slot)
                mm8(pbp, Xn, PTh)
                PTn = NB('PT%d' % (lv % 2), slot=slot)
                tt(PTn.t[:, 0:W], PT.t[:, 0:W], pbp.t[:, 0:W], ALU.add, [PT, pbp], [PTn])
                PThn = NB16('PTh%d' % (lv % 2), slot=slot)
                act(PThn.t[:, 0:W], PTn.t[:, 0:W], AF.Copy, [PTn], [PThn])
                PT = PTn; PTh = PThn
                yield
                Xb = Xn
                if lv < nlev:
                    XTb = XTn
            TinvT = PTh
            LakT = A['LakT']; ArbT = A['ArbT']; ArkT = A['ArkT']
            for c in range(nch):
                sl = slice(c * C, (c + 1) * C)
                zp = PSslot(slot)
                for hl in range(2):
                    pr = H2[hl]
                    op('pe', lambda e, pr=pr: e.matmul(zp.t[pr, 0:64], QsT.t[pr, sl], st_b.t[pr, :], start=True, stop=False), [QsT, st_b], [zp], inc=False)
                    op('pe', lambda e, pr=pr: e.matmul(zp.t[pr, 0:64], LakT.t[pr, sl], vt.t[pr, sl], start=False, stop=True), [LakT, vt], [zp], inc=(hl == 1))
                yield
                Zs = HB16(slot)
                act(Zs.t[:, 0:64], zp.t[:, 0:64], AF.Copy, [zp], [Zs])
                up = PSslot(slot)
                for hl in range(2):
                    pr = H2[hl]
                    op('pe', lambda e, pr=pr: e.matmul(up.t[pr, 0:64], TinvT.t[pr, sl], Zs.t[pr, 0:64], start=True, stop=True), [TinvT, Zs], [up], inc=(hl == 1))
                yield
                U = HB16(slot)
                ts(U.t[:, 0:64], up.t[:, 0:64], 1.0, None, ALU.mult, None, [up], [U])
                for hl in range(2):
                    pr = H2[hl]
                    op('pe', lambda e, pr=pr: e.matmul(ypb.t[pr, sl], st_b.t[pr, :], RsT.t[pr, sl], start=True, stop=False), [st_b, RsT], [ypb], inc=False)
                    op('pe', lambda e, pr=pr: e.matmul(ypb.t[pr, sl], U.t[pr, 0:64], ArbT.t[pr, sl], start=False, stop=False), [U, ArbT], [ypb], inc=False)
                    op('pe', lambda e, pr=pr: e.matmul(ypb.t[pr, sl], vt.t[pr, sl], ArkT.t[pr, sl], start=False, stop=True), [vt, ArkT], [ypb], inc=(hl == 1))
                yield
                sp_ = PSslot(slot)
                for hl in range(2):
                    pr = H2[hl]
                    op('pe', lambda e, pr=pr: e.matmul(sp_.t[pr, 0:64], pbt.t[pr, sl], U.t[pr, 0:64], start=True, stop=False), [pbt, U], [sp_], inc=False)
                    op('pe', lambda e, pr=pr: e.matmul(sp_.t[pr, 0:64], kbt.t[pr, sl], vt.t[pr, sl], start=False, stop=True), [kbt, vt], [sp_], inc=(hl == 1))
                stt(st_b.t[:, :], st_b.t[:, :], wc_ap(c), sp_.t[:, 0:64], ALU.mult, ALU.add, [st_b, WcB, sp_], [st_b])
                yield

        def scan_sample_pair(QsT, RsT, PbT, KbT, VT, arb, ark, WcB, wc16, stX):
            n = NS
            H2 = (slice(0, 64), slice(64, 128))
            QR = NB('sQR', 36, slot=2); PK = NB('sPK', 36, slot=2); UV = NB('sUV', 36, slot=2)
            q3 = QR.t[:, 0:2 * n].rearrange("p (s two) -> p s two", two=2)
            p3 = PK.t[:, 0:2 * n].rearrange("p (s two) -> p s two", two=2)
            u3 = UV.t[:, 0:2 * n].rearrange("p (s two) -> p s two", two=2)
            act(q3[:, :, 0], QsT.t[:, 0:n], AF.Copy, [QsT], [QR])
            ts(q3[:, :, 1], RsT.t[:, 0:n], 1.0, None, ALU.mult, None, [RsT], [QR])
            act(p3[:, :, 0], PbT.t[:, 0:n], AF.Copy, [PbT], [PK])
            ts(p3[:, :, 1], KbT.t[:, 0:n], 1.0, None, ALU.mult, None, [KbT], [PK])
            ts(u3[:, :, 1], VT.t[:, 0:n], 1.0, None, ALU.mult, None, [VT], [UV])
            pq = PS()
            for s_ in range(n):
                for hl in range(2):
                    pr = H2[hl]
                    op('pe', lambda e, s_=s_, pr=pr: e.matmul(pq.t[pr, 2 * s_:2 * s_ + 2], stX.t[pr, s_, :], QR.t[pr, 2 * s_:2 * s_ + 2], start=True, stop=True),
                       [stX, QR], [pq], inc=(s_ == n - 1 and hl == 1))
            pq3 = pq.t[:, 0:2 * n].rearrange("p (s two) -> p s two", two=2)
            act(u3[:, :, 0], pq3[:, :, 0], AF.Copy, [pq], [UV])
            Y = HB(); t2 = HB()
            tt(Y.t[:, 0:n], pq3[:, :, 0], arb.t[:, 0:n], ALU.mult, [pq, arb], [Y])
            tt(t2.t[:, 0:n], VT.t[:, 0:n], ark.t[:, 0:n], ALU.mult, [VT, ark], [t2])
            tt(Y.t[:, 0:n], Y.t[:, 0:n], t2.t[:, 0:n], ALU.add, [Y, t2], [Y])
            tt(Y.t[:, 0:n], Y.t[:, 0:n], pq3[:, :, 1], ALU.add, [Y, pq], [Y])
            gs = 8
            pkr = NB('sPKr', 512, slot=2); uvr = NB('sUVr', 512, slot=2)
            for g0 in range(0, n, gs):
                pt = PS(); pu = PS()
                for j in range(gs):
                    s_ = g0 + j
                    for hl in range(2):
                        pr = H2[hl]; p2 = slice(hl * 64, hl * 64 + 2)
                        op('pe', lambda e, j=j, s_=s_, pr=pr, p2=p2: e.matmul(pt.t[p2, j * 64:(j + 1) * 64], PK.t[pr, 2 * s_:2 * s_ + 2], ident[pr, pr], start=True, stop=True),
                           [PK, cm], [pt], inc=(j == gs - 1 and hl == 1))
                for j in range(gs):
                    s_ = g0 + j
                    for hl in range(2):
                        pr = H2[hl]; p2 = slice(hl * 64, hl * 64 + 2)
                        op('pe', lambda e, j=j, s_=s_, pr=pr, p2=p2: e.matmul(pu.t[p2, j * 64:(j + 1) * 64], UV.t[pr, 2 * s_:2 * s_ + 2], ident[pr, pr], start=True, stop=True),
                           [UV, cm], [pu], inc=(j == gs - 1 and hl == 1))
                for hl in range(2):
                    p2 = slice(hl * 64, hl * 64 + 2)
                    act(pkr.t[p2, 0:gs * 64], pt.t[p2, 0:gs * 64], AF.Copy, [pt], [pkr])
                    ts(uvr.t[p2, 0:gs * 64], pu.t[p2, 0:gs * 64], 1.0, None, ALU.mult, None, [pu], [uvr])
                pp = PS()
                for j in range(gs):
                    for hl in range(2):
                        pr = H2[hl]; p2 = slice(hl * 64, hl * 64 + 2)
                        op('pe', lambda e, j=j, pr=pr, p2=p2: e.matmul(pp.t[pr, j * 64:(j + 1) * 64], pkr.t[p2, j * 64:(j + 1) * 64], uvr.t[p2, j * 64:(j + 1) * 64], start=True, stop=True),
                           [pkr, uvr], [pp], inc=(j == gs - 1 and hl == 1))
                sv = stX.t[:, g0:g0 + gs, :]
                tt(sv, sv, wc16[:, g0:g0 + gs].unsqueeze(2).broadcast_to([128, gs, 64]), ALU.mult, [stX, WcB], [stX])
                tt(sv, sv, pp.t[:, 0:gs * 64].rearrange("p (s v) -> p s v", v=64), ALU.add, [stX, pp], [stX])
            return Y

        def do_tile(is_s, t0):
            pfx = 's_' if is_s else ''
            N = NS if is_s else NP
            C = 1 if is_s else 64
            nseq = NS if is_s else 1
            nch = N // C
            rows_list = [NS] if is_s else [128] * TCN
            xsrc = xS if is_s else xP
            psrc = pS if is_s else pP
            ydst = yS if is_s else yP
            last = is_s or (t0 + NP >= TP)
            for tc, rows in enumerate(rows_list):
                kb.dma('sp', xtok.t[:rows, tc, :], xsrc[t0 + tc * 128:t0 + tc * 128 + rows, :], writes=[xtok])
                kb.dma('sp', ptok.t[:rows, tc, :], psrc[t0 + tc * 128:t0 + tc * 128 + rows, :], writes=[ptok])
            CKP(pfx + 'load')
            norm_T(N, rows_list, GMIX)
            CKP(pfx + 'norm')
            if is_s:
                kb.dma('sp', tokS.t[:], stShift, writes=[tokS])
                for g in range(15):
                    pb = PS()
                    if g < 12:
                        col = (g // 4) * 512 + (g % 4) * 128; P_ = 128
                    elif g == 12:
                        col = 1536; P_ = 64
                    elif g == 13:
                        col = 1600; P_ = 64
                    else:
                        col = 1664; P_ = 128
                    tr(pb, pb.t[0:P_, 0:NS], tokS.t[0:NS, col:col + P_], NS, [tokS])
                    act(prevS.t[0:P_, g, :], pb.t[0:P_, 0:NS], AF.Copy, [pb], [prevS])
                for j in range(3):
                    kb.dma('sp', tokC.t[:], stConv[:, j, :], writes=[tokC])
                    for g in range(12):
                        pb = PS()
                        tr(pb, pb.t[0:128, 0:NS], tokC.t[0:NS, g * 128:(g + 1) * 128], NS, [tokC])
                        act(histS.t[:, j * 12 + g, :], pb.t[:, 0:NS], AF.Copy, [pb], [histS])
            wt = wload(w_in, 0, 8, 128, 1536, 256)
            pb = PS(); proj_fm(pb, pb.t[0:64, :N], wt, 0, 64, N)
            xw = HB()
            shift(64, N, pb, pb.t[0:64, :N], c64b.t[:, MU_XW:MU_XW + 1], omu64.t[:, 24:25], (car_xw, car_xw.t[:, 0:1]), xw, xw.t[0:64, :N], is_s,
                  prev_ap=prevS.t[0:64, 12, :], raw_ap=rawS.t[0:64, 12, :], raw_b=rawS)
            act(tw.t[:, :N], xw.t[0:64, :N], AF.Tanh, [xw], [tw])
            pb = PS(); proj_fm(pb, pb.t[0:64, :N], wt, 64, 64, N)
            shift(64, N, pb, pb.t[0:64, :N], c64b.t[:, MU_XAA:MU_XAA + 1], omu64.t[:, 25:26], (car_xw, car_xw.t[:, 1:2]), xaaS, xaaS.t[:, :N], is_s,
                  prev_ap=prevS.t[0:64, 13, :], raw_ap=rawS.t[0:64, 13, :], raw_b=rawS)
            pb = PS(); proj_fm(pb, pb.t[:, :N], wt, 128, 128, N)
            xg = HB()
            shift(128, N, pb, pb.t[:, :N], c128b.t[:, MU_XG:MU_XG + 1], omu128.t[:, 0:1], (car_xg, car_xg.t[:, 0:1]), xg, xg.t[:, :N], is_s,
                  prev_ap=prevS.t[:, 14, :], raw_ap=rawS.t[:, 14, :], raw_b=rawS)
            act(sgS.t[:, :N], xg.t[:, :N], AF.Sigmoid, [xg], [sgS])

            pend = []
            if (not is_s) and t0 == 0:
                pend = [lambda: convert('bra', w_bra, 512, D), lambda: convert('brb', w_brb, 512, D),
                        lambda: convert('gate', w_in[:, 3848:5896], D, 2048), lambda: convert('out', w_out, D, D),
                        lambda: convert('fg', w_fg, D, DFF), lambda: convert('fu', w_fu, D, DFF), lambda: convert('fd', w_fd, DFF, D),
                        lambda: convert('pg', w_pg, D, D), lambda: convert('pp', w_pp, 256, D)]
            CKP(pfx + 'lora')
            def rwkv_gen(hp, slot):
                T = lambda: HB(slot)
                PSs = lambda: PSslot(slot)
                wt1 = wload(w_in, 0, 8, 128, hp * 384, 256)
                wt2 = wload(w_in, 0, 8, 128, hp * 384 + 256, 128)
                rkv = []
                for j, nm_ in enumerate(('r', 'k', 'v')):
                    g = hp * 3 + j; gc_ = j * 4 + hp
                    pb = PSs()
                    if j < 2:
                        proj_fm(pb, pb.t[:, :N], wt1, j * 128, 128, N)
                    else:
                        proj_fm(pb, pb.t[:, :N], wt2, 0, 128, N)
                    o_ = NB(nm_, slot=slot)
                    shift(128, N, pb, pb.t[:, :N], c128r.t[:, g:g + 1], omu128r.t[:, g:g + 1], (car_rkv, car_rkv.t[:, gc_:gc_ + 1]), o_, o_.t[:, :N], is_s,
                          prev_ap=prevS.t[:, gc_, :], raw_ap=rawS.t[:, gc_, :], raw_b=rawS, tmp=T)
                    rkv.append(o_)
                r_, k_, v_ = rkv
                hs = slice(hp * 128, (hp + 1) * 128)
                cc_ = lambda base: c128r.t[:, base + hp:base + hp + 1]
                yield
                pb = PSs()
                mm(pb, pb.t[:, :N], w2b.t[:, hs], tw.t[:, :N], [w2b, tw], True, True)
                sg = T(); act(sg.t[:, :N], pb.t[:, :N], AF.Sigmoid, [pb, c128r], [sg], bias=cc_(W0r))
                yield
                cws = T()
                if C > 1:
                    op('dve', lambda e: e.tensor_tensor_scan(out=cws.t[:, :N], data0=scanm[:, :N], data1=sg.t[:, :N], initial=0.0, op0=ALU.mult, op1=ALU.add), [sg, cm], [cws])
                else:
                    ts(cws.t[:, :N], sg.t[:, :N], 1.0, None, ALU.mult, None, [sg], [cws])
                yield
                ew = NB('ew', slot=slot); act(ew.t[:, :N], cws.t[:, :N], AF.Exp, [cws], [ew], scale=-C0)
                ewi = NB('ewi', slot=slot); act(ewi.t[:, :N], cws.t[:, :N], AF.Exp, [cws], [ewi], scale=C0)
                yield
                cx = T(); tt(cx.t[:, :N], cws.t[:, :N], sg.t[:, :N], ALU.subtract, [cws, sg], [cx])
                ewx = NB('ewx', slot=slot); act(ewx.t[:, :N], cx.t[:, :N], AF.Exp, [cx], [ewx], scale=-C0)
                yield
                dC = T()
                cw3 = cws.t[:, :N].rearrange("p (c i) -> p c i", i=C)
                tt(dC.t[:, :N].rearrange("p (c i) -> p c i", i=C), cw3[:, :, C - 1:C].broadcast_to([128, nch, C]), cw3, ALU.subtract, [cws], [dC])
                ewC = NB('ewC', slot=slot); act(ewC.t[:, :N], dC.t[:, :N], AF.Exp, [dC], [ewC], scale=-C0)
                yield
                pb = PSs()
                mm(pb, pb.t[:, :N], a2b.t[:, hs], xaaS.t[:, :N], [a2b, xaaS], True, True)
                a_ = NB('a', slot=slot); act(a_.t[:, :N], pb.t[:, :N], AF.Sigmoid, [pb, c128r], [a_], bias=cc_(A0r))
                yield
                pb = PSs()
                mm(pb, pb.t[:, :N], g2b.t[:, hs], sgS.t[:, :N], [g2b, sgS], True, True)
                g_ = NB('g', slot=slot); act(g_.t[:, :N], pb.t[:, :N], AF.Copy, [pb], [g_])
                yield
                kks = T(); ts(kks.t[:, :N], k_.t[:, :N], cc_(KKr), None, ALU.mult, None, [k_, c128r], [kks])
                sq = HBW(slot); act(sq.t[:, :N], kks.t[:, :N], AF.Square, [kks], [sq])
                yield
                pb = PSs()
                mm(pb, pb.t[:, :N], bones16, sq.t[:, :N], [on16, sq], True, True)
                rs = T(); rsqrt(rs.t[:, :N], pb.t[:, :N], 1.0, 1e-6, [pb], [rs])
                yield
                kk = NB('kk', slot=slot); tt(kk.t[:, :N], kks.t[:, :N], rs.t[:, :N], ALU.mult, [kks, rs], [kk])
                t1 = T(); ts(t1.t[:, :N], a_.t[:, :N], -1.0, cc_(KAr), ALU.add, ALU.mult, [a_, c128r], [t1])
                yield
                km = NB('km', slot=slot); stt(km.t[:, :N], t1.t[:, :N], 1.0, k_.t[:, :N], ALU.add, ALU.mult, [t1, k_], [km])
                kka = NB('kka', slot=slot); tt(kka.t[:, :N], kk.t[:, :N], a_.t[:, :N], ALU.mult, [kk, a_], [kka])
                yield
                QsT = NB('QsT', slot=slot); stt(QsT.t[:, :N], kk.t[:, :N], -1.0, ewx.t[:, :N], ALU.mult, ALU.mult, [kk, ewx], [QsT])
                RsT = NB('RsT', slot=slot); tt(RsT.t[:, :N], r_.t[:, :N], ew.t[:, :N], ALU.mult, [r_, ew], [RsT])
                yield
                PnT = NB16('PnT', slot=slot); tt(PnT.t[:, :N], kka.t[:, :N], ewi.t[:, :N], ALU.mult, [kka, ewi], [PnT])
                KnT = NB16('KnT', slot=slot); tt(KnT.t[:, :N], km.t[:, :N], ewi.t[:, :N], ALU.mult, [km, ewi], [KnT])
                yield
                PbT = NB('PbT', slot=slot); tt(PbT.t[:, :N], kka.t[:, :N], ewC.t[:, :N], ALU.mult, [kka, ewC], [PbT])
                KbT = NB('KbT', slot=slot); tt(KbT.t[:, :N], km.t[:, :N], ewC.t[:, :N], ALU.mult, [km, ewC], [KbT])
                yield
                rk = HBW(slot); stt(rk.t[:, :N], r_.t[:, :N], cc_(RKr), km.t[:, :N], ALU.mult, ALU.mult, [r_, c128r, km], [rk])
                pb = PSs()
                mm(pb, pb.t[:, :N], bones16, rk.t[:, :N], [on16, rk], True, True)
                bon = NB('bon', slot=slot); tt(bon.t[:, :N], pb.t[:, :N], v_.t[:, :N], ALU.mult, [pb, v_], [bon])
                yield
                if is_s:
                    for hl_ in range(2):
                        kb.dma('sp', natS.t[0:64, :, hl_, :], stWkv[:, 2 * hp + hl_].rearrange("s v k -> v s k"), writes=[natS])
                    for s0 in range(0, NS, 4):
                        pb = PSs()
                        for s_ in range(s0, s0 + 4):
                            tr(pb, pb.t[:, (s_ - s0) * 64:(s_ - s0 + 1) * 64], natS.t[0:64, s_, :, :], 64, [natS], inc=(s_ == s0 + 3))
                        act(stRs.t[:, s0:s0 + 4, :], pb.t[:, 0:256].rearrange("k (s v) -> k s v", v=64), AF.Copy, [pb], [stRs])
                    stb = stRs
                    pr1 = T(); tt(pr1.t[:, :N], PnT.t[:, :N], RsT.t[:, :N], ALU.mult, [PnT, RsT], [pr1])
                    pr2 = T(); tt(pr2.t[:, :N], KnT.t[:, :N], RsT.t[:, :N], ALU.mult, [KnT, RsT], [pr2])
                    pb = PSs(); mm(pb, pb.t[:, :N], bones, pr1.t[:, :N], [cm2, pr1], True, True)
                    arb = T(); act(arb.t[:, :N], pb.t[:, :N], AF.Copy, [pb], [arb])
                    pb = PSs(); mm(pb, pb.t[:, :N], bones, pr2.t[:, :N], [cm2, pr2], True, True)
                    ark = T(); act(ark.t[:, :N], pb.t[:, :N], AF.Copy, [pb], [ark])
                    y = scan_sample_pair(QsT, RsT, PbT, KbT, v_, arb, ark, ew, ew.t[:, 0:NS], stRs)
                else:
                    CKP(pfx + 'rprep')
                    A = {}
                    H2 = (slice(0, 64), slice(64, 128))

                    def amat(name, Lb, Rb, mask):
                        pb = PSs()
                        for hl in range(2):
                            pr = H2[hl]
                            for c in range(nch):
                                sl = slice(c * C, (c + 1) * C)
                                op('pe', lambda e, sl=sl, pr=pr: e.matmul(pb.t[pr, sl], Lb.t[pr, sl], Rb.t[pr, sl], start=True, stop=True), [Lb, Rb], [pb],
                                   inc=(hl == 1 and c == nch - 1))
                        o_ = NB(name, slot=slot) if name == 'NT' else NB16(name, slot=slot)
                        tt(o_.t[:, 0:N].rearrange("p (c i) -> p c i", i=C), pb.t[:, 0:N].rearrange("p (c i) -> p c i", i=C),
                           mask.unsqueeze(1).broadcast_to([128, nch, C]), ALU.mult, [pb, cm], [o_])
                        A[name] = o_
                        if name == 'NT':
                            n16 = NB16('NT16', slot=slot)
                            act(n16.t[:, 0:N], o_.t[:, 0:N], AF.Copy, [o_], [n16])
                            A['NT16'] = n16
                    Qs16 = NB16('Qs16', slot=slot); act(Qs16.t[:, :N], QsT.t[:, :N], AF.Copy, [QsT], [Qs16])
                    Rs16 = NB16('Rs16', slot=slot); ts(Rs16.t[:, :N], RsT.t[:, :N], 1.0, None, ALU.mult, None, [RsT], [Rs16])
                    yield
                    amat('NT', PnT, Qs16, maskS2)
                    yield
                    amat('Nn16', Qs16, PnT, maskL2)
                    yield
                    amat('LakT', KnT, Qs16, maskS2)
                    yield
                    amat('ArbT', PnT, Rs16, maskI2)
                    yield
                    amat('ArkT', KnT, Rs16, maskI2)
                    yield
                    Pb16 = NB16('Pb16', slot=slot); act(Pb16.t[:, :N], PbT.t[:, :N], AF.Copy, [PbT], [Pb16])
                    Kb16 = NB16('Kb16', slot=slot); ts(Kb16.t[:, :N], KbT.t[:, :N], 1.0, None, ALU.mult, None, [KbT], [Kb16])
                    V16 = NB16('V16', slot=slot); act(V16.t[:, :N], v_.t[:, :N], AF.Copy, [v_], [V16])
                    yield
                    CKP(pfx + 'ramat')
                    stb = stR[hp]
                    ypb = ps[6 + slot]
                    yield from scan_pair(slot, N, C, QsT, RsT, Pb16, Kb16, V16, A, ew, lambda c: ew.t[:, (c + 1) * C - 1:(c + 1) * C], stb, ypb)
                    CKP(pfx + 'rscan')
                    y = T(); act(y.t[:, :N], ypb.t[:, :N], AF.Copy, [ypb], [y])
                yield
                pb = PSs(); mm(pb, pb.t[:, :N], bones, y.t[:, :N], [cm2, y], True, True)
                d_ = T(); stt(d_.t[:, :N], pb.t[:, :N], -1.0 / 64, y.t[:, :N], ALU.mult, ALU.add, [pb, y], [d_])
                yield
                d2 = HBW(slot); act(d2.t[:, :N], d_.t[:, :N], AF.Square, [d_], [d2])
                pb = PSs(); mm(pb, pb.t[:, :N], bones16, d2.t[:, :N], [on16, d2], True, True)
                rs2 = T(); rsqrt(rs2.t[:, :N], pb.t[:, :N], 1.0 / 64, 64e-5, [pb], [rs2])
                yield
                yn = T(); tt(yn.t[:, :N], d_.t[:, :N], rs2.t[:, :N], ALU.mult, [d_, rs2], [yn])
                ts(yn.t[:, :N], yn.t[:, :N], cc_(LNWr), cc_(LNBr), ALU.mult, ALU.add, [yn, c128r], [yn])
                yield
                tt(yn.t[:, :N], yn.t[:, :N], bon.t[:, :N], ALU.add, [yn, bon], [yn])
                tt(oa.t[:, hp, :N], yn.t[:, :N], g_.t[:, :N], ALU.mult, [yn, g_], [oa])
                yield
                if last:
                    if is_s:
                        for s0 in range(0, NS, 4):
                            pb = PSs()
                            for s_ in range(s0, s0 + 4):
                                tr(pb, pb.t[0:64, (s_ - s0) * 128:(s_ - s0 + 1) * 128], stRs.t[:, s_, :], 128, [stRs], inc=(s_ == s0 + 3))
                            act(natS.t[0:64, s0:s0 + 4, :, :], pb.t[0:64, 0:512].rearrange("v (s h k) -> v s h k", h=2, k=64), AF.Copy, [pb], [natS])
                        for hl_ in range(2):
                            kb.dma('sp', oWkvS[:, 2 * hp + hl_].rearrange("s v k -> v s k"), natS.t[0:64, :, hl_, :], reads=[natS], is_out=True)
                    else:
                        pb = PSs()
                        tr(pb, pb.t[0:64, 0:128], stR[hp].t[:, :], 128, [stR[hp]])
                        stg_ = T()
                        act(stg_.t[0:64, 0:128], pb.t[0:64, 0:128], AF.Copy, [pb], [stg_])
                        kb.dma('sp', oWkvP[0, 2 * hp:2 * hp + 2].rearrange("h v k -> v h k"), stg_.t[0:64, 0:128].rearrange("v (h k) -> v h k", k=64), reads=[stg_], is_out=True)

            CKP(pfx + 'rwkv')
            def gdn_gen(h, slot):
                T = lambda: HB(slot)
                PSs = lambda: PSslot(slot)
                wtb = wload(w_barep, 0, 8, 128, h * 128, 128)
                wta = wload(w_barep, 0, 8, 128, 512 + h * 128, 128)
                pb = PSs(); proj_fm(pb, pb.t[:, :N], wtb, 0, 128, N)
                Bb = NB('a', slot=slot); act(Bb.t[:, :N], pb.t[:, :N], AF.Sigmoid, [pb], [Bb])
                pb = PSs(); proj_fm(pb, pb.t[:, :N], wta, 0, 128, N)
                e1 = T(); act(e1.t[:, :N], pb.t[:, :N], AF.Exp, [pb, c128b], [e1], bias=c128b.t[:, DTB + h:DTB + h + 1])
                act(e1.t[:, :N], e1.t[:, :N], AF.Ln, [e1], [e1], bias=1.0)
                gl = T(); ts(gl.t[:, :N], e1.t[:, :N], nexpA.t[:, h:h + 1], None, ALU.mult, None, [e1, nexpA], [gl])
                G = NB('ewi', slot=slot)
                if C > 1:
                    op('dve', lambda e, G=G, gl=gl: e.tensor_tensor_scan(out=G.t[:, :N], data0=scanm[:, :N], data1=gl.t[:, :N], initial=0.0, op0=ALU.mult, op1=ALU.add), [gl, cm], [G])
                else:
                    ts(G.t[:, :N], gl.t[:, :N], 1.0, None, ALU.mult, None, [gl], [G])
                wt1 = wload(w_in, 0, 8, 128, 1792 + h * 512, 256)
                wt2 = wload(w_in, 0, 8, 128, 1792 + h * 512 + 256, 256)
                cs = []
                for j, nm_ in enumerate(('r', 'k', 'v')):
                    g = j * 4 + h
                    pb = PSs()
                    if j < 2:
                        proj_fm(pb, pb.t[:, :N], wt1, j * 128, 128, N)
                    else:
                        proj_fm(pb, pb.t[:, :N], wt2, 0, 128, N)
                    acc = T()
                    cwc = lambda tap, g=g: cvw.t[:, tap * 12 + g:tap * 12 + g + 1]
                    if is_s:
                        act(rawC.t[:, g, :], pb.t[:, :N], AF.Copy, [pb], [rawC])
                        ts(acc.t[:, :N], histS.t[:, 0 * 12 + g, :], cwc(0), None, ALU.mult, None, [histS, cvw], [acc])
                        stt(acc.t[:, :N], histS.t[:, 1 * 12 + g, :], cwc(1), acc.t[:, :N], ALU.mult, ALU.add, [histS, cvw, acc], [acc])
                        stt(acc.t[:, :N], histS.t[:, 2 * 12 + g, :], cwc(2), acc.t[:, :N], ALU.mult, ALU.add, [histS, cvw, acc], [acc])
                        stt(acc.t[:, :N], pb.t[:, :N], cwc(3), acc.t[:, :N], ALU.mult, ALU.add, [pb, cvw, acc], [acc])
                    else:
                        raw = T()
                        act(raw.t[:, 3:N + 3], pb.t[:, :N], AF.Copy, [pb], [raw])
                        act(raw.t[:, 0:3], car_cv.t[:, g, 0:3], AF.Copy, [car_cv], [raw])
                        act(car_cv.t[:, g, 0:3], raw.t[:, N:N + 3], AF.Copy, [raw], [car_cv])
                        ts(acc.t[:, :N], raw.t[:, 0:N], cwc(0), None, ALU.mult, None, [raw, cvw], [acc])
                        for tap in range(1, 4):
                            stt(acc.t[:, :N], raw.t[:, tap:tap + N], cwc(tap), acc.t[:, :N], ALU.mult, ALU.add, [raw, cvw, acc], [acc])
                    c_ = NB(nm_, slot=slot); act(c_.t[:, :N], acc.t[:, :N], AF.Silu, [acc], [c_])
                    cs.append(c_)
                pbz = PSs(); proj_fm(pbz, pbz.t[:, :N], wt2, 128, 128, N)
                zz = NB('g', slot=slot); act(zz.t[:, :N], pbz.t[:, :N], AF.Silu, [pbz], [zz])
                cq, ck, cv_ = cs
                yield

                def l2n(src, scale, name):
                    sq = HBW(slot); act(sq.t[:, :N], src.t[:, :N], AF.Square, [src], [sq])
                    pb = PSs(); mm(pb, pb.t[:, :N], ones16, sq.t[:, :N], [on16, sq], True, True)
                    rs = T(); rsqrt(rs.t[:, :N], pb.t[:, :N], 1.0, 1e-6, [pb], [rs])
                    o_ = NB(name, slot=slot); stt(o_.t[:, :N], src.t[:, :N], scale, rs.t[:, :N], ALU.mult, ALU.mult, [src, rs], [o_])
                    return o_
                qn = l2n(cq, float(128 ** -0.5), 'kk'); kn = l2n(ck, 1.0, 'km')
                yield
                N2, C2, nch2 = N, C, nch
                yield
                eg = NB('ew', slot=slot); act(eg.t[:, :N2], G.t[:, :N2], AF.Exp, [G], [eg])
                yield
                QsT = NB('QsT', slot=slot); tt(QsT.t[:, :N2], kn.t[:, :N2], eg.t[:, :N2], ALU.mult, [kn, eg], [QsT])
                yield
                RsT = NB('RsT', slot=slot); tt(RsT.t[:, :N2], qn.t[:, :N2], eg.t[:, :N2], ALU.mult, [qn, eg], [RsT])
                yield
                dC = T()
                yield
                G3 = G.t[:, :N2].rearrange("p (c i) -> p c i", i=C2)
                yield
                tt(dC.t[:, :N2].rearrange("p (c i) -> p c i", i=C2), G3[:, :, C2 - 1:C2].broadcast_to([128, nch2, C2]), G3, ALU.subtract, [G], [dC])
                yield
                egC = T(); act(egC.t[:, :N2], dC.t[:, :N2], AF.Exp, [dC], [egC])
                yield
                bE = T(); tt(bE.t[:, :N2], Bb.t[:, :N2], egC.t[:, :N2], ALU.mult, [Bb, egC], [bE])
                yield
                KbT = NB('KbT', slot=slot); tt(KbT.t[:, :N2], kn.t[:, :N2], bE.t[:, :N2], ALU.mult, [kn, bE], [KbT])
                yield
                PbT = NB('PbT', slot=slot); ts(PbT.t[:, :N2], KbT.t[:, :N2], -1.0, None, ALU.mult, None, [KbT], [PbT])
                yield
                if is_s:
                    kb.dma('sp', stGs.t[:], stGdn[:, h].rearrange("s k v -> k s v"), writes=[stGs])
                    stb = stGs
                    kq = T(); tt(kq.t[:, :N], kn.t[:, :N], qn.t[:, :N], ALU.mult, [kn, qn], [kq])
                    pb = PSs(); mm(pb, pb.t[:, :N], ones, kq.t[:, :N], [cm, kq], True, True)
                    arb = T(); stt(arb.t[:, :N], pb.t[:, :N], -1.0, Bb.t[:, :N], ALU.mult, ALU.mult, [pb, Bb], [arb])
                    ark = T(); ts(ark.t[:, :N], arb.t[:, :N], -1.0, None, ALU.mult, None, [arb], [ark])
                    o_ = scan_sample2(1, 128, 128, QsT, RsT, PbT, KbT, cv_, arb, ark, eg, eg.t[:, 0:NS], stGs)
                else:
                    gcT = NB('ewx', slot=slot); nbT = NB('ewC', slot=slot)
                    for (src, dst, sc) in ((G, gcT, 1.0), (Bb, nbT, -1.0)):
                        pb = PSs()
                        for c in range(nch2):
                            tr(pb, pb.t[0:C2, c * 32:(c + 1) * 32], src.t[0:32, c * C2:(c + 1) * C2], 32, [src], inc=(c == nch2 - 1))
                        ts(dst.t[0:C2, 0:nch2], pb.t[0:C2, 0:nch2 * 32].rearrange("p (c i) -> p c i", i=32)[:, :, 0], sc, None, ALU.mult, None, [pb], [dst])
                    E = NB('kka', slot=slot)
                    E3 = E.t[0:C2, :N2].rearrange("p (c i) -> p c i", i=C2)
                    tt(E3, G.t[0:C2, :N2].rearrange("p (c i) -> p c i", i=C2), gcT.t[0:C2, 0:nch2].unsqueeze(2).broadcast_to([C2, nch2, C2]), ALU.subtract, [G, gcT], [E])
                    tt(E3, E3, negI[0:C2, 0:C].unsqueeze(1).broadcast_to([C2, nch2, C2]), ALU.add, [E, cm], [E])
                    act(E.t[0:C2, :N2], E.t[0:C2, :N2], AF.Exp, [E], [E])
                    nb3 = nbT.t[0:C2, 0:nch2].unsqueeze(2).broadcast_to([C2, nch2, C2])
                    A = {}

                    def gmat(Rb, strict, n1, n2):
                        pb = PSs()
                        for c in range(nch2):
                            sl = slice(c * C2, (c + 1) * C2)
                            op('pe', lambda e, sl=sl: e.matmul(pb.t[0:C2, sl], kn.t[:, sl], Rb.t[:, sl], start=True, stop=True), [kn, Rb], [pb], inc=(c == nch2 - 1))
                        o_ = NB('NT' if strict else 'A32', slot=slot); o3 = o_.t[0:C2, :N2].rearrange("p (c i) -> p c i", i=C2)
                        tt(o3, pb.t[0:C2, :N2].rearrange("p (c i) -> p c i", i=C2), E3, ALU.mult, [pb, E], [o_])
                        if strict:
                            tt(o3, o3, maskS[0:C2, 0:C].unsqueeze(1).broadcast_to([C2, nch2, C2]), ALU.mult, [o_, cm], [o_])
                        tt(o3, o3, nb3, ALU.mult, [o_, nbT], [o_])
                        p_ = NB16(n1, slot=slot); act(p_.t[0:C2, :N2], o_.t[0:C2, :N2], AF.Copy, [o_], [p_])
                        n_ = NB16(n2, slot=slot); ts(n_.t[0:C2, :N2], o_.t[0:C2, :N2], -1.0, None, ALU.mult, None, [o_], [n_])
                        return o_, p_, n_
                    A['NT'], A['NT16'], A['LakT'] = gmat(kn, True, 'NT16', 'LakT')
                    _, A['ArbT'], A['ArkT'] = gmat(qn, False, 'ArbT', 'ArkT')
                    if is_s:
                        kb.dma('sp', stGs.t[:], stGdn[:, h].rearrange("s k v -> k s v"), writes=[stGs])
                        stb = stGs; stap = lambda s: stGs.t[:, s, :]
                    else:
                        stb = stG[h]; stap = lambda s, h=h: stG[h].t[:, :]
                    ypb = ps[6 + slot]
                    yield from scan2(slot, 1, 128, 128, N, C, QsT, RsT, PbT, KbT, cv_, A, eg, lambda c: eg.t[:, (c + 1) * C - 1:(c + 1) * C], stG[h], ypb)
                    osrc = ypb.t[:, 0:N2].rearrange("p (s two) -> p s two", two=2)[:, :, 0] if is_s else ypb.t[:, :N]
                    o_ = T(); act(o_.t[:, :N], osrc, AF.Copy, [ypb], [o_])
                sq = HBW(slot); act(sq.t[:, :N], o_.t[:, :N], AF.Square, [o_], [sq])
                yield
                pb = PSs(); mm(pb, pb.t[:, :N], ones16, sq.t[:, :N], [on16, sq], True, True)
                yield
                rs = T(); rsqrt(rs.t[:, :N], pb.t[:, :N], 1.0 / 128, 1e-6, [pb], [rs])
                yield
                on = T(); stt(on.t[:, :N], o_.t[:, :N], c128b.t[:, GDNN:GDNN + 1], rs.t[:, :N], ALU.mult, ALU.mult, [o_, c128b, rs], [on])
                yield
                tt(ob.t[:, h, :N], on.t[:, :N], zz.t[:, :N], ALU.mult, [on, zz], [ob])
                yield
                if last:
                    dstG = oGdnS if is_s else oGdnP
                    if is_s:
                        kb.dma('sp', dstG[:, h].rearrange("s k v -> k s v"), stGs.t[:], reads=[stGs], is_out=True)
                    else:
                        kb.dma('sp', dstG[0, h], stG[h].t[:, :], reads=[stG[h]], is_out=True)

            order = [x_ for x_ in [('r', 0), ('g', 0), ('r', 1), ('g', 1), ('r', 2), ('g', 2), ('r', 3), ('g', 3)] if x_[0] in ONLY[0]]
            mk = lambda kind, hh: (lambda sl_: (rwkv_gen(hh, sl_) if kind == 'r' else gdn_gen(hh, sl_)))
            queue = [mk(k_, h_) for (k_, h_) in order]
            if is_s or NSLOT[0] == 1:
                for f_ in queue:
                    for _ in f_(0):
                        pass
            else:
                active = [None, None]
                rounds = 0
                qs = [[mk(k_, h_) for (k_, h_) in order if k_ == 'r'], [mk(k_, h_) for (k_, h_) in order if k_ == 'g']]
                while qs[0] or qs[1] or any(a_ is not None for a_ in active):
                    rounds += 1
                    for sl_ in (0, 1):
                        if active[sl_] is None and qs[sl_] and not (sl_ == 1 and rounds < OFFS[0]):
                            active[sl_] = qs[sl_].pop(0)(sl_)
                            next(active[sl_])
                            for _ in range(2):
                                if pend:
                                    pend.pop(0)()
                        if active[sl_] is not None:
                            try:
                                next(active[sl_])
                            except StopIteration:
                                active[sl_] = None
            while pend:
                pend.pop(0)()
            if is_s:
                dbgdump('oa', oa.t[:, :, 0:NS], [oa]); dbgdump('ob', ob.t[:, :, 0:NS], [ob])
            CKP(pfx + 'gdn')
            for cb in range(4):
                wga = wloadc('gate', 0, 8, 128, cb * 256, 256)
                gas = []
                for m2 in range(2):
                    pga = PS(); proj_fm(pga, pga.t[:, :N], wga, m2 * 128, 128, N)
                    ga = HB(); act(ga.t[:, :N], pga.t[:, :N], AF.Sigmoid, [pga], [ga])
                    gas.append(ga)
                wgb = wloadc('gate', 0, 8, 128, 1024 + cb * 256, 256)
                gbs = []
                for m2 in range(2):
                    pgb = PS(); proj_fm(pgb, pgb.t[:, :N], wgb, m2 * 128, 128, N)
                    gb = HB(); act(gb.t[:, :N], pgb.t[:, :N], AF.Sigmoid, [pgb], [gb])
                    gbs.append(gb)
                wa = wloadc('bra', 0, 4, 128, cb * 256, 256)
                for m2 in range(2):
                    pa_ = PS(); proj_fm(pa_, pa_.t[:, :N], wa, m2 * 128, 128, N, nk=4, rhsb=oa)
                    tt(gas[m2].t[:, :N], gas[m2].t[:, :N], pa_.t[:, :N], ALU.mult, [gas[m2], pa_], [gas[m2]])
                wb = wloadc('brb', 0, 4, 128, cb * 256, 256)
                for m2 in range(2):
                    m = cb * 2 + m2
                    pbb = PS(); proj_fm(pbb, pbb.t[:, :N], wb, m2 * 128, 128, N, nk=4, rhsb=ob)
                    tt(gbs[m2].t[:, :N], gbs[m2].t[:, :N], pbb.t[:, :N], ALU.mult, [gbs[m2], pbb], [gbs[m2]])
                    tt(mixT.t[:, m, :N], gas[m2].t[:, :N], gbs[m2].t[:, :N], ALU.add, [gas[m2], gbs[m2]], [mixT])

            def tok_out(wname, nkc_list, lhsb, epilogue):
                for cb in range(4):
                    pbs = [PS() for _ in rows_list]
                    k0 = 0
                    tot = sum(nkc_list)
                    for nk in nkc_list:
                        wt = wloadc(wname, k0 * 128, nk, 128, cb * 256, 256)
                        for tc, rows in enumerate(rows_list):
                            for k in range(nk):
                                kk_ = k0 + k
                                mm(pbs[tc], pbs[tc].t[0:rows, 0:256], lhsb.t[:, kk_, tc * 128:tc * 128 + rows], wt.t[:, k, :], [lhsb, wt], kk_ == 0, kk_ == tot - 1)
                        k0 += nk
                    for tc, rows in enumerate(rows_list):
                        epilogue(tc, rows, cb, pbs[tc])

            def resid_add(tc, rows, cb, pb):
                sl = slice(cb * 256, (cb + 1) * 256)
                tt(xtok.t[0:rows, tc, sl], xtok.t[0:rows, tc, sl], pb.t[0:rows, 0:256], ALU.add, [xtok, pb], [xtok])

            if is_s:
                dbgdump('mix', mixT.t[:, :, 0:NS], [mixT])
            tok_out('out', [8], mixT, resid_add)
            if is_s:
                dbgdump('h1', xtok.t[0:NS, 0, :], [xtok])
            CKP(pfx + 'merge')
            norm_T(N, rows_list, GFFN)
            for cb in range(11):
                wg = wloadc('fg', 0, 8, 128, cb * 256, 256)
                wu = wloadc('fu', 0, 8, 128, cb * 256, 256)
                for m2 in range(2):
                    pg_ = PS(); proj_fm(pg_, pg_.t[:, :N], wg, m2 * 128, 128, N)
                    pu_ = PS(); proj_fm(pu_, pu_.t[:, :N], wu, m2 * 128, 128, N)
                    sg = HB(); act(sg.t[:, :N], pg_.t[:, :N], AF.Silu, [pg_], [sg])
                    tt(hfT.t[:, cb * 2 + m2, :N], sg.t[:, :N], pu_.t[:, :N], ALU.mult, [sg, pu_], [hfT])
            tok_out('fd', [8, 8, 6], hfT, resid_add)
            if is_s:
                dbgdump('h2', xtok.t[0:NS, 0, :], [xtok])
            CKP(pfx + 'ffn')
            norm_T(N, rows_list, GPLE)
            for tc, rows in enumerate(rows_list):
                for k in range(2):
                    pb = PS()
                    tr(pb, pb.t[:, 0:rows], ptok.t[0:rows, tc, k * 128:(k + 1) * 128], rows, [ptok])
                    act(peT.t[:, k, tc * 128:tc * 128 + rows], pb.t[:, 0:rows], AF.Copy, [pb], [peT])
            for cb in range(4):
                wg = wloadc('pg', 0, 8, 128, cb * 256, 256)
                wp = wloadc('pp', 0, 2, 128, cb * 256, 256)
                for tc, rows in enumerate(rows_list):
                    pg_ = PS(); pp_ = PS()
                    for k in range(8):
                        mm(pg_, pg_.t[0:rows, 0:256], uT.t[:, k, tc * 128:tc * 128 + rows], wg.t[:, k, :], [uT, wg], k == 0, k == 7)
                    for k in range(2):
                        mm(pp_, pp_.t[0:rows, 0:256], peT.t[:, k, tc * 128:tc * 128 + rows], wp.t[:, k, :], [peT, wp], k == 0, k == 1)
                    sg = HB(); act(sg.t[0:rows, 0:256], pg_.t[0:rows, 0:256], AF.Sigmoid, [pg_], [sg])
                    tt(sg.t[0:rows, 0:256], sg.t[0:rows, 0:256], pp_.t[0:rows, 0:256], ALU.mult, [sg, pp_], [sg])
                    sl = slice(cb * 256, (cb + 1) * 256)
                    tt(xtok.t[0:rows, tc, sl], xtok.t[0:rows, tc, sl], sg.t[0:rows, 0:256], ALU.add, [xtok, sg], [xtok])
            if is_s:
                dbgdump('h3', xtok.t[0:NS, 0, :], [xtok])
            CKP(pfx + 'ple')
            for tc, rows in enumerate(rows_list):
                act(xs.t[:rows, :], xtok.t[:rows, tc, :], AF.Square, [xtok], [xs, ss], accum_out=ss.t[:rows, tc:tc + 1])
            mr = max(rows_list); TC = len(rows_list)
            rsqrt(ss.t[:mr, 4:4 + TC], ss.t[:mr, 0:TC], 1.0 / D, 1e-6, [ss], [ss])
            for tc, rows in enumerate(rows_list):
                stt(xs.t[:rows, :], xtok.t[:rows, tc, :], ss.t[:rows, 4 + tc:5 + tc], gfb.t[:rows, :], ALU.mult, ALU.mult, [xtok, ss, gfb], [xs])
                kb.dma('sp', ydst[t0 + tc * 128:t0 + tc * 128 + rows, :], xs.t[:rows, :], reads=[xs], is_out=True)

        try:
          for ti in range(n_ptiles):
            do_tile(False, ti * NP)
          CKP('ptiles')
          pb = PS()
          tr(pb, pb.t[0:12, 0:128], car_rkv.t[:, 0:12], 128, [car_rkv])
          o1 = HB(); act(o1.t[0:12, 0:128], pb.t[0:12, 0:128], AF.Copy, [pb], [o1])
          kb.dma('sp', oShiftP[0, 0:1536].rearrange("(g p) -> g p", p=128), o1.t[0:12, 0:128], reads=[o1], is_out=True)
          pb = PS()
          tr(pb, pb.t[0:2, 0:64], car_xw.t[:, 0:2], 64, [car_xw])
          o2 = HB(); act(o2.t[0:2, 0:64], pb.t[0:2, 0:64], AF.Copy, [pb], [o2])
          kb.dma('sp', oShiftP[0, 1536:1664].rearrange("(g p) -> g p", p=64), o2.t[0:2, 0:64], reads=[o2], is_out=True)
          pb = PS()
          tr(pb, pb.t[0:2, 0:128], car_xg.t[:, 0:2], 128, [car_xg])
          o3 = HB(); act(o3.t[0:2, 0:128], pb.t[0:2, 0:128], AF.Copy, [pb], [o3])
          kb.dma('sp', oShiftP[0:1, 1664:1792], o3.t[0:1, 0:128], reads=[o3], is_out=True)
          for g3 in range(3):
              pb = PS()
              for gg in range(4):
                  g = g3 * 4 + gg
                  tr(pb, pb.t[0:4, gg * 128:(gg + 1) * 128], car_cv.t[:, g, :], 128, [car_cv], inc=(gg == 3))
              cst = NB(('Pb', 'Kb', 'Vt')[g3], 512, slot=0)
              act(cst.t[0:4, 0:512], pb.t[0:4, 0:512], AF.Copy, [pb], [cst])
              kb.dma('sp', oConvP[0, :, g3 * 512:(g3 + 1) * 512], cst.t[0:3, 0:512], reads=[cst], is_out=True)
          CKP('pouts')
          kb.barrier()
          do_tile(True, 0)
          CKP('stile')
          for jb in range(4):
              pb = PS()
              if jb < 3:
                  for h_ in range(4):
                      tr(pb, pb.t[0:NS, h_ * 128:(h_ + 1) * 128], rawS.t[:, jb * 4 + h_, :], 128, [rawS], inc=(h_ == 3))
                  act(tokS.t[0:NS, jb * 512:(jb + 1) * 512], pb.t[0:NS, 0:512], AF.Copy, [pb], [tokS])
              else:
                  tr(pb, pb.t[0:NS, 0:64], rawS.t[0:64, 12, :], 64, [rawS], inc=False)
                  tr(pb, pb.t[0:NS, 64:128], rawS.t[0:64, 13, :], 64, [rawS], inc=False)
                  tr(pb, pb.t[0:NS, 128:256], rawS.t[0:128, 14, :], 128, [rawS])
                  act(tokS.t[0:NS, 1536:1792], pb.t[0:NS, 0:256], AF.Copy, [pb], [tokS])
          kb.dma('sp', oShiftS[:, :], tokS.t[0:NS, :], reads=[tokS], is_out=True)
          kb.dma('sp', oConvS[:, 0:2, :], stConv[:, 1:3, :], is_out=True)
          for g3 in range(3):
              pb = PS()
              for gg in range(4):
                  g = g3 * 4 + gg
                  tr(pb, pb.t[0:NS, gg * 128:(gg + 1) * 128], rawC.t[:, g, :], 128, [rawC], inc=(gg == 3))
              act(tokC.t[0:NS, g3 * 512:(g3 + 1) * 512], pb.t[0:NS, 0:512], AF.Copy, [pb], [tokC])
          kb.dma('sp', oConvS[:, 2, :], tokC.t[0:NS, 0:1536], reads=[tokC], is_out=True)
        except _Stop:
            pass
        kb._wait('sp', kb.out_events)
    return nc


def host_consts():
    cm = np.zeros((128, 1024), np.float32)
    cm[:, 0:128] = np.eye(128, dtype=np.float32)
    j = np.arange(64)[:, None]; i = np.arange(64)[None, :]
    for h0 in (0, 64):
        cm[h0:h0 + 64, 128:192] = (j < i)
        cm[h0:h0 + 64, 192:256] = (j <= i)
        cm[h0:h0 + 64, 960:1024] = (j > i)
    cm[0:64, 256:320] = np.where(j <= i, 0.0, -30000.0)
    cm[:, 320:448] = 1.0
    sm = np.ones(512, np.float32); sm[::64] = 0.0
    cm[:, 448:960] = sm[None, :]
    return cm


def prep_weights(inp):
    w_in = inp['w_in'][0]
    perm = []
    for hp in range(4):
        for j in range(3):
            perm.extend(range(j * 512 + hp * 128, j * 512 + hp * 128 + 128))
    perm.extend(range(1536, 1792))
    for h in range(4):
        for j in range(3):
            perm.extend(range(1792 + j * 512 + h * 128, 1792 + j * 512 + h * 128 + 128))
        perm.extend(range(3328 + h * 128, 3328 + h * 128 + 128))
    perm.extend(range(3840, 5896))
    w_in_p = np.ascontiguousarray(w_in[:, perm])
    w_barep = np.ascontiguousarray(np.repeat(w_in[:, 3840:3848], 128, axis=1))
    mu = inp['mu_shift'][0]
    c64 = np.zeros((64, 80), np.float32)
    for h in range(8):
        for j in range(3):
            c64[:, h * 3 + j] = mu[j * 512 + h * 64: j * 512 + h * 64 + 64]
    c64[:, 24] = mu[1536:1600]; c64[:, 25] = mu[1600:1664]
    def hcol(v):
        return np.ascontiguousarray(v.reshape(8, 64).T)
    c64[:, 26:34] = hcol(inp['rw_w0'][0]); c64[:, 34:42] = hcol(inp['rw_a0'][0])
    c64[:, 42:50] = hcol(inp['rw_kk'][0]); c64[:, 50:58] = hcol(inp['rw_ka'][0])
    c64[:, 58:66] = hcol(inp['rw_rk'][0].reshape(-1)); c64[:, 66:74] = hcol(inp['rw_ln_w'][0])
    c128 = np.zeros((128, 64), np.float32)
    c128[:, 0] = mu[1664:1792]
    c128[:, 1:9] = inp['norm_mix'][0].reshape(8, 128).T
    c128[:, 9:17] = inp['norm_ffn'][0].reshape(8, 128).T
    c128[:, 17:25] = inp['norm_ple'][0].reshape(8, 128).T
    c128[:, 25] = inp['gdn_norm'][0]
    c128[:, 26:30] = inp['gdn_a_log'][0][None, :]
    c128[:, 30:34] = inp['gdn_dt_bias'][0][None, :]
    c128[0:64, 40:48] = hcol(inp['rw_ln_b'][0])
    c128r = np.zeros((128, 64), np.float32)
    for hp in range(4):
        for j in range(3):
            c128r[:, hp * 3 + j] = mu[j * 512 + hp * 128: j * 512 + hp * 128 + 128]
    pcol = lambda v: np.ascontiguousarray(np.asarray(v).reshape(4, 128).T)
    c128r[:, 12:16] = pcol(inp['rw_w0'][0]); c128r[:, 16:20] = pcol(inp['rw_a0'][0])
    c128r[:, 20:24] = pcol(inp['rw_kk'][0]); c128r[:, 24:28] = pcol(inp['rw_ka'][0])
    c128r[:, 28:32] = pcol(inp['rw_rk'][0].reshape(-1)); c128r[:, 32:36] = pcol(inp['rw_ln_w'][0]); c128r[:, 36:40] = pcol(inp['rw_ln_b'][0])
    cmat2 = np.zeros((128, 192), np.float32)
    cmat2[0:64, 0:64] = 1.0; cmat2[64:128, 64:128] = 1.0
    cmat2[0:64, 128:192] = np.eye(64); cmat2[64:128, 128:192] = np.eye(64)
    cv = inp['gdn_conv'][0]
    convw = np.zeros((128, 48), np.float32)
    for tap in range(4):
        convw[:, tap * 12:(tap + 1) * 12] = cv[tap].reshape(12, 128).T
    return dict(
        w_in=w_in_p, w_barep=w_barep, c64=c64, c128=c128, convw=convw,
        w2=np.ascontiguousarray(inp['rw_w2'][0]), a2=np.ascontiguousarray(inp['rw_a2'][0]), g2=np.ascontiguousarray(inp['rw_g2'][0]),
        w_bra=np.ascontiguousarray(inp['w_branch_a'][0]), w_brb=np.ascontiguousarray(inp['w_branch_b'][0]),
        w_out=np.ascontiguousarray(inp['w_out'][0]),
        w_fg=np.ascontiguousarray(inp['w_ffn_gate'][0]), w_fu=np.ascontiguousarray(inp['w_ffn_up'][0]),
        w_fd=np.ascontiguousarray(inp['w_ffn_down'][0]),
        w_pg=np.ascontiguousarray(inp['w_ple_gate'][0]), w_pp=np.ascontiguousarray(inp['w_ple_proj'][0]),
        gfin=np.ascontiguousarray(np.broadcast_to(inp['norm_final'][None, :], (128, D))),
        cmat=host_consts(), cmat2=cmat2, c128r=c128r,
    )


def make_in_maps(inp, n_cores, TP):
    shared = prep_weights(inp)
    maps = []
    for c in range(n_cores):
        m = dict(shared)
        m['xP'] = np.ascontiguousarray(inp['x_prompt'][c, :TP])
        m['pP'] = np.ascontiguousarray(inp['p_prompt'][0, c, :TP])
        sl = slice(c * NS, (c + 1) * NS)
        m['xS'] = np.ascontiguousarray(inp['x_sample'][sl, 0])
        m['pS'] = np.ascontiguousarray(inp['p_sample'][0, sl, 0])
        m['stShift'] = np.ascontiguousarray(inp['state_shift'][0, sl, 0])
        m['stWkv'] = np.ascontiguousarray(inp['state_wkv'][0, sl])
        m['stConv'] = np.ascontiguousarray(inp['state_conv'][0, sl])
        m['stGdn'] = np.ascontiguousarray(inp['state_gdn'][0, sl])
        maps.append(m)
    return maps


def gather(results, n_cores):
    cat = lambda k: np.concatenate([r[k] for r in results], axis=0)
    yP = np.stack([r['yP'] for r in results], axis=0)
    yS = cat('yS')[:, None, :]
    return (yP, yS,
            cat('oShiftP')[None, :, None, :], cat('oWkvP')[None], cat('oConvP')[None], cat('oGdnP')[None],
            cat('oShiftS')[None, :, None, :], cat('oWkvS')[None], cat('oConvS')[None], cat('oGdnS')[None])


def kernel(**inputs):
    inp = {k: np.asarray(v) for k, v in inputs.items()}
    n = 8
    TP = inp['x_prompt'].shape[1]
    nc = build(TP // NP)
    maps = make_in_maps(inp, n, TP)
    res = run_bass_kernel_spmd(nc, maps, core_ids=list(range(n)))
    outs = gather(res.results, n)
    return tuple(np.ascontiguousarray(o, dtype=np.float32) for o in outs)
```

```python
import contextlib
import numpy as np
import concourse.bass as bass
import concourse.mybir as mybir
from concourse.bass_utils import run_bass_kernel_spmd
from concourse.alu_op_type import AluOpType as ALU

F32 = mybir.dt.float32
BF16 = mybir.dt.bfloat16
AF = mybir.ActivationFunctionType

D = 1024
NS = 16
NP = 256
TCN = NP // 128
CW = 5896
DFF = 2816
C0 = float(np.exp(-0.5))


class _Stop(Exception):
    pass


STOP = [None]
NSLOT = [2]
ONLY = ['rg']
OFFS = [30]


def CKP(name):
    if STOP[0] == name:
        raise _Stop()


class Buf:
    def __init__(self, t):
        self.t = t
        self.w = None
        self.r = {}


class KB:
    def __init__(self, nc, es):
        self.nc = nc
        self.es = es
        self.eng = {'pe': nc.tensor, 'dve': nc.vector, 'act': nc.scalar, 'pool': nc.gpsimd, 'sp': nc.sync}
        self.semh = {}
        self.cnt = {}
        self.seen = {e: {} for e in self.eng}
        for e in self.eng:
            self.semh[e] = es.enter_context(nc.semaphore("s_" + e))
            self.cnt[e] = 0
        self.ndma = 20
        self.dcur = 0
        for i in range(self.ndma):
            k = "d%d" % i
            self.semh[k] = es.enter_context(nc.semaphore("s_" + k))
            self.cnt[k] = 0
        self.nbuf = 0
        self.out_events = []

    def sb(self, shape, dt=F32, name=None):
        self.nbuf += 1
        t = self.es.enter_context(self.nc.sbuf_tensor(name or ("b%d" % self.nbuf), list(shape), dt))
        return Buf(t)

    def psb(self, name):
        t = self.es.enter_context(self.nc.psum_tensor(name, [128, 512], F32))
        return Buf(t)

    def _wait(self, e, deps):
        engine = self.eng[e]
        for (s, v) in deps:
            if self.seen[e].get(s, 0) >= v:
                continue
            engine.wait_ge(self.semh[s], v)
            self.seen[e][s] = v

    def _deps(self, e, reads, writes):
        deps = []
        for b in reads:
            if b.w is not None:
                if not (e == 'pe' and b.w[0] == 'pe'):
                    deps.append(b.w)
        for b in writes:
            if b.w is not None and b.w[0] != e:
                deps.append(b.w)
            for s, v in b.r.items():
                if s != e:
                    deps.append((s, v))
        return deps

    def _record(self, ev, reads, writes):
        for b in reads:
            b.r[ev[0]] = max(b.r.get(ev[0], 0), ev[1])
        for b in writes:
            b.w = ev
            b.r = {}

    def op(self, e, fn, reads=(), writes=(), inc=True):
        self._wait(e, self._deps(e, reads, writes))
        ins = fn(self.eng[e])
        if inc:
            self.cnt[e] += 1
            ins.then_inc(self.semh[e], 1)
            ev = (e, self.cnt[e])
        else:
            ev = (e, self.cnt[e] + 1)
        self._record(ev, reads, writes)
        return ev

    def dma(self, q, out, in_, reads=(), writes=(), is_out=False):
        k = "d%d" % self.dcur
        self.dcur = (self.dcur + 1) % self.ndma
        deps = self._deps(q, reads, writes)
        if self.cnt[k] > 0:
            deps.append((k, self.cnt[k]))
        self._wait(q, deps)
        self.cnt[k] += 16
        with self.nc.allow_non_contiguous_dma(reason="layout"):
            self.eng[q].dma_start(out=out, in_=in_).then_inc(self.semh[k], 16)
        ev = (k, self.cnt[k])
        self._record(ev, reads, writes)
        if is_out:
            self.out_events.append(ev)
        return ev

    def barrier(self):
        evs = []
        for e in self.eng:
            if self.cnt[e] > 0:
                evs.append((e, self.cnt[e]))
        for i in range(self.ndma):
            k = "d%d" % i
            if self.cnt[k] > 0:
                evs.append((k, self.cnt[k]))
        for e in self.eng:
            self._wait(e, [ev for ev in evs if ev[0] != e])


def build(n_ptiles, dbg=()):
    TP = n_ptiles * NP
    nc = bass.Bass("TRN2", target_bir_lowering=False)

    def din(name, shape):
        return nc.dram_tensor(name, list(shape), F32, kind="ExternalInput").ap()

    def dout(name, shape):
        return nc.dram_tensor(name, list(shape), F32, kind="ExternalOutput").ap()

    xP = din("xP", [TP, D]); xS = din("xS", [NS, D])
    pP = din("pP", [TP, 256]); pS = din("pS", [NS, 256])
    stShift = din("stShift", [NS, 1792]); stWkv = din("stWkv", [NS, 8, 64, 64])
    stConv = din("stConv", [NS, 3, 1536]); stGdn = din("stGdn", [NS, 4, 128, 128])
    w_in = din("w_in", [D, CW]); w_barep = din("w_barep", [D, 1024])
    c64 = din("c64", [64, 80]); c128 = din("c128", [128, 64]); convw = din("convw", [128, 48])
    w2 = din("w2", [64, 512]); a2 = din("a2", [64, 512]); g2 = din("g2", [128, 512])
    w_bra = din("w_bra", [512, D]); w_brb = din("w_brb", [512, D]); w_out = din("w_out", [D, D])
    w_fg = din("w_fg", [D, DFF]); w_fu = din("w_fu", [D, DFF]); w_fd = din("w_fd", [DFF, D])
    w_pg = din("w_pg", [D, D]); w_pp = din("w_pp", [256, D])
    gfin = din("gfin", [128, D]); cmat = din("cmat", [128, 1024]); cmat2 = din("cmat2", [128, 192]); c128rd = din("c128r", [128, 64])
    yP = dout("yP", [TP, D]); yS = dout("yS", [NS, D])
    oShiftP = dout("oShiftP", [1, 1792]); oWkvP = dout("oWkvP", [1, 8, 64, 64])
    oConvP = dout("oConvP", [1, 3, 1536]); oGdnP = dout("oGdnP", [1, 4, 128, 128])
    oShiftS = dout("oShiftS", [NS, 1792]); oWkvS = dout("oWkvS", [NS, 8, 64, 64])
    oConvS = dout("oConvS", [NS, 3, 1536]); oGdnS = dout("oGdnS", [NS, 4, 128, 128])
    dbg_outs = {}
    for nm, shp in dbg:
        dbg_outs[nm] = nc.dram_tensor("dbg_" + nm, list(shp), BF16 if nm in ("oa", "ob", "mix") else F32, kind="ExternalOutput").ap()

    es = contextlib.ExitStack()
    with es:
        kb = KB(nc, es)
        op = kb.op
        cm = kb.sb([128, 1024]); kb.dma('sp', cm.t[:], cmat, writes=[cm])
        ident = cm.t[:, 0:128]
        maskS = cm.t[0:64, 128:192]
        maskI = cm.t[0:64, 192:256]
        negI = cm.t[0:64, 256:320]
        ones = cm.t[:, 320:448]
        scanm = cm.t[:, 448:960]
        maskS2 = cm.t[:, 128:192]; maskI2 = cm.t[:, 192:256]; maskL2 = cm.t[:, 960:1024]
        c64b = kb.sb([64, 80]); kb.dma('sp', c64b.t[:], c64, writes=[c64b])
        c128b = kb.sb([128, 64]); kb.dma('sp', c128b.t[:], c128, writes=[c128b])
        cvw = kb.sb([128, 48]); kb.dma('sp', cvw.t[:], convw, writes=[cvw])
        w2b = kb.sb([64, 512], BF16); kb.dma('pool', w2b.t[:], w2, writes=[w2b])
        a2b = kb.sb([64, 512], BF16); kb.dma('pool', a2b.t[:], a2, writes=[a2b])
        g2b = kb.sb([128, 512], BF16); kb.dma('pool', g2b.t[:], g2, writes=[g2b])
        gfb = kb.sb([128, D]); kb.dma('sp', gfb.t[:], gfin, writes=[gfb])
        MU_RKV, MU_XW, MU_XAA, W0, A0, KKc, KAc, RKc, LNW = 0, 24, 25, 26, 34, 42, 50, 58, 66
        MU_XG, GMIX, GFFN, GPLE, GDNN, ALOG, DTB, LNB = 0, 1, 9, 17, 25, 26, 30, 40
        c128r = kb.sb([128, 64]); kb.dma('sp', c128r.t[:], c128rd, writes=[c128r])
        cm2 = kb.sb([128, 192]); kb.dma('sp', cm2.t[:], cmat2, writes=[cm2])
        bones = cm2.t[:, 0:128]
        cm16 = kb.sb([128, 128], BF16)
        op('dve', lambda e: e.tensor_copy(out=cm16.t[:], in_=cm.t[:, 0:128]), [cm], [cm16])
        id16 = cm16.t[:, :]
        on16 = kb.sb([128, 256], BF16)
        op('dve', lambda e: e.tensor_copy(out=on16.t[:, 0:128], in_=cm.t[:, 320:448]), [cm], [on16])
        op('dve', lambda e: e.tensor_copy(out=on16.t[:, 128:256], in_=cm2.t[:, 0:128]), [cm2], [on16])
        ones16 = on16.t[:, 0:128]; bones16 = on16.t[:, 128:256]
        ident2 = cm2.t[:, 128:192]
        W0r, A0r, KKr, KAr, RKr, LNWr, LNBr = 12, 16, 20, 24, 28, 32, 36
        omu128r = kb.sb([128, 12])
        op('dve', lambda e: e.tensor_scalar(out=omu128r.t[:], in0=c128r.t[:, 0:12], scalar1=-1.0, scalar2=1.0, op0=ALU.mult, op1=ALU.add), [c128r], [omu128r])
        omu64 = kb.sb([64, 26])
        op('dve', lambda e: e.tensor_scalar(out=omu64.t[:], in0=c64b.t[:, 0:26], scalar1=-1.0, scalar2=1.0, op0=ALU.mult, op1=ALU.add), [c64b], [omu64])
        omu128 = kb.sb([128, 1])
        op('dve', lambda e: e.tensor_scalar(out=omu128.t[:], in0=c128b.t[:, 0:1], scalar1=-1.0, scalar2=1.0, op0=ALU.mult, op1=ALU.add), [c128b], [omu128])
        nexpA = kb.sb([128, 4])
        op('act', lambda e: e.activation(out=nexpA.t[:], in_=c128b.t[:, ALOG:ALOG + 4], func=AF.Exp), [c128b], [nexpA])
        op('dve', lambda e: e.tensor_scalar(out=nexpA.t[:], in0=nexpA.t[:], scalar1=-1.0, scalar2=None, op0=ALU.mult), [nexpA], [nexpA])

        ps = [kb.psb("ps%d" % i) for i in range(8)]
        pcur = [0]

        def PS():
            b = ps[pcur[0]]
            pcur[0] = (pcur[0] + 1) % 8
            return b
        pscur = [0, 0]

        def PSslot(slot):
            b = ps[slot * 3 + pscur[slot]]
            pscur[slot] = (pscur[slot] + 1) % 3
            return b

        NW = 10
        wring = [kb.sb([128, 8, 256], BF16, "wr%d" % i) for i in range(NW)]
        wcur = [0]

        def wload(w, r0, nk, pk, c0, ncols):
            b = wring[wcur[0]]
            wcur[0] = (wcur[0] + 1) % NW
            src = w[r0:r0 + nk * pk, c0:c0 + ncols].rearrange("(k p) c -> p k c", p=pk)
            kb.dma('pool', b.t[0:pk, 0:nk, 0:ncols], src, writes=[b])
            return b

        conv = {}

        def convert(name, src, R, Cc):
            dst = nc.dram_tensor("cw_" + name, [R, Cc], BF16, kind="Internal").ap()
            b = Buf(dst)
            for c0 in range(0, Cc, 1024):
                w_ = min(1024, Cc - c0)
                kb.dma('pool', dst[:, c0:c0 + w_], src[:, c0:c0 + w_], writes=[b])
            conv[name] = (dst, b)

        def wloadc(name, r0, nk, pk, c0, ncols):
            dst_, cb_ = conv[name]
            b = wring[wcur[0]]
            wcur[0] = (wcur[0] + 1) % NW
            src = dst_[r0:r0 + nk * pk, c0:c0 + ncols].rearrange("(k p) c -> p k c", p=pk)
            kb.dma('pool', b.t[0:pk, 0:nk, 0:ncols], src, reads=[cb_], writes=[b])
            return b

        RW = 11500
        region = kb.sb([128, RW], name="region")
        bumpP = [0]; bumpS = [0]

        def rview(bump, width, pat=None, parts=128, **kw):
            a_ = bump[0]; bump[0] += width
            assert bump[0] <= RW, (bump[0], RW)
            ap = region.t[0:parts, a_:a_ + width]
            if pat is not None:
                ap = ap.rearrange(pat, **kw)
            return Buf(ap)

        xtoks = [kb.sb([128, TCN, D], name="xtok0"), kb.sb([128, TCN, D], name="xtok1")]
        ptoks = None
        tix = [0]
        preloaded = {}
        xs = kb.sb([128, D], name="xs")
        uT = kb.sb([128, 8, NP], BF16, name="uT")
        ss = kb.sb([128, 8], name="ss")
        car_rkv = kb.sb([128, 12], name="car_rkv"); car_xw = kb.sb([64, 2], name="car_xw"); car_xg = kb.sb([128, 2], name="car_xg")
        car_cv = kb.sb([128, 12, 4], name="car_cv")
        for b_ in (car_rkv, car_xw, car_xg, car_cv):
            op('dve', lambda e, b_=b_: e.memset(b_.t[:], 0.0), [], [b_])
        stR = [kb.sb([128, 64], name="stR%d" % i) for i in range(4)]
        stG = [kb.sb([128, 128], name="stG%d" % i) for i in range(4)]
        for b_ in stR + stG:
            op('dve', lambda e, b_=b_: e.memset(b_.t[:], 0.0), [], [b_])
        stRs = rview(bumpS, NS * 64, "p (s v) -> p s v", v=64)
        stGs = rview(bumpS, NS * 128, "p (s v) -> p s v", v=128)
        tw = kb.sb([64, NP], BF16, name="tw"); xaaS = kb.sb([64, NP], BF16, name="xaaS"); sgS = kb.sb([128, NP], BF16, name="sgS")
        oa = kb.sb([128, 4, NP], BF16, name="oa"); ob = kb.sb([128, 4, NP], BF16, name="ob")
        mixT = kb.sb([128, 8, NP], BF16, name="mixT")
        hfT = kb.sb([128, 22, NP], BF16, name="hfT")
        peT = kb.sb([128, 2, NP], BF16, name="peT")
        ptoks = [kb.sb([128, TCN, 256], name="ptok0"), kb.sb([128, TCN, 256], name="ptok1")]
        prevS = rview(bumpS, 16 * NS, "p (g s) -> p g s", s=NS)
        rawS = rview(bumpS, 16 * NS, "p (g s) -> p g s", s=NS)
        histS = rview(bumpS, 36 * NS, "p (g s) -> p g s", s=NS)
        rawC = rview(bumpS, 12 * NS, "p (g s) -> p g s", s=NS)
        tokS = rview(bumpS, 1792, parts=NS)
        tokC = rview(bumpS, 1536, parts=NS)
        NHB = 7
        hb = [[kb.sb([128, NP + 4], name="hb%d" % i) for i in range(NHB)], [rview(bumpP, NP + 4) for i in range(6)]]
        hcur = [0, 0]

        def HB(slot=0):
            r_ = hb[slot]
            b = r_[hcur[slot]]
            hcur[slot] = (hcur[slot] + 1) % len(r_)
            return b
        named = {}

        def NB(name, width=NP + 4, slot=0):
            key = (name, slot)
            if key not in named:
                if slot == 0:
                    named[key] = kb.sb([128, width], name="n_" + name)
                elif slot == 1:
                    named[key] = rview(bumpP, width)
                else:
                    named[key] = rview(bumpS, width)
            return named[key]

        def act(out, in_, func, reads, writes, bias=None, scale=None, **kw):
            kws = dict(kw)
            if bias is not None:
                kws['bias'] = bias
            if scale is not None:
                kws['scale'] = scale
            return op('act', lambda e: e.activation(out=out, in_=in_, func=func, **kws), reads, writes)

        def tt(out, in0, in1, o, reads, writes, eng='dve'):
            return op(eng, lambda e: e.tensor_tensor(out=out, in0=in0, in1=in1, op=o), reads, writes)

        def ts(out, in0, s1, s2, o0, o1, reads, writes, eng='dve'):
            if o1 is None:
                return op(eng, lambda e: e.tensor_scalar(out=out, in0=in0, scalar1=s1, scalar2=None, op0=o0), reads, writes)
            return op(eng, lambda e: e.tensor_scalar(out=out, in0=in0, scalar1=s1, scalar2=s2, op0=o0, op1=o1), reads, writes)

        def stt(out, in0, sc, in1, o0, o1, reads, writes):
            return op('dve', lambda e: e.scalar_tensor_tensor(out=out, in0=in0, scalar=sc, in1=in1, op0=o0, op1=o1), reads, writes)

        def mm(pb, out, lhsT, rhs, reads, start, stop):
            return op('pe', lambda e: e.matmul(out, lhsT, rhs, start=start, stop=stop), reads, [pb], inc=stop)

        def tr(pb, out, in_, rows, reads, inc=True):
            return op('pe', lambda e: e.transpose(out, in_, ident[0:rows, 0:rows]), list(reads) + [cm], [pb], inc=inc)

        def rsqrt(out, in_, scale, eps, reads, writes):
            act(out, in_, AF.Ln, reads, writes, bias=eps, scale=scale)
            act(out, out, AF.Exp, writes, writes, scale=-0.5)

        def dbgdump(name, ap_sb, bufs):
            if name in dbg_outs:
                kb.dma('sp', dbg_outs[name], ap_sb, reads=bufs, is_out=True)

        def norm_T(N, rows_list, gcol, xtok):
            TC = len(rows_list)
            for tc, rows in enumerate(rows_list):
                act(xs.t[:rows, :], xtok.t[:rows, tc, :], AF.Square, [xtok], [xs, ss], accum_out=ss.t[:rows, tc:tc + 1])
            mr = max(rows_list)
            rsqrt(ss.t[:mr, 4:4 + TC], ss.t[:mr, 0:TC], 1.0 / D, 1e-6, [ss], [ss])
            pbs = [PS() for _ in range(8)]
            for tc, rows in enumerate(rows_list):
                ts(xs.t[:rows, :], xtok.t[:rows, tc, :], ss.t[:rows, 4 + tc:5 + tc], None, ALU.mult, None, [xtok, ss], [xs])
                for k in range(8):
                    tr(pbs[k], pbs[k].t[:, tc * 128:tc * 128 + rows], xs.t[:rows, k * 128:(k + 1) * 128], rows, [xs])
            for k in range(8):
                if k % 2 == 0:
                    ts(uT.t[:, k, :N], pbs[k].t[:, :N], c128b.t[:, gcol + k:gcol + k + 1], None, ALU.mult, None, [pbs[k], c128b], [uT])
                else:
                    act(uT.t[:, k, :N], pbs[k].t[:, :N], AF.Copy, [pbs[k], c128b], [uT], scale=c128b.t[:, gcol + k:gcol + k + 1])

        def proj_fm(pb, out_ap, wt, coff, M, N, nk=8, rhsb=None, pk=128):
            rb = rhsb or uT
            for k in range(nk):
                mm(pb, out_ap, wt.t[0:pk, k, coff:coff + M], rb.t[0:pk, k, :N], [wt, rb], k == 0, k == nk - 1)

        def shift(P, N, pb, psap, mu, omu, carry, out_b, out_ap, is_s, prev_ap=None, raw_ap=None, raw_b=None, tmp=HB):
            t1 = tmp()
            if is_s:
                act(t1.t[:P, :N], prev_ap, AF.Copy, [prevS], [t1], scale=mu)
                act(raw_ap, psap, AF.Copy, [pb], [raw_b])
            else:
                act(t1.t[:P, 1:N], psap[:, 0:N - 1], AF.Copy, [pb], [t1], scale=mu)
                act(t1.t[:P, 0:1], carry[1], AF.Copy, [carry[0]], [t1], scale=mu)
                act(carry[1], psap[:, N - 1:N], AF.Copy, [pb], [carry[0]])
            stt(out_ap, psap, omu, t1.t[:P, :N], ALU.mult, ALU.add, [pb, t1], [out_b])

        named16 = {}

        def NB16(name, width=NP + 4, slot=0):
            key = (name, slot)
            if key not in named16:
                if slot == 0:
                    named16[key] = kb.sb([128, width], BF16, name="h_" + name)
                else:
                    a_ = bumpP[0]; bumpP[0] += (width + 1) // 2
                    assert bumpP[0] <= RW
                    named16[key] = Buf(region.t[:, a_:a_ + (width + 1) // 2].bitcast(BF16))
            return named16[key]
        hb16 = [[NB16("r16_%d" % i, 132, sl_) for i in range(4)] for sl_ in (0, 1)]
        hbw = [[NB16("w16_%d" % i, NP + 4, sl_) for i in range(2)] for sl_ in (0, 1)]
        hbwcur = [0, 0]

        def HBW(slot=0):
            b = hbw[slot][hbwcur[slot]]
            hbwcur[slot] = (hbwcur[slot] + 1) % 2
            return b
        h16cur = [0, 0]

        def HB16(slot):
            b = hb16[slot][h16cur[slot]]
            h16cur[slot] = (h16cur[slot] + 1) % 4
            return b

        natS = rview(bumpS, NS * 128, "p (s h k) -> p s h k", parts=64, h=2, k=64)

        def scan(slot, Kd, Vd, N, C, nseq, QsT, RsT, PbT, KbT, VT, A, WcB, wc_ap, st_b, st_ap, ypb):
            nch = N // C
            cps = nch // nseq
            W = nch * C

            def tmaj(srcb, Fd, name):
                dst = NB16(name, 512, slot=slot)
                per = 512 // Fd
                for c0 in range(0, nch, per):
                    pb = PSslot(slot)
                    n = min(per, nch - c0)
                    for c in range(c0, c0 + n):
                        tr(pb, pb.t[0:C, (c - c0) * Fd:(c - c0 + 1) * Fd], srcb.t[0:Fd, c * C:(c + 1) * C], Fd, [srcb], inc=(c == c0 + n - 1))
                    act(dst.t[0:C, c0 * Fd:(c0 + n) * Fd], pb.t[0:C, 0:n * Fd], AF.Copy, [pb], [dst])
                return dst
            pre_t = (nch * max(Kd, Vd) <= 512)
            if pre_t:
                pbt = tmaj(PbT, Kd, 'Pb'); kbt = tmaj(KbT, Kd, 'Kb'); vt = tmaj(VT, Vd, 'Vt')
                gP = lambda c: pbt.t[0:C, c * Kd:(c + 1) * Kd]
                gK = lambda c: kbt.t[0:C, c * Kd:(c + 1) * Kd]
                gV = lambda c: vt.t[0:C, c * Vd:(c + 1) * Vd]
            yield
            TinvT = None
            if C > 1:
                NTb = A['NT']; NT16 = A['NT16']
                Xb = NB16('X0', slot=slot); pb = PSslot(slot)
                for c in range(nch):
                    tr(pb, pb.t[0:C, c * C:(c + 1) * C], NTb.t[0:C, c * C:(c + 1) * C], C, [NTb], inc=(c == nch - 1))
                act(Xb.t[0:C, 0:W], pb.t[0:C, 0:W], AF.Copy, [pb], [Xb])
                XTb = NT16
                PT = NB('PT0', slot=slot)
                tt(PT.t[0:C, 0:W].rearrange("p (c i) -> p c i", i=C), NTb.t[0:C, 0:W].rearrange("p (c i) -> p c i", i=C),
                   cm.t[0:C, 0:C].unsqueeze(1).broadcast_to([C, nch, C]), ALU.add, [NTb, cm], [PT])
                PTh = NB16('PTh0', slot=slot)
                act(PTh.t[0:C, 0:W], PT.t[0:C, 0:W], AF.Copy, [PT], [PTh])
                yield
                nlev = {64: 5, 2: 0}[C]
                for lv in range(1, nlev + 1):
                    pbx = PSslot(slot)
                    for c in range(nch):
                        sl = slice(c * C, (c + 1) * C)
                        op('pe', lambda e, sl=sl, pbx=pbx, XTb=XTb, Xb=Xb: e.matmul(pbx.t[0:C, sl], XTb.t[0:C, sl], Xb.t[0:C, sl], start=True, stop=True), [XTb, Xb], [pbx], inc=(c == nch - 1))
                    Xn = NB16('X%d' % (lv % 2), slot=slot)
                    XTn = None
                    if lv < nlev:
                        pbt_ = PSslot(slot)
                        for c in range(nch):
                            sl = slice(c * C, (c + 1) * C)
                            op('pe', lambda e, sl=sl, pbt_=pbt_, XTb=XTb, Xb=Xb: e.matmul(pbt_.t[0:C, sl], Xb.t[0:C, sl], XTb.t[0:C, sl], start=True, stop=True), [XTb, Xb], [pbt_], inc=(c == nch - 1))
                    act(Xn.t[0:C, 0:W], pbx.t[0:C, 0:W], AF.Copy, [pbx], [Xn])
                    yield
                    if lv < nlev:
                        XTn = NB16('XT%d' % (lv % 2), slot=slot)
                        ts(XTn.t[0:C, 0:W], pbt_.t[0:C, 0:W], 1.0, None, ALU.mult, None, [pbt_], [XTn])
                    pbp = PSslot(slot)
                    for c in range(nch):
                        sl = slice(c * C, (c + 1) * C)
                        op('pe', lambda e, sl=sl, pbp=pbp, Xn=Xn, PTh=PTh: e.matmul(pbp.t[0:C, sl], Xn.t[0:C, sl], PTh.t[0:C, sl], start=True, stop=True), [Xn, PTh], [pbp], inc=(c == nch - 1))
                    PTn = NB('PT%d' % (lv % 2), slot=slot)
                    tt(PTn.t[0:C, 0:W], PT.t[0:C, 0:W], pbp.t[0:C, 0:W], ALU.add, [PT, pbp], [PTn])
                    PThn = NB16('PTh%d' % (lv % 2), slot=slot)
                    act(PThn.t[0:C, 0:W], PTn.t[0:C, 0:W], AF.Copy, [PTn], [PThn])
                    PT = PTn; PTh = PThn
                    yield
                    Xb = Xn
                    if lv < nlev:
                        XTb = XTn
                PT = PTh
                TinvT = PT
            for s in range(nseq):
                for cc in range(cps):
                    c = s * cps + cc
                    sl = slice(c * C, (c + 1) * C)
                    stap = st_ap(s)
                    if pre_t:
                        aP = gP(c); aK = gK(c); aV = gV(c)
                    else:
                        pb = PSslot(slot)
                        tr(pb, pb.t[0:C, 0:Kd], PbT.t[0:Kd, sl], Kd, [PbT], inc=False)
                        tr(pb, pb.t[0:C, 128:128 + Kd], KbT.t[0:Kd, sl], Kd, [KbT], inc=False)
                        tr(pb, pb.t[0:C, 256:256 + Vd], VT.t[0:Vd, sl], Vd, [VT])
                        row = HB(slot)
                        act(row.t[0:C, 0:256], pb.t[0:C, 0:256], AF.Copy, [pb], [row])
                        rowv = HB(slot)
                        act(rowv.t[0:C, 0:128], pb.t[0:C, 256:384], AF.Copy, [pb], [rowv])
                        aP = row.t[0:C, 0:Kd]; aK = row.t[0:C, 128:128 + Kd]; aV = rowv.t[0:C, 0:Vd]
                        pbt = row; kbt = row; vt = rowv
                    zp = PSslot(slot)
                    if C > 1:
                        op('pe', lambda e: e.matmul(zp.t[0:C, 0:Vd], QsT.t[0:Kd, sl], stap, start=True, stop=False), [QsT, st_b], [zp], inc=False)
                        op('pe', lambda e: e.matmul(zp.t[0:C, 0:Vd], A['LakT'].t[0:C, sl], aV, start=False, stop=True), [A['LakT'], vt], [zp])
                        yield
                        Zs = HB16(slot)
                        act(Zs.t[0:C, 0:Vd], zp.t[0:C, 0:Vd], AF.Copy, [zp], [Zs])
                        up = PSslot(slot)
                        op('pe', lambda e: e.matmul(up.t[0:C, 0:Vd], TinvT.t[0:C, sl], Zs.t[0:C, 0:Vd], start=True, stop=True), [TinvT, Zs], [up])
                        yield
                        U = HB16(slot)
                        ts(U.t[0:C, 0:Vd], up.t[0:C, 0:Vd], 1.0, None, ALU.mult, None, [up], [U])
                    else:
                        op('pe', lambda e: e.matmul(zp.t[0:C, 0:Vd], QsT.t[0:Kd, sl], stap, start=True, stop=True), [QsT, st_b], [zp])
                        U = HB(slot)
                        act(U.t[0:C, 0:Vd], zp.t[0:C, 0:Vd], AF.Copy, [zp], [U])
                    op('pe', lambda e: e.matmul(ypb.t[0:Vd, sl], stap, RsT.t[0:Kd, sl], start=True, stop=False), [st_b, RsT], [ypb], inc=False)
                    op('pe', lambda e: e.matmul(ypb.t[0:Vd, sl], U.t[0:C, 0:Vd], A['ArbT'].t[0:C, sl], start=False, stop=False), [U, A['ArbT']], [ypb], inc=False)
                    op('pe', lambda e: e.matmul(ypb.t[0:Vd, sl], aV, A['ArkT'].t[0:C, sl], start=False, stop=True), [vt, A['ArkT']], [ypb])
                    yield
                    sp_ = PSslot(slot)
                    op('pe', lambda e: e.matmul(sp_.t[0:Kd, 0:Vd], aP, U.t[0:C, 0:Vd], start=True, stop=False), [pbt, U], [sp_], inc=False)
                    op('pe', lambda e: e.matmul(sp_.t[0:Kd, 0:Vd], aK, aV, start=False, stop=True), [kbt, vt], [sp_])
                    stt(stap, stap, wc_ap(c), sp_.t[0:Kd, 0:Vd], ALU.mult, ALU.add, [st_b, WcB, sp_], [st_b])
                    yield

        def scan_sample(Kd, Vd, QsT, RsT, PbT, KbT, VT, arb, ark, WcB, wc16, stX):
            n = NS
            QR = NB('sQR', 36, slot=2); PK = NB('sPK', 36, slot=2); UV = NB('sUV', 36, slot=2)
            q3 = QR.t[0:Kd, 0:2 * n].rearrange("p (s two) -> p s two", two=2)
            p3 = PK.t[0:Kd, 0:2 * n].rearrange("p (s two) -> p s two", two=2)
            u3 = UV.t[0:Vd, 0:2 * n].rearrange("p (s two) -> p s two", two=2)
            act(q3[:, :, 0], QsT.t[0:Kd, 0:n], AF.Copy, [QsT], [QR])
            ts(q3[:, :, 1], RsT.t[0:Kd, 0:n], 1.0, None, ALU.mult, None, [RsT], [QR])
            act(p3[:, :, 0], PbT.t[0:Kd, 0:n], AF.Copy, [PbT], [PK])
            ts(p3[:, :, 1], KbT.t[0:Kd, 0:n], 1.0, None, ALU.mult, None, [KbT], [PK])
            ts(u3[:, :, 1], VT.t[0:Vd, 0:n], 1.0, None, ALU.mult, None, [VT], [UV])
            pq = PS()
            for s_ in range(n):
                op('pe', lambda e, s_=s_: e.matmul(pq.t[0:Vd, 2 * s_:2 * s_ + 2], stX.t[:, s_, :], QR.t[0:Kd, 2 * s_:2 * s_ + 2], start=True, stop=True),
                   [stX, QR], [pq], inc=(s_ == n - 1))
            pq3 = pq.t[0:Vd, 0:2 * n].rearrange("p (s two) -> p s two", two=2)
            act(u3[:, :, 0], pq3[:, :, 0], AF.Copy, [pq], [UV])
            Y = HB(); t2 = HB()
            tt(Y.t[0:Vd, 0:n], pq3[:, :, 0], arb.t[0:Vd, 0:n], ALU.mult, [pq, arb], [Y])
            tt(t2.t[0:Vd, 0:n], VT.t[0:Vd, 0:n], ark.t[0:Vd, 0:n], ALU.mult, [VT, ark], [t2])
            tt(Y.t[0:Vd, 0:n], Y.t[0:Vd, 0:n], t2.t[0:Vd, 0:n], ALU.add, [Y, t2], [Y])
            tt(Y.t[0:Vd, 0:n], Y.t[0:Vd, 0:n], pq3[:, :, 1], ALU.add, [Y, pq], [Y])
            gs = 512 // max(Kd, Vd)
            pkr = NB('sPKr', 512, slot=2); uvr = NB('sUVr', 512, slot=2)
            for g0 in range(0, n, gs):
                pt = PS()
                for j in range(gs):
                    s_ = g0 + j
                    tr(pt, pt.t[0:2, j * Kd:(j + 1) * Kd], PK.t[0:Kd, 2 * s_:2 * s_ + 2], Kd, [PK], inc=(j == gs - 1))
                act(pkr.t[0:2, 0:gs * Kd], pt.t[0:2, 0:gs * Kd], AF.Copy, [pt], [pkr])
                pu = PS()
                for j in range(gs):
                    s_ = g0 + j
                    tr(pu, pu.t[0:2, j * Vd:(j + 1) * Vd], UV.t[0:Vd, 2 * s_:2 * s_ + 2], Vd, [UV], inc=(j == gs - 1))
                ts(uvr.t[0:2, 0:gs * Vd], pu.t[0:2, 0:gs * Vd], 1.0, None, ALU.mult, None, [pu], [uvr])
                pp = PS()
                for j in range(gs):
                    op('pe', lambda e, j=j: e.matmul(pp.t[0:Kd, j * Vd:(j + 1) * Vd], pkr.t[0:2, j * Kd:(j + 1) * Kd], uvr.t[0:2, j * Vd:(j + 1) * Vd], start=True, stop=True),
                       [pkr, uvr], [pp], inc=(j == gs - 1))
                sv = stX.t[:, g0:g0 + gs, :]
                tt(sv, sv, wc16[:, g0:g0 + gs].unsqueeze(2).broadcast_to([Kd, gs, Vd]), ALU.mult, [stX, WcB], [stX])
                tt(sv, sv, pp.t[0:Kd, 0:gs * Vd].rearrange("p (s v) -> p s v", v=Vd), ALU.add, [stX, pp], [stX])
            return Y

        def scan2(slot, nh, Kd, Vd, N, C, QsT, RsT, PbT, KbT, VT, A, WcB, wc_ap, st_b, ypb):
            nch = N // C
            nb = nh * nch
            W = nb * C

            def tmaj(srcb, name):
                dst = NB16(name, 516, slot=slot)
                pb = PSslot(slot)
                for c in range(nch):
                    tr(pb, pb.t[0:C, c * 128:(c + 1) * 128], srcb.t[0:128, c * C:(c + 1) * C], 128, [srcb], inc=(c == nch - 1))
                act(dst.t[0:C, 0:nch * 128], pb.t[0:C, 0:nch * 128], AF.Copy, [pb], [dst])
                return dst
            pbt = tmaj(PbT, 'Pb'); kbt = tmaj(KbT, 'Kb'); vt = tmaj(VT, 'Vt')
            yield
            NTb = A['NT']; NT16 = A['NT16']
            Xb = NB16('X0', 516, slot=slot); pb = PSslot(slot)
            for b in range(nb):
                tr(pb, pb.t[0:C, b * C:(b + 1) * C], NTb.t[0:C, b * C:(b + 1) * C], C, [NTb], inc=(b == nb - 1))
            act(Xb.t[0:C, 0:W], pb.t[0:C, 0:W], AF.Copy, [pb], [Xb])
            XTb = NT16
            PT = NB('PT0', 516, slot=slot)
            tt(PT.t[0:C, 0:W].rearrange("p (c i) -> p c i", i=C), NTb.t[0:C, 0:W].rearrange("p (c i) -> p c i", i=C),
               cm.t[0:C, 0:C].unsqueeze(1).broadcast_to([C, nb, C]), ALU.add, [NTb, cm], [PT])
            PTh = NB16('PTh0', 516, slot=slot)
            act(PTh.t[0:C, 0:W], PT.t[0:C, 0:W], AF.Copy, [PT], [PTh])
            yield
            nlev = 5
            for lv in range(1, nlev + 1):
                pbx = PSslot(slot)
                for b in range(nb):
                    sl = slice(b * C, (b + 1) * C)
                    op('pe', lambda e, sl=sl, pbx=pbx, XTb=XTb, Xb=Xb: e.matmul(pbx.t[0:C, sl], XTb.t[0:C, sl], Xb.t[0:C, sl], start=True, stop=True), [XTb, Xb], [pbx], inc=(b == nb - 1))
                Xn = NB16('X%d' % (lv % 2), 516, slot=slot)
                XTn = None
                if lv < nlev:
                    pbt_ = PSslot(slot)
                    for b in range(nb):
                        sl = slice(b * C, (b + 1) * C)
                        op('pe', lambda e, sl=sl, pbt_=pbt_, XTb=XTb, Xb=Xb: e.matmul(pbt_.t[0:C, sl], Xb.t[0:C, sl], XTb.t[0:C, sl], start=True, stop=True), [XTb, Xb], [pbt_], inc=(b == nb - 1))
                act(Xn.t[0:C, 0:W], pbx.t[0:C, 0:W], AF.Copy, [pbx], [Xn])
                yield
                if lv < nlev:
                    XTn = NB16('XT%d' % (lv % 2), 516, slot=slot)
                    ts(XTn.t[0:C, 0:W], pbt_.t[0:C, 0:W], 1.0, None, ALU.mult, None, [pbt_], [XTn])
                pbp = PSslot(slot)
                for b in range(nb):
                    sl = slice(b * C, (b + 1) * C)
                    op('pe', lambda e, sl=sl, pbp=pbp, Xn=Xn, PTh=PTh: e.matmul(pbp.t[0:C, sl], Xn.t[0:C, sl], PTh.t[0:C, sl], start=True, stop=True), [Xn, PTh], [pbp], inc=(b == nb - 1))
                PTn = NB('PT%d' % (lv % 2), 516, slot=slot)
                tt(PTn.t[0:C, 0:W], PT.t[0:C, 0:W], pbp.t[0:C, 0:W], ALU.add, [PT, pbp], [PTn])
                PThn = NB16('PTh%d' % (lv % 2), 516, slot=slot)
                act(PThn.t[0:C, 0:W], PTn.t[0:C, 0:W], AF.Copy, [PTn], [PThn])
                PT = PTn; PTh = PThn
                yield
                Xb = Xn
                if lv < nlev:
                    XTb = XTn
            TinvT = PTh
            LakT = A['LakT']; ArbT = A['ArbT']; ArkT = A['ArkT']
            for c in range(nch):
                sl = slice(c * C, (c + 1) * C)
                zp = PSslot(slot)
                for hl in range(nh):
                    pr = slice(hl * Kd, (hl + 1) * Kd); vc = slice(hl * Vd, (hl + 1) * Vd)
                    bsl = slice((hl * nch + c) * C, (hl * nch + c + 1) * C)
                    aV = vt.t[0:C, c * 128 + hl * Vd:c * 128 + (hl + 1) * Vd]
                    op('pe', lambda e, pr=pr, vc=vc: e.matmul(zp.t[0:C, vc], QsT.t[pr, sl], st_b.t[pr, :], start=True, stop=False), [QsT, st_b], [zp], inc=False)
                    op('pe', lambda e, vc=vc, bsl=bsl, aV=aV: e.matmul(zp.t[0:C, vc], LakT.t[0:C, bsl], aV, start=False, stop=True), [LakT, vt], [zp], inc=(hl == nh - 1))
                yield
                Zs = HB16(slot)
                act(Zs.t[0:C, 0:128], zp.t[0:C, 0:128], AF.Copy, [zp], [Zs])
                up = PSslot(slot)
                for hl in range(nh):
                    vc = slice(hl * Vd, (hl + 1) * Vd)
                    bsl = slice((hl * nch + c) * C, (hl * nch + c + 1) * C)
                    op('pe', lambda e, vc=vc, bsl=bsl: e.matmul(up.t[0:C, vc], TinvT.t[0:C, bsl], Zs.t[0:C, vc], start=True, stop=True), [TinvT, Zs], [up], inc=(hl == nh - 1))
                yield
                U = HB16(slot)
                ts(U.t[0:C, 0:128], up.t[0:C, 0:128], 1.0, None, ALU.mult, None, [up], [U])
                for hl in range(nh):
                    pr = slice(hl * Kd, (hl + 1) * Kd); vc = slice(hl * Vd, (hl + 1) * Vd)
                    bsl = slice((hl * nch + c) * C, (hl * nch + c + 1) * C)
                    aV = vt.t[0:C, c * 128 + hl * Vd:c * 128 + (hl + 1) * Vd]
                    op('pe', lambda e, pr=pr, vc=vc: e.matmul(ypb.t[vc, sl], st_b.t[pr, :], RsT.t[pr, sl], start=True, stop=False), [st_b, RsT], [ypb], inc=False)
                    op('pe', lambda e, vc=vc, bsl=bsl: e.matmul(ypb.t[vc, sl], U.t[0:C, vc], ArbT.t[0:C, bsl], start=False, stop=False), [U, ArbT], [ypb], inc=False)
                    op('pe', lambda e, vc=vc, bsl=bsl, aV=aV: e.matmul(ypb.t[vc, sl], aV, ArkT.t[0:C, bsl], start=False, stop=True), [vt, ArkT], [ypb], inc=(hl == nh - 1))
                yield
                sp_ = PSslot(slot)
                for hl in range(nh):
                    pr = slice(hl * Kd, (hl + 1) * Kd); vc = slice(hl * Vd, (hl + 1) * Vd)
                    aP = pbt.t[0:C, c * 128 + hl * Kd:c * 128 + (hl + 1) * Kd]
                    aK = kbt.t[0:C, c * 128 + hl * Kd:c * 128 + (hl + 1) * Kd]
                    aV = vt.t[0:C, c * 128 + hl * Vd:c * 128 + (hl + 1) * Vd]
                    op('pe', lambda e, pr=pr, vc=vc, aP=aP: e.matmul(sp_.t[pr, 0:Vd], aP, U.t[0:C, vc], start=True, stop=False), [pbt, U], [sp_], inc=False)
                    op('pe', lambda e, pr=pr, aK=aK, aV=aV: e.matmul(sp_.t[pr, 0:Vd], aK, aV, start=False, stop=True), [kbt, vt], [sp_], inc=(hl == nh - 1))
                stt(st_b.t[:, :], st_b.t[:, :], wc_ap(c), sp_.t[0:128, 0:Vd], ALU.mult, ALU.add, [st_b, WcB, sp_], [st_b])
                yield

        def scan_sample2(nh, Kd, Vd, QsT, RsT, PbT, KbT, VT, arb, ark, WcB, wc16, stX):
            n = NS
            QR = NB('sQR', 36, slot=2); PK = NB('sPK', 36, slot=2); UV = NB('sUV', 36, slot=2)
            q3 = QR.t[:, 0:2 * n].rearrange("p (s two) -> p s two", two=2)
            p3 = PK.t[:, 0:2 * n].rearrange("p (s two) -> p s two", two=2)
            u3 = UV.t[:, 0:2 * n].rearrange("p (s two) -> p s two", two=2)
            act(q3[:, :, 0], QsT.t[:, 0:n], AF.Copy, [QsT], [QR])
            ts(q3[:, :, 1], RsT.t[:, 0:n], 1.0, None, ALU.mult, None, [RsT], [QR])
            act(p3[:, :, 0], PbT.t[:, 0:n], AF.Copy, [PbT], [PK])
            ts(p3[:, :, 1], KbT.t[:, 0:n], 1.0, None, ALU.mult, None, [KbT], [PK])
            ts(u3[:, :, 1], VT.t[:, 0:n], 1.0, None, ALU.mult, None, [VT], [UV])
            pq = PS()
            for s_ in range(n):
                for hl in range(nh):
                    pr = slice(hl * Kd, (hl + 1) * Kd); vc = slice(hl * Vd, (hl + 1) * Vd)
                    op('pe', lambda e, s_=s_, pr=pr, vc=vc: e.matmul(pq.t[vc, 2 * s_:2 * s_ + 2], stX.t[pr, s_, :], QR.t[pr, 2 * s_:2 * s_ + 2], start=True, stop=True),
                       [stX, QR], [pq], inc=(s_ == n - 1 and hl == nh - 1))
            pq3 = pq.t[:, 0:2 * n].rearrange("p (s two) -> p s two", two=2)
            act(u3[:, :, 0], pq3[:, :, 0], AF.Copy, [pq], [UV])
            Y = HB(); t2 = HB()
            tt(Y.t[:, 0:n], pq3[:, :, 0], arb.t[:, 0:n], ALU.mult, [pq, arb], [Y])
            tt(t2.t[:, 0:n], VT.t[:, 0:n], ark.t[:, 0:n], ALU.mult, [VT, ark], [t2])
            tt(Y.t[:, 0:n], Y.t[:, 0:n], t2.t[:, 0:n], ALU.add, [Y, t2], [Y])
            tt(Y.t[:, 0:n], Y.t[:, 0:n], pq3[:, :, 1], ALU.add, [Y, pq], [Y])
            gs = 4
            pkr = NB('sPKr', 512, slot=2); uvr = NB('sUVr', 512, slot=2)
            for g0 in range(0, n, gs):
                pt = PS()
                for j in range(gs):
                    s_ = g0 + j
                    tr(pt, pt.t[0:2, j * 128:(j + 1) * 128], PK.t[:, 2 * s_:2 * s_ + 2], 128, [PK], inc=(j == gs - 1))
                act(pkr.t[0:2, 0:gs * 128], pt.t[0:2, 0:gs * 128], AF.Copy, [pt], [pkr])
                pu = PS()
                for j in range(gs):
                    s_ = g0 + j
                    tr(pu, pu.t[0:2, j * 128:(j + 1) * 128], UV.t[:, 2 * s_:2 * s_ + 2], 128, [UV], inc=(j == gs - 1))
                ts(uvr.t[0:2, 0:gs * 128], pu.t[0:2, 0:gs * 128], 1.0, None, ALU.mult, None, [pu], [uvr])
                pp = PS()
                for j in range(gs):
                    for hl in range(nh):
                        pr = slice(hl * Kd, (hl + 1) * Kd)
                        op('pe', lambda e, j=j, hl=hl, pr=pr: e.matmul(pp.t[pr, j * Vd:(j + 1) * Vd], pkr.t[0:2, j * 128 + hl * Kd:j * 128 + (hl + 1) * Kd],
                                                                    uvr.t[0:2, j * 128 + hl * Vd:j * 128 + (hl + 1) * Vd], start=True, stop=True),
                           [pkr, uvr], [pp], inc=(j == gs - 1 and hl == nh - 1))
                sv = stX.t[:, g0:g0 + gs, :]
                tt(sv, sv, wc16[:, g0:g0 + gs].unsqueeze(2).broadcast_to([128, gs, Vd]), ALU.mult, [stX, WcB], [stX])
                tt(sv, sv, pp.t[:, 0:gs * Vd].rearrange("p (s v) -> p s v", v=Vd), ALU.add, [stX, pp], [stX])
            return Y

        def scan_pair(slot, N, C, QsT, RsT, Pb16, Kb16, V16, A, WcB, wc_ap, st_b, ypb):
            nch = N // C
            W = nch * C
            H2 = (slice(0, 64), slice(64, 128))

            def tmaj(src16, name):
                dst = NB16(name, slot=slot)
                pb = PSslot(slot)
                for hl in range(2):
                    pr = H2[hl]
                    for c in range(nch):
                        sl = slice(c * C, (c + 1) * C)
                        op('pe', lambda e, pr=pr, sl=sl: e.matmul(pb.t[pr, sl], src16.t[pr, sl], id16[pr, pr], start=True, stop=True), [src16, cm16], [pb],
                           inc=(hl == 1 and c == nch - 1))
                act(dst.t[:, 0:W], pb.t[:, 0:W], AF.Copy, [pb], [dst])
                return dst
            pbt = tmaj(Pb16, 'Pb'); kbt = tmaj(Kb16, 'Kb'); vt = tmaj(V16, 'Vt')
            yield
            NTb = A['NT']; XTb = A['NT16']; Xb = A['Nn16']
            PT = NB('PT0', slot=slot)
            tt(PT.t[:, 0:W].rearrange("p (c i) -> p c i", i=C), NTb.t[:, 0:W].rearrange("p (c i) -> p c i", i=C),
               ident2.unsqueeze(1).broadcast_to([128, nch, C]), ALU.add, [NTb, cm], [PT])
            PTh = NB16('PTh0', slot=slot)
            act(PTh.t[:, 0:W], PT.t[:, 0:W], AF.Copy, [PT], [PTh])
            yield
            nlev = 5

            def mm8(pbo, L_, R_):
                for hl in range(2):
                    pr = H2[hl]
                    for c in range(nch):
                        sl = slice(c * C, (c + 1) * C)
                        op('pe', lambda e, pr=pr, sl=sl: e.matmul(pbo.t[pr, sl], L_.t[pr, sl], R_.t[pr, sl], start=True, stop=True), [L_, R_], [pbo],
                           inc=(hl == 1 and c == nch - 1))
            for lv in range(1, nlev + 1):
                pbx = PSslot(slot)
                mm8(pbx, XTb, Xb)
                Xn = NB16('X%d' % (lv % 2), slot=slot)
                XTn = None
                if lv < nlev:
                    pbt_ = PSslot(slot)
                    mm8(pbt_, Xb, XTb)
                act(Xn.t[:, 0:W], pbx.t[:, 0:W], AF.Copy, [pbx], [Xn])
                yield
                if lv < nlev:
                    XTn = NB16('XT%d' % (lv % 2), slot=slot)
                    ts(XTn.t[:, 0:W], pbt_.t[:, 0:W], 1.0, None, ALU.mult, None, [pbt_], [XTn])
                pbp = PSslot(slot)
                mm8(pbp, Xn, PTh)
                PTn = NB('PT%d' % (lv % 2), slot=slot)
                tt(PTn.t[:, 0:W], PT.t[:, 0:W], pbp.t[:, 0:W], ALU.add, [PT, pbp], [PTn])
                PThn = NB16('PTh%d' % (lv % 2), slot=slot)
                act(PThn.t[:, 0:W], PTn.t[:, 0:W], AF.Copy, [PTn], [PThn])
                PT = PTn; PTh = PThn
                yield
                Xb = Xn
                if lv < nlev:
                    XTb = XTn
            TinvT = PTh
            LakT = A['LakT']; ArbT = A['ArbT']; ArkT = A['ArkT']
            for c in range(nch):
                sl = slice(c * C, (c + 1) * C)
                zp = PSslot(slot)
                for hl in range(2):
                    pr = H2[hl]
                    op('pe', lambda e, pr=pr: e.matmul(zp.t[pr, 0:64], QsT.t[pr, sl], st_b.t[pr, :], start=True, stop=False), [QsT, st_b], [zp], inc=False)
                    op('pe', lambda e, pr=pr: e.matmul(zp.t[pr, 0:64], LakT.t[pr, sl], vt.t[pr, sl], start=False, stop=True), [LakT, vt], [zp], inc=(hl == 1))
                yield
                Zs = HB16(slot)
                act(Zs.t[:, 0:64], zp.t[:, 0:64], AF.Copy, [zp], [Zs])
                up = PSslot(slot)
                for hl in range(2):
                    pr = H2[hl]
                    op('pe', lambda e, pr=pr: e.matmul(up.t[pr, 0:64], TinvT.t[pr, sl], Zs.t[pr, 0:64], start=True, stop=True), [TinvT, Zs], [up], inc=(hl == 1))
                yield
                U = HB16(slot)
                ts(U.t[:, 0:64], up.t[:, 0:64], 1.0, None, ALU.mult, None, [up], [U])
                for hl in range(2):
                    pr = H2[hl]
                    op('pe', lambda e, pr=pr: e.matmul(ypb.t[pr, sl], st_b.t[pr, :], RsT.t[pr, sl], start=True, stop=False), [st_b, RsT], [ypb], inc=False)
                    op('pe', lambda e, pr=pr: e.matmul(ypb.t[pr, sl], U.t[pr, 0:64], ArbT.t[pr, sl], start=False, stop=False), [U, ArbT], [ypb], inc=False)
                    op('pe', lambda e, pr=pr: e.matmul(ypb.t[pr, sl], vt.t[pr, sl], ArkT.t[pr, sl], start=False, stop=True), [vt, ArkT], [ypb], inc=(hl == 1))
                yield
                sp_ = PSslot(slot)
                for hl in range(2):
                    pr = H2[hl]
                    op('pe', lambda e, pr=pr: e.matmul(sp_.t[pr, 0:64], pbt.t[pr, sl], U.t[pr, 0:64], start=True, stop=False), [pbt, U], [sp_], inc=False)
                    op('pe', lambda e, pr=pr: e.matmul(sp_.t[pr, 0:64], kbt.t[pr, sl], vt.t[pr, sl], start=False, stop=True), [kbt, vt], [sp_], inc=(hl == 1))
                stt(st_b.t[:, :], st_b.t[:, :], wc_ap(c), sp_.t[:, 0:64], ALU.mult, ALU.add, [st_b, WcB, sp_], [st_b])
                yield

        def scan_sample_pair(QsT, RsT, PbT, KbT, VT, arb, ark, WcB, wc16, stX):
            n = NS
            H2 = (slice(0, 64), slice(64, 128))
            QR = NB('sQR', 36, slot=2); PK = NB('sPK', 36, slot=2); UV = NB('sUV', 36, slot=2)
            q3 = QR.t[:, 0:2 * n].rearrange("p (s two) -> p s two", two=2)
            p3 = PK.t[:, 0:2 * n].rearrange("p (s two) -> p s two", two=2)
            u3 = UV.t[:, 0:2 * n].rearrange("p (s two) -> p s two", two=2)
            act(q3[:, :, 0], QsT.t[:, 0:n], AF.Copy, [QsT], [QR])
            ts(q3[:, :, 1], RsT.t[:, 0:n], 1.0, None, ALU.mult, None, [RsT], [QR])
            act(p3[:, :, 0], PbT.t[:, 0:n], AF.Copy, [PbT], [PK])
            ts(p3[:, :, 1], KbT.t[:, 0:n], 1.0, None, ALU.mult, None, [KbT], [PK])
            ts(u3[:, :, 1], VT.t[:, 0:n], 1.0, None, ALU.mult, None, [VT], [UV])
            pq = PS()
            for s_ in range(n):
                for hl in range(2):
                    pr = H2[hl]
                    op('pe', lambda e, s_=s_, pr=pr: e.matmul(pq.t[pr, 2 * s_:2 * s_ + 2], stX.t[pr, s_, :], QR.t[pr, 2 * s_:2 * s_ + 2], start=True, stop=True),
                       [stX, QR], [pq], inc=(s_ == n - 1 and hl == 1))
            pq3 = pq.t[:, 0:2 * n].rearrange("p (s two) -> p s two", two=2)
            act(u3[:, :, 0], pq3[:, :, 0], AF.Copy, [pq], [UV])
            Y = HB(); t2 = HB()
            tt(Y.t[:, 0:n], pq3[:, :, 0], arb.t[:, 0:n], ALU.mult, [pq, arb], [Y])
            tt(t2.t[:, 0:n], VT.t[:, 0:n], ark.t[:, 0:n], ALU.mult, [VT, ark], [t2])
            tt(Y.t[:, 0:n], Y.t[:, 0:n], t2.t[:, 0:n], ALU.add, [Y, t2], [Y])
            tt(Y.t[:, 0:n], Y.t[:, 0:n], pq3[:, :, 1], ALU.add, [Y, pq], [Y])
            gs = 8
            pkr = NB('sPKr', 512, slot=2); uvr = NB('sUVr', 512, slot=2)
            for g0 in range(0, n, gs):
                pt = PS(); pu = PS()
                for j in range(gs):
                    s_ = g0 + j
                    for hl in range(2):
                        pr = H2[hl]; p2 = slice(hl * 64, hl * 64 + 2)
                        op('pe', lambda e, j=j, s_=s_, pr=pr, p2=p2: e.matmul(pt.t[p2, j * 64:(j + 1) * 64], PK.t[pr, 2 * s_:2 * s_ + 2], ident[pr, pr], start=True, stop=True),
                           [PK, cm], [pt], inc=(j == gs - 1 and hl == 1))
                for j in range(gs):
                    s_ = g0 + j
                    for hl in range(2):
                        pr = H2[hl]; p2 = slice(hl * 64, hl * 64 + 2)
                        op('pe', lambda e, j=j, s_=s_, pr=pr, p2=p2: e.matmul(pu.t[p2, j * 64:(j + 1) * 64], UV.t[pr, 2 * s_:2 * s_ + 2], ident[pr, pr], start=True, stop=True),
                           [UV, cm], [pu], inc=(j == gs - 1 and hl == 1))
                for hl in range(2):
                    p2 = slice(hl * 64, hl * 64 + 2)
                    act(pkr.t[p2, 0:gs * 64], pt.t[p2, 0:gs * 64], AF.Copy, [pt], [pkr])
                    ts(uvr.t[p2, 0:gs * 64], pu.t[p2, 0:gs * 64], 1.0, None, ALU.mult, None, [pu], [uvr])
                pp = PS()
                for j in range(gs):
                    for hl in range(2):
                        pr = H2[hl]; p2 = slice(hl * 64, hl * 64 + 2)
                        op('pe', lambda e, j=j, pr=pr, p2=p2: e.matmul(pp.t[pr, j * 64:(j + 1) * 64], pkr.t[p2, j * 64:(j + 1) * 64], uvr.t[p2, j * 64:(j + 1) * 64], start=True, stop=True),
                           [pkr, uvr], [pp], inc=(j == gs - 1 and hl == 1))
                sv = stX.t[:, g0:g0 + gs, :]
                tt(sv, sv, wc16[:, g0:g0 + gs].unsqueeze(2).broadcast_to([128, gs, 64]), ALU.mult, [stX, WcB], [stX])
                tt(sv, sv, pp.t[:, 0:gs * 64].rearrange("p (s v) -> p s v", v=64), ALU.add, [stX, pp], [stX])
            return Y

        def load_inputs(i, is_s, t0):
            rows_list = [NS] if is_s else [128] * TCN
            xsrc = xS if is_s else xP
            psrc = pS if is_s else pP
            for tc, rows in enumerate(rows_list):
                kb.dma('sp', xtoks[i].t[:rows, tc, :], xsrc[t0 + tc * 128:t0 + tc * 128 + rows, :], writes=[xtoks[i]])
                kb.dma('sp', ptoks[i].t[:rows, tc, :], psrc[t0 + tc * 128:t0 + tc * 128 + rows, :], writes=[ptoks[i]])

        def do_tile(is_s, t0, nxt=None):
            cur = tix[0] % 2
            tix[0] += 1
            xtok = xtoks[cur]; ptok = ptoks[cur]
            pfx = 's_' if is_s else ''
            N = NS if is_s else NP
            C = 1 if is_s else 64
            nseq = NS if is_s else 1
            nch = N // C
            rows_list = [NS] if is_s else [128] * TCN
            xsrc = xS if is_s else xP
            psrc = pS if is_s else pP
            ydst = yS if is_s else yP
            last = is_s or (t0 + NP >= TP)
            if not preloaded.pop((is_s, t0), False):
                load_inputs(cur, is_s, t0)
            CKP(pfx + 'load')
            norm_T(N, rows_list, GMIX, xtok)
            CKP(pfx + 'norm')
            if is_s:
                kb.dma('sp', tokS.t[:], stShift, writes=[tokS])
                for g in range(15):
                    pb = PS()
                    if g < 12:
                        col = (g // 4) * 512 + (g % 4) * 128; P_ = 128
                    elif g == 12:
                        col = 1536; P_ = 64
                    elif g == 13:
                        col = 1600; P_ = 64
                    else:
                        col = 1664; P_ = 128
                    tr(pb, pb.t[0:P_, 0:NS], tokS.t[0:NS, col:col + P_], NS, [tokS])
                    act(prevS.t[0:P_, g, :], pb.t[0:P_, 0:NS], AF.Copy, [pb], [prevS])
                for j in range(3):
                    kb.dma('sp', tokC.t[:], stConv[:, j, :], writes=[tokC])
                    for g in range(12):
                        pb = PS()
                        tr(pb, pb.t[0:128, 0:NS], tokC.t[0:NS, g * 128:(g + 1) * 128], NS, [tokC])
                        act(histS.t[:, j * 12 + g, :], pb.t[:, 0:NS], AF.Copy, [pb], [histS])
            wt = wload(w_in, 0, 8, 128, 1536, 256)
            pb = PS(); proj_fm(pb, pb.t[0:64, :N], wt, 0, 64, N)
            xw = HB()
            shift(64, N, pb, pb.t[0:64, :N], c64b.t[:, MU_XW:MU_XW + 1], omu64.t[:, 24:25], (car_xw, car_xw.t[:, 0:1]), xw, xw.t[0:64, :N], is_s,
                  prev_ap=prevS.t[0:64, 12, :], raw_ap=rawS.t[0:64, 12, :], raw_b=rawS)
            act(tw.t[:, :N], xw.t[0:64, :N], AF.Tanh, [xw], [tw])
            pb = PS(); proj_fm(pb, pb.t[0:64, :N], wt, 64, 64, N)
            shift(64, N, pb, pb.t[0:64, :N], c64b.t[:, MU_XAA:MU_XAA + 1], omu64.t[:, 25:26], (car_xw, car_xw.t[:, 1:2]), xaaS, xaaS.t[:, :N], is_s,
                  prev_ap=prevS.t[0:64, 13, :], raw_ap=rawS.t[0:64, 13, :], raw_b=rawS)
            pb = PS(); proj_fm(pb, pb.t[:, :N], wt, 128, 128, N)
            xg = HB()
            shift(128, N, pb, pb.t[:, :N], c128b.t[:, MU_XG:MU_XG + 1], omu128.t[:, 0:1], (car_xg, car_xg.t[:, 0:1]), xg, xg.t[:, :N], is_s,
                  prev_ap=prevS.t[:, 14, :], raw_ap=rawS.t[:, 14, :], raw_b=rawS)
            act(sgS.t[:, :N], xg.t[:, :N], AF.Sigmoid, [xg], [sgS])

            pend = []
            if (not is_s) and t0 == 0:
                pend = [lambda: convert('bra', w_bra, 512, D), lambda: convert('brb', w_brb, 512, D),
                        lambda: convert('gate', w_in[:, 3848:5896], D, 2048), lambda: convert('out', w_out, D, D),
                        lambda: convert('fg', w_fg, D, DFF), lambda: convert('fu', w_fu, D, DFF), lambda: convert('fd', w_fd, DFF, D),
                        lambda: convert('pg', w_pg, D, D), lambda: convert('pp', w_pp, 256, D)]
            CKP(pfx + 'lora')
            def rwkv_gen(hp, slot):
                T = lambda: HB(slot)
                PSs = lambda: PSslot(slot)
                wt1 = wload(w_in, 0, 8, 128, hp * 384, 256)
                wt2 = wload(w_in, 0, 8, 128, hp * 384 + 256, 128)
                rkv = []
                for j, nm_ in enumerate(('r', 'k', 'v')):
                    g = hp * 3 + j; gc_ = j * 4 + hp
                    pb = PSs()
                    if j < 2:
                        proj_fm(pb, pb.t[:, :N], wt1, j * 128, 128, N)
                    else:
                        proj_fm(pb, pb.t[:, :N], wt2, 0, 128, N)
                    o_ = NB(nm_, slot=slot)
                    shift(128, N, pb, pb.t[:, :N], c128r.t[:, g:g + 1], omu128r.t[:, g:g + 1], (car_rkv, car_rkv.t[:, gc_:gc_ + 1]), o_, o_.t[:, :N], is_s,
                          prev_ap=prevS.t[:, gc_, :], raw_ap=rawS.t[:, gc_, :], raw_b=rawS, tmp=T)
                    rkv.append(o_)
                r_, k_, v_ = rkv
                hs = slice(hp * 128, (hp + 1) * 128)
                cc_ = lambda base: c128r.t[:, base + hp:base + hp + 1]
                yield
                pb = PSs()
                mm(pb, pb.t[:, :N], w2b.t[:, hs], tw.t[:, :N], [w2b, tw], True, True)
                sg = T(); act(sg.t[:, :N], pb.t[:, :N], AF.Sigmoid, [pb, c128r], [sg], bias=cc_(W0r))
                yield
                cws = T()
                if C > 1:
                    op('dve', lambda e: e.tensor_tensor_scan(out=cws.t[:, :N], data0=scanm[:, :N], data1=sg.t[:, :N], initial=0.0, op0=ALU.mult, op1=ALU.add), [sg, cm], [cws])
                else:
                    ts(cws.t[:, :N], sg.t[:, :N], 1.0, None, ALU.mult, None, [sg], [cws])
                yield
                ew = NB('ew', slot=slot); act(ew.t[:, :N], cws.t[:, :N], AF.Exp, [cws], [ew], scale=-C0)
                ewi = NB('ewi', slot=slot); act(ewi.t[:, :N], cws.t[:, :N], AF.Exp, [cws], [ewi], scale=C0)
                yield
                cx = T(); tt(cx.t[:, :N], cws.t[:, :N], sg.t[:, :N], ALU.subtract, [cws, sg], [cx])
                ewx = NB('ewx', slot=slot); act(ewx.t[:, :N], cx.t[:, :N], AF.Exp, [cx], [ewx], scale=-C0)
                yield
                dC = T()
                cw3 = cws.t[:, :N].rearrange("p (c i) -> p c i", i=C)
                tt(dC.t[:, :N].rearrange("p (c i) -> p c i", i=C), cw3[:, :, C - 1:C].broadcast_to([128, nch, C]), cw3, ALU.subtract, [cws], [dC])
                ewC = NB('ewC', slot=slot); act(ewC.t[:, :N], dC.t[:, :N], AF.Exp, [dC], [ewC], scale=-C0)
                yield
                pb = PSs()
                mm(pb, pb.t[:, :N], a2b.t[:, hs], xaaS.t[:, :N], [a2b, xaaS], True, True)
                a_ = NB('a', slot=slot); act(a_.t[:, :N], pb.t[:, :N], AF.Sigmoid, [pb, c128r], [a_], bias=cc_(A0r))
                yield
                pb = PSs()
                mm(pb, pb.t[:, :N], g2b.t[:, hs], sgS.t[:, :N], [g2b, sgS], True, True)
                g_ = NB('g', slot=slot); act(g_.t[:, :N], pb.t[:, :N], AF.Copy, [pb], [g_])
                yield
                kks = T(); ts(kks.t[:, :N], k_.t[:, :N], cc_(KKr), None, ALU.mult, None, [k_, c128r], [kks])
                sq = HBW(slot); act(sq.t[:, :N], kks.t[:, :N], AF.Square, [kks], [sq])
                yield
                pb = PSs()
                mm(pb, pb.t[:, :N], bones16, sq.t[:, :N], [on16, sq], True, True)
                rs = T(); rsqrt(rs.t[:, :N], pb.t[:, :N], 1.0, 1e-6, [pb], [rs])
                yield
                kk = NB('kk', slot=slot); tt(kk.t[:, :N], kks.t[:, :N], rs.t[:, :N], ALU.mult, [kks, rs], [kk])
                t1 = T(); ts(t1.t[:, :N], a_.t[:, :N], -1.0, cc_(KAr), ALU.add, ALU.mult, [a_, c128r], [t1])
                yield
                km = NB('km', slot=slot); stt(km.t[:, :N], t1.t[:, :N], 1.0, k_.t[:, :N], ALU.add, ALU.mult, [t1, k_], [km])
                kka = NB('kka', slot=slot); tt(kka.t[:, :N], kk.t[:, :N], a_.t[:, :N], ALU.mult, [kk, a_], [kka])
                yield
                QsT = NB('QsT', slot=slot); stt(QsT.t[:, :N], kk.t[:, :N], -1.0, ewx.t[:, :N], ALU.mult, ALU.mult, [kk, ewx], [QsT])
                RsT = NB('RsT', slot=slot); tt(RsT.t[:, :N], r_.t[:, :N], ew.t[:, :N], ALU.mult, [r_, ew], [RsT])
                yield
                PnT = NB16('PnT', slot=slot); tt(PnT.t[:, :N], kka.t[:, :N], ewi.t[:, :N], ALU.mult, [kka, ewi], [PnT])
                KnT = NB16('KnT', slot=slot); tt(KnT.t[:, :N], km.t[:, :N], ewi.t[:, :N], ALU.mult, [km, ewi], [KnT])
                yield
                PbT = NB('PbT', slot=slot); tt(PbT.t[:, :N], kka.t[:, :N], ewC.t[:, :N], ALU.mult, [kka, ewC], [PbT])
                KbT = NB('KbT', slot=slot); tt(KbT.t[:, :N], km.t[:, :N], ewC.t[:, :N], ALU.mult, [km, ewC], [KbT])
                yield
                rk = HBW(slot); stt(rk.t[:, :N], r_.t[:, :N], cc_(RKr), km.t[:, :N], ALU.mult, ALU.mult, [r_, c128r, km], [rk])
                pb = PSs()
                mm(pb, pb.t[:, :N], bones16, rk.t[:, :N], [on16, rk], True, True)
                bon = NB('bon', slot=slot); tt(bon.t[:, :N], pb.t[:, :N], v_.t[:, :N], ALU.mult, [pb, v_], [bon])
                yield
                if is_s:
                    for hl_ in range(2):
                        kb.dma('sp', natS.t[0:64, :, hl_, :], stWkv[:, 2 * hp + hl_].rearrange("s v k -> v s k"), writes=[natS])
                    for s0 in range(0, NS, 4):
                        pb = PSs()
                        for s_ in range(s0, s0 + 4):
                            tr(pb, pb.t[:, (s_ - s0) * 64:(s_ - s0 + 1) * 64], natS.t[0:64, s_, :, :], 64, [natS], inc=(s_ == s0 + 3))
                        act(stRs.t[:, s0:s0 + 4, :], pb.t[:, 0:256].rearrange("k (s v) -> k s v", v=64), AF.Copy, [pb], [stRs])
                    stb = stRs
                    pr1 = T(); tt(pr1.t[:, :N], PnT.t[:, :N], RsT.t[:, :N], ALU.mult, [PnT, RsT], [pr1])
                    pr2 = T(); tt(pr2.t[:, :N], KnT.t[:, :N], RsT.t[:, :N], ALU.mult, [KnT, RsT], [pr2])
                    pb = PSs(); mm(pb, pb.t[:, :N], bones, pr1.t[:, :N], [cm2, pr1], True, True)
                    arb = T(); act(arb.t[:, :N], pb.t[:, :N], AF.Copy, [pb], [arb])
                    pb = PSs(); mm(pb, pb.t[:, :N], bones, pr2.t[:, :N], [cm2, pr2], True, True)
                    ark = T(); act(ark.t[:, :N], pb.t[:, :N], AF.Copy, [pb], [ark])
                    y = scan_sample_pair(QsT, RsT, PbT, KbT, v_, arb, ark, ew, ew.t[:, 0:NS], stRs)
                else:
                    CKP(pfx + 'rprep')
                    A = {}
                    H2 = (slice(0, 64), slice(64, 128))

                    def amat(name, Lb, Rb, mask):
                        pb = PSs()
                        for hl in range(2):
                            pr = H2[hl]
                            for c in range(nch):
                                sl = slice(c * C, (c + 1) * C)
                                op('pe', lambda e, sl=sl, pr=pr: e.matmul(pb.t[pr, sl], Lb.t[pr, sl], Rb.t[pr, sl], start=True, stop=True), [Lb, Rb], [pb],
                                   inc=(hl == 1 and c == nch - 1))
                        o_ = NB(name, slot=slot) if name == 'NT' else NB16(name, slot=slot)
                        tt(o_.t[:, 0:N].rearrange("p (c i) -> p c i", i=C), pb.t[:, 0:N].rearrange("p (c i) -> p c i", i=C),
                           mask.unsqueeze(1).broadcast_to([128, nch, C]), ALU.mult, [pb, cm], [o_])
                        A[name] = o_
                        if name == 'NT':
                            n16 = NB16('NT16', slot=slot)
                            act(n16.t[:, 0:N], o_.t[:, 0:N], AF.Copy, [o_], [n16])
                            A['NT16'] = n16
                    Qs16 = NB16('Qs16', slot=slot); act(Qs16.t[:, :N], QsT.t[:, :N], AF.Copy, [QsT], [Qs16])
                    Rs16 = NB16('Rs16', slot=slot); ts(Rs16.t[:, :N], RsT.t[:, :N], 1.0, None, ALU.mult, None, [RsT], [Rs16])
                    yield
                    amat('NT', PnT, Qs16, maskS2)
                    yield
                    amat('Nn16', Qs16, PnT, maskL2)
                    yield
                    amat('LakT', KnT, Qs16, maskS2)
                    yield
                    amat('ArbT', PnT, Rs16, maskI2)
                    yield
                    amat('ArkT', KnT, Rs16, maskI2)
                    yield
                    Pb16 = NB16('Pb16', slot=slot); act(Pb16.t[:, :N], PbT.t[:, :N], AF.Copy, [PbT], [Pb16])
                    Kb16 = NB16('Kb16', slot=slot); ts(Kb16.t[:, :N], KbT.t[:, :N], 1.0, None, ALU.mult, None, [KbT], [Kb16])
                    V16 = NB16('V16', slot=slot); act(V16.t[:, :N], v_.t[:, :N], AF.Copy, [v_], [V16])
                    yield
                    CKP(pfx + 'ramat')
                    stb = stR[hp]
                    ypb = ps[6 + slot]
                    yield from scan_pair(slot, N, C, QsT, RsT, Pb16, Kb16, V16, A, ew, lambda c: ew.t[:, (c + 1) * C - 1:(c + 1) * C], stb, ypb)
                    CKP(pfx + 'rscan')
                    y = T(); act(y.t[:, :N], ypb.t[:, :N], AF.Copy, [ypb], [y])
                yield
                pb = PSs(); mm(pb, pb.t[:, :N], bones, y.t[:, :N], [cm2, y], True, True)
                d_ = T(); stt(d_.t[:, :N], pb.t[:, :N], -1.0 / 64, y.t[:, :N], ALU.mult, ALU.add, [pb, y], [d_])
                yield
                d2 = HBW(slot); act(d2.t[:, :N], d_.t[:, :N], AF.Square, [d_], [d2])
                pb = PSs(); mm(pb, pb.t[:, :N], bones16, d2.t[:, :N], [on16, d2], True, True)
                rs2 = T(); rsqrt(rs2.t[:, :N], pb.t[:, :N], 1.0 / 64, 64e-5, [pb], [rs2])
                yield
                yn = T(); tt(yn.t[:, :N], d_.t[:, :N], rs2.t[:, :N], ALU.mult, [d_, rs2], [yn])
                ts(yn.t[:, :N], yn.t[:, :N], cc_(LNWr), cc_(LNBr), ALU.mult, ALU.add, [yn, c128r], [yn])
                yield
                tt(yn.t[:, :N], yn.t[:, :N], bon.t[:, :N], ALU.add, [yn, bon], [yn])
                tt(oa.t[:, hp, :N], yn.t[:, :N], g_.t[:, :N], ALU.mult, [yn, g_], [oa])
                yield
                if last:
                    if is_s:
                        for s0 in range(0, NS, 4):
                            pb = PSs()
                            for s_ in range(s0, s0 + 4):
                                tr(pb, pb.t[0:64, (s_ - s0) * 128:(s_ - s0 + 1) * 128], stRs.t[:, s_, :], 128, [stRs], inc=(s_ == s0 + 3))
                            act(natS.t[0:64, s0:s0 + 4, :, :], pb.t[0:64, 0:512].rearrange("v (s h k) -> v s h k", h=2, k=64), AF.Copy, [pb], [natS])
                        for hl_ in range(2):
                            kb.dma('sp', oWkvS[:, 2 * hp + hl_].rearrange("s v k -> v s k"), natS.t[0:64, :, hl_, :], reads=[natS], is_out=True)
                    else:
                        pb = PSs()
                        tr(pb, pb.t[0:64, 0:128], stR[hp].t[:, :], 128, [stR[hp]])
                        stg_ = T()
                        act(stg_.t[0:64, 0:128], pb.t[0:64, 0:128], AF.Copy, [pb], [stg_])
                        kb.dma('sp', oWkvP[0, 2 * hp:2 * hp + 2].rearrange("h v k -> v h k"), stg_.t[0:64, 0:128].rearrange("v (h k) -> v h k", k=64), reads=[stg_], is_out=True)

            CKP(pfx + 'rwkv')
            def gdn_gen(h, slot):
                T = lambda: HB(slot)
                PSs = lambda: PSslot(slot)
                wtb = wload(w_barep, 0, 8, 128, h * 128, 128)
                wta = wload(w_barep, 0, 8, 128, 512 + h * 128, 128)
                pb = PSs(); proj_fm(pb, pb.t[:, :N], wtb, 0, 128, N)
                Bb = NB('a', slot=slot); act(Bb.t[:, :N], pb.t[:, :N], AF.Sigmoid, [pb], [Bb])
                pb = PSs(); proj_fm(pb, pb.t[:, :N], wta, 0, 128, N)
                e1 = T(); act(e1.t[:, :N], pb.t[:, :N], AF.Exp, [pb, c128b], [e1], bias=c128b.t[:, DTB + h:DTB + h + 1])
                act(e1.t[:, :N], e1.t[:, :N], AF.Ln, [e1], [e1], bias=1.0)
                gl = T(); ts(gl.t[:, :N], e1.t[:, :N], nexpA.t[:, h:h + 1], None, ALU.mult, None, [e1, nexpA], [gl])
                G = NB('ewi', slot=slot)
                if C > 1:
                    op('dve', lambda e, G=G, gl=gl: e.tensor_tensor_scan(out=G.t[:, :N], data0=scanm[:, :N], data1=gl.t[:, :N], initial=0.0, op0=ALU.mult, op1=ALU.add), [gl, cm], [G])
                else:
                    ts(G.t[:, :N], gl.t[:, :N], 1.0, None, ALU.mult, None, [gl], [G])
                wt1 = wload(w_in, 0, 8, 128, 1792 + h * 512, 256)
                wt2 = wload(w_in, 0, 8, 128, 1792 + h * 512 + 256, 256)
                cs = []
                for j, nm_ in enumerate(('r', 'k', 'v')):
                    g = j * 4 + h
                    pb = PSs()
                    if j < 2:
                        proj_fm(pb, pb.t[:, :N], wt1, j * 128, 128, N)
                    else:
                        proj_fm(pb, pb.t[:, :N], wt2, 0, 128, N)
                    acc = T()
                    cwc = lambda tap, g=g: cvw.t[:, tap * 12 + g:tap * 12 + g + 1]
                    if is_s:
                        act(rawC.t[:, g, :], pb.t[:, :N], AF.Copy, [pb], [rawC])
                        ts(acc.t[:, :N], histS.t[:, 0 * 12 + g, :], cwc(0), None, ALU.mult, None, [histS, cvw], [acc])
                        stt(acc.t[:, :N], histS.t[:, 1 * 12 + g, :], cwc(1), acc.t[:, :N], ALU.mult, ALU.add, [histS, cvw, acc], [acc])
                        stt(acc.t[:, :N], histS.t[:, 2 * 12 + g, :], cwc(2), acc.t[:, :N], ALU.mult, ALU.add, [histS, cvw, acc], [acc])
                        stt(acc.t[:, :N], pb.t[:, :N], cwc(3), acc.t[:, :N], ALU.mult, ALU.add, [pb, cvw, acc], [acc])
                    else:
                        raw = T()
                        act(raw.t[:, 3:N + 3], pb.t[:, :N], AF.Copy, [pb], [raw])
                        act(raw.t[:, 0:3], car_cv.t[:, g, 0:3], AF.Copy, [car_cv], [raw])
                        act(car_cv.t[:, g, 0:3], raw.t[:, N:N + 3], AF.Copy, [raw], [car_cv])
                        ts(acc.t[:, :N], raw.t[:, 0:N], cwc(0), None, ALU.mult, None, [raw, cvw], [acc])
                        for tap in range(1, 4):
                            stt(acc.t[:, :N], raw.t[:, tap:tap + N], cwc(tap), acc.t[:, :N], ALU.mult, ALU.add, [raw, cvw, acc], [acc])
                    c_ = NB(nm_, slot=slot); act(c_.t[:, :N], acc.t[:, :N], AF.Silu, [acc], [c_])
                    cs.append(c_)
                pbz = PSs(); proj_fm(pbz, pbz.t[:, :N], wt2, 128, 128, N)
                zz = NB('g', slot=slot); act(zz.t[:, :N], pbz.t[:, :N], AF.Silu, [pbz], [zz])
                cq, ck, cv_ = cs
                yield

                def l2n(src, scale, name):
                    sq = HBW(slot); act(sq.t[:, :N], src.t[:, :N], AF.Square, [src], [sq])
                    pb = PSs(); mm(pb, pb.t[:, :N], ones16, sq.t[:, :N], [on16, sq], True, True)
                    rs = T(); rsqrt(rs.t[:, :N], pb.t[:, :N], 1.0, 1e-6, [pb], [rs])
                    o_ = NB(name, slot=slot); stt(o_.t[:, :N], src.t[:, :N], scale, rs.t[:, :N], ALU.mult, ALU.mult, [src, rs], [o_])
                    return o_
                qn = l2n(cq, float(128 ** -0.5), 'kk'); kn = l2n(ck, 1.0, 'km')
                yield
                N2, C2, nch2 = N, C, nch
                yield
                eg = NB('ew', slot=slot); act(eg.t[:, :N2], G.t[:, :N2], AF.Exp, [G], [eg])
                yield
                QsT = NB('QsT', slot=slot); tt(QsT.t[:, :N2], kn.t[:, :N2], eg.t[:, :N2], ALU.mult, [kn, eg], [QsT])
                yield
                RsT = NB('RsT', slot=slot); tt(RsT.t[:, :N2], qn.t[:, :N2], eg.t[:, :N2], ALU.mult, [qn, eg], [RsT])
                yield
                dC = T()
                yield
                G3 = G.t[:, :N2].rearrange("p (c i) -> p c i", i=C2)
                yield
                tt(dC.t[:, :N2].rearrange("p (c i) -> p c i", i=C2), G3[:, :, C2 - 1:C2].broadcast_to([128, nch2, C2]), G3, ALU.subtract, [G], [dC])
                yield
                egC = T(); act(egC.t[:, :N2], dC.t[:, :N2], AF.Exp, [dC], [egC])
                yield
                bE = T(); tt(bE.t[:, :N2], Bb.t[:, :N2], egC.t[:, :N2], ALU.mult, [Bb, egC], [bE])
                yield
                KbT = NB('KbT', slot=slot); tt(KbT.t[:, :N2], kn.t[:, :N2], bE.t[:, :N2], ALU.mult, [kn, bE], [KbT])
                yield
                PbT = NB('PbT', slot=slot); ts(PbT.t[:, :N2], KbT.t[:, :N2], -1.0, None, ALU.mult, None, [KbT], [PbT])
                yield
                if is_s:
                    kb.dma('sp', stGs.t[:], stGdn[:, h].rearrange("s k v -> k s v"), writes=[stGs])
                    stb = stGs
                    kq = T(); tt(kq.t[:, :N], kn.t[:, :N], qn.t[:, :N], ALU.mult, [kn, qn], [kq])
                    pb = PSs(); mm(pb, pb.t[:, :N], ones, kq.t[:, :N], [cm, kq], True, True)
                    arb = T(); stt(arb.t[:, :N], pb.t[:, :N], -1.0, Bb.t[:, :N], ALU.mult, ALU.mult, [pb, Bb], [arb])
                    ark = T(); ts(ark.t[:, :N], arb.t[:, :N], -1.0, None, ALU.mult, None, [arb], [ark])
                    o_ = scan_sample2(1, 128, 128, QsT, RsT, PbT, KbT, cv_, arb, ark, eg, eg.t[:, 0:NS], stGs)
                else:
                    gcT = NB('ewx', slot=slot); nbT = NB('ewC', slot=slot)
                    for (src, dst, sc) in ((G, gcT, 1.0), (Bb, nbT, -1.0)):
                        pb = PSs()
                        for c in range(nch2):
                            tr(pb, pb.t[0:C2, c * 32:(c + 1) * 32], src.t[0:32, c * C2:(c + 1) * C2], 32, [src], inc=(c == nch2 - 1))
                        ts(dst.t[0:C2, 0:nch2], pb.t[0:C2, 0:nch2 * 32].rearrange("p (c i) -> p c i", i=32)[:, :, 0], sc, None, ALU.mult, None, [pb], [dst])
                    E = NB('kka', slot=slot)
                    E3 = E.t[0:C2, :N2].rearrange("p (c i) -> p c i", i=C2)
                    tt(E3, G.t[0:C2, :N2].rearrange("p (c i) -> p c i", i=C2), gcT.t[0:C2, 0:nch2].unsqueeze(2).broadcast_to([C2, nch2, C2]), ALU.subtract, [G, gcT], [E])
                    tt(E3, E3, negI[0:C2, 0:C].unsqueeze(1).broadcast_to([C2, nch2, C2]), ALU.add, [E, cm], [E])
                    act(E.t[0:C2, :N2], E.t[0:C2, :N2], AF.Exp, [E], [E])
                    nb3 = nbT.t[0:C2, 0:nch2].unsqueeze(2).broadcast_to([C2, nch2, C2])
                    A = {}

                    def gmat(Rb, strict, n1, n2):
                        pb = PSs()
                        for c in range(nch2):
                            sl = slice(c * C2, (c + 1) * C2)
                            op('pe', lambda e, sl=sl: e.matmul(pb.t[0:C2, sl], kn.t[:, sl], Rb.t[:, sl], start=True, stop=True), [kn, Rb], [pb], inc=(c == nch2 - 1))
                        o_ = NB('NT' if strict else 'A32', slot=slot); o3 = o_.t[0:C2, :N2].rearrange("p (c i) -> p c i", i=C2)
                        tt(o3, pb.t[0:C2, :N2].rearrange("p (c i) -> p c i", i=C2), E3, ALU.mult, [pb, E], [o_])
                        if strict:
                            tt(o3, o3, maskS[0:C2, 0:C].unsqueeze(1).broadcast_to([C2, nch2, C2]), ALU.mult, [o_, cm], [o_])
                        tt(o3, o3, nb3, ALU.mult, [o_, nbT], [o_])
                        p_ = NB16(n1, slot=slot); act(p_.t[0:C2, :N2], o_.t[0:C2, :N2], AF.Copy, [o_], [p_])
                        n_ = NB16(n2, slot=slot); ts(n_.t[0:C2, :N2], o_.t[0:C2, :N2], -1.0, None, ALU.mult, None, [o_], [n_])
                        return o_, p_, n_
                    A['NT'], A['NT16'], A['LakT'] = gmat(kn, True, 'NT16', 'LakT')
                    _, A['ArbT'], A['ArkT'] = gmat(qn, False, 'ArbT', 'ArkT')
                    if is_s:
                        kb.dma('sp', stGs.t[:], stGdn[:, h].rearrange("s k v -> k s v"), writes=[stGs])
                        stb = stGs; stap = lambda s: stGs.t[:, s, :]
                    else:
                        stb = stG[h]; stap = lambda s, h=h: stG[h].t[:, :]
                    ypb = ps[6 + slot]
                    yield from scan2(slot, 1, 128, 128, N, C, QsT, RsT, PbT, KbT, cv_, A, eg, lambda c: eg.t[:, (c + 1) * C - 1:(c + 1) * C], stG[h], ypb)
                    osrc = ypb.t[:, 0:N2].rearrange("p (s two) -> p s two", two=2)[:, :, 0] if is_s else ypb.t[:, :N]
                    o_ = T(); act(o_.t[:, :N], osrc, AF.Copy, [ypb], [o_])
                sq = HBW(slot); act(sq.t[:, :N], o_.t[:, :N], AF.Square, [o_], [sq])
                yield
                pb = PSs(); mm(pb, pb.t[:, :N], ones16, sq.t[:, :N], [on16, sq], True, True)
                yield
                rs = T(); rsqrt(rs.t[:, :N], pb.t[:, :N], 1.0 / 128, 1e-6, [pb], [rs])
                yield
                on = T(); stt(on.t[:, :N], o_.t[:, :N], c128b.t[:, GDNN:GDNN + 1], rs.t[:, :N], ALU.mult, ALU.mult, [o_, c128b, rs], [on])
                yield
                tt(ob.t[:, h, :N], on.t[:, :N], zz.t[:, :N], ALU.mult, [on, zz], [ob])
                yield
                if last:
                    dstG = oGdnS if is_s else oGdnP
                    if is_s:
                        kb.dma('sp', dstG[:, h].rearrange("s k v -> k s v"), stGs.t[:], reads=[stGs], is_out=True)
                    else:
                        kb.dma('sp', dstG[0, h], stG[h].t[:, :], reads=[stG[h]], is_out=True)

            order = [x_ for x_ in [('r', 0), ('g', 0), ('r', 1), ('g', 1), ('r', 2), ('g', 2), ('r', 3), ('g', 3)] if x_[0] in ONLY[0]]
            mk = lambda kind, hh: (lambda sl_: (rwkv_gen(hh, sl_) if kind == 'r' else gdn_gen(hh, sl_)))
            queue = [mk(k_, h_) for (k_, h_) in order]
            if is_s or NSLOT[0] == 1:
                for f_ in queue:
                    for _ in f_(0):
                        pass
            else:
                active = [None, None]
                rounds = 0
                qs = [[mk(k_, h_) for (k_, h_) in order if k_ == 'r'], [mk(k_, h_) for (k_, h_) in order if k_ == 'g']]
                while qs[0] or qs[1] or any(a_ is not None for a_ in active):
                    rounds += 1
                    for sl_ in (0, 1):
                        if active[sl_] is None and qs[sl_] and not (sl_ == 1 and rounds < OFFS[0]):
                            active[sl_] = qs[sl_].pop(0)(sl_)
                            next(active[sl_])
                            for _ in range(2):
                                if pend:
                                    pend.pop(0)()
                        if active[sl_] is not None:
                            try:
                                next(active[sl_])
                            except StopIteration:
                                active[sl_] = None
            while pend:
                pend.pop(0)()
            if is_s:
                dbgdump('oa', oa.t[:, :, 0:NS], [oa]); dbgdump('ob', ob.t[:, :, 0:NS], [ob])
            CKP(pfx + 'gdn')
            if nxt is not None:
                load_inputs(1 - cur, nxt[0], nxt[1])
                preloaded[nxt] = True
            for cb in range(4):
                wga = wloadc('gate', 0, 8, 128, cb * 256, 256)
                gas = []
                for m2 in range(2):
                    pga = PS(); proj_fm(pga, pga.t[:, :N], wga, m2 * 128, 128, N)
                    ga = HB(); act(ga.t[:, :N], pga.t[:, :N], AF.Sigmoid, [pga], [ga])
                    gas.append(ga)
                wgb = wloadc('gate', 0, 8, 128, 1024 + cb * 256, 256)
                gbs = []
                for m2 in range(2):
                    pgb = PS(); proj_fm(pgb, pgb.t[:, :N], wgb, m2 * 128, 128, N)
                    gb = HB(); act(gb.t[:, :N], pgb.t[:, :N], AF.Sigmoid, [pgb], [gb])
                    gbs.append(gb)
                wa = wloadc('bra', 0, 4, 128, cb * 256, 256)
                for m2 in range(2):
                    pa_ = PS(); proj_fm(pa_, pa_.t[:, :N], wa, m2 * 128, 128, N, nk=4, rhsb=oa)
                    tt(gas[m2].t[:, :N], gas[m2].t[:, :N], pa_.t[:, :N], ALU.mult, [gas[m2], pa_], [gas[m2]])
                wb = wloadc('brb', 0, 4, 128, cb * 256, 256)
                for m2 in range(2):
                    m = cb * 2 + m2
                    pbb = PS(); proj_fm(pbb, pbb.t[:, :N], wb, m2 * 128, 128, N, nk=4, rhsb=ob)
                    tt(gbs[m2].t[:, :N], gbs[m2].t[:, :N], pbb.t[:, :N], ALU.mult, [gbs[m2], pbb], [gbs[m2]])
                    tt(mixT.t[:, m, :N], gas[m2].t[:, :N], gbs[m2].t[:, :N], ALU.add, [gas[m2], gbs[m2]], [mixT])

            def tok_out(wname, nkc_list, lhsb, epilogue):
                for cb in range(4):
                    pbs = [PS() for _ in rows_list]
                    k0 = 0
                    tot = sum(nkc_list)
                    for nk in nkc_list:
                        wt = wloadc(wname, k0 * 128, nk, 128, cb * 256, 256)
                        for tc, rows in enumerate(rows_list):
                            for k in range(nk):
                                kk_ = k0 + k
                                mm(pbs[tc], pbs[tc].t[0:rows, 0:256], lhsb.t[:, kk_, tc * 128:tc * 128 + rows], wt.t[:, k, :], [lhsb, wt], kk_ == 0, kk_ == tot - 1)
                        k0 += nk
                    for tc, rows in enumerate(rows_list):
                        epilogue(tc, rows, cb, pbs[tc])

            def resid_add(tc, rows, cb, pb):
                sl = slice(cb * 256, (cb + 1) * 256)
                tt(xtok.t[0:rows, tc, sl], xtok.t[0:rows, tc, sl], pb.t[0:rows, 0:256], ALU.add, [xtok, pb], [xtok])

            if is_s:
                dbgdump('mix', mixT.t[:, :, 0:NS], [mixT])
            tok_out('out', [8], mixT, resid_add)
            if is_s:
                dbgdump('h1', xtok.t[0:NS, 0, :], [xtok])
            CKP(pfx + 'merge')
            norm_T(N, rows_list, GFFN, xtok)
            for cb in range(11):
                wg = wloadc('fg', 0, 8, 128, cb * 256, 256)
                wu = wloadc('fu', 0, 8, 128, cb * 256, 256)
                for m2 in range(2):
                    pg_ = PS(); proj_fm(pg_, pg_.t[:, :N], wg, m2 * 128, 128, N)
                    pu_ = PS(); proj_fm(pu_, pu_.t[:, :N], wu, m2 * 128, 128, N)
                    sg = HB(); act(sg.t[:, :N], pg_.t[:, :N], AF.Silu, [pg_], [sg])
                    tt(hfT.t[:, cb * 2 + m2, :N], sg.t[:, :N], pu_.t[:, :N], ALU.mult, [sg, pu_], [hfT])
            tok_out('fd', [8, 8, 6], hfT, resid_add)
            if is_s:
                dbgdump('h2', xtok.t[0:NS, 0, :], [xtok])
            CKP(pfx + 'ffn')
            norm_T(N, rows_list, GPLE, xtok)
            for tc, rows in enumerate(rows_list):
                for k in range(2):
                    pb = PS()
                    tr(pb, pb.t[:, 0:rows], ptok.t[0:rows, tc, k * 128:(k + 1) * 128], rows, [ptok])
                    act(peT.t[:, k, tc * 128:tc * 128 + rows], pb.t[:, 0:rows], AF.Copy, [pb], [peT])
            for cb in range(4):
                wg = wloadc('pg', 0, 8, 128, cb * 256, 256)
                wp = wloadc('pp', 0, 2, 128, cb * 256, 256)
                for tc, rows in enumerate(rows_list):
                    pg_ = PS(); pp_ = PS()
                    for k in range(8):
                        mm(pg_, pg_.t[0:rows, 0:256], uT.t[:, k, tc * 128:tc * 128 + rows], wg.t[:, k, :], [uT, wg], k == 0, k == 7)
                    for k in range(2):
                        mm(pp_, pp_.t[0:rows, 0:256], peT.t[:, k, tc * 128:tc * 128 + rows], wp.t[:, k, :], [peT, wp], k == 0, k == 1)
                    sg = HB(); act(sg.t[0:rows, 0:256], pg_.t[0:rows, 0:256], AF.Sigmoid, [pg_], [sg])
                    tt(sg.t[0:rows, 0:256], sg.t[0:rows, 0:256], pp_.t[0:rows, 0:256], ALU.mult, [sg, pp_], [sg])
                    sl = slice(cb * 256, (cb + 1) * 256)
                    tt(xtok.t[0:rows, tc, sl], xtok.t[0:rows, tc, sl], sg.t[0:rows, 0:256], ALU.add, [xtok, sg], [xtok])
            if is_s:
                dbgdump('h3', xtok.t[0:NS, 0, :], [xtok])
            CKP(pfx + 'ple')
            for tc, rows in enumerate(rows_list):
                act(xs.t[:rows, :], xtok.t[:rows, tc, :], AF.Square, [xtok], [xs, ss], accum_out=ss.t[:rows, tc:tc + 1])
            mr = max(rows_list); TC = len(rows_list)
            rsqrt(ss.t[:mr, 4:4 + TC], ss.t[:mr, 0:TC], 1.0 / D, 1e-6, [ss], [ss])
            for tc, rows in enumerate(rows_list):
                stt(xs.t[:rows, :], xtok.t[:rows, tc, :], ss.t[:rows, 4 + tc:5 + tc], gfb.t[:rows, :], ALU.mult, ALU.mult, [xtok, ss, gfb], [xs])
                kb.dma('sp', ydst[t0 + tc * 128:t0 + tc * 128 + rows, :], xs.t[:rows, :], reads=[xs], is_out=True)

        try:
          for ti in range(n_ptiles):
            do_tile(False, ti * NP, nxt=((False, (ti + 1) * NP) if ti + 1 < n_ptiles else (True, 0)))
          CKP('ptiles')
          pb = PS()
          tr(pb, pb.t[0:12, 0:128], car_rkv.t[:, 0:12], 128, [car_rkv])
          o1 = HB(); act(o1.t[0:12, 0:128], pb.t[0:12, 0:128], AF.Copy, [pb], [o1])
          kb.dma('sp', oShiftP[0, 0:1536].rearrange("(g p) -> g p", p=128), o1.t[0:12, 0:128], reads=[o1], is_out=True)
          pb = PS()
          tr(pb, pb.t[0:2, 0:64], car_xw.t[:, 0:2], 64, [car_xw])
          o2 = HB(); act(o2.t[0:2, 0:64], pb.t[0:2, 0:64], AF.Copy, [pb], [o2])
          kb.dma('sp', oShiftP[0, 1536:1664].rearrange("(g p) -> g p", p=64), o2.t[0:2, 0:64], reads=[o2], is_out=True)
          pb = PS()
          tr(pb, pb.t[0:2, 0:128], car_xg.t[:, 0:2], 128, [car_xg])
          o3 = HB(); act(o3.t[0:2, 0:128], pb.t[0:2, 0:128], AF.Copy, [pb], [o3])
          kb.dma('sp', oShiftP[0:1, 1664:1792], o3.t[0:1, 0:128], reads=[o3], is_out=True)
          for g3 in range(3):
              pb = PS()
              for gg in range(4):
                  g = g3 * 4 + gg
                  tr(pb, pb.t[0:4, gg * 128:(gg + 1) * 128], car_cv.t[:, g, :], 128, [car_cv], inc=(gg == 3))
              cst = NB(('Pb', 'Kb', 'Vt')[g3], 512, slot=0)
              act(cst.t[0:4, 0:512], pb.t[0:4, 0:512], AF.Copy, [pb], [cst])
              kb.dma('sp', oConvP[0, :, g3 * 512:(g3 + 1) * 512], cst.t[0:3, 0:512], reads=[cst], is_out=True)
          CKP('pouts')
          kb.barrier()
          do_tile(True, 0)
          CKP('stile')
          for jb in range(4):
              pb = PS()
              if jb < 3:
                  for h_ in range(4):
                      tr(pb, pb.t[0:NS, h_ * 128:(h_ + 1) * 128], rawS.t[:, jb * 4 + h_, :], 128, [rawS], inc=(h_ == 3))
                  act(tokS.t[0:NS, jb * 512:(jb + 1) * 512], pb.t[0:NS, 0:512], AF.Copy, [pb], [tokS])
              else:
                  tr(pb, pb.t[0:NS, 0:64], rawS.t[0:64, 12, :], 64, [rawS], inc=False)
                  tr(pb, pb.t[0:NS, 64:128], rawS.t[0:64, 13, :], 64, [rawS], inc=False)
                  tr(pb, pb.t[0:NS, 128:256], rawS.t[0:128, 14, :], 128, [rawS])
                  act(tokS.t[0:NS, 1536:1792], pb.t[0:NS, 0:256], AF.Copy, [pb], [tokS])
          kb.dma('sp', oShiftS[:, :], tokS.t[0:NS, :], reads=[tokS], is_out=True)
          kb.dma('sp', oConvS[:, 0:2, :], stConv[:, 1:3, :], is_out=True)
          for g3 in range(3):
              pb = PS()
              for gg in range(4):
                  g = g3 * 4 + gg
                  tr(pb, pb.t[0:NS, gg * 128:(gg + 1) * 128], rawC.t[:, g, :], 128, [rawC], inc=(gg == 3))
              act(tokC.t[0:NS, g3 * 512:(g3 + 1) * 512], pb.t[0:NS, 0:512], AF.Copy, [pb], [tokC])
          kb.dma('sp', oConvS[:, 2, :], tokC.t[0:NS, 0:1536], reads=[tokC], is_out=True)
        except _Stop:
            pass
        kb._wait('sp', kb.out_events)
    return nc


def host_consts():
    cm = np.zeros((128, 1024), np.float32)
    cm[:, 0:128] = np.eye(128, dtype=np.float32)
    j = np.arange(64)[:, None]; i = np.arange(64)[None, :]
    for h0 in (0, 64):
        cm[h0:h0 + 64, 128:192] = (j < i)
        cm[h0:h0 + 64, 192:256] = (j <= i)
        cm[h0:h0 + 64, 960:1024] = (j > i)
    cm[0:64, 256:320] = np.where(j <= i, 0.0, -30000.0)
    cm[:, 320:448] = 1.0
    sm = np.ones(512, np.float32); sm[::64] = 0.0
    cm[:, 448:960] = sm[None, :]
    return cm


def prep_weights(inp):
    w_in = inp['w_in'][0]
    perm = []
    for hp in range(4):
        for j in range(3):
            perm.extend(range(j * 512 + hp * 128, j * 512 + hp * 128 + 128))
    perm.extend(range(1536, 1792))
    for h in range(4):
        for j in range(3):
            perm.extend(range(1792 + j * 512 + h * 128, 1792 + j * 512 + h * 128 + 128))
        perm.extend(range(3328 + h * 128, 3328 + h * 128 + 128))
    perm.extend(range(3840, 5896))
    w_in_p = np.ascontiguousarray(w_in[:, perm])
    w_barep = np.ascontiguousarray(np.repeat(w_in[:, 3840:3848], 128, axis=1))
    mu = inp['mu_shift'][0]
    c64 = np.zeros((64, 80), np.float32)
    for h in range(8):
        for j in range(3):
            c64[:, h * 3 + j] = mu[j * 512 + h * 64: j * 512 + h * 64 + 64]
    c64[:, 24] = mu[1536:1600]; c64[:, 25] = mu[1600:1664]
    def hcol(v):
        return np.ascontiguousarray(v.reshape(8, 64).T)
    c64[:, 26:34] = hcol(inp['rw_w0'][0]); c64[:, 34:42] = hcol(inp['rw_a0'][0])
    c64[:, 42:50] = hcol(inp['rw_kk'][0]); c64[:, 50:58] = hcol(inp['rw_ka'][0])
    c64[:, 58:66] = hcol(inp['rw_rk'][0].reshape(-1)); c64[:, 66:74] = hcol(inp['rw_ln_w'][0])
    c128 = np.zeros((128, 64), np.float32)
    c128[:, 0] = mu[1664:1792]
    c128[:, 1:9] = inp['norm_mix'][0].reshape(8, 128).T
    c128[:, 9:17] = inp['norm_ffn'][0].reshape(8, 128).T
    c128[:, 17:25] = inp['norm_ple'][0].reshape(8, 128).T
    c128[:, 25] = inp['gdn_norm'][0]
    c128[:, 26:30] = inp['gdn_a_log'][0][None, :]
    c128[:, 30:34] = inp['gdn_dt_bias'][0][None, :]
    c128[0:64, 40:48] = hcol(inp['rw_ln_b'][0])
    c128r = np.zeros((128, 64), np.float32)
    for hp in range(4):
        for j in range(3):
            c128r[:, hp * 3 + j] = mu[j * 512 + hp * 128: j * 512 + hp * 128 + 128]
    pcol = lambda v: np.ascontiguousarray(np.asarray(v).reshape(4, 128).T)
    c128r[:, 12:16] = pcol(inp['rw_w0'][0]); c128r[:, 16:20] = pcol(inp['rw_a0'][0])
    c128r[:, 20:24] = pcol(inp['rw_kk'][0]); c128r[:, 24:28] = pcol(inp['rw_ka'][0])
    c128r[:, 28:32] = pcol(inp['rw_rk'][0].reshape(-1)); c128r[:, 32:36] = pcol(inp['rw_ln_w'][0]); c128r[:, 36:40] = pcol(inp['rw_ln_b'][0])
    cmat2 = np.zeros((128, 192), np.float32)
    cmat2[0:64, 0:64] = 1.0; cmat2[64:128, 64:128] = 1.0
    cmat2[0:64, 128:192] = np.eye(64); cmat2[64:128, 128:192] = np.eye(64)
    cv = inp['gdn_conv'][0]
    convw = np.zeros((128, 48), np.float32)
    for tap in range(4):
        convw[:, tap * 12:(tap + 1) * 12] = cv[tap].reshape(12, 128).T
    return dict(
        w_in=w_in_p, w_barep=w_barep, c64=c64, c128=c128, convw=convw,
        w2=np.ascontiguousarray(inp['rw_w2'][0]), a2=np.ascontiguousarray(inp['rw_a2'][0]), g2=np.ascontiguousarray(inp['rw_g2'][0]),
        w_bra=np.ascontiguousarray(inp['w_branch_a'][0]), w_brb=np.ascontiguousarray(inp['w_branch_b'][0]),
        w_out=np.ascontiguousarray(inp['w_out'][0]),
        w_fg=np.ascontiguousarray(inp['w_ffn_gate'][0]), w_fu=np.ascontiguousarray(inp['w_ffn_up'][0]),
        w_fd=np.ascontiguousarray(inp['w_ffn_down'][0]),
        w_pg=np.ascontiguousarray(inp['w_ple_gate'][0]), w_pp=np.ascontiguousarray(inp['w_ple_proj'][0]),
        gfin=np.ascontiguousarray(np.broadcast_to(inp['norm_final'][None, :], (128, D))),
        cmat=host_consts(), cmat2=cmat2, c128r=c128r,
    )


def make_in_maps(inp, n_cores, TP):
    shared = prep_weights(inp)
    maps = []
    for c in range(n_cores):
        m = dict(shared)
        m['xP'] = np.ascontiguousarray(inp['x_prompt'][c, :TP])
        m['pP'] = np.ascontiguousarray(inp['p_prompt'][0, c, :TP])
        sl = slice(c * NS, (c + 1) * NS)
        m['xS'] = np.ascontiguousarray(inp['x_sample'][sl, 0])
        m['pS'] = np.ascontiguousarray(inp['p_sample'][0, sl, 0])
        m['stShift'] = np.ascontiguousarray(inp['state_shift'][0, sl, 0])
        m['stWkv'] = np.ascontiguousarray(inp['state_wkv'][0, sl])
        m['stConv'] = np.ascontiguousarray(inp['state_conv'][0, sl])
        m['stGdn'] = np.ascontiguousarray(inp['state_gdn'][0, sl])
        maps.append(m)
    return maps


def gather(results, n_cores):
    cat = lambda k: np.concatenate([r[k] for r in results], axis=0)
    yP = np.stack([r['yP'] for r in results], axis=0)
    yS = cat('yS')[:, None, :]
    return (yP, yS,
            cat('oShiftP')[None, :, None, :], cat('oWkvP')[None], cat('oConvP')[None], cat('oGdnP')[None],
            cat('oShiftS')[None, :, None, :], cat('oWkvS')[None], cat('oConvS')[None], cat('oGdnS')[None])


def kernel(**inputs):
    inp = {k: np.asarray(v) for k, v in inputs.items()}
    n = 8
    TP = inp['x_prompt'].shape[1]
    nc = build(TP // NP)
    maps = make_in_maps(inp, n, TP)
    res = run_bass_kernel_spmd(nc, maps, core_ids=list(range(n)))
    outs = gather(res.results, n)
    return tuple(np.ascontiguousarray(o, dtype=np.float32) for o in outs)
```
